# Optimizing a Trainium2 kernel written in Bass

```python
import jax, jax.numpy as jnp
from jax import lax
import numpy as np

D_MODEL = 1024
BATCH = 32
SEQ = 256
DEPTH = 1
DEC_BATCH = 8
DEC_SEQ = 4096
PAST_LEN = 256

GRID_W = 64
MIX_WIDTH = D_MODEL
A_WIDTH = MIX_WIDTH // 2
B_WIDTH = MIX_WIDTH - A_WIDTH
HEAD_DIM = 64
A_HEADS = A_WIDTH // HEAD_DIM
DECAY_LORA = 64
ICLR_LORA = 64
IN_WIDTH = 4 * A_WIDTH + 4 * B_WIDTH
IN_SPLITS = tuple(int(s) for s in np.cumsum([A_WIDTH] * 4 + [B_WIDTH] * 3))
NORM_EPS = 1e-6
GN_EPS = 64e-5

kernel_name = "bidir_rwkv7_shortconv_hybrid_dit_step"


def _rmsnorm(x, g):
    xf = x.astype(jnp.float32)
    y = xf * lax.rsqrt(jnp.mean(xf * xf, axis=-1, keepdims=True) + NORM_EPS)
    return (y * g.astype(jnp.float32)).astype(x.dtype)


def _centred_taps(p, axis):
    n = p.shape[axis]
    pad = [(0, 0)] * p.ndim
    pad[axis] = (1, 1)
    pp = jnp.pad(p, pad)
    return (lax.slice_in_dim(pp, 0, n, axis=axis), lax.slice_in_dim(pp, 2, n + 2, axis=axis))


def _token_shift(p, mu, is_latent):
    b, t, ch = p.shape
    if is_latent:
        q = p.reshape(b, t // GRID_W, GRID_W, ch)
        axis = 2
    else:
        q = p
        axis = 1
    prev, nxt = _centred_taps(q, axis)
    return (q + mu * (0.5 * (prev + nxt) - q)).reshape(b, t, ch)


def _short_conv(u, w, is_latent):
    if not is_latent:
        prev, nxt = _centred_taps(u, 1)
        return w[0] * prev + w[1] * u + w[2] * nxt
    b, t, ch = u.shape
    half = ch // 2
    g = u.reshape(b, t // GRID_W, GRID_W, ch)
    ph, nh = _centred_taps(g[..., :half], 2)
    pv, nv = _centred_taps(g[..., half:], 1)
    prev = jnp.concatenate([ph, pv], axis=-1)
    nxt = jnp.concatenate([nh, nv], axis=-1)
    return (w[0] * prev + w[1] * g + w[2] * nxt).reshape(b, t, ch)


def _heads(z):
    return z.reshape(z.shape[:-1] + (A_HEADS, HEAD_DIM))


def _wkv(h, r, k, v, s0, lp):
    f32 = jnp.float32
    b, t, _ = h.shape
    hf = h.astype(f32)
    lw = jnp.tanh(jnp.einsum('btd,edl->betl', hf, lp['decay_down'].astype(f32)))
    w = lp['decay_w0'].astype(f32)[None, :, None, :] + jnp.einsum('betl,elc->betc', lw, lp['decay_up'].astype(f32))
    decay = jnp.exp(-jnp.exp(-jax.nn.softplus(-w) - 0.5))
    la = jnp.einsum('btd,edl->betl', hf, lp['iclr_down'].astype(f32))
    a = jax.nn.sigmoid(lp['iclr_bias'].astype(f32)[None, :, None, :]
                       + jnp.einsum('betl,elc->betc', la, lp['iclr_up'].astype(f32)))
    rf, kf, vf = r.astype(f32), k.astype(f32), v.astype(f32)
    kk = _heads(kf * lp['kk_scale'].astype(f32))
    kk = kk * lax.rsqrt(jnp.maximum(jnp.sum(kk * kk, axis=-1, keepdims=True), 1e-24))
    kd = kf[:, None] * (1.0 + (a - 1.0) * lp['ka_scale'].astype(f32))
    rh, vh, kdh, ah, dh = _heads(rf), _heads(vf), _heads(kd), _heads(a), _heads(decay)

    def both(z):
        return jnp.stack([z, z], axis=1)

    def orient(z):
        return jnp.moveaxis(jnp.stack([z[:, 0], jnp.flip(z[:, 1], axis=1)], axis=1), 2, 0)

    xs = (orient(both(rh)), orient(dh), orient(kdh), orient(both(vh)), orient(both(kk)), orient(ah))

    def step(S, inp):
        r_t, w_t, k_t, v_t, kk_t, a_t = inp
        s_kk = jnp.einsum('bdhij,bdhj->bdhi', S, kk_t)
        S = (S * w_t[..., None, :] - s_kk[..., :, None] * (kk_t * a_t)[..., None, :]
             + v_t[..., :, None] * k_t[..., None, :])
        return S, jnp.einsum('bdhij,bdhj->bdhi', S, r_t)

    s_final, ys = lax.scan(step, s0.astype(f32), xs)
    ys = jnp.moveaxis(ys, 0, 2)
    y = ys[:, 0] + jnp.flip(ys[:, 1], axis=1)
    mean = jnp.mean(y, axis=-1, keepdims=True)
    var = jnp.mean(jnp.square(y - mean), axis=-1, keepdims=True)
    y = ((y - mean) * lax.rsqrt(var + GN_EPS)).reshape(b, t, A_WIDTH)
    y = y * lp['gn_w'].astype(f32) + lp['gn_b'].astype(f32)
    bonus = jnp.sum(rh[:, None] * kdh * _heads(lp['bonus_rk'].astype(f32)), axis=(1, -1))
    y = y + (bonus[..., None] * vh).reshape(b, t, A_WIDTH)
    return y, s_final


def _layer(x, mod, s0, is_latent, lp):
    shift, scale, gate = jnp.split(mod, 3, axis=-1)
    h = _rmsnorm(x, lp['norm_g']) * (1.0 + scale) + shift
    proj = h @ lp['w_in']
    r, k, v, g_a, b_gate, c_gate, u_conv, g_b = jnp.split(proj, IN_SPLITS, axis=-1)
    rkv = _token_shift(jnp.concatenate([r, k, v], axis=-1), lp['shift_mu'].reshape(-1), is_latent)
    r, k, v = jnp.split(rkv, 3, axis=-1)
    ya, s_final = _wkv(h, r, k, v, s0, lp)
    ya = ya.astype(x.dtype) * jax.nn.silu(g_a)
    yb = b_gate * _short_conv(c_gate * u_conv, lp['conv_w'], is_latent) * jax.nn.silu(g_b)
    u = jnp.concatenate([ya, yb], axis=-1) @ lp['w_out']
    return x + gate * u, s_final


def setup_inputs(seed: int = 0) -> dict:
    key = jax.random.key(seed)
    ks = jax.random.split(key, 32)
    f32 = jnp.float32
    L = DEPTH

    def nrm(k, shape, s):
        return jax.random.normal(k, shape, f32) * s

    return {
        "x_prompt": nrm(ks[0], (BATCH, SEQ, D_MODEL), 1.0),
        "x_sample": nrm(ks[1], (DEC_BATCH, DEC_SEQ, D_MODEL), 1.0),
        "c": nrm(ks[2], (DEC_BATCH, D_MODEL), 1.0),
        "state_wkv": nrm(ks[3], (DEC_BATCH, DEPTH, 2, A_HEADS, HEAD_DIM, HEAD_DIM), 0.5),
        "c_ctx": nrm(ks[4], (D_MODEL,), 1.0),
        "w_ada": nrm(ks[5], (L, D_MODEL, 3 * D_MODEL), 0.5 * D_MODEL ** -0.5),
        "b_ada": nrm(ks[6], (L, 3 * D_MODEL), 0.02),
        "norm_g": 1.0 + nrm(ks[7], (L, D_MODEL), 0.02),
        "w_in": nrm(ks[8], (L, D_MODEL, IN_WIDTH), D_MODEL ** -0.5),
        "shift_mu": jax.random.uniform(ks[9], (L, 3, A_WIDTH), f32),
        "decay_w0": jax.random.uniform(ks[10], (L, 2, A_WIDTH), f32, -4.0, 1.0),
        "decay_down": nrm(ks[11], (L, 2, D_MODEL, DECAY_LORA), D_MODEL ** -0.5),
        "decay_up": nrm(ks[12], (L, 2, DECAY_LORA, A_WIDTH), 0.5 * DECAY_LORA ** -0.5),
        "iclr_bias": nrm(ks[13], (L, 2, A_WIDTH), 0.5),
        "iclr_down": nrm(ks[14], (L, 2, D_MODEL, ICLR_LORA), D_MODEL ** -0.5),
        "iclr_up": nrm(ks[15], (L, 2, ICLR_LORA, A_WIDTH), 0.5 * ICLR_LORA ** -0.5),
        "kk_scale": 0.85 + nrm(ks[16], (L, A_WIDTH), 0.05),
        "ka_scale": 1.0 + nrm(ks[17], (L, A_WIDTH), 0.05),
        "bonus_rk": nrm(ks[18], (L, A_WIDTH), 0.1),
        "gn_w": 1.0 + nrm(ks[19], (L, A_WIDTH), 0.02),
        "gn_b": nrm(ks[20], (L, A_WIDTH), 0.02),
        "conv_w": nrm(ks[21], (L, 3, B_WIDTH), 3.0 ** -0.5),
        "w_out": nrm(ks[22], (L, MIX_WIDTH, D_MODEL), MIX_WIDTH ** -0.5),
        "final_g": 1.0 + nrm(ks[23], (D_MODEL,), 0.02),
    }


def reference(x_prompt, x_sample, c, state_wkv, c_ctx, w_ada, b_ada, norm_g, w_in, shift_mu,
              decay_w0, decay_down, decay_up, iclr_bias, iclr_down, iclr_up, kk_scale, ka_scale,
              bonus_rk, gn_w, gn_b, conv_w, w_out, final_g):
    ctx = x_prompt
    lat = x_sample
    ctx_states = []
    for l in range(DEPTH):
        lp = dict(norm_g=norm_g[l], w_in=w_in[l], shift_mu=shift_mu[l], decay_w0=decay_w0[l],
                  decay_down=decay_down[l], decay_up=decay_up[l], iclr_bias=iclr_bias[l],
                  iclr_down=iclr_down[l], iclr_up=iclr_up[l], kk_scale=kk_scale[l],
                  ka_scale=ka_scale[l], bonus_rk=bonus_rk[l], gn_w=gn_w[l], gn_b=gn_b[l],
                  conv_w=conv_w[l], w_out=w_out[l])
        mod_ctx = (jax.nn.silu(c_ctx) @ w_ada[l] + b_ada[l])[None, None, :]
        mod_lat = (jax.nn.silu(c) @ w_ada[l] + b_ada[l])[:, None, :]
        s_zero = jnp.zeros((ctx.shape[0], 2, A_HEADS, HEAD_DIM, HEAD_DIM), jnp.float32)
        ctx, s_ctx = _layer(ctx, mod_ctx, s_zero, False, lp)
        ctx_states.append(s_ctx.astype(x_prompt.dtype))
        lat, _ = _layer(lat, mod_lat, state_wkv[:, l], True, lp)
    y_prompt = _rmsnorm(ctx, final_g)
    y_sample = _rmsnorm(lat, final_g)
    new_state_wkv = jnp.stack(ctx_states, axis=1)
    return (y_prompt, y_sample, new_state_wkv)
```

```python
import os
import numpy as np
import concourse.bass as bass
import concourse.mybir as mybir
from concourse.bass_utils import run_bass_kernel_spmd
from contextlib import ExitStack

F32, BF16, I32 = mybir.dt.float32, mybir.dt.bfloat16, mybir.dt.int32
ALU = mybir.AluOpType
AF = mybir.ActivationFunctionType
C0 = 0.6065306597126334
NORM_EPS = 1e-6
GN_EPS = 64e-5
NDS = 24
BG_INJECT = os.environ.get("BG_INJECT", "1") == "1"
PUMP_EVERY = int(os.environ.get("PUMP_EVERY", "2"))
KCUT = int(os.environ.get("KCUT", "5"))
RAW_ONLY = os.environ.get("RAW_ONLY", "0") == "1"
SELF_SKIP = tuple(os.environ.get("SELF_SKIP", "pe").split(","))


class Buf:
    __slots__ = ("w", "r")

    def __init__(self):
        self.w = None
        self.r = {}


class T:
    def __init__(self, t):
        self.t = t
        self.b = Buf()

    def __getitem__(self, idx):
        return self.t[idx]


def _b(x):
    return x.b if hasattr(x, "b") else x


class Sched:
    def __init__(self, nc, es):
        self.nc = nc
        self.eng = {"pe": nc.tensor, "dve": nc.vector, "act": nc.scalar, "pool": nc.gpsimd, "sp": nc.sync}
        self.sem = {k: es.enter_context(nc.semaphore("s_" + k)) for k in self.eng}
        self.cnt = {k: 0 for k in self.eng}
        self.dsem = [es.enter_context(nc.semaphore("d%d" % i)) for i in range(NDS)]
        self.dcnt = [0] * NDS
        self.dnext = 0
        self.dnext2 = 0
        self.waited = {}
        self.nwait = 0
        self.off = False
        self.nops = {k: 0 for k in self.eng}
        self.marks = []

    def _semh(self, key):
        return self.sem[key] if isinstance(key, str) else self.dsem[key]

    def _wait(self, e, key, val):
        if val <= 0:
            return
        if e == key and e in SELF_SKIP:
            return
        if self.waited.get((e, key), 0) >= val:
            return
        self.eng[e].wait_ge(self._semh(key), val)
        self.waited[(e, key)] = val
        self.nwait += 1

    def _deps(self, e, reads, writes):
        for b in reads:
            b = _b(b)
            if b.w:
                self._wait(e, *b.w)
        raw_only = RAW_ONLY and e in ("act", "dve")
        for b in writes:
            b = _b(b)
            if b.w and not (raw_only and b.w[0] == e):
                self._wait(e, *b.w)
            for k, v in b.r.items():
                if raw_only and k == e:
                    continue
                self._wait(e, k, v)

    def _mark(self, key, tgt, reads, writes):
        for b in writes:
            b = _b(b)
            b.w = (key, tgt)
            b.r = {}
        for b in reads:
            b = _b(b)
            if b.r.get(key, 0) < tgt:
                b.r[key] = tgt

    def op(self, e, fn, reads=(), writes=(), inc=True):
        if self.off:
            return
        self._deps(e, reads, writes)
        inst = fn(self.eng[e])
        self.nops[e] += 1
        tgt = self.cnt[e] + 1
        if inc:
            inst.then_inc(self.sem[e], 1)
            self.cnt[e] = tgt
        self._mark(e, tgt, reads, writes)

    def dma(self, e, out, in_, reads=(), writes=()):
        if self.off:
            return
        if e == "pool":
            i = NDS - 8 + self.dnext2
            self.dnext2 = (self.dnext2 + 1) % 8
        else:
            i = self.dnext
            self.dnext = (i + 1) % (NDS - 8)
        self._deps(e, reads, writes)
        self._wait(e, i, self.dcnt[i])
        self.eng[e].dma_start(out=out, in_=in_).then_inc(self.dsem[i], 16)
        self.dcnt[i] += 16
        self._mark(i, self.dcnt[i], reads, writes)

    def mark(self, label):
        self.marks.append((label, dict(self.nops)))

    def barrier(self):
        if self.off:
            return
        for e in self.eng:
            for k in self.eng:
                if k != e:
                    self._wait(e, k, self.cnt[k])
            for i in range(NDS):
                self._wait(e, i, self.dcnt[i])

    def finish(self, e="sp"):
        for i in range(NDS):
            self._wait(e, i, self.dcnt[i])


class _Stop(Exception):
    pass


def build_program(NLB, NCS, debug=False, stop=None):
    SS = []

    def ckpt(n):
        SS[0].mark(n)
        if stop is not None and n == stop:
            SS[0].off = True

    NB = NLB + NCS
    NCH = 2 * NB
    NTOK = NB * 256
    nc = bass.Bass("TRN2", target_bir_lowering=False)

    def din(name, shape, dt=F32):
        return nc.dram_tensor(name, list(shape), dt, kind="ExternalInput").ap()

    def dout(name, shape, dt=F32):
        return nc.dram_tensor(name, list(shape), dt, kind="ExternalOutput").ap()

    def dscr(name, shape, dt):
        return nc.dram_tensor(name, list(shape), dt, kind=("ExternalOutput" if debug else "Internal")).ap()

    xs = din("xs", [NTOK, 1024])
    cT = din("cT", [128, 16])
    st0 = din("st0", [64, 16, 64])
    w_ada = din("w_ada", [128, 8, 3072])
    b_ada2 = din("b_ada2", [2, 3072])
    ngc_d = din("ngc", [128, 8])
    fg_bc = din("fg_bc", [128, 1024])
    w_in = din("w_in", [128, 8, 4096])
    lora_dn = din("lora_dn", [128, 8, 256])
    dec_up = din("dec_up", [128, 512])
    icl_up = din("icl_up", [128, 512])
    cols_d = din("cols", [128, 60])
    w_out = din("w_out", [128, 8, 1024])
    ys = dout("ys", [NTOK, 1024])
    so = dout("so", [max(NCS, 1), 64, 16, 64])

    hT_d = dscr("hT_d", [NB, 128, 8 * 256], BF16)
    Yl_d = dscr("Yl_d", [NCH, 128, 512], F32)
    Qh_d = dscr("Qh_d", [NCH, 64, 2048], BF16)
    X_d = dscr("X_d", [NCH, 64, 1024], BF16)
    D_d = dscr("D_d", [NCH, 64, 1024], F32)
    sg_d = dscr("sg_d", [NB, 128, 1024], F32)
    bv_d = dscr("bv_d", [NB, 128, 1024], F32)
    S_d = dscr("S_d", [NCH, 64, 1024], BF16)
    mod_d = dscr("mod_d", [2, 1024], F32)
    mod_b = Buf()
    hT_b = [Buf() for _ in range(NB)]
    Yl_b = [Buf() for _ in range(NCH)]
    Qh_b = [Buf() for _ in range(NCH)]
    X_b = [Buf() for _ in range(NCH)]
    D_b = [Buf() for _ in range(NCH)]
    sg_b = [Buf() for _ in range(NB)]
    bv_b = [Buf() for _ in range(NB)]
    S_b = [[Buf(), Buf()] for _ in range(NCH)]
    ys_b = Buf()
    so_b = Buf()

    with ExitStack() as es:
        S = Sched(nc, es)
        SS.append(S)

        try:
            def sbt(es_, name, shape, dt):
                return T(es_.enter_context(nc.sbuf_tensor("sb_" + name, list(shape), dt)))

            PS = [T(es.enter_context(nc.psum_tensor("ps%d" % i, [128, 512], F32))) for i in range(8)]

            def mm(out, lhsT, rhs, R, W, start=True, stop=True, inc=True):
                S.op("pe", lambda e: e.matmul(out, lhsT=lhsT, rhs=rhs, start=start, stop=stop, skip_group_check=True),
                     reads=R, writes=W, inc=inc)

            def tt(eng, out, a, b, op, R, W):
                S.op(eng, lambda e: e.tensor_tensor(out=out, in0=a, in1=b, op=op), reads=R, writes=W)

            def ts(eng, out, a, s1, s2, op0, op1, R, W):
                if op1 is None:
                    S.op(eng, lambda e: e.tensor_scalar(out=out, in0=a, scalar1=s1, scalar2=None, op0=op0), reads=R, writes=W)
                else:
                    S.op(eng, lambda e: e.tensor_scalar(out=out, in0=a, scalar1=s1, scalar2=s2, op0=op0, op1=op1),
                         reads=R, writes=W)

            def stt(out, a, sc, b, op0, op1, R, W):
                S.op("dve", lambda e: e.scalar_tensor_tensor(out=out, in0=a, scalar=sc, in1=b, op0=op0, op1=op1),
                     reads=R, writes=W)

            def act(out, in_, func, R, W, bias=None, scale=None, accum=None):
                kw = {}
                if bias is not None:
                    kw["bias"] = bias
                if scale is not None:
                    kw["scale"] = scale
                if accum is not None:
                    kw["accum_out"] = accum
                S.op("act", lambda e: e.activation(out=out, in_=in_, func=func, **kw), reads=R, writes=W)

            def sigm(out, in_, R, W, nbias=None):
                act(out, in_, AF.Exp, R, W, scale=-1.0, bias=nbias)
                act(out, out, AF.Ln, W, W, bias=1.0)
                act(out, out, AF.Exp, W, W, scale=-1.0)

            def cp(eng, out, in_, R, W):
                if eng == "act":
                    S.op("act", lambda e: e.copy(out=out, in_=in_), reads=R, writes=W)
                else:
                    S.op(eng, lambda e: e.tensor_copy(out=out, in_=in_), reads=R, writes=W)

            ioi = sbt(es, "ioi", [128, 128], I32)
            iof = sbt(es, "iof", [128, 128], F32)
            identf = sbt(es, "identf", [128, 128], F32)
            identb = sbt(es, "identb", [128, 128], BF16)
            bones = sbt(es, "bones", [128, 128], F32)
            ones = sbt(es, "ones", [128, 128], F32)
            MK = {k: sbt(es, "mk_" + k, [128, 512], BF16) for k in ("LT", "GT", "LE", "GE", "NLT", "NGT")}
            cols = sbt(es, "cols", [128, 60], F32)
            dcols = sbt(es, "dcols", [128, 44], F32)
            w_bf = sbt(es, "w_bf", [128, 8, 2048], BF16)
            WLh = sbt(es, "WLh", [64, NB * 32], F32)

            S.op("pool", lambda e: e.iota(ioi[:], pattern=[[1, 128]], base=0, channel_multiplier=-1), writes=[ioi])
            cp("dve", iof[:], ioi[:], [ioi], [iof])
            ts("dve", identf[:], iof[:], 0.0, None, ALU.is_equal, None, [iof], [identf])
            cp("dve", identb[:], identf[:], [identf], [identb])
            S.op("dve", lambda e: e.memset(ones[:], 1.0), writes=[ones])
            S.op("dve", lambda e: e.memset(bones[:], 0.0), writes=[bones])
            S.op("dve", lambda e: e.memset(bones[0:64, 0:64], 1.0), writes=[bones])
            S.op("dve", lambda e: e.memset(bones[64:128, 64:128], 1.0), writes=[bones])
            for j in range(4):
                sl = slice(j * 128, (j + 1) * 128)
                ts("dve", MK["LT"][:, sl], iof[:], 0.0, None, ALU.is_gt, None, [iof], [MK["LT"]])
                ts("dve", MK["GT"][:, sl], iof[:], 0.0, None, ALU.is_lt, None, [iof], [MK["GT"]])
                ts("dve", MK["LE"][:, sl], iof[:], 0.0, None, ALU.is_ge, None, [iof], [MK["LE"]])
                ts("dve", MK["GE"][:, sl], iof[:], 0.0, None, ALU.is_le, None, [iof], [MK["GE"]])
                ts("dve", MK["NLT"][:, sl], iof[:], 0.0, -1.0, ALU.is_gt, ALU.mult, [iof], [MK["NLT"]])
                ts("dve", MK["NGT"][:, sl], iof[:], 0.0, -1.0, ALU.is_lt, ALU.mult, [iof], [MK["NGT"]])
            S.dma("sp", cols[:], cols_d, writes=[cols])
            ts("dve", dcols[:, 0:12], cols[:, 0:12], -1.0, 1.0, ALU.mult, ALU.add, [cols], [dcols])
            ts("dve", dcols[:, 12:24], cols[:, 0:12], 0.5, None, ALU.mult, None, [cols], [dcols])
            ts("dve", dcols[:, 24:28], cols[:, 32:36], -1.0, 1.0, ALU.mult, ALU.add, [cols], [dcols])
            ts("dve", dcols[:, 28:44], cols[:, 12:28], -1.0, None, ALU.mult, None, [cols], [dcols])
            ckpt(1)

            def col(i):
                return cols[:, i:i + 1]

            def dcol(i):
                return dcols[:, i:i + 1]

            with ExitStack() as e1:
                g1c = [sbt(e1, "g1c%d" % s, [128, 8], F32) for s in range(2)]
                shc = [sbt(e1, "shc%d" % s, [128, 8], F32) for s in range(2)]
                lora_bf = sbt(e1, "lora_bf", [128, 8, 256], BF16)
                dup_bf = sbt(e1, "dup_bf", [128, 512], BF16)
                iup_bf = sbt(e1, "iup_bf", [128, 512], BF16)
                Sf0 = sbt(e1, "Sf0", [64, 16, 64], F32)
                with ExitStack() as e0:
                    stg = [sbt(e0, "stg%d" % i, [128, 3072], F32) for i in range(2)]
                    ngt = sbt(e0, "ngt", [128, 8], F32)
                    cTt = sbt(e0, "cTt", [128, 16], F32)
                    scT = sbt(e0, "scT", [128, 16], F32)
                    modv = sbt(e0, "modv", [2, 3072], F32)
                    bad = sbt(e0, "bad", [2, 3072], F32)
                    st_in = sbt(e0, "st_in", [64, 16, 64], F32)
                    for k in range(8):
                        g = stg[k % 2]
                        S.dma("sp", g[:, 0:2048], w_in[:, k, 0:2048], writes=[g])
                        cp("act" if k % 2 == 0 else "dve", w_bf[:, k, :], g[:, 0:2048], [g], [w_bf])
                    g = stg[0]
                    S.dma("sp", g[:, 0:2048], lora_dn.rearrange("p k n -> p (k n)"), writes=[g])
                    cp("dve", lora_bf[:, :, :].rearrange("p k n -> p (k n)"), g[:, 0:2048], [g], [lora_bf])
                    g = stg[1]
                    S.dma("sp", g[:, 0:512], dec_up, writes=[g])
                    S.dma("sp", g[:, 512:1024], icl_up, writes=[g])
                    cp("dve", dup_bf[:], g[:, 0:512], [g], [dup_bf])
                    cp("dve", iup_bf[:], g[:, 512:1024], [g], [iup_bf])
                    ckpt(2)
                    S.dma("sp", cTt[:], cT, writes=[cTt])
                    act(scT[:], cTt[:], AF.Silu, [cTt], [scT])
                    S.dma("sp", bad[:], b_ada2, writes=[bad])
                    S.dma("sp", ngt[:], ngc_d, writes=[ngt])
                    for k in range(8):
                        g = stg[k % 2]
                        S.dma("sp", g[:], w_ada[:, k, :], writes=[g])
                        for n in range(6):
                            mm(PS[n][0:2, :], scT[:, 2 * k:2 * k + 2], g[:, n * 512:(n + 1) * 512], [scT, g], [PS[n]],
                               start=(k == 0), stop=(k == 7))
                    for n in range(6):
                        tt("dve", modv[:, n * 512:(n + 1) * 512], PS[n][0:2, :], bad[:, n * 512:(n + 1) * 512], ALU.add,
                           [PS[n], bad], [modv])
                    S.dma("sp", mod_d, modv[:, 2048:3072], reads=[modv], writes=[mod_b])
                    for part in range(2):
                        for k in range(8):
                            c0 = (part * 8 + k) * 2
                            mm(PS[0][:, c0:c0 + 2], modv[0:2, part * 1024 + k * 128:part * 1024 + (k + 1) * 128],
                               identf[0:2, 0:2], [modv, identf], [PS[0]], inc=(part == 1 and k == 7))
                    mview = PS[0][:, 0:32].rearrange("p (a k s) -> p a k s", a=2, k=8)
                    for s in range(2):
                        cp("dve", shc[s][:], mview[:, 0, :, s], [PS[0]], [shc[s]])
                        stt(g1c[s][:], mview[:, 1, :, s], 1.0, ngt[:], ALU.add, ALU.mult, [PS[0], ngt], [g1c[s]])
                    ckpt(3)
                    S.dma("sp", st_in[:], st0, writes=[st_in])
                    for hd in range(16):
                        p = PS[hd // 8]
                        mm(p[0:64, (hd % 8) * 64:(hd % 8 + 1) * 64], st_in[:, hd, :], identf[0:64, 0:64], [st_in, identf], [p],
                           inc=(hd % 8 == 7))
                    for hf in range(2):
                        cp("dve", Sf0[:, hf * 8:(hf + 1) * 8, :].rearrange("p a b -> p (a b)"), PS[hf][0:64, :], [PS[hf]], [Sf0])

                S.barrier()
                ckpt(4)
                e1b = ExitStack()
                e1b.__enter__()
                xts = [sbt(e1b, "xt0", [128, 1024], F32)] * 2
                hb = sbt(e1b, "hb", [128, 1024], BF16)
                ss = sbt(e1b, "ss", [128, 4], F32)
                hT = sbt(e1b, "hT", [128, 8, 256], BF16)
                ppL = sbt(e1b, "ppL", [128, 4, 66], F32)
                ppC = sbt(e1b, "ppC", [128, 1, 258], F32)
                nbt = sbt(e1b, "nbt", [128, 256], F32)
                RKV = [[sbt(e1b, "rkv%d_%d" % (q, cb), [128, 256], F32) for cb in range(4)] for q in range(3)]
                vbf = [sbt(e1b, "vbf%d" % cb, [128, 256], BF16) for cb in range(4)]
                sgT = sbt(e1b, "sgT", [128, 4, 256], F32)
                bvT = sbt(e1b, "bvT", [128, 4, 256], F32)
                lwd = sbt(e1b, "lwd", [128, 256], BF16)
                lwi = sbt(e1b, "lwi", [128, 256], BF16)
                TMPC = [{n: sbt(e1b, "tmc%d_%s" % (i, n), [128, 256], F32) for n in ["sq", "kk", "rkd"]} for i in range(2)]
                TMPC[0]["rn"] = TMPC[1]["rn"] = sbt(e1b, "tmc_rn", [128, 256], F32)
                TMP = TMPC[0]
                TMPE = [{n: sbt(e1b, "tm0_%s" % n, [128, 256] if n != "tcol" else [128, 4], F32)
                         for n in ["sig", "pi", "px", "E1", "E2", "E3", "E4", "a", "bq", "kd", "tcol"]}]
                TMPE.append(TMPE[0])
                WLc = [sbt(e1b, "WLc%d" % i, [128, 4], F32) for i in range(2)]
                FM4 = {n: [[[sbt(e1b, "fm_%s%d%d%d" % (n, par, e, cb), [128, 256], BF16) for cb in range(4)] for e in range(2)]
                           for par in range(2)] for n in ("Qt", "KKt", "Kt", "Bt")}
                FMs = {n: [[sbt(e1b, "fm_%s%d%d" % (n, e, cb), [128, 256], BF16) for cb in range(4)] for e in range(2)]
                       for n in ("Kh", "Bh")}

                def FMt(n, par, e, cb):
                    return FMs[n][e][cb] if n in FMs else FM4[n][par][e][cb]

                Khtm = [[[sbt(e1b, "khtm%d%d%d" % (par, ck, e), [128, 512], BF16) for e in range(2)] for ck in range(2)]
                        for par in range(2)]
                Bhtm = [[[sbt(e1b, "bhtm%d%d%d" % (par, ck, e), [128, 512], BF16) for e in range(2)] for ck in range(2)]
                        for par in range(2)]
                Vtm = [[sbt(e1b, "vtm%d%d" % (par, ck), [128, 512], BF16) for ck in range(2)] for par in range(2)]
                XT = [[sbt(e1b, "XT%d%d" % (e, i), [128, 512], F32) for i in range(2)] for e in range(2)]
                XM = [[sbt(e1b, "XM%d%d" % (e, i), [128, 512], F32) for i in range(2)] for e in range(2)]
                AakT = [sbt(e1b, "AakT%d" % e, [128, 512], BF16) for e in range(2)]
                AqbT = [sbt(e1b, "AqbT%d" % e, [128, 512], BF16) for e in range(2)]
                AqkT = [sbt(e1b, "AqkT%d" % e, [128, 512], BF16) for e in range(2)]
                Zb = [sbt(e1b, "Zb%d" % e, [128, 512], F32) for e in range(2)]
                UGn = [sbt(e1b, "UGn%d" % e, [128, 512], BF16) for e in range(2)]
                Ylt = sbt(e1b, "Ylt", [128, 512], F32)
                Qht = sbt(e1b, "Qht", [64, 2048], BF16)
                Xst = sbt(e1b, "Xst", [64, 1024], BF16)
                Dst = sbt(e1b, "Dst", [64, 1024], F32)
                S.op("dve", lambda e: e.memset(ppL[:, :, :].rearrange("p a b -> p (a b)"), 0.0), writes=[ppL])
                S.op("dve", lambda e: e.memset(ppC[:, :, :].rearrange("p a b -> p (a b)"), 0.0), writes=[ppC])

                PB = PS[7]

                def ab_gen(blk):
                    lat = blk < NLB
                    s = 0 if lat else 1
                    tok0 = blk * 256
                    for i in range(2):
                        xt = xts[i]
                        S.dma("sp", xt[:], xs[tok0 + i * 128:tok0 + (i + 1) * 128, :], writes=[xt])
                        act(hb[:], xt[:], AF.Square, [xt], [hb, ss], accum=ss[:, 0:1])
                        ts("dve", ss[:, 1:2], ss[:, 0:1], 1.0 / 1024, NORM_EPS, ALU.mult, ALU.add, [ss], [ss])
                        act(ss[:, 2:3], ss[:, 1:2], AF.Ln, [ss], [ss])
                        act(ss[:, 3:4], ss[:, 2:3], AF.Exp, [ss], [ss], scale=-0.5)
                        yield
                        ts("dve", hb[:], xt[:], ss[:, 3:4], None, ALU.mult, None, [xt, ss], [hb])
                        yield
                        for half in range(2):
                            p = PB
                            for k4 in range(4):
                                k = half * 4 + k4
                                mm(p[:, k4 * 128:(k4 + 1) * 128], hb[:, k * 128:(k + 1) * 128], identb[:], [hb, identb], [p],
                                   inc=(k4 == 3))
                            for k4 in range(4):
                                k = half * 4 + k4
                                dsth = hT[:, k, i * 128:(i + 1) * 128]
                                srcp = p[:, k4 * 128:(k4 + 1) * 128]
                                if k4 % 2 == 0:
                                    act(dsth, srcp, AF.Identity, [p, g1c[s], shc[s]], [hT], scale=g1c[s][:, k:k + 1],
                                        bias=shc[s][:, k:k + 1])
                                else:
                                    ts("dve", dsth, srcp, g1c[s][:, k:k + 1], shc[s][:, k:k + 1], ALU.mult, ALU.add,
                                       [p, g1c[s], shc[s]], [hT])
                            yield
                    S.dma("pool", hT_d[blk], hT[:, :, :].rearrange("p k t -> p (k t)"), reads=[hT], writes=[hT_b[blk]])
                    pp = ppL if lat else ppC
                    R_, W_ = (4, 64) if lat else (1, 256)

                    def v3(ap):
                        return ap.rearrange("p (r w) -> p r w", r=R_)

                    hs = slice(0, 256)
                    for cbg in range(16):
                        p = PB
                        for k in range(8):
                            mm(p[:, hs], w_bf[:, k, cbg * 128:(cbg + 1) * 128], hT[:, k, :], [w_bf, hT], [p],
                               start=(k == 0), stop=(k == 7), inc=(k == 7))
                        q, cb = cbg // 4, cbg % 4
                        if q < 3:
                            cp("act", pp[:, :, 1:W_ + 1], v3(p[:, hs]), [p], [pp])
                            tt("dve", v3(nbt[:]), pp[:, :, 0:W_], pp[:, :, 2:W_ + 2], ALU.add, [pp], [nbt])
                            dst = RKV[q][cb]
                            act(v3(dst[:]), pp[:, :, 1:W_ + 1], AF.Identity, [pp, dcols], [dst], scale=dcol(q * 4 + cb))
                            stt(dst[:], nbt[:], dcol(12 + q * 4 + cb), dst[:], ALU.mult, ALU.add, [nbt, dcols, dst], [dst])
                            if q == 2:
                                cp("pool", vbf[cb][:], dst[:], [dst], [vbf[cb]])
                        else:
                            sigm(sgT[:, cb, :], p[:, hs], [p], [sgT])
                            tt("dve", sgT[:, cb, :], p[:, hs], sgT[:, cb, :], ALU.mult, [p, sgT], [sgT])
                        yield
                    S.dma("pool", sg_d[blk], sgT[:, :, :].rearrange("p a b -> p (a b)"), reads=[sgT], writes=[sg_b[blk]])
                    for mb in range(2):
                        p = PB
                        for k in range(8):
                            mm(p[:, hs], lora_bf[:, k, mb * 128:(mb + 1) * 128], hT[:, k, :], [lora_bf, hT], [p],
                               start=(k == 0), stop=(k == 7), inc=(k == 7))
                        if mb == 0:
                            tq = TMP["sq"]
                            act(tq[:], p[:, hs], AF.Exp, [p], [tq], scale=-2.0)
                            act(tq[:], tq[:], AF.Ln, [tq], [tq], bias=1.0)
                            act(tq[:], tq[:], AF.Exp, [tq], [tq], scale=-1.0)
                            ts("dve", lwd[:], tq[:], 2.0, -1.0, ALU.mult, ALU.add, [tq], [lwd])
                        else:
                            cp("dve", lwi[:], p[:, hs], [p], [lwi])
                        yield

                def cd_gen(blk):
                    par = blk % 2
                    p5 = PB

                    def c_kk(cb):
                        if False:
                            yield
                        k_ = RKV[1][cb]
                        T_ = TMPC[cb % 2]
                        act(T_["sq"][:], k_[:], AF.Square, [k_, cols], [T_["sq"]], scale=col(28 + cb))
                        mm(p5[:, 0:256], bones[:], T_["sq"][:], [bones, T_["sq"]], [p5])
                        ts("dve", T_["rn"][:], p5[:, 0:256], 1e-24, None, ALU.max, None, [p5], [T_["rn"]])
                        act(T_["rn"][:], T_["rn"][:], AF.Ln, [T_["rn"]], [T_["rn"]])
                        act(T_["rn"][:], T_["rn"][:], AF.Exp, [T_["rn"]], [T_["rn"]], scale=-0.5)
                        stt(T_["kk"][:], k_[:], col(28 + cb), T_["rn"][:], ALU.mult, ALU.mult, [k_, cols, T_["rn"]],
                            [T_["kk"]])

                    def c_front(cb, e):
                        TE = TMPE[e]
                        es_ = slice(e * 64, (e + 1) * 64)
                        pz = PB
                        mm(pz[:, 0:256], dup_bf[es_, cb * 128:(cb + 1) * 128], lwd[es_, :], [dup_bf, lwd], [pz])
                        mm(pz[:, 256:512], iup_bf[es_, cb * 128:(cb + 1) * 128], lwi[es_, :], [iup_bf, lwi], [pz])
                        sig, pi, px, tcl = TE["sig"], TE["pi"], TE["px"], TE["tcol"]
                        act(sig[:], pz[:, 0:256], AF.Exp, [pz, dcols], [sig], scale=-1.0, bias=dcol(28 + e * 4 + cb))
                        act(TE["a"][:], pz[:, 256:512], AF.Exp, [pz, dcols], [TE["a"]], scale=-1.0, bias=dcol(36 + e * 4 + cb))
                        act(sig[:], sig[:], AF.Ln, [sig], [sig], bias=1.0)
                        act(sig[:], sig[:], AF.Exp, [sig], [sig], scale=-1.0)
                        yield
                        for ck in range(2):
                            tc = slice(ck * 128, (ck + 1) * 128)
                            S.op("dve", lambda e_: e_.tensor_tensor_scan(out=pi[:, tc], data0=ones[:, 0:128],
                                                                         data1=sig[:, tc], initial=0.0,
                                                                         op0=ALU.mult, op1=ALU.add),
                                 reads=[ones, sig], writes=[pi])
                        tt("pool", px[:], pi[:], sig[:], ALU.subtract, [pi, sig], [px])
                        ts("dve", tcl[:, 0:2], pi[:, 127:256:128], -C0, None, ALU.mult, None, [pi], [tcl])
                        ts("dve", tcl[:, 2:4], pi[:, 127:256:128], C0, None, ALU.mult, None, [pi], [tcl])
                        WL_ = WLc[cb % 2]
                        act(WL_[:, e * 2:e * 2 + 2], tcl[:, 0:2], AF.Exp, [tcl], [WL_])
                        E1, E2, E3, E4 = TE["E1"], TE["E2"], TE["E3"], TE["E4"]
                        if e == 0:
                            act(E1[:], pi[:], AF.Exp, [pi], [E1], scale=-C0)
                            act(E2[:], px[:], AF.Exp, [px], [E2], scale=-C0)
                            act(E3[:], pi[:], AF.Exp, [pi], [E3], scale=C0)
                            for ck in range(2):
                                tc = slice(ck * 128, (ck + 1) * 128)
                                act(E4[:, tc], pi[:, tc], AF.Exp, [pi, tcl], [E4], scale=C0, bias=tcl[:, ck:ck + 1])
                        else:
                            for ck in range(2):
                                tc = slice(ck * 128, (ck + 1) * 128)
                                act(E1[:, tc], px[:, tc], AF.Exp, [px, tcl], [E1], scale=C0, bias=tcl[:, ck:ck + 1])
                                act(E2[:, tc], pi[:, tc], AF.Exp, [pi, tcl], [E2], scale=C0, bias=tcl[:, ck:ck + 1])
                                act(E3[:, tc], px[:, tc], AF.Exp, [px, tcl], [E3], scale=-C0, bias=tcl[:, 2 + ck:3 + ck])
                            act(E4[:], px[:], AF.Exp, [px], [E4], scale=-C0)
                        act(TE["a"][:], TE["a"][:], AF.Ln, [TE["a"]], [TE["a"]], bias=1.0)
                        act(TE["a"][:], TE["a"][:], AF.Exp, [TE["a"]], [TE["a"]], scale=-1.0)

                    def c_back(cb, e):
                        TE = TMPE[e]
                        T_ = TMPC[cb % 2]
                        r_, k_ = RKV[0][cb], RKV[1][cb]
                        E1, E2, E3, E4 = TE["E1"], TE["E2"], TE["E3"], TE["E4"]
                        a_, bq, kd, rkd = TE["a"], TE["bq"], TE["kd"], T_["rkd"]
                        tt("pool", bq[:], T_["kk"][:], a_[:], ALU.mult, [T_["kk"], a_], [bq])
                        ts("dve", kd[:], a_[:], col(32 + cb), dcol(24 + cb), ALU.mult, ALU.add, [a_, cols, dcols], [kd])
                        tt("dve", kd[:], kd[:], k_[:], ALU.mult, [kd, k_], [kd])
                        f = lambda n: FMt(n, par, e, cb)
                        tt("dve", f("Qt")[:], r_[:], E1[:], ALU.mult, [r_, E1], [f("Qt")])
                        tt("pool", f("KKt")[:], T_["kk"][:], E2[:], ALU.mult, [T_["kk"], E2], [f("KKt")])
                        tt("dve", f("Kt")[:], kd[:], E3[:], ALU.mult, [kd, E3], [f("Kt")])
                        yield
                        tt("pool", f("Bt")[:], bq[:], E3[:], ALU.mult, [bq, E3], [f("Bt")])
                        tt("dve", f("Kh")[:], kd[:], E4[:], ALU.mult, [kd, E4], [f("Kh")])
                        tt("pool", f("Bh")[:], bq[:], E4[:], ALU.mult, [bq, E4], [f("Bh")])
                        if e == 0:
                            tt("dve", rkd[:], r_[:], kd[:], ALU.mult, [r_, kd], [rkd])
                        else:
                            tt("dve", T_["sq"][:], r_[:], kd[:], ALU.mult, [r_, kd], [T_["sq"]])
                            stt(rkd[:], rkd[:], 1.0, T_["sq"][:], ALU.mult, ALU.add, [rkd, T_["sq"]], [rkd])

                    def c_tail(cb):
                        if False:
                            yield
                        T_ = TMPC[cb % 2]
                        v_ = RKV[2][cb]
                        rkd = T_["rkd"]
                        WL_ = WLc[cb % 2]
                        for hh in range(2):
                            mm(p5[0:64, 256 + hh * 4:256 + hh * 4 + 4], identf[:, hh * 64:(hh + 1) * 64], WL_[:, 0:4],
                               [identf, WL_], [p5])
                        for hh in range(2):
                            h = cb * 2 + hh
                            for e in range(2):
                                c0 = ((blk * 2 + e) * 8 + h) * 2
                                cp("dve", WLh[:, c0:c0 + 2], p5[0:64, 256 + hh * 4 + e * 2:256 + hh * 4 + e * 2 + 2], [p5], [WLh])
                        ts("dve", rkd[:], rkd[:], col(36 + cb), None, ALU.mult, None, [rkd, cols], [rkd])
                        mm(p5[:, 0:256], bones[:], rkd[:], [bones, rkd], [p5])
                        tt("dve", bvT[:, cb, :], p5[:, 0:256], v_[:], ALU.mult, [p5, v_], [bvT])

                    for cb in range(4):
                        yield from c_kk(cb)
                        yield
                        for e in range(2):
                            yield from c_front(cb, e)
                            yield
                            yield from c_back(cb, e)
                            yield
                        yield from c_tail(cb)
                        yield
                    S.dma("pool", bv_d[blk], bvT[:, :, :].rearrange("p a b -> p (a b)"), reads=[bvT], writes=[bv_b[blk]])

                    for ck in range(2):
                        tc = slice(ck * 128, (ck + 1) * 128)
                        jobs = [(Vtm[par][ck], vbf)] + [(Khtm[par][ck][e], FMs["Kh"][e]) for e in range(2)] + \
                               [(Bhtm[par][ck][e], FMs["Bh"][e]) for e in range(2)]
                        for ji, (dst, src) in enumerate(jobs):
                            p = PB
                            for cb in range(4):
                                mm(p[:, cb * 128:(cb + 1) * 128], src[cb][:, tc], identb[:], [src[cb], identb], [p],
                                   inc=(cb == 3))
                            cp("act" if ji % 2 == 0 else "dve", dst[:], p[:, :], [p], [dst])
                            yield


                def abcd_gen(blk):
                    yield from ab_gen(blk)
                    yield from cd_gen(blk)

                bg = {}

                def pump():
                    g = bg.get("g")
                    if g is not None:
                        try:
                            next(g)
                        except StopIteration:
                            bg["g"] = None

                def phase1_block(blk):
                    lat = blk < NLB
                    s = 0 if lat else 1
                    tok0 = blk * 256
                    if blk == 0:
                        bg["g"] = abcd_gen(0)
                    while bg.get("g") is not None:
                        pump()
                    ckpt(5)
                    if blk + 1 < NB and BG_INJECT:
                        bg["g"] = abcd_gen(blk + 1)
                    npump = [0]
                    par = blk % 2
                    ckpt(8)
                    for ck in range(2):
                        chunk = blk * 2 + ck
                        tc = slice(ck * 128, (ck + 1) * 128)
                        yps = PS[6]
                        for hg in range(2):
                            def fm(n, e, hq):
                                return FM4[n][par][e][hq][hg * 64:(hg + 1) * 64, tc]

                            def fmR(n, e, hq):
                                return [FM4[n][par][e][hq]]

                            def chain(e):
                                bA, bB, zps = PS[3 * e], PS[3 * e + 1], PS[3 * e + 2]
                                if e == 0:
                                    mSTn, mSn, mST, mIT = MK["NLT"], MK["NGT"], MK["LT"], MK["LE"]
                                else:
                                    mSTn, mSn, mST, mIT = MK["NGT"], MK["NLT"], MK["GT"], MK["GE"]
                                hsl = lambda hq: slice(hq * 128, (hq + 1) * 128)
                                for hq in range(4):
                                    mm(bA[:, hsl(hq)], fm("Bt", e, hq), fm("KKt", e, hq), fmR("Bt", e, hq) + fmR("KKt", e, hq),
                                       [bA], inc=(hq == 3))
                                tt("dve", XT[e][0][:], bA[:, :], mSTn[:], ALU.mult, [bA, mSTn], [XT[e][0]])
                                for hq in range(4):
                                    mm(bB[:, hsl(hq)], fm("KKt", e, hq), fm("Bt", e, hq), fmR("Bt", e, hq) + fmR("KKt", e, hq),
                                       [bB], inc=(hq == 3))
                                tt("dve", XM[e][0][:], bB[:, :], mSn[:], ALU.mult, [bB, mSn], [XM[e][0]])
                                yield
                                for hq in range(4):
                                    mm(bA[:, hsl(hq)], fm("Kt", e, hq), fm("KKt", e, hq), fmR("Kt", e, hq) + fmR("KKt", e, hq),
                                       [bA], inc=(hq == 3))
                                tt("dve", AakT[e][:], bA[:, :], mST[:], ALU.mult, [bA, mST], [AakT[e]])
                                for hq in range(4):
                                    h = hq * 2 + hg
                                    hh = hg
                                    mm(zps[:, hq * 128:hq * 128 + 64], AakT[e][:, hsl(hq)], Vtm[par][ck][:, h * 64:(h + 1) * 64],
                                       [AakT[e], Vtm[par][ck]], [zps], start=(hq == 0), stop=False, inc=False)
                                    mm(zps[:, hq * 128 + 64:(hq + 1) * 128], fm("KKt", e, hq),
                                       identb[hh * 64:(hh + 1) * 64, hh * 64:(hh + 1) * 64], fmR("KKt", e, hq) + [identb], [zps],
                                       start=False, stop=False, inc=(hq == 3))
                                cp("act", Zb[e][:], zps[:, :], [zps], [Zb[e]])
                                yield
                                for hq in range(4):
                                    mm(bB[:, hsl(hq)], fm("Bt", e, hq), fm("Qt", e, hq), fmR("Bt", e, hq) + fmR("Qt", e, hq),
                                       [bB], inc=(hq == 3))
                                tt("dve", AqbT[e][:], bB[:, :], mIT[:], ALU.mult, [bB, mIT], [AqbT[e]])
                                for hq in range(4):
                                    mm(bA[:, hsl(hq)], fm("Kt", e, hq), fm("Qt", e, hq), fmR("Kt", e, hq) + fmR("Qt", e, hq),
                                       [bA], inc=(hq == 3))
                                tt("dve", AqkT[e][:], bA[:, :], mIT[:], ALU.mult, [bA, mIT], [AqkT[e]])
                                yield
                                def xtv(i, lev_):
                                    if lev_ < KCUT:
                                        return XT[e][i][:, :], XM[e][i][:, :]
                                    return XT[e][i][:, :].bitcast(BF16)[:, 0:512], XM[e][i][:, :].bitcast(BF16)[:, 0:512]

                                for lev in range(7):
                                    cur, nxt = lev % 2, (lev + 1) % 2
                                    xt_c, xm_c = xtv(cur, lev)
                                    zsrc = Zb[e] if lev < KCUT else AakT[e]
                                    for hq in range(4):
                                        mm(zps[:, hsl(hq)], xt_c[:, hsl(hq)], zsrc[:, hsl(hq)], [XT[e][cur], zsrc], [zps],
                                           start=False, stop=(lev == 6), inc=(hq == 3))
                                    if lev < 6:
                                        xt_n, xm_n = xtv(nxt, lev + 1)
                                        for hq in range(4):
                                            mm(bA[:, hsl(hq)], xm_c[:, hsl(hq)], xt_c[:, hsl(hq)],
                                               [XM[e][cur], XT[e][cur]], [bA], inc=(hq == 3))
                                        cp("dve", xt_n, bA[:, :], [bA], [XT[e][nxt]])
                                        if lev < 5:
                                            for hq in range(4):
                                                mm(bB[:, hsl(hq)], xt_c[:, hsl(hq)], xm_c[:, hsl(hq)],
                                                   [XM[e][cur], XT[e][cur]], [bB], inc=(hq == 3))
                                            cp("act", xm_n, bB[:, :], [bB], [XM[e][nxt]])
                                        zdst = Zb[e] if lev + 1 < KCUT else AakT[e]
                                        cp("act", zdst[:], zps[:, :], [zps], [zdst])
                                    yield
                                act(UGn[e][:], zps[:, :], AF.Identity, [zps], [UGn[e]], scale=-1.0)
                                for hq in range(4):
                                    h = hq * 2 + hg
                                    hh = hg
                                    hc = slice(h * 64, (h + 1) * 64)
                                    mm(yps[:, hc], AqbT[e][:, hsl(hq)], UGn[e][:, hq * 128:hq * 128 + 64], [AqbT[e], UGn[e]],
                                       [yps], start=(e == 0 and hg == 0 and hq == 0), stop=False, inc=False)
                                    mm(yps[:, hc], AqkT[e][:, hsl(hq)], Vtm[par][ck][:, hc], [AqkT[e], Vtm[par][ck]], [yps],
                                       start=False, stop=(e == 1), inc=(hq == 3))
                                for hq in range(4):
                                    hh = hg
                                    mm(bA[0:64, hsl(hq)], identb[hh * 64:(hh + 1) * 64, hh * 64:(hh + 1) * 64], fm("Qt", e, hq),
                                       fmR("Qt", e, hq) + [identb], [bA], start=True, stop=False, inc=False)
                                    mm(bA[0:64, hsl(hq)], UGn[e][:, hq * 128 + 64:(hq + 1) * 128], AqbT[e][:, hsl(hq)],
                                       [UGn[e], AqbT[e]], [bA], start=False, stop=True, inc=(hq == 3))
                                cp("act", Qht[:, :].rearrange("p (e q g t) -> p e q g t", e=2, q=4, g=2)[:, e, :, hg, :],
                                   bA[0:64, :].rearrange("p (q t) -> p q t", q=4), [bA], [Qht])
                                for hq in range(4):
                                    h = hq * 2 + hg
                                    hc = slice(h * 64, (h + 1) * 64)
                                    mm(bB[0:64, hq * 64:(hq + 1) * 64], UGn[e][:, hq * 128 + 64:(hq + 1) * 128], Bhtm[par][ck][e][:, hc],
                                       [UGn[e], Bhtm[par][ck][e]], [bB], inc=False)
                                    mm(bB[0:64, 256 + hq * 64:256 + (hq + 1) * 64], Bhtm[par][ck][e][:, hc],
                                       UGn[e][:, hq * 128:hq * 128 + 64], [UGn[e], Bhtm[par][ck][e]], [bB], start=True, stop=False,
                                       inc=False)
                                    mm(bB[0:64, 256 + hq * 64:256 + (hq + 1) * 64], Khtm[par][ck][e][:, hc], Vtm[par][ck][:, hc],
                                       [Khtm[par][ck][e], Vtm[par][ck]], [bB], start=False, stop=True, inc=(hq == 3))
                                cp("dve", Xst[:, :].rearrange("p (e q g c) -> p e q g c", e=2, q=4, g=2)[:, e, :, hg, :],
                                   bB[0:64, 0:256].rearrange("p (q c) -> p q c", q=4), [bB], [Xst])
                                cp("dve", Dst[:, :].rearrange("p (e q g c) -> p e q g c", e=2, q=4, g=2)[:, e, :, hg, :],
                                   bB[0:64, 256:512].rearrange("p (q c) -> p q c", q=4), [bB], [Dst])
                                yield

                            gens = [chain(0), chain(1)]
                            alive = [True, True]
                            while any(alive):
                                for gi in range(2):
                                    if alive[gi]:
                                        try:
                                            next(gens[gi])
                                        except StopIteration:
                                            alive[gi] = False
                                        npump[0] += 1
                                        if npump[0] % PUMP_EVERY == 0:
                                            pump()
                        cp("act", Ylt[:], yps[:, :], [yps], [Ylt])
                        S.dma("pool", Yl_d[chunk], Ylt[:], reads=[Ylt], writes=[Yl_b[chunk]])
                        S.dma("pool", Qh_d[chunk], Qht[:], reads=[Qht], writes=[Qh_b[chunk]])
                        S.dma("pool", X_d[chunk], Xst[:], reads=[Xst], writes=[X_b[chunk]])
                        S.dma("pool", D_d[chunk], Dst[:], reads=[Dst], writes=[D_b[chunk]])

                for blk in range(NB):
                    phase1_block(blk)
                    if blk + 1 < NB and not BG_INJECT:
                        bg["g"] = abcd_gen(blk + 1)
                    ckpt(9)

                e1b.close()
                S.barrier()
                ckpt(10)
                Sf = sbt(e1, "Sf", [64, 16, 64], F32)
                Sb = sbt(e1, "Sb", [64, 16, 64], BF16)
                Xl = [sbt(e1, "Xl%d" % i, [64, 16, 64], BF16) for i in range(2)]
                Dl = [sbt(e1, "Dl%d" % i, [64, 16, 64], F32) for i in range(2)]
                Sfin = sbt(e1, "Sfin", [64, 16, 64], F32)
                SfB = [Buf() for _ in range(16)]

                def flat(t_, a=None, b=None):
                    ap = t_[:, :, :] if a is None else t_[:, a:b, :]
                    return ap.rearrange("p a b -> p (a b)")

                def run_seq(chunks, init_from_state, seq_out):
                    n = len(chunks)
                    if init_from_state:
                        cp("dve", flat(Sf), flat(Sf0), [Sf0], SfB)
                    else:
                        S.op("dve", lambda e: e.memset(flat(Sf), 0.0), writes=SfB)
                    cp("dve", flat(Sb), flat(Sf), SfB, [Sb])
                    for st in range(n):
                        cf, cbw = chunks[st], chunks[n - 1 - st]
                        S.dma("pool", S_d[cf][:, 0:512], flat(Sb, 0, 8), reads=[Sb], writes=[S_b[cf][0]])
                        S.dma("pool", S_d[cbw][:, 512:1024], flat(Sb, 8, 16), reads=[Sb], writes=[S_b[cbw][1]])
                        xl, dl = Xl[st % 2], Dl[st % 2]
                        S.dma("sp", flat(xl, 0, 8), X_d[cf][:, 0:512], reads=[X_b[cf]], writes=[xl])
                        S.dma("sp", flat(xl, 8, 16), X_d[cbw][:, 512:1024], reads=[X_b[cbw]], writes=[xl])
                        S.dma("sp", flat(dl, 0, 8), D_d[cf][:, 0:512], reads=[D_b[cf]], writes=[dl])
                        S.dma("sp", flat(dl, 8, 16), D_d[cbw][:, 512:1024], reads=[D_b[cbw]], writes=[dl])
                        for hd in range(16):
                            p = PS[hd // 8]
                            mm(p[0:64, (hd % 8) * 64:(hd % 8 + 1) * 64], xl[:, hd, :], Sb[:, hd, :], [xl, Sb], [p],
                               inc=(hd % 8 == 7))
                        for hd in range(16):
                            e, h = hd // 8, hd % 8
                            cch = cf if e == 0 else cbw
                            blk_, ck_ = cch // 2, cch % 2
                            c0 = ((blk_ * 2 + e) * 8 + h) * 2 + ck_
                            p = PS[hd // 8]
                            stt(Sf[:, hd, :], Sf[:, hd, :], WLh[:, c0:c0 + 1], p[0:64, (hd % 8) * 64:(hd % 8 + 1) * 64],
                                ALU.mult, ALU.add, [SfB[hd], WLh, p], [SfB[hd]])
                        tt("dve", flat(Sf), flat(Sf), flat(dl), ALU.add, SfB + [dl], SfB)
                        cp("act", flat(Sb), flat(Sf), SfB, [Sb])
                    if seq_out is not None:
                        for hd in range(16):
                            p = PS[2 + hd // 8]
                            mm(p[0:64, (hd % 8) * 64:(hd % 8 + 1) * 64], Sf[:, hd, :], identf[0:64, 0:64], [SfB[hd], identf], [p],
                               inc=(hd % 8 == 7))
                        for hf in range(2):
                            cp("dve", flat(Sfin, hf * 8, hf * 8 + 8), PS[2 + hf][0:64, :], [PS[2 + hf]], [Sfin])
                        S.dma("pool", so[seq_out], Sfin[:, :, :], reads=[Sfin])

                if NLB > 0:
                    run_seq(list(range(0, 2 * NLB)), True, None)
                for cs in range(NCS):
                    b0 = 2 * (NLB + cs)
                    run_seq([b0, b0 + 1], False, cs)

            ckpt(11)
            S.barrier()
            with ExitStack() as e3:
                wo_bf = sbt(e3, "wo_bf", [128, 8, 1024], BF16)
                FG = sbt(e3, "FG", [128, 1024], F32)
                stg3 = [sbt(e3, "stg3_%d" % i, [128, 2048], F32) for i in range(2)]
                for k in range(8):
                    g = stg3[k % 2]
                    S.dma("sp", g[:], w_in[:, k, 2048:4096], writes=[g])
                    cp("act" if k % 2 == 0 else "dve", w_bf[:, k, :], g[:], [g], [w_bf])
                for k in range(8):
                    g = stg3[k % 2]
                    S.dma("sp", g[:, 0:1024], w_out[:, k, :], writes=[g])
                    cp("act" if k % 2 == 0 else "dve", wo_bf[:, k, :], g[:, 0:1024], [g], [wo_bf])
                S.dma("sp", FG[:], fg_bc, writes=[FG])
                GATE = [sbt(e3, "GATE_%d" % s, [128, 1024], F32) for s in range(2)]
                modg = sbt(e3, "modg", [2, 1024], F32)
                sel3 = [sbt(e3, "sel3_%d" % s, [2, 128], F32) for s in range(2)]
                S.dma("sp", modg[:], mod_d, reads=[mod_b], writes=[modg])
                for s in range(2):
                    ts("dve", sel3[s][:], ones[0:2, :], identf[0:2, s:s + 1], None, ALU.mult, None, [ones, identf], [sel3[s]])
                    for n in range(2):
                        p = PS[s * 2 + n]
                        mm(p[:, :], sel3[s][:], modg[:, n * 512:(n + 1) * 512], [sel3[s], modg], [p])
                        cp("act", GATE[s][:, n * 512:(n + 1) * 512], p[:, :], [p], [GATE[s]])
                hTw2 = [sbt(e3, "hTw%d" % i, [128, 8, 384], BF16) for i in range(2)]
                x3 = [sbt(e3, "x3_%d" % i, [128, 1024], F32) for i in range(2)]
                Gt = [sbt(e3, "Gt%d" % cb, [128, 256], F32) for cb in range(4)]
                tmpc = sbt(e3, "tmpc", [128, 384], F32)
                cuA = sbt(e3, "cuA", [128, 4, 66], F32)
                cuB = sbt(e3, "cuB", [128, 384], F32)
                cuC = sbt(e3, "cuC", [128, 258], F32)
                cacc = sbt(e3, "cacc", [128, 256], F32)
                catT = sbt(e3, "catT", [128, 8, 256], BF16)
                Qhl2 = [sbt(e3, "Qhl%d" % i, [64, 2048], BF16) for i in range(2)]
                Sl2 = [sbt(e3, "Sl%d" % i, [64, 1024], BF16) for i in range(2)]
                Yll2 = [sbt(e3, "Yll%d" % i, [128, 512], F32) for i in range(2)]
                Yt = sbt(e3, "Yt", [128, 512], F32)
                gnY = sbt(e3, "gnY", [128, 512], BF16)
                bst = sbt(e3, "bst", [128, 8, 6], F32)
                mv = sbt(e3, "mv", [128, 8, 2], F32)
                rs = sbt(e3, "rs", [128, 8], F32)
                sgl2 = [sbt(e3, "sgl%d" % i, [128, 4, 256], F32) for i in range(2)]
                bvl2 = [sbt(e3, "bvl%d" % i, [128, 4, 256], F32) for i in range(2)]
                yat = sbt(e3, "yat", [128, 128], F32)
                yo = sbt(e3, "yo", [128, 1024], F32)
                junk3 = sbt(e3, "junk3", [128, 1024], BF16)
                ss3 = sbt(e3, "ss3", [128, 4], F32)
                S.op("dve", lambda e: e.memset(cuA[:, :, :].rearrange("p a b -> p (a b)"), 0.0), writes=[cuA])
                S.op("dve", lambda e: e.memset(cuC[:], 0.0), writes=[cuC])

                def hflat(a, b):
                    return hTw[:, :, a:b]

                def phase3_block(blk):
                    hTw, sgl, bvl = hTw2[blk % 2], sgl2[blk % 2], bvl2[blk % 2]
                    lat = blk < NLB
                    s = 0 if lat else 1
                    tok0 = blk * 256
                    hv = lambda b_: hT_d[b_].rearrange("p (k t) -> p k t", k=8)
                    if lat:
                        if blk == 0:
                            S.op("dve", lambda e: e.memset(hTw[:, :, 0:64], 0.0), writes=[hTw])
                        else:
                            S.dma("sp", hTw[:, :, 0:64], hv(blk - 1)[:, :, 192:256], reads=[hT_b[blk - 1]], writes=[hTw])
                        if blk == NLB - 1:
                            S.op("dve", lambda e: e.memset(hTw[:, :, 320:384], 0.0), writes=[hTw])
                        else:
                            S.dma("sp", hTw[:, :, 320:384], hv(blk + 1)[:, :, 0:64], reads=[hT_b[blk + 1]], writes=[hTw])
                    S.dma("sp", hTw[:, :, 64:320], hv(blk), reads=[hT_b[blk]], writes=[hTw])
                    S.dma("sp", sgl[:, :, :].rearrange("p a b -> p (a b)"), sg_d[blk], reads=[sg_b[blk]], writes=[sgl])
                    S.dma("sp", bvl[:, :, :].rearrange("p a b -> p (a b)"), bv_d[blk], reads=[bv_b[blk]], writes=[bvl])
                    for cb in range(4):
                        pb, pg = PS[(cb % 2) * 2], PS[(cb % 2) * 2 + 1]
                        hs = slice(0, 256)
                        for k in range(8):
                            mm(pb[:, hs], w_bf[:, k, cb * 128:(cb + 1) * 128], hTw[:, k, 64:320], [w_bf, hTw], [pb],
                               start=(k == 0), stop=(k == 7), inc=(k == 7))
                        for k in range(8):
                            mm(pg[:, hs], w_bf[:, k, (12 + cb) * 128:(13 + cb) * 128], hTw[:, k, 64:320], [w_bf, hTw], [pg],
                               start=(k == 0), stop=(k == 7), inc=(k == 7))
                        sigm(tmpc[:, 0:256], pg[:, hs], [pg], [tmpc])
                        tt("dve", tmpc[:, 0:256], pg[:, hs], tmpc[:, 0:256], ALU.mult, [pg, tmpc], [tmpc])
                        tt("dve", Gt[cb][:], pb[:, hs], tmpc[:, 0:256], ALU.mult, [pb, tmpc], [Gt[cb]])
                    for cb in range(4):
                        pc, pu = PS[4 + (cb % 2)], PS[6 + (cb % 2)]
                        wide = lat and cb >= 2
                        n0, n1 = (0, 384) if wide else (64, 320)
                        N = n1 - n0
                        for k in range(8):
                            mm(pc[:, 0:N], w_bf[:, k, (4 + cb) * 128:(5 + cb) * 128], hTw[:, k, n0:n1], [w_bf, hTw], [pc],
                               start=(k == 0), stop=(k == 7), inc=(k == 7))
                        for k in range(8):
                            mm(pu[:, 0:N], w_bf[:, k, (8 + cb) * 128:(9 + cb) * 128], hTw[:, k, n0:n1], [w_bf, hTw], [pu],
                               start=(k == 0), stop=(k == 7), inc=(k == 7))
                        cp("act", tmpc[:, 0:N], pc[:, 0:N], [pc], [tmpc])
                        cw = [col(48 + j * 4 + cb) for j in range(3)]
                        if not lat:
                            tt("dve", cuC[:, 1:257], tmpc[:, 0:256], pu[:, 0:256], ALU.mult, [tmpc, pu], [cuC])
                            prev, ctr, nxt, cub = cuC[:, 0:256], cuC[:, 1:257], cuC[:, 2:258], cuC
                            accv = cacc[:]
                        elif wide:
                            tt("dve", cuB[:], tmpc[:, 0:384], pu[:, 0:384], ALU.mult, [tmpc, pu], [cuB])
                            prev, ctr, nxt, cub = cuB[:, 0:256], cuB[:, 64:320], cuB[:, 128:384], cuB
                            accv = cacc[:]
                        else:
                            tt("dve", cuA[:, :, 1:65], tmpc[:, 0:256].rearrange("p (r w) -> p r w", r=4),
                               pu[:, 0:256].rearrange("p (r w) -> p r w", r=4), ALU.mult, [tmpc, pu], [cuA])
                            prev, ctr, nxt, cub = cuA[:, :, 0:64], cuA[:, :, 1:65], cuA[:, :, 2:66], cuA
                            accv = cacc[:].rearrange("p (r w) -> p r w", r=4)
                        ts("dve", accv, ctr, cw[1], None, ALU.mult, None, [cub, cols], [cacc])
                        stt(accv, prev, cw[0], accv, ALU.mult, ALU.add, [cub, cols, cacc], [cacc])
                        stt(accv, nxt, cw[2], accv, ALU.mult, ALU.add, [cub, cols, cacc], [cacc])
                        tt("pool", catT[:, 4 + cb, :], cacc[:], Gt[cb][:], ALU.mult, [cacc, Gt[cb]], [catT])
                    for ck in range(2):
                        chunk = blk * 2 + ck
                        tc = slice(ck * 128, (ck + 1) * 128)
                        Qhl, Sl, Yll = Qhl2[ck], Sl2[ck], Yll2[ck]
                        S.dma("sp", Qhl[:], Qh_d[chunk], reads=[Qh_b[chunk]], writes=[Qhl])
                        S.dma("sp", Sl[:], S_d[chunk], reads=[S_b[chunk][0], S_b[chunk][1]], writes=[Sl])
                        S.dma("sp", Yll[:], Yl_d[chunk], reads=[Yl_b[chunk]], writes=[Yll])
                        xt = x3[ck]
                        S.dma("sp", xt[:], xs[tok0 + ck * 128:tok0 + (ck + 1) * 128, :], writes=[xt])
                        yp = PS[6]
                        for h in range(8):
                            for e in range(2):
                                mm(yp[:, h * 64:(h + 1) * 64], Qhl[:, (e * 8 + h) * 128:(e * 8 + h + 1) * 128],
                                   Sl[:, (e * 8 + h) * 64:(e * 8 + h + 1) * 64], [Qhl, Sl], [yp], start=(e == 0), stop=(e == 1),
                                   inc=(h == 7 and e == 1))
                        tt("dve", Yt[:], yp[:, :], Yll[:], ALU.add, [yp, Yll], [Yt])
                        for h in range(8):
                            S.op("dve", lambda e_: e_.bn_stats(out=bst[:, h, :], in_=Yt[:, h * 64:(h + 1) * 64]), reads=[Yt],
                                 writes=[bst])
                        for h in range(8):
                            S.op("dve", lambda e_: e_.bn_aggr(out=mv[:, h, :], in_=bst[:, h, :]), reads=[bst], writes=[mv])
                        ts("dve", rs[:], mv[:, :, 1], GN_EPS, None, ALU.add, None, [mv], [rs])
                        act(rs[:], rs[:], AF.Ln, [rs], [rs])
                        act(rs[:], rs[:], AF.Exp, [rs], [rs], scale=-0.5)
                        for h in range(8):
                            ts("dve", gnY[:, h * 64:(h + 1) * 64], Yt[:, h * 64:(h + 1) * 64], mv[:, h, 0:1], rs[:, h:h + 1],
                               ALU.subtract, ALU.mult, [Yt, mv, rs], [gnY])
                        pt = PS[7]
                        for cb in range(4):
                            mm(pt[:, cb * 128:(cb + 1) * 128], gnY[:, cb * 128:(cb + 1) * 128], identb[:], [gnY, identb], [pt],
                               inc=(cb == 3))
                        for cb in range(4):
                            ts("dve", yat[:], pt[:, cb * 128:(cb + 1) * 128], col(40 + cb), col(44 + cb), ALU.mult, ALU.add,
                               [pt, cols], [yat])
                            tt("pool", yat[:], yat[:], bvl[:, cb, tc], ALU.add, [yat, bvl], [yat])
                            tt("pool", catT[:, cb, tc], yat[:], sgl[:, cb, tc], ALU.mult, [yat, sgl], [catT])
                        for n in range(2):
                            po = PS[4 + n]
                            for m in range(8):
                                mm(po[:, :], catT[:, m, tc], wo_bf[:, m, n * 512:(n + 1) * 512], [catT, wo_bf], [po],
                                   start=(m == 0), stop=(m == 7), inc=(m == 7))
                            hs = slice(n * 512, (n + 1) * 512)
                            tt("dve", yo[:, hs], po[:, :], GATE[s][:, hs], ALU.mult, [po, GATE[s]], [yo])
                        tt("pool", yo[:], yo[:], xt[:], ALU.add, [yo, xt], [yo])
                        act(junk3[:], yo[:], AF.Square, [yo], [junk3, ss3], accum=ss3[:, 0:1])
                        ts("dve", ss3[:, 1:2], ss3[:, 0:1], 1.0 / 1024, NORM_EPS, ALU.mult, ALU.add, [ss3], [ss3])
                        act(ss3[:, 2:3], ss3[:, 1:2], AF.Ln, [ss3], [ss3])
                        act(ss3[:, 3:4], ss3[:, 2:3], AF.Exp, [ss3], [ss3], scale=-0.5)
                        stt(yo[:], yo[:], ss3[:, 3:4], FG[:], ALU.mult, ALU.mult, [yo, ss3, FG], [yo])
                        S.dma("pool", ys[tok0 + ck * 128:tok0 + (ck + 1) * 128, :], yo[:], reads=[yo])

                for blk in range(NB):
                    phase3_block(blk)
                    ckpt(12)
        except _Stop:
            pass
        S.off = False
        S.finish("sp")
        S.finish("pool")
    nc._marks = SS[0].marks
    return nc


def _prep_shared(inp):
    f = np.float32
    d = {}
    d["w_ada"] = np.ascontiguousarray(inp["w_ada"][0].reshape(8, 128, 3072).transpose(1, 0, 2), dtype=f)
    d["b_ada2"] = np.ascontiguousarray(np.stack([inp["b_ada"][0], inp["b_ada"][0]], 0), dtype=f)
    d["ngc"] = np.ascontiguousarray(np.asarray(inp["norm_g"][0], dtype=f).reshape(8, 128).T)
    d["fg_bc"] = np.ascontiguousarray(np.broadcast_to(inp["final_g"][None, :], (128, 1024)), dtype=f)
    d["w_in"] = np.ascontiguousarray(inp["w_in"][0].reshape(8, 128, 4096).transpose(1, 0, 2), dtype=f)
    ld = np.concatenate([inp["decay_down"][0, 0], inp["decay_down"][0, 1], inp["iclr_down"][0, 0], inp["iclr_down"][0, 1]],
                        axis=1)
    d["lora_dn"] = np.ascontiguousarray(ld.reshape(8, 128, 256).transpose(1, 0, 2), dtype=f)
    d["dec_up"] = np.ascontiguousarray(inp["decay_up"][0].reshape(128, 512), dtype=f)
    d["icl_up"] = np.ascontiguousarray(inp["iclr_up"][0].reshape(128, 512), dtype=f)

    def c4(v):
        return np.asarray(v, dtype=f).reshape(4, 128).T

    cl = [c4(inp["shift_mu"][0, q]) for q in range(3)]
    cl += [c4(inp["decay_w0"][0, e]) for e in range(2)]
    cl += [c4(inp["iclr_bias"][0, e]) for e in range(2)]
    cl += [c4(inp["kk_scale"][0]), c4(inp["ka_scale"][0]), c4(inp["bonus_rk"][0]), c4(inp["gn_w"][0]), c4(inp["gn_b"][0])]
    cl += [c4(inp["conv_w"][0, j]) for j in range(3)]
    d["cols"] = np.ascontiguousarray(np.concatenate(cl, axis=1), dtype=f)
    d["w_out"] = np.ascontiguousarray(inp["w_out"][0].reshape(8, 128, 1024).transpose(1, 0, 2), dtype=f)
    return d


def _core_inputs(shared, x_lat, x_ctx, c_lat, c_ctx, st):
    f = np.float32
    m = dict(shared)
    parts = []
    if x_lat is not None:
        parts.append(np.asarray(x_lat, dtype=f).reshape(-1, 1024))
    if x_ctx is not None and len(x_ctx):
        parts.append(np.asarray(x_ctx, dtype=f).reshape(-1, 1024))
    m["xs"] = np.ascontiguousarray(np.concatenate(parts, 0))
    cv = np.stack([np.asarray(c_lat, dtype=f), np.asarray(c_ctx, dtype=f)], 0)
    m["cT"] = np.ascontiguousarray(cv.reshape(2, 8, 128).transpose(2, 1, 0).reshape(128, 16))
    m["st0"] = np.ascontiguousarray(np.asarray(st, dtype=f).transpose(2, 0, 1, 3).reshape(64, 16, 64))
    return m


_PROG = {}


def kernel(**inputs):
    inp = {k: np.asarray(v) for k, v in inputs.items()}
    NCORES = 8
    NLB, NCS = 16, 4
    shared = _prep_shared(inp)
    in_maps = []
    for b in range(NCORES):
        in_maps.append(_core_inputs(shared, inp["x_sample"][b], inp["x_prompt"][4 * b:4 * b + 4], inp["c"][b], inp["c_ctx"],
                                    inp["state_wkv"][b, 0]))
    key = (NLB, NCS)
    if key not in _PROG:
        _PROG[key] = build_program(NLB, NCS)
    res = run_bass_kernel_spmd(_PROG[key], in_maps, core_ids=list(range(NCORES)))
    y_prompt = np.zeros((32, 256, 1024), np.float32)
    y_sample = np.zeros((8, 4096, 1024), np.float32)
    new_state = np.zeros((32, 1, 2, 8, 64, 64), np.float32)
    for b in range(NCORES):
        r = res.results[b]
        ysb = np.asarray(r["ys"])
        y_sample[b] = ysb[:4096]
        y_prompt[4 * b:4 * b + 4] = ysb[4096:].reshape(4, 256, 1024)
        sob = np.asarray(r["so"]).reshape(4, 64, 2, 8, 64)
        new_state[4 * b:4 * b + 4, 0] = sob.transpose(0, 2, 3, 1, 4)
    return (y_prompt, y_sample, new_state)
```

```python
import os
import numpy as np
import concourse.bass as bass
import concourse.mybir as mybir
from concourse.bass_utils import run_bass_kernel_spmd
from contextlib import ExitStack

F32, BF16, I32 = mybir.dt.float32, mybir.dt.bfloat16, mybir.dt.int32
ALU = mybir.AluOpType
AF = mybir.ActivationFunctionType
C0 = 0.6065306597126334
NORM_EPS = 1e-6
GN_EPS = 64e-5
NDS = 24
BG_INJECT = os.environ.get("BG_INJECT", "1") == "1"
PUMP_EVERY = int(os.environ.get("PUMP_EVERY", "1"))
KCUT = int(os.environ.get("KCUT", "5"))
RAW_ONLY = os.environ.get("RAW_ONLY", "0") == "1"
SELF_SKIP = tuple(os.environ.get("SELF_SKIP", "pe").split(","))


class Buf:
    __slots__ = ("w", "r")

    def __init__(self):
        self.w = None
        self.r = {}


class T:
    def __init__(self, t):
        self.t = t
        self.b = Buf()

    def __getitem__(self, idx):
        return self.t[idx]


def _b(x):
    return x.b if hasattr(x, "b") else x


class Sched:
    def __init__(self, nc, es):
        self.nc = nc
        self.eng = {"pe": nc.tensor, "dve": nc.vector, "act": nc.scalar, "pool": nc.gpsimd, "sp": nc.sync}
        self.sem = {k: es.enter_context(nc.semaphore("s_" + k)) for k in self.eng}
        self.cnt = {k: 0 for k in self.eng}
        self.dsem = [es.enter_context(nc.semaphore("d%d" % i)) for i in range(NDS)]
        self.dcnt = [0] * NDS
        self.dnext = 0
        self.dnext2 = 0
        self.waited = {}
        self.nwait = 0
        self.off = False
        self.nops = {k: 0 for k in self.eng}
        self.marks = []

    def _semh(self, key):
        return self.sem[key] if isinstance(key, str) else self.dsem[key]

    def _wait(self, e, key, val):
        if val <= 0:
            return
        if e == key and e in SELF_SKIP:
            return
        if self.waited.get((e, key), 0) >= val:
            return
        self.eng[e].wait_ge(self._semh(key), val)
        self.waited[(e, key)] = val
        self.nwait += 1

    def _deps(self, e, reads, writes):
        for b in reads:
            b = _b(b)
            if b.w:
                self._wait(e, *b.w)
        raw_only = RAW_ONLY and e in ("act", "dve")
        for b in writes:
            b = _b(b)
            if b.w and not (raw_only and b.w[0] == e):
                self._wait(e, *b.w)
            for k, v in b.r.items():
                if raw_only and k == e:
                    continue
                self._wait(e, k, v)

    def _mark(self, key, tgt, reads, writes):
        for b in writes:
            b = _b(b)
            b.w = (key, tgt)
            b.r = {}
        for b in reads:
            b = _b(b)
            if b.r.get(key, 0) < tgt:
                b.r[key] = tgt

    def op(self, e, fn, reads=(), writes=(), inc=True):
        if self.off:
            return
        self._deps(e, reads, writes)
        inst = fn(self.eng[e])
        self.nops[e] += 1
        tgt = self.cnt[e] + 1
        if inc:
            inst.then_inc(self.sem[e], 1)
            self.cnt[e] = tgt
        self._mark(e, tgt, reads, writes)

    def dma(self, e, out, in_, reads=(), writes=()):
        if self.off:
            return
        if e == "pool":
            i = NDS - 8 + self.dnext2
            self.dnext2 = (self.dnext2 + 1) % 8
        else:
            i = self.dnext
            self.dnext = (i + 1) % (NDS - 8)
        self._deps(e, reads, writes)
        self._wait(e, i, self.dcnt[i])
        self.eng[e].dma_start(out=out, in_=in_).then_inc(self.dsem[i], 16)
        self.dcnt[i] += 16
        self._mark(i, self.dcnt[i], reads, writes)

    def mark(self, label):
        self.marks.append((label, dict(self.nops)))

    def barrier(self):
        if self.off:
            return
        for e in self.eng:
            for k in self.eng:
                if k != e:
                    self._wait(e, k, self.cnt[k])
            for i in range(NDS):
                self._wait(e, i, self.dcnt[i])

    def finish(self, e="sp"):
        for i in range(NDS):
            self._wait(e, i, self.dcnt[i])


class _Stop(Exception):
    pass


def build_program(NLB, NCS, debug=False, stop=None):
    SS = []

    def ckpt(n):
        SS[0].mark(n)
        if stop is not None and n == stop:
            SS[0].off = True

    NB = NLB + NCS
    NCH = 2 * NB
    NTOK = NB * 256
    nc = bass.Bass("TRN2", target_bir_lowering=False)

    def din(name, shape, dt=F32):
        return nc.dram_tensor(name, list(shape), dt, kind="ExternalInput").ap()

    def dout(name, shape, dt=F32):
        return nc.dram_tensor(name, list(shape), dt, kind="ExternalOutput").ap()

    def dscr(name, shape, dt):
        return nc.dram_tensor(name, list(shape), dt, kind=("ExternalOutput" if debug else "Internal")).ap()

    xs = din("xs", [NTOK, 1024])
    cT = din("cT", [128, 16])
    st0 = din("st0", [64, 16, 64])
    w_ada = din("w_ada", [128, 8, 3072])
    b_ada2 = din("b_ada2", [2, 3072])
    ngc_d = din("ngc", [128, 8])
    fg_bc = din("fg_bc", [128, 1024])
    w_in = din("w_in", [128, 8, 4096])
    lora_dn = din("lora_dn", [128, 8, 256])
    dec_up = din("dec_up", [128, 512])
    icl_up = din("icl_up", [128, 512])
    cols_d = din("cols", [128, 60])
    w_out = din("w_out", [128, 8, 1024])
    ys = dout("ys", [NTOK, 1024])
    so = dout("so", [max(NCS, 1), 64, 16, 64])

    hT_d = dscr("hT_d", [NB, 128, 8 * 256], BF16)
    Yl_d = dscr("Yl_d", [NCH, 128, 512], F32)
    Qh_d = dscr("Qh_d", [NCH, 64, 2048], BF16)
    X_d = dscr("X_d", [NCH, 64, 1024], BF16)
    D_d = dscr("D_d", [NCH, 64, 1024], F32)
    sg_d = dscr("sg_d", [NB, 128, 1024], F32)
    bv_d = dscr("bv_d", [NB, 128, 1024], F32)
    S_d = dscr("S_d", [NCH, 64, 1024], BF16)
    mod_d = dscr("mod_d", [2, 1024], F32)
    mod_b = Buf()
    hT_b = [Buf() for _ in range(NB)]
    Yl_b = [Buf() for _ in range(NCH)]
    Qh_b = [Buf() for _ in range(NCH)]
    X_b = [Buf() for _ in range(NCH)]
    D_b = [Buf() for _ in range(NCH)]
    sg_b = [Buf() for _ in range(NB)]
    bv_b = [Buf() for _ in range(NB)]
    S_b = [[Buf(), Buf()] for _ in range(NCH)]
    ys_b = Buf()
    so_b = Buf()

    with ExitStack() as es:
        S = Sched(nc, es)
        SS.append(S)

        try:
            def sbt(es_, name, shape, dt):
                return T(es_.enter_context(nc.sbuf_tensor("sb_" + name, list(shape), dt)))

            PS = [T(es.enter_context(nc.psum_tensor("ps%d" % i, [128, 512], F32))) for i in range(8)]

            def mm(out, lhsT, rhs, R, W, start=True, stop=True, inc=True):
                S.op("pe", lambda e: e.matmul(out, lhsT=lhsT, rhs=rhs, start=start, stop=stop, skip_group_check=True),
                     reads=R, writes=W, inc=inc)

            def tt(eng, out, a, b, op, R, W):
                S.op(eng, lambda e: e.tensor_tensor(out=out, in0=a, in1=b, op=op), reads=R, writes=W)

            def ts(eng, out, a, s1, s2, op0, op1, R, W):
                if op1 is None:
                    S.op(eng, lambda e: e.tensor_scalar(out=out, in0=a, scalar1=s1, scalar2=None, op0=op0), reads=R, writes=W)
                else:
                    S.op(eng, lambda e: e.tensor_scalar(out=out, in0=a, scalar1=s1, scalar2=s2, op0=op0, op1=op1),
                         reads=R, writes=W)

            def stt(out, a, sc, b, op0, op1, R, W):
                S.op("dve", lambda e: e.scalar_tensor_tensor(out=out, in0=a, scalar=sc, in1=b, op0=op0, op1=op1),
                     reads=R, writes=W)

            def act(out, in_, func, R, W, bias=None, scale=None, accum=None):
                kw = {}
                if bias is not None:
                    kw["bias"] = bias
                if scale is not None:
                    kw["scale"] = scale
                if accum is not None:
                    kw["accum_out"] = accum
                S.op("act", lambda e: e.activation(out=out, in_=in_, func=func, **kw), reads=R, writes=W)

            def sigm(out, in_, R, W, nbias=None):
                act(out, in_, AF.Exp, R, W, scale=-1.0, bias=nbias)
                act(out, out, AF.Ln, W, W, bias=1.0)
                act(out, out, AF.Exp, W, W, scale=-1.0)

            def cp(eng, out, in_, R, W):
                if eng == "act":
                    S.op("act", lambda e: e.copy(out=out, in_=in_), reads=R, writes=W)
                else:
                    S.op(eng, lambda e: e.tensor_copy(out=out, in_=in_), reads=R, writes=W)

            ioi = sbt(es, "ioi", [128, 128], I32)
            iof = sbt(es, "iof", [128, 128], F32)
            identf = sbt(es, "identf", [128, 128], F32)
            identb = sbt(es, "identb", [128, 128], BF16)
            bones = sbt(es, "bones", [128, 128], F32)
            ones = sbt(es, "ones", [128, 128], F32)
            MK = {k: sbt(es, "mk_" + k, [128, 512], BF16) for k in ("LT", "GT", "LE", "GE", "NLT", "NGT")}
            cols = sbt(es, "cols", [128, 60], F32)
            dcols = sbt(es, "dcols", [128, 44], F32)
            w_bf = sbt(es, "w_bf", [128, 8, 2048], BF16)
            WLh = sbt(es, "WLh", [64, NB * 32], F32)

            S.op("pool", lambda e: e.iota(ioi[:], pattern=[[1, 128]], base=0, channel_multiplier=-1), writes=[ioi])
            cp("dve", iof[:], ioi[:], [ioi], [iof])
            ts("dve", identf[:], iof[:], 0.0, None, ALU.is_equal, None, [iof], [identf])
            cp("dve", identb[:], identf[:], [identf], [identb])
            S.op("dve", lambda e: e.memset(ones[:], 1.0), writes=[ones])
            S.op("dve", lambda e: e.memset(bones[:], 0.0), writes=[bones])
            S.op("dve", lambda e: e.memset(bones[0:64, 0:64], 1.0), writes=[bones])
            S.op("dve", lambda e: e.memset(bones[64:128, 64:128], 1.0), writes=[bones])
            for j in range(4):
                sl = slice(j * 128, (j + 1) * 128)
                ts("dve", MK["LT"][:, sl], iof[:], 0.0, None, ALU.is_gt, None, [iof], [MK["LT"]])
                ts("dve", MK["GT"][:, sl], iof[:], 0.0, None, ALU.is_lt, None, [iof], [MK["GT"]])
                ts("dve", MK["LE"][:, sl], iof[:], 0.0, None, ALU.is_ge, None, [iof], [MK["LE"]])
                ts("dve", MK["GE"][:, sl], iof[:], 0.0, None, ALU.is_le, None, [iof], [MK["GE"]])
                ts("dve", MK["NLT"][:, sl], iof[:], 0.0, -1.0, ALU.is_gt, ALU.mult, [iof], [MK["NLT"]])
                ts("dve", MK["NGT"][:, sl], iof[:], 0.0, -1.0, ALU.is_lt, ALU.mult, [iof], [MK["NGT"]])
            S.dma("sp", cols[:], cols_d, writes=[cols])
            ts("dve", dcols[:, 0:12], cols[:, 0:12], -1.0, 1.0, ALU.mult, ALU.add, [cols], [dcols])
            ts("dve", dcols[:, 12:24], cols[:, 0:12], 0.5, None, ALU.mult, None, [cols], [dcols])
            ts("dve", dcols[:, 24:28], cols[:, 32:36], -1.0, 1.0, ALU.mult, ALU.add, [cols], [dcols])
            ts("dve", dcols[:, 28:44], cols[:, 12:28], -1.0, None, ALU.mult, None, [cols], [dcols])
            ckpt(1)

            def col(i):
                return cols[:, i:i + 1]

            def dcol(i):
                return dcols[:, i:i + 1]

            with ExitStack() as e1:
                g1c = [sbt(e1, "g1c%d" % s, [128, 8], F32) for s in range(2)]
                shc = [sbt(e1, "shc%d" % s, [128, 8], F32) for s in range(2)]
                lora_bf = sbt(e1, "lora_bf", [128, 8, 256], BF16)
                dup_bf = sbt(e1, "dup_bf", [128, 512], BF16)
                iup_bf = sbt(e1, "iup_bf", [128, 512], BF16)
                Sf0 = sbt(e1, "Sf0", [64, 16, 64], F32)
                with ExitStack() as e0:
                    stg = [sbt(e0, "stg%d" % i, [128, 3072], F32) for i in range(2)]
                    ngt = sbt(e0, "ngt", [128, 8], F32)
                    cTt = sbt(e0, "cTt", [128, 16], F32)
                    scT = sbt(e0, "scT", [128, 16], F32)
                    modv = sbt(e0, "modv", [2, 3072], F32)
                    bad = sbt(e0, "bad", [2, 3072], F32)
                    st_in = sbt(e0, "st_in", [64, 16, 64], F32)
                    for k in range(8):
                        g = stg[k % 2]
                        S.dma("sp", g[:, 0:2048], w_in[:, k, 0:2048], writes=[g])
                        cp("act" if k % 2 == 0 else "dve", w_bf[:, k, :], g[:, 0:2048], [g], [w_bf])
                    g = stg[0]
                    S.dma("sp", g[:, 0:2048], lora_dn.rearrange("p k n -> p (k n)"), writes=[g])
                    cp("dve", lora_bf[:, :, :].rearrange("p k n -> p (k n)"), g[:, 0:2048], [g], [lora_bf])
                    g = stg[1]
                    S.dma("sp", g[:, 0:512], dec_up, writes=[g])
                    S.dma("sp", g[:, 512:1024], icl_up, writes=[g])
                    cp("dve", dup_bf[:], g[:, 0:512], [g], [dup_bf])
                    cp("dve", iup_bf[:], g[:, 512:1024], [g], [iup_bf])
                    ckpt(2)
                    S.dma("sp", cTt[:], cT, writes=[cTt])
                    act(scT[:], cTt[:], AF.Silu, [cTt], [scT])
                    S.dma("sp", bad[:], b_ada2, writes=[bad])
                    S.dma("sp", ngt[:], ngc_d, writes=[ngt])
                    for k in range(8):
                        g = stg[k % 2]
                        S.dma("sp", g[:], w_ada[:, k, :], writes=[g])
                        for n in range(6):
                            mm(PS[n][0:2, :], scT[:, 2 * k:2 * k + 2], g[:, n * 512:(n + 1) * 512], [scT, g], [PS[n]],
                               start=(k == 0), stop=(k == 7))
                    for n in range(6):
                        tt("dve", modv[:, n * 512:(n + 1) * 512], PS[n][0:2, :], bad[:, n * 512:(n + 1) * 512], ALU.add,
                           [PS[n], bad], [modv])
                    S.dma("sp", mod_d, modv[:, 2048:3072], reads=[modv], writes=[mod_b])
                    for part in range(2):
                        for k in range(8):
                            c0 = (part * 8 + k) * 2
                            mm(PS[0][:, c0:c0 + 2], modv[0:2, part * 1024 + k * 128:part * 1024 + (k + 1) * 128],
                               identf[0:2, 0:2], [modv, identf], [PS[0]], inc=(part == 1 and k == 7))
                    mview = PS[0][:, 0:32].rearrange("p (a k s) -> p a k s", a=2, k=8)
                    for s in range(2):
                        cp("dve", shc[s][:], mview[:, 0, :, s], [PS[0]], [shc[s]])
                        stt(g1c[s][:], mview[:, 1, :, s], 1.0, ngt[:], ALU.add, ALU.mult, [PS[0], ngt], [g1c[s]])
                    ckpt(3)
                    S.dma("sp", st_in[:], st0, writes=[st_in])
                    for hd in range(16):
                        p = PS[hd // 8]
                        mm(p[0:64, (hd % 8) * 64:(hd % 8 + 1) * 64], st_in[:, hd, :], identf[0:64, 0:64], [st_in, identf], [p],
                           inc=(hd % 8 == 7))
                    for hf in range(2):
                        cp("dve", Sf0[:, hf * 8:(hf + 1) * 8, :].rearrange("p a b -> p (a b)"), PS[hf][0:64, :], [PS[hf]], [Sf0])

                S.barrier()
                ckpt(4)
                e1b = ExitStack()
                e1b.__enter__()
                xts = [sbt(e1b, "xt0", [128, 1024], F32)] * 2
                hb = sbt(e1b, "hb", [128, 1024], BF16)
                ss = sbt(e1b, "ss", [128, 4], F32)
                hT = sbt(e1b, "hT", [128, 8, 256], BF16)
                ppL = sbt(e1b, "ppL", [128, 4, 66], F32)
                ppC = sbt(e1b, "ppC", [128, 1, 258], F32)
                nbt = sbt(e1b, "nbt", [128, 256], F32)
                RKV = [[sbt(e1b, "rkv%d_%d" % (q, cb), [128, 256], F32) for cb in range(4)] for q in range(3)]
                vbf = [sbt(e1b, "vbf%d" % cb, [128, 256], BF16) for cb in range(4)]
                sgT = sbt(e1b, "sgT", [128, 4, 256], F32)
                bvT = sbt(e1b, "bvT", [128, 4, 256], F32)
                lwd = sbt(e1b, "lwd", [128, 256], BF16)
                lwi = sbt(e1b, "lwi", [128, 256], BF16)
                TMPC = [{n: sbt(e1b, "tmc%d_%s" % (i, n), [128, 256], F32) for n in ["sq", "kk", "rkd"]} for i in range(2)]
                TMPC[0]["rn"] = TMPC[1]["rn"] = sbt(e1b, "tmc_rn", [128, 256], F32)
                TMP = TMPC[0]
                TMPE = [{n: sbt(e1b, "tm0_%s" % n, [128, 256] if n != "tcol" else [128, 4], F32)
                         for n in ["sig", "pi", "px", "E1", "E2", "E3", "E4", "a", "bq", "kd", "tcol"]}]
                TMPE.append(TMPE[0])
                WLc = [sbt(e1b, "WLc%d" % i, [128, 4], F32) for i in range(2)]
                FM4 = {n: [[[sbt(e1b, "fm_%s%d%d%d" % (n, par, e, cb), [128, 256], BF16) for cb in range(4)] for e in range(2)]
                           for par in range(2)] for n in ("Qt", "KKt", "Kt", "Bt")}
                FMs = {n: [[sbt(e1b, "fm_%s%d%d" % (n, e, cb), [128, 256], BF16) for cb in range(4)] for e in range(2)]
                       for n in ("Kh", "Bh")}

                def FMt(n, par, e, cb):
                    return FMs[n][e][cb] if n in FMs else FM4[n][par][e][cb]

                Khtm = [[[sbt(e1b, "khtm%d%d%d" % (par, ck, e), [128, 512], BF16) for e in range(2)] for ck in range(2)]
                        for par in range(2)]
                Bhtm = [[[sbt(e1b, "bhtm%d%d%d" % (par, ck, e), [128, 512], BF16) for e in range(2)] for ck in range(2)]
                        for par in range(2)]
                Vtm = [[sbt(e1b, "vtm%d%d" % (par, ck), [128, 512], BF16) for ck in range(2)] for par in range(2)]
                XT = [[sbt(e1b, "XT%d%d" % (e, i), [128, 512], F32) for i in range(2)] for e in range(2)]
                XM = [[sbt(e1b, "XM%d%d" % (e, i), [128, 512], F32) for i in range(2)] for e in range(2)]
                AakT = [sbt(e1b, "AakT%d" % e, [128, 512], BF16) for e in range(2)]
                AqbT = [sbt(e1b, "AqbT%d" % e, [128, 512], BF16) for e in range(2)]
                AqkT = [sbt(e1b, "AqkT%d" % e, [128, 512], BF16) for e in range(2)]
                Zb = [sbt(e1b, "Zb%d" % e, [128, 512], F32) for e in range(2)]
                UGn = [sbt(e1b, "UGn%d" % e, [128, 512], BF16) for e in range(2)]
                Ylt = sbt(e1b, "Ylt", [128, 512], F32)
                Qht = sbt(e1b, "Qht", [64, 2048], BF16)
                Xst = sbt(e1b, "Xst", [64, 1024], BF16)
                Dst = sbt(e1b, "Dst", [64, 1024], F32)
                S.op("dve", lambda e: e.memset(ppL[:, :, :].rearrange("p a b -> p (a b)"), 0.0), writes=[ppL])
                S.op("dve", lambda e: e.memset(ppC[:, :, :].rearrange("p a b -> p (a b)"), 0.0), writes=[ppC])

                PB = PS[7]

                def ab_gen(blk):
                    lat = blk < NLB
                    s = 0 if lat else 1
                    tok0 = blk * 256
                    for i in range(2):
                        xt = xts[i]
                        S.dma("sp", xt[:], xs[tok0 + i * 128:tok0 + (i + 1) * 128, :], writes=[xt])
                        yield
                        act(hb[:], xt[:], AF.Square, [xt], [hb, ss], accum=ss[:, 0:1])
                        ts("dve", ss[:, 1:2], ss[:, 0:1], 1.0 / 1024, NORM_EPS, ALU.mult, ALU.add, [ss], [ss])
                        act(ss[:, 2:3], ss[:, 1:2], AF.Ln, [ss], [ss])
                        act(ss[:, 3:4], ss[:, 2:3], AF.Exp, [ss], [ss], scale=-0.5)
                        yield
                        ts("pool", hb[:], xt[:], ss[:, 3:4], None, ALU.mult, None, [xt, ss], [hb])
                        yield
                        for half in range(2):
                            p = PB
                            for k4 in range(4):
                                k = half * 4 + k4
                                mm(p[:, k4 * 128:(k4 + 1) * 128], hb[:, k * 128:(k + 1) * 128], identb[:], [hb, identb], [p],
                                   inc=(k4 == 3))
                            for k4 in range(4):
                                k = half * 4 + k4
                                dsth = hT[:, k, i * 128:(i + 1) * 128]
                                srcp = p[:, k4 * 128:(k4 + 1) * 128]
                                if k4 % 2 == 0:
                                    act(dsth, srcp, AF.Identity, [p, g1c[s], shc[s]], [hT], scale=g1c[s][:, k:k + 1],
                                        bias=shc[s][:, k:k + 1])
                                else:
                                    ts("dve", dsth, srcp, g1c[s][:, k:k + 1], shc[s][:, k:k + 1], ALU.mult, ALU.add,
                                       [p, g1c[s], shc[s]], [hT])
                            yield
                    S.dma("pool", hT_d[blk], hT[:, :, :].rearrange("p k t -> p (k t)"), reads=[hT], writes=[hT_b[blk]])
                    pp = ppL if lat else ppC
                    R_, W_ = (4, 64) if lat else (1, 256)

                    def v3(ap):
                        return ap.rearrange("p (r w) -> p r w", r=R_)

                    hs = slice(0, 256)
                    for cbg in range(16):
                        p = PB
                        for k in range(8):
                            mm(p[:, hs], w_bf[:, k, cbg * 128:(cbg + 1) * 128], hT[:, k, :], [w_bf, hT], [p],
                               start=(k == 0), stop=(k == 7), inc=(k == 7))
                        q, cb = cbg // 4, cbg % 4
                        if q < 3:
                            cp("act", pp[:, :, 1:W_ + 1], v3(p[:, hs]), [p], [pp])
                            tt("dve", v3(nbt[:]), pp[:, :, 0:W_], pp[:, :, 2:W_ + 2], ALU.add, [pp], [nbt])
                            dst = RKV[q][cb]
                            act(v3(dst[:]), pp[:, :, 1:W_ + 1], AF.Identity, [pp, dcols], [dst], scale=dcol(q * 4 + cb))
                            stt(dst[:], nbt[:], dcol(12 + q * 4 + cb), dst[:], ALU.mult, ALU.add, [nbt, dcols, dst], [dst])
                            if q == 2:
                                cp("pool", vbf[cb][:], dst[:], [dst], [vbf[cb]])
                        else:
                            sigm(sgT[:, cb, :], p[:, hs], [p], [sgT])
                            tt("dve", sgT[:, cb, :], p[:, hs], sgT[:, cb, :], ALU.mult, [p, sgT], [sgT])
                        yield
                    S.dma("pool", sg_d[blk], sgT[:, :, :].rearrange("p a b -> p (a b)"), reads=[sgT], writes=[sg_b[blk]])
                    for mb in range(2):
                        p = PB
                        for k in range(8):
                            mm(p[:, hs], lora_bf[:, k, mb * 128:(mb + 1) * 128], hT[:, k, :], [lora_bf, hT], [p],
                               start=(k == 0), stop=(k == 7), inc=(k == 7))
                        if mb == 0:
                            tq = TMP["sq"]
                            act(tq[:], p[:, hs], AF.Exp, [p], [tq], scale=-2.0)
                            act(tq[:], tq[:], AF.Ln, [tq], [tq], bias=1.0)
                            act(tq[:], tq[:], AF.Exp, [tq], [tq], scale=-1.0)
                            ts("dve", lwd[:], tq[:], 2.0, -1.0, ALU.mult, ALU.add, [tq], [lwd])
                        else:
                            cp("dve", lwi[:], p[:, hs], [p], [lwi])
                        yield

                def cd_gen(blk):
                    par = blk % 2
                    p5 = PB

                    def c_kk(cb):
                        if False:
                            yield
                        k_ = RKV[1][cb]
                        T_ = TMPC[cb % 2]
                        act(T_["sq"][:], k_[:], AF.Square, [k_, cols], [T_["sq"]], scale=col(28 + cb))
                        mm(p5[:, 0:256], bones[:], T_["sq"][:], [bones, T_["sq"]], [p5])
                        ts("dve", T_["rn"][:], p5[:, 0:256], 1e-24, None, ALU.max, None, [p5], [T_["rn"]])
                        act(T_["rn"][:], T_["rn"][:], AF.Ln, [T_["rn"]], [T_["rn"]])
                        act(T_["rn"][:], T_["rn"][:], AF.Exp, [T_["rn"]], [T_["rn"]], scale=-0.5)
                        stt(T_["kk"][:], k_[:], col(28 + cb), T_["rn"][:], ALU.mult, ALU.mult, [k_, cols, T_["rn"]],
                            [T_["kk"]])

                    def c_front(cb, e):
                        TE = TMPE[e]
                        es_ = slice(e * 64, (e + 1) * 64)
                        pz = PB
                        mm(pz[:, 0:256], dup_bf[es_, cb * 128:(cb + 1) * 128], lwd[es_, :], [dup_bf, lwd], [pz])
                        mm(pz[:, 256:512], iup_bf[es_, cb * 128:(cb + 1) * 128], lwi[es_, :], [iup_bf, lwi], [pz])
                        sig, pi, px, tcl = TE["sig"], TE["pi"], TE["px"], TE["tcol"]
                        act(sig[:], pz[:, 0:256], AF.Exp, [pz, dcols], [sig], scale=-1.0, bias=dcol(28 + e * 4 + cb))
                        act(TE["a"][:], pz[:, 256:512], AF.Exp, [pz, dcols], [TE["a"]], scale=-1.0, bias=dcol(36 + e * 4 + cb))
                        act(sig[:], sig[:], AF.Ln, [sig], [sig], bias=1.0)
                        act(sig[:], sig[:], AF.Exp, [sig], [sig], scale=-1.0)
                        yield
                        for ck in range(2):
                            tc = slice(ck * 128, (ck + 1) * 128)
                            S.op("dve", lambda e_: e_.tensor_tensor_scan(out=pi[:, tc], data0=ones[:, 0:128],
                                                                         data1=sig[:, tc], initial=0.0,
                                                                         op0=ALU.mult, op1=ALU.add),
                                 reads=[ones, sig], writes=[pi])
                        tt("dve", px[:], pi[:], sig[:], ALU.subtract, [pi, sig], [px])
                        ts("dve", tcl[:, 0:2], pi[:, 127:256:128], -C0, None, ALU.mult, None, [pi], [tcl])
                        ts("dve", tcl[:, 2:4], pi[:, 127:256:128], C0, None, ALU.mult, None, [pi], [tcl])
                        yield
                        WL_ = WLc[cb % 2]
                        act(WL_[:, e * 2:e * 2 + 2], tcl[:, 0:2], AF.Exp, [tcl], [WL_])
                        E1, E2, E3, E4 = TE["E1"], TE["E2"], TE["E3"], TE["E4"]
                        if e == 0:
                            act(E1[:], pi[:], AF.Exp, [pi], [E1], scale=-C0)
                            act(E2[:], px[:], AF.Exp, [px], [E2], scale=-C0)
                            act(E3[:], pi[:], AF.Exp, [pi], [E3], scale=C0)
                            for ck in range(2):
                                tc = slice(ck * 128, (ck + 1) * 128)
                                act(E4[:, tc], pi[:, tc], AF.Exp, [pi, tcl], [E4], scale=C0, bias=tcl[:, ck:ck + 1])
                        else:
                            for ck in range(2):
                                tc = slice(ck * 128, (ck + 1) * 128)
                                act(E1[:, tc], px[:, tc], AF.Exp, [px, tcl], [E1], scale=C0, bias=tcl[:, ck:ck + 1])
                                act(E2[:, tc], pi[:, tc], AF.Exp, [pi, tcl], [E2], scale=C0, bias=tcl[:, ck:ck + 1])
                                act(E3[:, tc], px[:, tc], AF.Exp, [px, tcl], [E3], scale=-C0, bias=tcl[:, 2 + ck:3 + ck])
                            act(E4[:], px[:], AF.Exp, [px], [E4], scale=-C0)
                        yield
                        act(TE["a"][:], TE["a"][:], AF.Ln, [TE["a"]], [TE["a"]], bias=1.0)
                        act(TE["a"][:], TE["a"][:], AF.Exp, [TE["a"]], [TE["a"]], scale=-1.0)

                    def c_back(cb, e):
                        TE = TMPE[e]
                        T_ = TMPC[cb % 2]
                        r_, k_ = RKV[0][cb], RKV[1][cb]
                        E1, E2, E3, E4 = TE["E1"], TE["E2"], TE["E3"], TE["E4"]
                        a_, bq, kd, rkd = TE["a"], TE["bq"], TE["kd"], T_["rkd"]
                        tt("pool", bq[:], T_["kk"][:], a_[:], ALU.mult, [T_["kk"], a_], [bq])
                        ts("dve", kd[:], a_[:], col(32 + cb), dcol(24 + cb), ALU.mult, ALU.add, [a_, cols, dcols], [kd])
                        tt("dve", kd[:], kd[:], k_[:], ALU.mult, [kd, k_], [kd])
                        f = lambda n: FMt(n, par, e, cb)
                        tt("dve", f("Qt")[:], r_[:], E1[:], ALU.mult, [r_, E1], [f("Qt")])
                        tt("pool", f("KKt")[:], T_["kk"][:], E2[:], ALU.mult, [T_["kk"], E2], [f("KKt")])
                        tt("dve", f("Kt")[:], kd[:], E3[:], ALU.mult, [kd, E3], [f("Kt")])
                        yield
                        tt("pool", f("Bt")[:], bq[:], E3[:], ALU.mult, [bq, E3], [f("Bt")])
                        tt("dve", f("Kh")[:], kd[:], E4[:], ALU.mult, [kd, E4], [f("Kh")])
                        tt("pool", f("Bh")[:], bq[:], E4[:], ALU.mult, [bq, E4], [f("Bh")])
                        if e == 0:
                            tt("dve", rkd[:], r_[:], kd[:], ALU.mult, [r_, kd], [rkd])
                        else:
                            tt("dve", T_["sq"][:], r_[:], kd[:], ALU.mult, [r_, kd], [T_["sq"]])
                            stt(rkd[:], rkd[:], 1.0, T_["sq"][:], ALU.mult, ALU.add, [rkd, T_["sq"]], [rkd])

                    def c_tail(cb):
                        if False:
                            yield
                        T_ = TMPC[cb % 2]
                        v_ = RKV[2][cb]
                        rkd = T_["rkd"]
                        WL_ = WLc[cb % 2]
                        for hh in range(2):
                            mm(p5[0:64, 256 + hh * 4:256 + hh * 4 + 4], identf[:, hh * 64:(hh + 1) * 64], WL_[:, 0:4],
                               [identf, WL_], [p5])
                        for hh in range(2):
                            h = cb * 2 + hh
                            for e in range(2):
                                c0 = ((blk * 2 + e) * 8 + h) * 2
                                cp("dve", WLh[:, c0:c0 + 2], p5[0:64, 256 + hh * 4 + e * 2:256 + hh * 4 + e * 2 + 2], [p5], [WLh])
                        ts("dve", rkd[:], rkd[:], col(36 + cb), None, ALU.mult, None, [rkd, cols], [rkd])
                        mm(p5[:, 0:256], bones[:], rkd[:], [bones, rkd], [p5])
                        tt("dve", bvT[:, cb, :], p5[:, 0:256], v_[:], ALU.mult, [p5, v_], [bvT])

                    for cb in range(4):
                        yield from c_kk(cb)
                        yield
                        for e in range(2):
                            yield from c_front(cb, e)
                            yield
                            yield from c_back(cb, e)
                            yield
                        yield from c_tail(cb)
                        yield
                    S.dma("pool", bv_d[blk], bvT[:, :, :].rearrange("p a b -> p (a b)"), reads=[bvT], writes=[bv_b[blk]])

                    for ck in range(2):
                        tc = slice(ck * 128, (ck + 1) * 128)
                        jobs = [(Vtm[par][ck], vbf)] + [(Khtm[par][ck][e], FMs["Kh"][e]) for e in range(2)] + \
                               [(Bhtm[par][ck][e], FMs["Bh"][e]) for e in range(2)]
                        for ji, (dst, src) in enumerate(jobs):
                            p = PB
                            for cb in range(4):
                                mm(p[:, cb * 128:(cb + 1) * 128], src[cb][:, tc], identb[:], [src[cb], identb], [p],
                                   inc=(cb == 3))
                            cp("act" if ji % 2 == 0 else "dve", dst[:], p[:, :], [p], [dst])
                            yield


                def abcd_gen(blk):
                    yield from ab_gen(blk)
                    yield from cd_gen(blk)

                bg = {}

                def pump():
                    g = bg.get("g")
                    if g is not None:
                        try:
                            next(g)
                        except StopIteration:
                            bg["g"] = None

                def phase1_block(blk):
                    lat = blk < NLB
                    s = 0 if lat else 1
                    tok0 = blk * 256
                    if blk == 0:
                        bg["g"] = abcd_gen(0)
                    while bg.get("g") is not None:
                        pump()
                    ckpt(5)
                    if blk + 1 < NB and BG_INJECT:
                        bg["g"] = abcd_gen(blk + 1)
                    npump = [0]
                    par = blk % 2
                    ckpt(8)
                    for ck in range(2):
                        chunk = blk * 2 + ck
                        tc = slice(ck * 128, (ck + 1) * 128)
                        yps = PS[6]
                        for hg in range(2):
                            def fm(n, e, hq):
                                return FM4[n][par][e][hq][hg * 64:(hg + 1) * 64, tc]

                            def fmR(n, e, hq):
                                return [FM4[n][par][e][hq]]

                            def chain(e):
                                bA, bB, zps = PS[3 * e], PS[3 * e + 1], PS[3 * e + 2]
                                if e == 0:
                                    mSTn, mSn, mST, mIT = MK["NLT"], MK["NGT"], MK["LT"], MK["LE"]
                                else:
                                    mSTn, mSn, mST, mIT = MK["NGT"], MK["NLT"], MK["GT"], MK["GE"]
                                hsl = lambda hq: slice(hq * 128, (hq + 1) * 128)
                                for hq in range(4):
                                    mm(bA[:, hsl(hq)], fm("Bt", e, hq), fm("KKt", e, hq), fmR("Bt", e, hq) + fmR("KKt", e, hq),
                                       [bA], inc=(hq == 3))
                                tt("dve", XT[e][0][:], bA[:, :], mSTn[:], ALU.mult, [bA, mSTn], [XT[e][0]])
                                for hq in range(4):
                                    mm(bB[:, hsl(hq)], fm("KKt", e, hq), fm("Bt", e, hq), fmR("Bt", e, hq) + fmR("KKt", e, hq),
                                       [bB], inc=(hq == 3))
                                tt("dve", XM[e][0][:], bB[:, :], mSn[:], ALU.mult, [bB, mSn], [XM[e][0]])
                                yield
                                for hq in range(4):
                                    mm(bA[:, hsl(hq)], fm("Kt", e, hq), fm("KKt", e, hq), fmR("Kt", e, hq) + fmR("KKt", e, hq),
                                       [bA], inc=(hq == 3))
                                tt("dve", AakT[e][:], bA[:, :], mST[:], ALU.mult, [bA, mST], [AakT[e]])
                                for hq in range(4):
                                    h = hq * 2 + hg
                                    hh = hg
                                    mm(zps[:, hq * 128:hq * 128 + 64], AakT[e][:, hsl(hq)], Vtm[par][ck][:, h * 64:(h + 1) * 64],
                                       [AakT[e], Vtm[par][ck]], [zps], start=(hq == 0), stop=False, inc=False)
                                    mm(zps[:, hq * 128 + 64:(hq + 1) * 128], fm("KKt", e, hq),
                                       identb[hh * 64:(hh + 1) * 64, hh * 64:(hh + 1) * 64], fmR("KKt", e, hq) + [identb], [zps],
                                       start=False, stop=False, inc=(hq == 3))
                                cp("act", Zb[e][:], zps[:, :], [zps], [Zb[e]])
                                yield
                                for hq in range(4):
                                    mm(bB[:, hsl(hq)], fm("Bt", e, hq), fm("Qt", e, hq), fmR("Bt", e, hq) + fmR("Qt", e, hq),
                                       [bB], inc=(hq == 3))
                                cp("act", AqbT[e][:], bB[:, :], [bB], [AqbT[e]])
                                tt("pool", AqbT[e][:], AqbT[e][:], mIT[:], ALU.mult, [AqbT[e], mIT], [AqbT[e]])
                                for hq in range(4):
                                    mm(bA[:, hsl(hq)], fm("Kt", e, hq), fm("Qt", e, hq), fmR("Kt", e, hq) + fmR("Qt", e, hq),
                                       [bA], inc=(hq == 3))
                                cp("act", AqkT[e][:], bA[:, :], [bA], [AqkT[e]])
                                tt("pool", AqkT[e][:], AqkT[e][:], mIT[:], ALU.mult, [AqkT[e], mIT], [AqkT[e]])
                                yield
                                def xtv(i, lev_):
                                    if lev_ < KCUT:
                                        return XT[e][i][:, :], XM[e][i][:, :]
                                    return XT[e][i][:, :].bitcast(BF16)[:, 0:512], XM[e][i][:, :].bitcast(BF16)[:, 0:512]

                                for lev in range(7):
                                    cur, nxt = lev % 2, (lev + 1) % 2
                                    xt_c, xm_c = xtv(cur, lev)
                                    zsrc = Zb[e] if lev < KCUT else AakT[e]
                                    for hq in range(4):
                                        mm(zps[:, hsl(hq)], xt_c[:, hsl(hq)], zsrc[:, hsl(hq)], [XT[e][cur], zsrc], [zps],
                                           start=False, stop=(lev == 6), inc=(hq == 3))
                                    if lev < 6:
                                        xt_n, xm_n = xtv(nxt, lev + 1)
                                        for hq in range(4):
                                            mm(bA[:, hsl(hq)], xm_c[:, hsl(hq)], xt_c[:, hsl(hq)],
                                               [XM[e][cur], XT[e][cur]], [bA], inc=(hq == 3))
                                        cp("dve", xt_n, bA[:, :], [bA], [XT[e][nxt]])
                                        if lev < 5:
                                            for hq in range(4):
                                                mm(bB[:, hsl(hq)], xt_c[:, hsl(hq)], xm_c[:, hsl(hq)],
                                                   [XM[e][cur], XT[e][cur]], [bB], inc=(hq == 3))
                                            cp("act", xm_n, bB[:, :], [bB], [XM[e][nxt]])
                                        zdst = Zb[e] if lev + 1 < KCUT else AakT[e]
                                        cp("act", zdst[:], zps[:, :], [zps], [zdst])
                                    yield
                                act(UGn[e][:], zps[:, :], AF.Identity, [zps], [UGn[e]], scale=-1.0)
                                for hq in range(4):
                                    h = hq * 2 + hg
                                    hh = hg
                                    hc = slice(h * 64, (h + 1) * 64)
                                    mm(yps[:, hc], AqbT[e][:, hsl(hq)], UGn[e][:, hq * 128:hq * 128 + 64], [AqbT[e], UGn[e]],
                                       [yps], start=(e == 0 and hg == 0 and hq == 0), stop=False, inc=False)
                                    mm(yps[:, hc], AqkT[e][:, hsl(hq)], Vtm[par][ck][:, hc], [AqkT[e], Vtm[par][ck]], [yps],
                                       start=False, stop=(e == 1), inc=(hq == 3))
                                for hq in range(4):
                                    hh = hg
                                    mm(bA[0:64, hsl(hq)], identb[hh * 64:(hh + 1) * 64, hh * 64:(hh + 1) * 64], fm("Qt", e, hq),
                                       fmR("Qt", e, hq) + [identb], [bA], start=True, stop=False, inc=False)
                                    mm(bA[0:64, hsl(hq)], UGn[e][:, hq * 128 + 64:(hq + 1) * 128], AqbT[e][:, hsl(hq)],
                                       [UGn[e], AqbT[e]], [bA], start=False, stop=True, inc=(hq == 3))
                                cp("act", Qht[:, :].rearrange("p (e q g t) -> p e q g t", e=2, q=4, g=2)[:, e, :, hg, :],
                                   bA[0:64, :].rearrange("p (q t) -> p q t", q=4), [bA], [Qht])
                                for hq in range(4):
                                    h = hq * 2 + hg
                                    hc = slice(h * 64, (h + 1) * 64)
                                    mm(bB[0:64, hq * 64:(hq + 1) * 64], UGn[e][:, hq * 128 + 64:(hq + 1) * 128], Bhtm[par][ck][e][:, hc],
                                       [UGn[e], Bhtm[par][ck][e]], [bB], inc=False)
                                    mm(bB[0:64, 256 + hq * 64:256 + (hq + 1) * 64], Bhtm[par][ck][e][:, hc],
                                       UGn[e][:, hq * 128:hq * 128 + 64], [UGn[e], Bhtm[par][ck][e]], [bB], start=True, stop=False,
                                       inc=False)
                                    mm(bB[0:64, 256 + hq * 64:256 + (hq + 1) * 64], Khtm[par][ck][e][:, hc], Vtm[par][ck][:, hc],
                                       [Khtm[par][ck][e], Vtm[par][ck]], [bB], start=False, stop=True, inc=(hq == 3))
                                cp("dve", Xst[:, :].rearrange("p (e q g c) -> p e q g c", e=2, q=4, g=2)[:, e, :, hg, :],
                                   bB[0:64, 0:256].rearrange("p (q c) -> p q c", q=4), [bB], [Xst])
                                cp("dve", Dst[:, :].rearrange("p (e q g c) -> p e q g c", e=2, q=4, g=2)[:, e, :, hg, :],
                                   bB[0:64, 256:512].rearrange("p (q c) -> p q c", q=4), [bB], [Dst])
                                yield

                            gens = [chain(0), chain(1)]
                            alive = [True, True]
                            while any(alive):
                                for gi in range(2):
                                    if alive[gi]:
                                        try:
                                            next(gens[gi])
                                        except StopIteration:
                                            alive[gi] = False
                                        npump[0] += 1
                                        if npump[0] % PUMP_EVERY == 0:
                                            pump()
                        cp("act", Ylt[:], yps[:, :], [yps], [Ylt])
                        S.dma("pool", Yl_d[chunk], Ylt[:], reads=[Ylt], writes=[Yl_b[chunk]])
                        S.dma("pool", Qh_d[chunk], Qht[:], reads=[Qht], writes=[Qh_b[chunk]])
                        S.dma("pool", X_d[chunk], Xst[:], reads=[Xst], writes=[X_b[chunk]])
                        S.dma("pool", D_d[chunk], Dst[:], reads=[Dst], writes=[D_b[chunk]])

                for blk in range(NB):
                    phase1_block(blk)
                    if blk + 1 < NB and not BG_INJECT:
                        bg["g"] = abcd_gen(blk + 1)
                    ckpt(9)

                e1b.close()
                S.barrier()
                ckpt(10)
                Sf = sbt(e1, "Sf", [64, 16, 64], F32)
                Sb = sbt(e1, "Sb", [64, 16, 64], BF16)
                Xl = [sbt(e1, "Xl%d" % i, [64, 16, 64], BF16) for i in range(2)]
                Dl = [sbt(e1, "Dl%d" % i, [64, 16, 64], F32) for i in range(2)]
                Sfin = sbt(e1, "Sfin", [64, 16, 64], F32)
                SfB = [Buf() for _ in range(16)]

                def flat(t_, a=None, b=None):
                    ap = t_[:, :, :] if a is None else t_[:, a:b, :]
                    return ap.rearrange("p a b -> p (a b)")

                def run_seq(chunks, init_from_state, seq_out):
                    n = len(chunks)
                    if init_from_state:
                        cp("dve", flat(Sf), flat(Sf0), [Sf0], SfB)
                    else:
                        S.op("dve", lambda e: e.memset(flat(Sf), 0.0), writes=SfB)
                    cp("dve", flat(Sb), flat(Sf), SfB, [Sb])
                    for st in range(n):
                        cf, cbw = chunks[st], chunks[n - 1 - st]
                        S.dma("pool", S_d[cf][:, 0:512], flat(Sb, 0, 8), reads=[Sb], writes=[S_b[cf][0]])
                        S.dma("pool", S_d[cbw][:, 512:1024], flat(Sb, 8, 16), reads=[Sb], writes=[S_b[cbw][1]])
                        xl, dl = Xl[st % 2], Dl[st % 2]
                        S.dma("sp", flat(xl, 0, 8), X_d[cf][:, 0:512], reads=[X_b[cf]], writes=[xl])
                        S.dma("sp", flat(xl, 8, 16), X_d[cbw][:, 512:1024], reads=[X_b[cbw]], writes=[xl])
                        S.dma("sp", flat(dl, 0, 8), D_d[cf][:, 0:512], reads=[D_b[cf]], writes=[dl])
                        S.dma("sp", flat(dl, 8, 16), D_d[cbw][:, 512:1024], reads=[D_b[cbw]], writes=[dl])
                        for hd in range(16):
                            p = PS[hd // 8]
                            mm(p[0:64, (hd % 8) * 64:(hd % 8 + 1) * 64], xl[:, hd, :], Sb[:, hd, :], [xl, Sb], [p],
                               inc=(hd % 8 == 7))
                        for hd in range(16):
                            e, h = hd // 8, hd % 8
                            cch = cf if e == 0 else cbw
                            blk_, ck_ = cch // 2, cch % 2
                            c0 = ((blk_ * 2 + e) * 8 + h) * 2 + ck_
                            p = PS[hd // 8]
                            stt(Sf[:, hd, :], Sf[:, hd, :], WLh[:, c0:c0 + 1], p[0:64, (hd % 8) * 64:(hd % 8 + 1) * 64],
                                ALU.mult, ALU.add, [SfB[hd], WLh, p], [SfB[hd]])
                        tt("dve", flat(Sf), flat(Sf), flat(dl), ALU.add, SfB + [dl], SfB)
                        cp("act", flat(Sb), flat(Sf), SfB, [Sb])
                    if seq_out is not None:
                        for hd in range(16):
                            p = PS[2 + hd // 8]
                            mm(p[0:64, (hd % 8) * 64:(hd % 8 + 1) * 64], Sf[:, hd, :], identf[0:64, 0:64], [SfB[hd], identf], [p],
                               inc=(hd % 8 == 7))
                        for hf in range(2):
                            cp("dve", flat(Sfin, hf * 8, hf * 8 + 8), PS[2 + hf][0:64, :], [PS[2 + hf]], [Sfin])
                        S.dma("pool", so[seq_out], Sfin[:, :, :], reads=[Sfin])

                if NLB > 0:
                    run_seq(list(range(0, 2 * NLB)), True, None)
                for cs in range(NCS):
                    b0 = 2 * (NLB + cs)
                    run_seq([b0, b0 + 1], False, cs)

            ckpt(11)
            S.barrier()
            with ExitStack() as e3:
                wo_bf = sbt(e3, "wo_bf", [128, 8, 1024], BF16)
                FG = sbt(e3, "FG", [128, 1024], F32)
                stg3 = [sbt(e3, "stg3_%d" % i, [128, 2048], F32) for i in range(2)]
                for k in range(8):
                    g = stg3[k % 2]
                    S.dma("sp", g[:], w_in[:, k, 2048:4096], writes=[g])
                    cp("act" if k % 2 == 0 else "dve", w_bf[:, k, :], g[:], [g], [w_bf])
                for k in range(8):
                    g = stg3[k % 2]
                    S.dma("sp", g[:, 0:1024], w_out[:, k, :], writes=[g])
                    cp("act" if k % 2 == 0 else "dve", wo_bf[:, k, :], g[:, 0:1024], [g], [wo_bf])
                S.dma("sp", FG[:], fg_bc, writes=[FG])
                GATE = [sbt(e3, "GATE_%d" % s, [128, 1024], F32) for s in range(2)]
                modg = sbt(e3, "modg", [2, 1024], F32)
                sel3 = [sbt(e3, "sel3_%d" % s, [2, 128], F32) for s in range(2)]
                S.dma("sp", modg[:], mod_d, reads=[mod_b], writes=[modg])
                for s in range(2):
                    ts("dve", sel3[s][:], ones[0:2, :], identf[0:2, s:s + 1], None, ALU.mult, None, [ones, identf], [sel3[s]])
                    for n in range(2):
                        p = PS[s * 2 + n]
                        mm(p[:, :], sel3[s][:], modg[:, n * 512:(n + 1) * 512], [sel3[s], modg], [p])
                        cp("act", GATE[s][:, n * 512:(n + 1) * 512], p[:, :], [p], [GATE[s]])
                hTw2 = [sbt(e3, "hTw%d" % i, [128, 8, 384], BF16) for i in range(2)]
                x3 = [sbt(e3, "x3_%d" % i, [128, 1024], F32) for i in range(2)]
                Gt = [sbt(e3, "Gt%d" % cb, [128, 256], F32) for cb in range(4)]
                tmpc = sbt(e3, "tmpc", [128, 384], F32)
                cuA = sbt(e3, "cuA", [128, 4, 66], F32)
                cuB = sbt(e3, "cuB", [128, 384], F32)
                cuC = sbt(e3, "cuC", [128, 258], F32)
                cacc = sbt(e3, "cacc", [128, 256], F32)
                catT = sbt(e3, "catT", [128, 8, 256], BF16)
                Qhl2 = [sbt(e3, "Qhl%d" % i, [64, 2048], BF16) for i in range(2)]
                Sl2 = [sbt(e3, "Sl%d" % i, [64, 1024], BF16) for i in range(2)]
                Yll2 = [sbt(e3, "Yll%d" % i, [128, 512], F32) for i in range(2)]
                Yt = sbt(e3, "Yt", [128, 512], F32)
                gnY = sbt(e3, "gnY", [128, 512], BF16)
                bst = sbt(e3, "bst", [128, 8, 6], F32)
                mv = sbt(e3, "mv", [128, 8, 2], F32)
                rs = sbt(e3, "rs", [128, 8], F32)
                sgl2 = [sbt(e3, "sgl%d" % i, [128, 4, 256], F32) for i in range(2)]
                bvl2 = [sbt(e3, "bvl%d" % i, [128, 4, 256], F32) for i in range(2)]
                yat = sbt(e3, "yat", [128, 128], F32)
                yo = sbt(e3, "yo", [128, 1024], F32)
                junk3 = sbt(e3, "junk3", [128, 1024], BF16)
                ss3 = sbt(e3, "ss3", [128, 4], F32)
                S.op("dve", lambda e: e.memset(cuA[:, :, :].rearrange("p a b -> p (a b)"), 0.0), writes=[cuA])
                S.op("dve", lambda e: e.memset(cuC[:], 0.0), writes=[cuC])

                def hflat(a, b):
                    return hTw[:, :, a:b]

                def phase3_block(blk):
                    hTw, sgl, bvl = hTw2[blk % 2], sgl2[blk % 2], bvl2[blk % 2]
                    lat = blk < NLB
                    s = 0 if lat else 1
                    tok0 = blk * 256
                    hv = lambda b_: hT_d[b_].rearrange("p (k t) -> p k t", k=8)
                    if lat:
                        if blk == 0:
                            S.op("dve", lambda e: e.memset(hTw[:, :, 0:64], 0.0), writes=[hTw])
                        else:
                            S.dma("sp", hTw[:, :, 0:64], hv(blk - 1)[:, :, 192:256], reads=[hT_b[blk - 1]], writes=[hTw])
                        if blk == NLB - 1:
                            S.op("dve", lambda e: e.memset(hTw[:, :, 320:384], 0.0), writes=[hTw])
                        else:
                            S.dma("sp", hTw[:, :, 320:384], hv(blk + 1)[:, :, 0:64], reads=[hT_b[blk + 1]], writes=[hTw])
                    S.dma("sp", hTw[:, :, 64:320], hv(blk), reads=[hT_b[blk]], writes=[hTw])
                    S.dma("sp", sgl[:, :, :].rearrange("p a b -> p (a b)"), sg_d[blk], reads=[sg_b[blk]], writes=[sgl])
                    S.dma("sp", bvl[:, :, :].rearrange("p a b -> p (a b)"), bv_d[blk], reads=[bv_b[blk]], writes=[bvl])
                    for cb in range(4):
                        pb, pg = PS[(cb % 2) * 2], PS[(cb % 2) * 2 + 1]
                        hs = slice(0, 256)
                        for k in range(8):
                            mm(pb[:, hs], w_bf[:, k, cb * 128:(cb + 1) * 128], hTw[:, k, 64:320], [w_bf, hTw], [pb],
                               start=(k == 0), stop=(k == 7), inc=(k == 7))
                        for k in range(8):
                            mm(pg[:, hs], w_bf[:, k, (12 + cb) * 128:(13 + cb) * 128], hTw[:, k, 64:320], [w_bf, hTw], [pg],
                               start=(k == 0), stop=(k == 7), inc=(k == 7))
                        sigm(tmpc[:, 0:256], pg[:, hs], [pg], [tmpc])
                        tt("dve", tmpc[:, 0:256], pg[:, hs], tmpc[:, 0:256], ALU.mult, [pg, tmpc], [tmpc])
                        tt("dve", Gt[cb][:], pb[:, hs], tmpc[:, 0:256], ALU.mult, [pb, tmpc], [Gt[cb]])
                    for cb in range(4):
                        pc, pu = PS[4 + (cb % 2)], PS[6 + (cb % 2)]
                        wide = lat and cb >= 2
                        n0, n1 = (0, 384) if wide else (64, 320)
                        N = n1 - n0
                        for k in range(8):
                            mm(pc[:, 0:N], w_bf[:, k, (4 + cb) * 128:(5 + cb) * 128], hTw[:, k, n0:n1], [w_bf, hTw], [pc],
                               start=(k == 0), stop=(k == 7), inc=(k == 7))
                        for k in range(8):
                            mm(pu[:, 0:N], w_bf[:, k, (8 + cb) * 128:(9 + cb) * 128], hTw[:, k, n0:n1], [w_bf, hTw], [pu],
                               start=(k == 0), stop=(k == 7), inc=(k == 7))
                        cp("act", tmpc[:, 0:N], pc[:, 0:N], [pc], [tmpc])
                        cw = [col(48 + j * 4 + cb) for j in range(3)]
                        if not lat:
                            tt("dve", cuC[:, 1:257], tmpc[:, 0:256], pu[:, 0:256], ALU.mult, [tmpc, pu], [cuC])
                            prev, ctr, nxt, cub = cuC[:, 0:256], cuC[:, 1:257], cuC[:, 2:258], cuC
                            accv = cacc[:]
                        elif wide:
                            tt("dve", cuB[:], tmpc[:, 0:384], pu[:, 0:384], ALU.mult, [tmpc, pu], [cuB])
                            prev, ctr, nxt, cub = cuB[:, 0:256], cuB[:, 64:320], cuB[:, 128:384], cuB
                            accv = cacc[:]
                        else:
                            tt("dve", cuA[:, :, 1:65], tmpc[:, 0:256].rearrange("p (r w) -> p r w", r=4),
                               pu[:, 0:256].rearrange("p (r w) -> p r w", r=4), ALU.mult, [tmpc, pu], [cuA])
                            prev, ctr, nxt, cub = cuA[:, :, 0:64], cuA[:, :, 1:65], cuA[:, :, 2:66], cuA
                            accv = cacc[:].rearrange("p (r w) -> p r w", r=4)
                        ts("dve", accv, ctr, cw[1], None, ALU.mult, None, [cub, cols], [cacc])
                        stt(accv, prev, cw[0], accv, ALU.mult, ALU.add, [cub, cols, cacc], [cacc])
                        stt(accv, nxt, cw[2], accv, ALU.mult, ALU.add, [cub, cols, cacc], [cacc])
                        tt("pool", catT[:, 4 + cb, :], cacc[:], Gt[cb][:], ALU.mult, [cacc, Gt[cb]], [catT])
                    for ck in range(2):
                        chunk = blk * 2 + ck
                        tc = slice(ck * 128, (ck + 1) * 128)
                        Qhl, Sl, Yll = Qhl2[ck], Sl2[ck], Yll2[ck]
                        S.dma("sp", Qhl[:], Qh_d[chunk], reads=[Qh_b[chunk]], writes=[Qhl])
                        S.dma("sp", Sl[:], S_d[chunk], reads=[S_b[chunk][0], S_b[chunk][1]], writes=[Sl])
                        S.dma("sp", Yll[:], Yl_d[chunk], reads=[Yl_b[chunk]], writes=[Yll])
                        xt = x3[ck]
                        S.dma("sp", xt[:], xs[tok0 + ck * 128:tok0 + (ck + 1) * 128, :], writes=[xt])
                        yp = PS[6]
                        for h in range(8):
                            for e in range(2):
                                mm(yp[:, h * 64:(h + 1) * 64], Qhl[:, (e * 8 + h) * 128:(e * 8 + h + 1) * 128],
                                   Sl[:, (e * 8 + h) * 64:(e * 8 + h + 1) * 64], [Qhl, Sl], [yp], start=(e == 0), stop=(e == 1),
                                   inc=(h == 7 and e == 1))
                        tt("dve", Yt[:], yp[:, :], Yll[:], ALU.add, [yp, Yll], [Yt])
                        for h in range(8):
                            S.op("dve", lambda e_: e_.bn_stats(out=bst[:, h, :], in_=Yt[:, h * 64:(h + 1) * 64]), reads=[Yt],
                                 writes=[bst])
                        for h in range(8):
                            S.op("dve", lambda e_: e_.bn_aggr(out=mv[:, h, :], in_=bst[:, h, :]), reads=[bst], writes=[mv])
                        ts("dve", rs[:], mv[:, :, 1], GN_EPS, None, ALU.add, None, [mv], [rs])
                        act(rs[:], rs[:], AF.Ln, [rs], [rs])
                        act(rs[:], rs[:], AF.Exp, [rs], [rs], scale=-0.5)
                        for h in range(8):
                            ts("dve", gnY[:, h * 64:(h + 1) * 64], Yt[:, h * 64:(h + 1) * 64], mv[:, h, 0:1], rs[:, h:h + 1],
                               ALU.subtract, ALU.mult, [Yt, mv, rs], [gnY])
                        pt = PS[7]
                        for cb in range(4):
                            mm(pt[:, cb * 128:(cb + 1) * 128], gnY[:, cb * 128:(cb + 1) * 128], identb[:], [gnY, identb], [pt],
                               inc=(cb == 3))
                        for cb in range(4):
                            ts("dve", yat[:], pt[:, cb * 128:(cb + 1) * 128], col(40 + cb), col(44 + cb), ALU.mult, ALU.add,
                               [pt, cols], [yat])
                            tt("pool", yat[:], yat[:], bvl[:, cb, tc], ALU.add, [yat, bvl], [yat])
                            tt("pool", catT[:, cb, tc], yat[:], sgl[:, cb, tc], ALU.mult, [yat, sgl], [catT])
                        for n in range(2):
                            po = PS[4 + n]
                            for m in range(8):
                                mm(po[:, :], catT[:, m, tc], wo_bf[:, m, n * 512:(n + 1) * 512], [catT, wo_bf], [po],
                                   start=(m == 0), stop=(m == 7), inc=(m == 7))
                            hs = slice(n * 512, (n + 1) * 512)
                            tt("dve", yo[:, hs], po[:, :], GATE[s][:, hs], ALU.mult, [po, GATE[s]], [yo])
                        tt("pool", yo[:], yo[:], xt[:], ALU.add, [yo, xt], [yo])
                        act(junk3[:], yo[:], AF.Square, [yo], [junk3, ss3], accum=ss3[:, 0:1])
                        ts("dve", ss3[:, 1:2], ss3[:, 0:1], 1.0 / 1024, NORM_EPS, ALU.mult, ALU.add, [ss3], [ss3])
                        act(ss3[:, 2:3], ss3[:, 1:2], AF.Ln, [ss3], [ss3])
                        act(ss3[:, 3:4], ss3[:, 2:3], AF.Exp, [ss3], [ss3], scale=-0.5)
                        stt(yo[:], yo[:], ss3[:, 3:4], FG[:], ALU.mult, ALU.mult, [yo, ss3, FG], [yo])
                        S.dma("pool", ys[tok0 + ck * 128:tok0 + (ck + 1) * 128, :], yo[:], reads=[yo])

                for blk in range(NB):
                    phase3_block(blk)
                    ckpt(12)
        except _Stop:
            pass
        S.off = False
        S.finish("sp")
        S.finish("pool")
    nc._marks = SS[0].marks
    return nc


def _prep_shared(inp):
    f = np.float32
    d = {}
    d["w_ada"] = np.ascontiguousarray(inp["w_ada"][0].reshape(8, 128, 3072).transpose(1, 0, 2), dtype=f)
    d["b_ada2"] = np.ascontiguousarray(np.stack([inp["b_ada"][0], inp["b_ada"][0]], 0), dtype=f)
    d["ngc"] = np.ascontiguousarray(np.asarray(inp["norm_g"][0], dtype=f).reshape(8, 128).T)
    d["fg_bc"] = np.ascontiguousarray(np.broadcast_to(inp["final_g"][None, :], (128, 1024)), dtype=f)
    d["w_in"] = np.ascontiguousarray(inp["w_in"][0].reshape(8, 128, 4096).transpose(1, 0, 2), dtype=f)
    ld = np.concatenate([inp["decay_down"][0, 0], inp["decay_down"][0, 1], inp["iclr_down"][0, 0], inp["iclr_down"][0, 1]],
                        axis=1)
    d["lora_dn"] = np.ascontiguousarray(ld.reshape(8, 128, 256).transpose(1, 0, 2), dtype=f)
    d["dec_up"] = np.ascontiguousarray(inp["decay_up"][0].reshape(128, 512), dtype=f)
    d["icl_up"] = np.ascontiguousarray(inp["iclr_up"][0].reshape(128, 512), dtype=f)

    def c4(v):
        return np.asarray(v, dtype=f).reshape(4, 128).T

    cl = [c4(inp["shift_mu"][0, q]) for q in range(3)]
    cl += [c4(inp["decay_w0"][0, e]) for e in range(2)]
    cl += [c4(inp["iclr_bias"][0, e]) for e in range(2)]
    cl += [c4(inp["kk_scale"][0]), c4(inp["ka_scale"][0]), c4(inp["bonus_rk"][0]), c4(inp["gn_w"][0]), c4(inp["gn_b"][0])]
    cl += [c4(inp["conv_w"][0, j]) for j in range(3)]
    d["cols"] = np.ascontiguousarray(np.concatenate(cl, axis=1), dtype=f)
    d["w_out"] = np.ascontiguousarray(inp["w_out"][0].reshape(8, 128, 1024).transpose(1, 0, 2), dtype=f)
    return d


def _core_inputs(shared, x_lat, x_ctx, c_lat, c_ctx, st):
    f = np.float32
    m = dict(shared)
    parts = []
    if x_lat is not None:
        parts.append(np.asarray(x_lat, dtype=f).reshape(-1, 1024))
    if x_ctx is not None and len(x_ctx):
        parts.append(np.asarray(x_ctx, dtype=f).reshape(-1, 1024))
    m["xs"] = np.ascontiguousarray(np.concatenate(parts, 0))
    cv = np.stack([np.asarray(c_lat, dtype=f), np.asarray(c_ctx, dtype=f)], 0)
    m["cT"] = np.ascontiguousarray(cv.reshape(2, 8, 128).transpose(2, 1, 0).reshape(128, 16))
    m["st0"] = np.ascontiguousarray(np.asarray(st, dtype=f).transpose(2, 0, 1, 3).reshape(64, 16, 64))
    return m


_PROG = {}


def kernel(**inputs):
    inp = {k: np.asarray(v) for k, v in inputs.items()}
    NCORES = 8
    NLB, NCS = 16, 4
    shared = _prep_shared(inp)
    in_maps = []
    for b in range(NCORES):
        in_maps.append(_core_inputs(shared, inp["x_sample"][b], inp["x_prompt"][4 * b:4 * b + 4], inp["c"][b], inp["c_ctx"],
                                    inp["state_wkv"][b, 0]))
    key = (NLB, NCS)
    if key not in _PROG:
        _PROG[key] = build_program(NLB, NCS)
    res = run_bass_kernel_spmd(_PROG[key], in_maps, core_ids=list(range(NCORES)))
    y_prompt = np.zeros((32, 256, 1024), np.float32)
    y_sample = np.zeros((8, 4096, 1024), np.float32)
    new_state = np.zeros((32, 1, 2, 8, 64, 64), np.float32)
    for b in range(NCORES):
        r = res.results[b]
        ysb = np.asarray(r["ys"])
        y_sample[b] = ysb[:4096]
        y_prompt[4 * b:4 * b + 4] = ysb[4096:].reshape(4, 256, 1024)
        sob = np.asarray(r["so"]).reshape(4, 64, 2, 8, 64)
        new_state[4 * b:4 * b + 4, 0] = sob.transpose(0, 2, 3, 1, 4)
    return (y_prompt, y_sample, new_state)
```

```python
import os
import numpy as np
import concourse.bass as bass
import concourse.mybir as mybir
from concourse.bass_utils import run_bass_kernel_spmd
from contextlib import ExitStack

F32, BF16, I32 = mybir.dt.float32, mybir.dt.bfloat16, mybir.dt.int32
ALU = mybir.AluOpType
AF = mybir.ActivationFunctionType
C0 = 0.6065306597126334
NORM_EPS = 1e-6
GN_EPS = 64e-5
NDS = 24
BG_INJECT = os.environ.get("BG_INJECT", "1") == "1"
PUMP_EVERY = int(os.environ.get("PUMP_EVERY", "1"))
KCUT = int(os.environ.get("KCUT", "5"))
RAW_ONLY = os.environ.get("RAW_ONLY", "0") == "1"
SELF_SKIP = tuple(os.environ.get("SELF_SKIP", "pe").split(","))


class Buf:
    __slots__ = ("w", "r")

    def __init__(self):
        self.w = None
        self.r = {}


class T:
    def __init__(self, t):
        self.t = t
        self.b = Buf()

    def __getitem__(self, idx):
        return self.t[idx]


def _b(x):
    return x.b if hasattr(x, "b") else x


class Sched:
    def __init__(self, nc, es):
        self.nc = nc
        self.eng = {"pe": nc.tensor, "dve": nc.vector, "act": nc.scalar, "pool": nc.gpsimd, "sp": nc.sync}
        self.sem = {k: es.enter_context(nc.semaphore("s_" + k)) for k in self.eng}
        self.cnt = {k: 0 for k in self.eng}
        self.dsem = [es.enter_context(nc.semaphore("d%d" % i)) for i in range(NDS)]
        self.dcnt = [0] * NDS
        self.dnext = 0
        self.dnext2 = 0
        self.waited = {}
        self.nwait = 0
        self.off = False
        self.nops = {k: 0 for k in self.eng}
        self.marks = []

    def _semh(self, key):
        return self.sem[key] if isinstance(key, str) else self.dsem[key]

    def _wait(self, e, key, val):
        if val <= 0:
            return
        if e == key and e in SELF_SKIP:
            return
        if self.waited.get((e, key), 0) >= val:
            return
        self.eng[e].wait_ge(self._semh(key), val)
        self.waited[(e, key)] = val
        self.nwait += 1

    def _deps(self, e, reads, writes):
        for b in reads:
            b = _b(b)
            if b.w:
                self._wait(e, *b.w)
        raw_only = RAW_ONLY and e in ("act", "dve")
        for b in writes:
            b = _b(b)
            if b.w and not (raw_only and b.w[0] == e):
                self._wait(e, *b.w)
            for k, v in b.r.items():
                if raw_only and k == e:
                    continue
                self._wait(e, k, v)

    def _mark(self, key, tgt, reads, writes):
        for b in writes:
            b = _b(b)
            b.w = (key, tgt)
            b.r = {}
        for b in reads:
            b = _b(b)
            if b.r.get(key, 0) < tgt:
                b.r[key] = tgt

    def op(self, e, fn, reads=(), writes=(), inc=True):
        if self.off:
            return
        self._deps(e, reads, writes)
        inst = fn(self.eng[e])
        self.nops[e] += 1
        tgt = self.cnt[e] + 1
        if inc:
            inst.then_inc(self.sem[e], 1)
            self.cnt[e] = tgt
        self._mark(e, tgt, reads, writes)

    def dma(self, e, out, in_, reads=(), writes=()):
        if self.off:
            return
        if e == "pool":
            i = NDS - 8 + self.dnext2
            self.dnext2 = (self.dnext2 + 1) % 8
        else:
            i = self.dnext
            self.dnext = (i + 1) % (NDS - 8)
        self._deps(e, reads, writes)
        self._wait(e, i, self.dcnt[i])
        self.eng[e].dma_start(out=out, in_=in_).then_inc(self.dsem[i], 16)
        self.dcnt[i] += 16
        self._mark(i, self.dcnt[i], reads, writes)

    def mark(self, label):
        self.marks.append((label, dict(self.nops)))

    def barrier(self):
        if self.off:
            return
        for e in self.eng:
            for k in self.eng:
                if k != e:
                    self._wait(e, k, self.cnt[k])
            for i in range(NDS):
                self._wait(e, i, self.dcnt[i])

    def finish(self, e="sp"):
        for i in range(NDS):
            self._wait(e, i, self.dcnt[i])


class _Stop(Exception):
    pass


def build_program(NLB, NCS, debug=False, stop=None):
    SS = []

    def ckpt(n):
        SS[0].mark(n)
        if stop is not None and n == stop:
            SS[0].off = True

    NB = NLB + NCS
    NCH = 2 * NB
    NTOK = NB * 256
    nc = bass.Bass("TRN2", target_bir_lowering=False)

    def din(name, shape, dt=F32):
        return nc.dram_tensor(name, list(shape), dt, kind="ExternalInput").ap()

    def dout(name, shape, dt=F32):
        return nc.dram_tensor(name, list(shape), dt, kind="ExternalOutput").ap()

    def dscr(name, shape, dt):
        return nc.dram_tensor(name, list(shape), dt, kind=("ExternalOutput" if debug else "Internal")).ap()

    xs = din("xs", [NTOK, 1024])
    cT = din("cT", [128, 16])
    st0 = din("st0", [64, 16, 64])
    w_ada = din("w_ada", [128, 8, 3072])
    b_ada2 = din("b_ada2", [2, 3072])
    ngc_d = din("ngc", [128, 8])
    fg_bc = din("fg_bc", [128, 1024])
    w_in = din("w_in", [128, 8, 4096])
    lora_dn = din("lora_dn", [128, 8, 256])
    dec_up = din("dec_up", [128, 512])
    icl_up = din("icl_up", [128, 512])
    cols_d = din("cols", [128, 60])
    w_out = din("w_out", [128, 8, 1024])
    ys = dout("ys", [NTOK, 1024])
    so = dout("so", [max(NCS, 1), 64, 16, 64])

    hT_d = dscr("hT_d", [NB, 128, 8 * 256], BF16)
    Yl_d = dscr("Yl_d", [NCH, 128, 512], F32)
    Qh_d = dscr("Qh_d", [NCH, 64, 2048], BF16)
    X_d = dscr("X_d", [NCH, 64, 1024], BF16)
    D_d = dscr("D_d", [NCH, 64, 1024], F32)
    sg_d = dscr("sg_d", [NB, 128, 1024], F32)
    bv_d = dscr("bv_d", [NB, 128, 1024], F32)
    S_d = dscr("S_d", [NCH, 64, 1024], BF16)
    mod_d = dscr("mod_d", [2, 1024], F32)
    mod_b = Buf()
    hT_b = [Buf() for _ in range(NB)]
    Yl_b = [Buf() for _ in range(NCH)]
    Qh_b = [Buf() for _ in range(NCH)]
    X_b = [Buf() for _ in range(NCH)]
    D_b = [Buf() for _ in range(NCH)]
    sg_b = [Buf() for _ in range(NB)]
    bv_b = [Buf() for _ in range(NB)]
    S_b = [[Buf(), Buf()] for _ in range(NCH)]
    ys_b = Buf()
    so_b = Buf()

    with ExitStack() as es:
        S = Sched(nc, es)
        SS.append(S)

        try:
            def sbt(es_, name, shape, dt):
                return T(es_.enter_context(nc.sbuf_tensor("sb_" + name, list(shape), dt)))

            PS = [T(es.enter_context(nc.psum_tensor("ps%d" % i, [128, 512], F32))) for i in range(8)]

            def mm(out, lhsT, rhs, R, W, start=True, stop=True, inc=True):
                S.op("pe", lambda e: e.matmul(out, lhsT=lhsT, rhs=rhs, start=start, stop=stop, skip_group_check=True),
                     reads=R, writes=W, inc=inc)

            def tt(eng, out, a, b, op, R, W):
                S.op(eng, lambda e: e.tensor_tensor(out=out, in0=a, in1=b, op=op), reads=R, writes=W)

            def ts(eng, out, a, s1, s2, op0, op1, R, W):
                if op1 is None:
                    S.op(eng, lambda e: e.tensor_scalar(out=out, in0=a, scalar1=s1, scalar2=None, op0=op0), reads=R, writes=W)
                else:
                    S.op(eng, lambda e: e.tensor_scalar(out=out, in0=a, scalar1=s1, scalar2=s2, op0=op0, op1=op1),
                         reads=R, writes=W)

            def stt(out, a, sc, b, op0, op1, R, W):
                S.op("dve", lambda e: e.scalar_tensor_tensor(out=out, in0=a, scalar=sc, in1=b, op0=op0, op1=op1),
                     reads=R, writes=W)

            def act(out, in_, func, R, W, bias=None, scale=None, accum=None):
                kw = {}
                if bias is not None:
                    kw["bias"] = bias
                if scale is not None:
                    kw["scale"] = scale
                if accum is not None:
                    kw["accum_out"] = accum
                S.op("act", lambda e: e.activation(out=out, in_=in_, func=func, **kw), reads=R, writes=W)

            def sigm(out, in_, R, W, nbias=None):
                act(out, in_, AF.Exp, R, W, scale=-1.0, bias=nbias)
                act(out, out, AF.Ln, W, W, bias=1.0)
                act(out, out, AF.Exp, W, W, scale=-1.0)

            def cp(eng, out, in_, R, W):
                if eng == "act":
                    S.op("act", lambda e: e.copy(out=out, in_=in_), reads=R, writes=W)
                else:
                    S.op(eng, lambda e: e.tensor_copy(out=out, in_=in_), reads=R, writes=W)

            ioi = sbt(es, "ioi", [128, 128], I32)
            iof = sbt(es, "iof", [128, 128], F32)
            identf = sbt(es, "identf", [128, 128], F32)
            identb = sbt(es, "identb", [128, 128], BF16)
            bones = sbt(es, "bones", [128, 128], F32)
            ones = sbt(es, "ones", [128, 128], F32)
            MK = {k: sbt(es, "mk_" + k, [128, 512], BF16) for k in ("LT", "GT", "LE", "GE", "NLT", "NGT")}
            cols = sbt(es, "cols", [128, 60], F32)
            dcols = sbt(es, "dcols", [128, 44], F32)
            w_bf = sbt(es, "w_bf", [128, 8, 2048], BF16)
            WLh = sbt(es, "WLh", [64, NB * 32], F32)

            S.op("pool", lambda e: e.iota(ioi[:], pattern=[[1, 128]], base=0, channel_multiplier=-1), writes=[ioi])
            cp("dve", iof[:], ioi[:], [ioi], [iof])
            ts("dve", identf[:], iof[:], 0.0, None, ALU.is_equal, None, [iof], [identf])
            cp("dve", identb[:], identf[:], [identf], [identb])
            S.op("dve", lambda e: e.memset(ones[:], 1.0), writes=[ones])
            S.op("dve", lambda e: e.memset(bones[:], 0.0), writes=[bones])
            S.op("dve", lambda e: e.memset(bones[0:64, 0:64], 1.0), writes=[bones])
            S.op("dve", lambda e: e.memset(bones[64:128, 64:128], 1.0), writes=[bones])
            for j in range(4):
                sl = slice(j * 128, (j + 1) * 128)
                ts("dve", MK["LT"][:, sl], iof[:], 0.0, None, ALU.is_gt, None, [iof], [MK["LT"]])
                ts("dve", MK["GT"][:, sl], iof[:], 0.0, None, ALU.is_lt, None, [iof], [MK["GT"]])
                ts("dve", MK["LE"][:, sl], iof[:], 0.0, None, ALU.is_ge, None, [iof], [MK["LE"]])
                ts("dve", MK["GE"][:, sl], iof[:], 0.0, None, ALU.is_le, None, [iof], [MK["GE"]])
                ts("dve", MK["NLT"][:, sl], iof[:], 0.0, -1.0, ALU.is_gt, ALU.mult, [iof], [MK["NLT"]])
                ts("dve", MK["NGT"][:, sl], iof[:], 0.0, -1.0, ALU.is_lt, ALU.mult, [iof], [MK["NGT"]])
            S.dma("sp", cols[:], cols_d, writes=[cols])
            ts("dve", dcols[:, 0:12], cols[:, 0:12], -1.0, 1.0, ALU.mult, ALU.add, [cols], [dcols])
            ts("dve", dcols[:, 12:24], cols[:, 0:12], 0.5, None, ALU.mult, None, [cols], [dcols])
            ts("dve", dcols[:, 24:28], cols[:, 32:36], -1.0, 1.0, ALU.mult, ALU.add, [cols], [dcols])
            ts("dve", dcols[:, 28:44], cols[:, 12:28], -1.0, None, ALU.mult, None, [cols], [dcols])
            ckpt(1)

            def col(i):
                return cols[:, i:i + 1]

            def dcol(i):
                return dcols[:, i:i + 1]

            with ExitStack() as e1:
                g1c = [sbt(e1, "g1c%d" % s, [128, 8], F32) for s in range(2)]
                shc = [sbt(e1, "shc%d" % s, [128, 8], F32) for s in range(2)]
                lora_bf = sbt(e1, "lora_bf", [128, 8, 256], BF16)
                dup_bf = sbt(e1, "dup_bf", [128, 512], BF16)
                iup_bf = sbt(e1, "iup_bf", [128, 512], BF16)
                Sf0 = sbt(e1, "Sf0", [64, 16, 64], F32)
                with ExitStack() as e0:
                    stg = [sbt(e0, "stg%d" % i, [128, 3072], F32) for i in range(2)]
                    ngt = sbt(e0, "ngt", [128, 8], F32)
                    cTt = sbt(e0, "cTt", [128, 16], F32)
                    scT = sbt(e0, "scT", [128, 16], F32)
                    modv = sbt(e0, "modv", [2, 3072], F32)
                    bad = sbt(e0, "bad", [2, 3072], F32)
                    st_in = sbt(e0, "st_in", [64, 16, 64], F32)
                    for k in range(8):
                        g = stg[k % 2]
                        S.dma("sp", g[:, 0:2048], w_in[:, k, 0:2048], writes=[g])
                        cp("act" if k % 2 == 0 else "dve", w_bf[:, k, :], g[:, 0:2048], [g], [w_bf])
                    g = stg[0]
                    S.dma("sp", g[:, 0:2048], lora_dn.rearrange("p k n -> p (k n)"), writes=[g])
                    cp("dve", lora_bf[:, :, :].rearrange("p k n -> p (k n)"), g[:, 0:2048], [g], [lora_bf])
                    g = stg[1]
                    S.dma("sp", g[:, 0:512], dec_up, writes=[g])
                    S.dma("sp", g[:, 512:1024], icl_up, writes=[g])
                    cp("dve", dup_bf[:], g[:, 0:512], [g], [dup_bf])
                    cp("dve", iup_bf[:], g[:, 512:1024], [g], [iup_bf])
                    ckpt(2)
                    S.dma("sp", cTt[:], cT, writes=[cTt])
                    act(scT[:], cTt[:], AF.Silu, [cTt], [scT])
                    S.dma("sp", bad[:], b_ada2, writes=[bad])
                    S.dma("sp", ngt[:], ngc_d, writes=[ngt])
                    for k in range(8):
                        g = stg[k % 2]
                        S.dma("sp", g[:], w_ada[:, k, :], writes=[g])
                        for n in range(6):
                            mm(PS[n][0:2, :], scT[:, 2 * k:2 * k + 2], g[:, n * 512:(n + 1) * 512], [scT, g], [PS[n]],
                               start=(k == 0), stop=(k == 7))
                    for n in range(6):
                        tt("dve", modv[:, n * 512:(n + 1) * 512], PS[n][0:2, :], bad[:, n * 512:(n + 1) * 512], ALU.add,
                           [PS[n], bad], [modv])
                    S.dma("sp", mod_d, modv[:, 2048:3072], reads=[modv], writes=[mod_b])
                    for part in range(2):
                        for k in range(8):
                            c0 = (part * 8 + k) * 2
                            mm(PS[0][:, c0:c0 + 2], modv[0:2, part * 1024 + k * 128:part * 1024 + (k + 1) * 128],
                               identf[0:2, 0:2], [modv, identf], [PS[0]], inc=(part == 1 and k == 7))
                    mview = PS[0][:, 0:32].rearrange("p (a k s) -> p a k s", a=2, k=8)
                    for s in range(2):
                        cp("dve", shc[s][:], mview[:, 0, :, s], [PS[0]], [shc[s]])
                        stt(g1c[s][:], mview[:, 1, :, s], 1.0, ngt[:], ALU.add, ALU.mult, [PS[0], ngt], [g1c[s]])
                    ckpt(3)
                    S.dma("sp", st_in[:], st0, writes=[st_in])
                    for hd in range(16):
                        p = PS[hd // 8]
                        mm(p[0:64, (hd % 8) * 64:(hd % 8 + 1) * 64], st_in[:, hd, :], identf[0:64, 0:64], [st_in, identf], [p],
                           inc=(hd % 8 == 7))
                    for hf in range(2):
                        cp("dve", Sf0[:, hf * 8:(hf + 1) * 8, :].rearrange("p a b -> p (a b)"), PS[hf][0:64, :], [PS[hf]], [Sf0])

                S.barrier()
                ckpt(4)
                e1b = ExitStack()
                e1b.__enter__()
                xts = [sbt(e1b, "xt0", [128, 1024], F32)] * 2
                hb = sbt(e1b, "hb", [128, 1024], BF16)
                ss = sbt(e1b, "ss", [128, 4], F32)
                hT = sbt(e1b, "hT", [128, 8, 256], BF16)
                ppL = sbt(e1b, "ppL", [128, 4, 66], F32)
                ppC = sbt(e1b, "ppC", [128, 1, 258], F32)
                nbt = sbt(e1b, "nbt", [128, 256], F32)
                RKV = [[sbt(e1b, "rkv%d_%d" % (q, cb), [128, 256], F32) for cb in range(4)] for q in range(3)]
                vbf = [sbt(e1b, "vbf%d" % cb, [128, 256], BF16) for cb in range(4)]
                sgT = sbt(e1b, "sgT", [128, 4, 256], F32)
                bvT = sbt(e1b, "bvT", [128, 4, 256], F32)
                lwd = sbt(e1b, "lwd", [128, 256], BF16)
                lwi = sbt(e1b, "lwi", [128, 256], BF16)
                TMPC = [{n: sbt(e1b, "tmc%d_%s" % (i, n), [128, 256], F32) for n in ["sq", "kk", "rkd"]} for i in range(2)]
                TMPC[0]["rn"] = TMPC[1]["rn"] = sbt(e1b, "tmc_rn", [128, 256], F32)
                TMP = TMPC[0]
                TMPE = [{n: sbt(e1b, "tm0_%s" % n, [128, 256] if n != "tcol" else [128, 4], F32)
                         for n in ["sig", "pi", "px", "E1", "E2", "E3", "E4", "a", "bq", "kd", "tcol"]}]
                TMPE.append(TMPE[0])
                WLc = [sbt(e1b, "WLc%d" % i, [128, 4], F32) for i in range(2)]
                FM4 = {n: [[[sbt(e1b, "fm_%s%d%d%d" % (n, par, e, cb), [128, 256], BF16) for cb in range(4)] for e in range(2)]
                           for par in range(2)] for n in ("Qt", "KKt", "Kt", "Bt")}
                FMs = {n: [[sbt(e1b, "fm_%s%d%d" % (n, e, cb), [128, 256], BF16) for cb in range(4)] for e in range(2)]
                       for n in ("Kh", "Bh")}

                def FMt(n, par, e, cb):
                    return FMs[n][e][cb] if n in FMs else FM4[n][par][e][cb]

                Khtm = [[[sbt(e1b, "khtm%d%d%d" % (par, ck, e), [128, 512], BF16) for e in range(2)] for ck in range(2)]
                        for par in range(2)]
                Bhtm = [[[sbt(e1b, "bhtm%d%d%d" % (par, ck, e), [128, 512], BF16) for e in range(2)] for ck in range(2)]
                        for par in range(2)]
                Vtm = [[sbt(e1b, "vtm%d%d" % (par, ck), [128, 512], BF16) for ck in range(2)] for par in range(2)]
                XT = [[sbt(e1b, "XT%d%d" % (e, i), [128, 512], F32) for i in range(2)] for e in range(2)]
                XM = [[sbt(e1b, "XM%d%d" % (e, i), [128, 512], F32) for i in range(2)] for e in range(2)]
                AakT = [sbt(e1b, "AakT%d" % e, [128, 512], BF16) for e in range(2)]
                AqbT = [sbt(e1b, "AqbT%d" % e, [128, 512], BF16) for e in range(2)]
                AqkT = [sbt(e1b, "AqkT%d" % e, [128, 512], BF16) for e in range(2)]
                Zb = [sbt(e1b, "Zb%d" % e, [128, 512], F32) for e in range(2)]
                UGn = [sbt(e1b, "UGn%d" % e, [128, 512], BF16) for e in range(2)]
                Ylt = sbt(e1b, "Ylt", [128, 512], F32)
                Qht = sbt(e1b, "Qht", [64, 2048], BF16)
                Xst = sbt(e1b, "Xst", [64, 1024], BF16)
                Dst = sbt(e1b, "Dst", [64, 1024], F32)
                S.op("dve", lambda e: e.memset(ppL[:, :, :].rearrange("p a b -> p (a b)"), 0.0), writes=[ppL])
                S.op("dve", lambda e: e.memset(ppC[:, :, :].rearrange("p a b -> p (a b)"), 0.0), writes=[ppC])

                PB = PS[7]

                def ab_gen(blk):
                    lat = blk < NLB
                    s = 0 if lat else 1
                    tok0 = blk * 256
                    for i in range(2):
                        xt = xts[i]
                        S.dma("sp", xt[:], xs[tok0 + i * 128:tok0 + (i + 1) * 128, :], writes=[xt])
                        yield
                        act(hb[:], xt[:], AF.Square, [xt], [hb, ss], accum=ss[:, 0:1])
                        ts("dve", ss[:, 1:2], ss[:, 0:1], 1.0 / 1024, NORM_EPS, ALU.mult, ALU.add, [ss], [ss])
                        act(ss[:, 2:3], ss[:, 1:2], AF.Ln, [ss], [ss])
                        act(ss[:, 3:4], ss[:, 2:3], AF.Exp, [ss], [ss], scale=-0.5)
                        yield
                        ts("dve", hb[:], xt[:], ss[:, 3:4], None, ALU.mult, None, [xt, ss], [hb])
                        yield
                        for half in range(2):
                            p = PB
                            for k4 in range(4):
                                k = half * 4 + k4
                                mm(p[:, k4 * 128:(k4 + 1) * 128], hb[:, k * 128:(k + 1) * 128], identb[:], [hb, identb], [p],
                                   inc=(k4 == 3))
                            for k4 in range(4):
                                k = half * 4 + k4
                                dsth = hT[:, k, i * 128:(i + 1) * 128]
                                srcp = p[:, k4 * 128:(k4 + 1) * 128]
                                if k4 % 2 == 0:
                                    act(dsth, srcp, AF.Identity, [p, g1c[s], shc[s]], [hT], scale=g1c[s][:, k:k + 1],
                                        bias=shc[s][:, k:k + 1])
                                else:
                                    ts("dve", dsth, srcp, g1c[s][:, k:k + 1], shc[s][:, k:k + 1], ALU.mult, ALU.add,
                                       [p, g1c[s], shc[s]], [hT])
                            yield
                    S.dma("pool", hT_d[blk], hT[:, :, :].rearrange("p k t -> p (k t)"), reads=[hT], writes=[hT_b[blk]])
                    pp = ppL if lat else ppC
                    R_, W_ = (4, 64) if lat else (1, 256)

                    def v3(ap):
                        return ap.rearrange("p (r w) -> p r w", r=R_)

                    hs = slice(0, 256)
                    for cbg in range(16):
                        p = PB
                        for k in range(8):
                            mm(p[:, hs], w_bf[:, k, cbg * 128:(cbg + 1) * 128], hT[:, k, :], [w_bf, hT], [p],
                               start=(k == 0), stop=(k == 7), inc=(k == 7))
                        q, cb = cbg // 4, cbg % 4
                        if q < 3:
                            cp("act", pp[:, :, 1:W_ + 1], v3(p[:, hs]), [p], [pp])
                            tt("dve", v3(nbt[:]), pp[:, :, 0:W_], pp[:, :, 2:W_ + 2], ALU.add, [pp], [nbt])
                            dst = RKV[q][cb]
                            act(v3(dst[:]), pp[:, :, 1:W_ + 1], AF.Identity, [pp, dcols], [dst], scale=dcol(q * 4 + cb))
                            stt(dst[:], nbt[:], dcol(12 + q * 4 + cb), dst[:], ALU.mult, ALU.add, [nbt, dcols, dst], [dst])
                            if q == 2:
                                cp("pool", vbf[cb][:], dst[:], [dst], [vbf[cb]])
                        else:
                            sigm(sgT[:, cb, :], p[:, hs], [p], [sgT])
                            tt("dve", sgT[:, cb, :], p[:, hs], sgT[:, cb, :], ALU.mult, [p, sgT], [sgT])
                        yield
                    S.dma("pool", sg_d[blk], sgT[:, :, :].rearrange("p a b -> p (a b)"), reads=[sgT], writes=[sg_b[blk]])
                    for mb in range(2):
                        p = PB
                        for k in range(8):
                            mm(p[:, hs], lora_bf[:, k, mb * 128:(mb + 1) * 128], hT[:, k, :], [lora_bf, hT], [p],
                               start=(k == 0), stop=(k == 7), inc=(k == 7))
                        if mb == 0:
                            tq = TMP["sq"]
                            act(tq[:], p[:, hs], AF.Exp, [p], [tq], scale=-2.0)
                            act(tq[:], tq[:], AF.Ln, [tq], [tq], bias=1.0)
                            act(tq[:], tq[:], AF.Exp, [tq], [tq], scale=-1.0)
                            ts("dve", lwd[:], tq[:], 2.0, -1.0, ALU.mult, ALU.add, [tq], [lwd])
                        else:
                            cp("dve", lwi[:], p[:, hs], [p], [lwi])
                        yield

                def cd_gen(blk):
                    par = blk % 2
                    p5 = PB

                    def c_kk(cb):
                        if False:
                            yield
                        k_ = RKV[1][cb]
                        T_ = TMPC[cb % 2]
                        act(T_["sq"][:], k_[:], AF.Square, [k_, cols], [T_["sq"]], scale=col(28 + cb))
                        mm(p5[:, 0:256], bones[:], T_["sq"][:], [bones, T_["sq"]], [p5])
                        ts("dve", T_["rn"][:], p5[:, 0:256], 1e-24, None, ALU.max, None, [p5], [T_["rn"]])
                        act(T_["rn"][:], T_["rn"][:], AF.Ln, [T_["rn"]], [T_["rn"]])
                        act(T_["rn"][:], T_["rn"][:], AF.Exp, [T_["rn"]], [T_["rn"]], scale=-0.5)
                        stt(T_["kk"][:], k_[:], col(28 + cb), T_["rn"][:], ALU.mult, ALU.mult, [k_, cols, T_["rn"]],
                            [T_["kk"]])

                    def c_front(cb, e):
                        TE = TMPE[e]
                        es_ = slice(e * 64, (e + 1) * 64)
                        pz = PB
                        mm(pz[:, 0:256], dup_bf[es_, cb * 128:(cb + 1) * 128], lwd[es_, :], [dup_bf, lwd], [pz])
                        mm(pz[:, 256:512], iup_bf[es_, cb * 128:(cb + 1) * 128], lwi[es_, :], [iup_bf, lwi], [pz])
                        sig, pi, px, tcl = TE["sig"], TE["pi"], TE["px"], TE["tcol"]
                        act(sig[:], pz[:, 0:256], AF.Exp, [pz, dcols], [sig], scale=-1.0, bias=dcol(28 + e * 4 + cb))
                        act(TE["a"][:], pz[:, 256:512], AF.Exp, [pz, dcols], [TE["a"]], scale=-1.0, bias=dcol(36 + e * 4 + cb))
                        act(sig[:], sig[:], AF.Ln, [sig], [sig], bias=1.0)
                        act(sig[:], sig[:], AF.Exp, [sig], [sig], scale=-1.0)
                        yield
                        for ck in range(2):
                            tc = slice(ck * 128, (ck + 1) * 128)
                            S.op("dve", lambda e_: e_.tensor_tensor_scan(out=pi[:, tc], data0=ones[:, 0:128],
                                                                         data1=sig[:, tc], initial=0.0,
                                                                         op0=ALU.mult, op1=ALU.add),
                                 reads=[ones, sig], writes=[pi])
                        tt("dve", px[:], pi[:], sig[:], ALU.subtract, [pi, sig], [px])
                        ts("dve", tcl[:, 0:2], pi[:, 127:256:128], -C0, None, ALU.mult, None, [pi], [tcl])
                        ts("dve", tcl[:, 2:4], pi[:, 127:256:128], C0, None, ALU.mult, None, [pi], [tcl])
                        yield
                        WL_ = WLc[cb % 2]
                        act(WL_[:, e * 2:e * 2 + 2], tcl[:, 0:2], AF.Exp, [tcl], [WL_])
                        E1, E2, E3, E4 = TE["E1"], TE["E2"], TE["E3"], TE["E4"]
                        if e == 0:
                            act(E1[:], pi[:], AF.Exp, [pi], [E1], scale=-C0)
                            act(E2[:], px[:], AF.Exp, [px], [E2], scale=-C0)
                            act(E3[:], pi[:], AF.Exp, [pi], [E3], scale=C0)
                            for ck in range(2):
                                tc = slice(ck * 128, (ck + 1) * 128)
                                act(E4[:, tc], pi[:, tc], AF.Exp, [pi, tcl], [E4], scale=C0, bias=tcl[:, ck:ck + 1])
                        else:
                            for ck in range(2):
                                tc = slice(ck * 128, (ck + 1) * 128)
                                act(E1[:, tc], px[:, tc], AF.Exp, [px, tcl], [E1], scale=C0, bias=tcl[:, ck:ck + 1])
                                act(E2[:, tc], pi[:, tc], AF.Exp, [pi, tcl], [E2], scale=C0, bias=tcl[:, ck:ck + 1])
                                act(E3[:, tc], px[:, tc], AF.Exp, [px, tcl], [E3], scale=-C0, bias=tcl[:, 2 + ck:3 + ck])
                            act(E4[:], px[:], AF.Exp, [px], [E4], scale=-C0)
                        yield
                        act(TE["a"][:], TE["a"][:], AF.Ln, [TE["a"]], [TE["a"]], bias=1.0)
                        act(TE["a"][:], TE["a"][:], AF.Exp, [TE["a"]], [TE["a"]], scale=-1.0)

                    def c_back(cb, e):
                        TE = TMPE[e]
                        T_ = TMPC[cb % 2]
                        r_, k_ = RKV[0][cb], RKV[1][cb]
                        E1, E2, E3, E4 = TE["E1"], TE["E2"], TE["E3"], TE["E4"]
                        a_, bq, kd, rkd = TE["a"], TE["bq"], TE["kd"], T_["rkd"]
                        tt("pool", bq[:], T_["kk"][:], a_[:], ALU.mult, [T_["kk"], a_], [bq])
                        ts("dve", kd[:], a_[:], col(32 + cb), dcol(24 + cb), ALU.mult, ALU.add, [a_, cols, dcols], [kd])
                        tt("dve", kd[:], kd[:], k_[:], ALU.mult, [kd, k_], [kd])
                        f = lambda n: FMt(n, par, e, cb)
                        tt("dve", f("Qt")[:], r_[:], E1[:], ALU.mult, [r_, E1], [f("Qt")])
                        tt("pool", f("KKt")[:], T_["kk"][:], E2[:], ALU.mult, [T_["kk"], E2], [f("KKt")])
                        tt("dve", f("Kt")[:], kd[:], E3[:], ALU.mult, [kd, E3], [f("Kt")])
                        yield
                        tt("pool", f("Bt")[:], bq[:], E3[:], ALU.mult, [bq, E3], [f("Bt")])
                        tt("dve", f("Kh")[:], kd[:], E4[:], ALU.mult, [kd, E4], [f("Kh")])
                        tt("pool", f("Bh")[:], bq[:], E4[:], ALU.mult, [bq, E4], [f("Bh")])
                        if e == 0:
                            tt("dve", rkd[:], r_[:], kd[:], ALU.mult, [r_, kd], [rkd])
                        else:
                            tt("dve", T_["sq"][:], r_[:], kd[:], ALU.mult, [r_, kd], [T_["sq"]])
                            stt(rkd[:], rkd[:], 1.0, T_["sq"][:], ALU.mult, ALU.add, [rkd, T_["sq"]], [rkd])

                    def c_tail(cb):
                        if False:
                            yield
                        T_ = TMPC[cb % 2]
                        v_ = RKV[2][cb]
                        rkd = T_["rkd"]
                        WL_ = WLc[cb % 2]
                        for hh in range(2):
                            mm(p5[0:64, 256 + hh * 4:256 + hh * 4 + 4], identf[:, hh * 64:(hh + 1) * 64], WL_[:, 0:4],
                               [identf, WL_], [p5])
                        for hh in range(2):
                            h = cb * 2 + hh
                            for e in range(2):
                                c0 = ((blk * 2 + e) * 8 + h) * 2
                                cp("dve", WLh[:, c0:c0 + 2], p5[0:64, 256 + hh * 4 + e * 2:256 + hh * 4 + e * 2 + 2], [p5], [WLh])
                        ts("dve", rkd[:], rkd[:], col(36 + cb), None, ALU.mult, None, [rkd, cols], [rkd])
                        mm(p5[:, 0:256], bones[:], rkd[:], [bones, rkd], [p5])
                        tt("dve", bvT[:, cb, :], p5[:, 0:256], v_[:], ALU.mult, [p5, v_], [bvT])

                    for cb in range(4):
                        yield from c_kk(cb)
                        yield
                        for e in range(2):
                            yield from c_front(cb, e)
                            yield
                            yield from c_back(cb, e)
                            yield
                        yield from c_tail(cb)
                        yield
                    S.dma("pool", bv_d[blk], bvT[:, :, :].rearrange("p a b -> p (a b)"), reads=[bvT], writes=[bv_b[blk]])

                    for ck in range(2):
                        tc = slice(ck * 128, (ck + 1) * 128)
                        jobs = [(Vtm[par][ck], vbf)] + [(Khtm[par][ck][e], FMs["Kh"][e]) for e in range(2)] + \
                               [(Bhtm[par][ck][e], FMs["Bh"][e]) for e in range(2)]
                        for ji, (dst, src) in enumerate(jobs):
                            p = PB
                            for cb in range(4):
                                mm(p[:, cb * 128:(cb + 1) * 128], src[cb][:, tc], identb[:], [src[cb], identb], [p],
                                   inc=(cb == 3))
                            cp("act" if ji % 2 == 0 else "dve", dst[:], p[:, :], [p], [dst])
                            yield


                def abcd_gen(blk):
                    yield from ab_gen(blk)
                    yield from cd_gen(blk)

                bg = {}

                def pump():
                    g = bg.get("g")
                    if g is not None:
                        try:
                            next(g)
                        except StopIteration:
                            bg["g"] = None

                def phase1_block(blk):
                    lat = blk < NLB
                    s = 0 if lat else 1
                    tok0 = blk * 256
                    if blk == 0:
                        bg["g"] = abcd_gen(0)
                    while bg.get("g") is not None:
                        pump()
                    ckpt(5)
                    if blk + 1 < NB and BG_INJECT:
                        bg["g"] = abcd_gen(blk + 1)
                    npump = [0]
                    par = blk % 2
                    ckpt(8)
                    for ck in range(2):
                        chunk = blk * 2 + ck
                        tc = slice(ck * 128, (ck + 1) * 128)
                        yps = PS[6]
                        for hg in range(2):
                            def fm(n, e, hq):
                                return FM4[n][par][e][hq][hg * 64:(hg + 1) * 64, tc]

                            def fmR(n, e, hq):
                                return [FM4[n][par][e][hq]]

                            def chain(e):
                                bA, bB, zps = PS[3 * e], PS[3 * e + 1], PS[3 * e + 2]
                                if e == 0:
                                    mSTn, mSn, mST, mIT = MK["NLT"], MK["NGT"], MK["LT"], MK["LE"]
                                else:
                                    mSTn, mSn, mST, mIT = MK["NGT"], MK["NLT"], MK["GT"], MK["GE"]
                                hsl = lambda hq: slice(hq * 128, (hq + 1) * 128)
                                for hq in range(4):
                                    mm(bA[:, hsl(hq)], fm("Bt", e, hq), fm("KKt", e, hq), fmR("Bt", e, hq) + fmR("KKt", e, hq),
                                       [bA], inc=(hq == 3))
                                tt("dve", XT[e][0][:], bA[:, :], mSTn[:], ALU.mult, [bA, mSTn], [XT[e][0]])
                                for hq in range(4):
                                    mm(bB[:, hsl(hq)], fm("KKt", e, hq), fm("Bt", e, hq), fmR("Bt", e, hq) + fmR("KKt", e, hq),
                                       [bB], inc=(hq == 3))
                                tt("dve", XM[e][0][:], bB[:, :], mSn[:], ALU.mult, [bB, mSn], [XM[e][0]])
                                yield
                                for hq in range(4):
                                    mm(bA[:, hsl(hq)], fm("Kt", e, hq), fm("KKt", e, hq), fmR("Kt", e, hq) + fmR("KKt", e, hq),
                                       [bA], inc=(hq == 3))
                                tt("dve", AakT[e][:], bA[:, :], mST[:], ALU.mult, [bA, mST], [AakT[e]])
                                for hq in range(4):
                                    h = hq * 2 + hg
                                    hh = hg
                                    mm(zps[:, hq * 128:hq * 128 + 64], AakT[e][:, hsl(hq)], Vtm[par][ck][:, h * 64:(h + 1) * 64],
                                       [AakT[e], Vtm[par][ck]], [zps], start=(hq == 0), stop=False, inc=False)
                                    mm(zps[:, hq * 128 + 64:(hq + 1) * 128], fm("KKt", e, hq),
                                       identb[hh * 64:(hh + 1) * 64, hh * 64:(hh + 1) * 64], fmR("KKt", e, hq) + [identb], [zps],
                                       start=False, stop=False, inc=(hq == 3))
                                cp("act", Zb[e][:], zps[:, :], [zps], [Zb[e]])
                                yield
                                for hq in range(4):
                                    mm(bB[:, hsl(hq)], fm("Bt", e, hq), fm("Qt", e, hq), fmR("Bt", e, hq) + fmR("Qt", e, hq),
                                       [bB], inc=(hq == 3))
                                cp("act", AqbT[e][:], bB[:, :], [bB], [AqbT[e]])
                                tt("pool", AqbT[e][:], AqbT[e][:], mIT[:], ALU.mult, [AqbT[e], mIT], [AqbT[e]])
                                for hq in range(4):
                                    mm(bA[:, hsl(hq)], fm("Kt", e, hq), fm("Qt", e, hq), fmR("Kt", e, hq) + fmR("Qt", e, hq),
                                       [bA], inc=(hq == 3))
                                cp("act", AqkT[e][:], bA[:, :], [bA], [AqkT[e]])
                                tt("pool", AqkT[e][:], AqkT[e][:], mIT[:], ALU.mult, [AqkT[e], mIT], [AqkT[e]])
                                yield
                                def xtv(i, lev_):
                                    if lev_ < KCUT:
                                        return XT[e][i][:, :], XM[e][i][:, :]
                                    return XT[e][i][:, :].bitcast(BF16)[:, 0:512], XM[e][i][:, :].bitcast(BF16)[:, 0:512]

                                for lev in range(7):
                                    cur, nxt = lev % 2, (lev + 1) % 2
                                    xt_c, xm_c = xtv(cur, lev)
                                    zsrc = Zb[e] if lev < KCUT else AakT[e]
                                    for hq in range(4):
                                        mm(zps[:, hsl(hq)], xt_c[:, hsl(hq)], zsrc[:, hsl(hq)], [XT[e][cur], zsrc], [zps],
                                           start=False, stop=(lev == 6), inc=(hq == 3))
                                    if lev < 6:
                                        xt_n, xm_n = xtv(nxt, lev + 1)
                                        for hq in range(4):
                                            mm(bA[:, hsl(hq)], xm_c[:, hsl(hq)], xt_c[:, hsl(hq)],
                                               [XM[e][cur], XT[e][cur]], [bA], inc=(hq == 3))
                                        cp("dve", xt_n, bA[:, :], [bA], [XT[e][nxt]])
                                        if lev < 5:
                                            for hq in range(4):
                                                mm(bB[:, hsl(hq)], xt_c[:, hsl(hq)], xm_c[:, hsl(hq)],
                                                   [XM[e][cur], XT[e][cur]], [bB], inc=(hq == 3))
                                            cp("act", xm_n, bB[:, :], [bB], [XM[e][nxt]])
                                        zdst = Zb[e] if lev + 1 < KCUT else AakT[e]
                                        cp("act", zdst[:], zps[:, :], [zps], [zdst])
                                    yield
                                act(UGn[e][:], zps[:, :], AF.Identity, [zps], [UGn[e]], scale=-1.0)
                                for hq in range(4):
                                    h = hq * 2 + hg
                                    hh = hg
                                    hc = slice(h * 64, (h + 1) * 64)
                                    mm(yps[:, hc], AqbT[e][:, hsl(hq)], UGn[e][:, hq * 128:hq * 128 + 64], [AqbT[e], UGn[e]],
                                       [yps], start=(e == 0 and hg == 0 and hq == 0), stop=False, inc=False)
                                    mm(yps[:, hc], AqkT[e][:, hsl(hq)], Vtm[par][ck][:, hc], [AqkT[e], Vtm[par][ck]], [yps],
                                       start=False, stop=(e == 1), inc=(hq == 3))
                                for hq in range(4):
                                    hh = hg
                                    mm(bA[0:64, hsl(hq)], identb[hh * 64:(hh + 1) * 64, hh * 64:(hh + 1) * 64], fm("Qt", e, hq),
                                       fmR("Qt", e, hq) + [identb], [bA], start=True, stop=False, inc=False)
                                    mm(bA[0:64, hsl(hq)], UGn[e][:, hq * 128 + 64:(hq + 1) * 128], AqbT[e][:, hsl(hq)],
                                       [UGn[e], AqbT[e]], [bA], start=False, stop=True, inc=(hq == 3))
                                cp("act", Qht[:, :].rearrange("p (e q g t) -> p e q g t", e=2, q=4, g=2)[:, e, :, hg, :],
                                   bA[0:64, :].rearrange("p (q t) -> p q t", q=4), [bA], [Qht])
                                for hq in range(4):
                                    h = hq * 2 + hg
                                    hc = slice(h * 64, (h + 1) * 64)
                                    mm(bB[0:64, hq * 64:(hq + 1) * 64], UGn[e][:, hq * 128 + 64:(hq + 1) * 128], Bhtm[par][ck][e][:, hc],
                                       [UGn[e], Bhtm[par][ck][e]], [bB], inc=False)
                                    mm(bB[0:64, 256 + hq * 64:256 + (hq + 1) * 64], Bhtm[par][ck][e][:, hc],
                                       UGn[e][:, hq * 128:hq * 128 + 64], [UGn[e], Bhtm[par][ck][e]], [bB], start=True, stop=False,
                                       inc=False)
                                    mm(bB[0:64, 256 + hq * 64:256 + (hq + 1) * 64], Khtm[par][ck][e][:, hc], Vtm[par][ck][:, hc],
                                       [Khtm[par][ck][e], Vtm[par][ck]], [bB], start=False, stop=True, inc=(hq == 3))
                                cp("dve", Xst[:, :].rearrange("p (e q g c) -> p e q g c", e=2, q=4, g=2)[:, e, :, hg, :],
                                   bB[0:64, 0:256].rearrange("p (q c) -> p q c", q=4), [bB], [Xst])
                                cp("dve", Dst[:, :].rearrange("p (e q g c) -> p e q g c", e=2, q=4, g=2)[:, e, :, hg, :],
                                   bB[0:64, 256:512].rearrange("p (q c) -> p q c", q=4), [bB], [Dst])
                                yield

                            gens = [chain(0), chain(1)]
                            alive = [True, True]
                            while any(alive):
                                for gi in range(2):
                                    if alive[gi]:
                                        try:
                                            next(gens[gi])
                                        except StopIteration:
                                            alive[gi] = False
                                        npump[0] += 1
                                        if npump[0] % PUMP_EVERY == 0:
                                            pump()
                        cp("act", Ylt[:], yps[:, :], [yps], [Ylt])
                        S.dma("pool", Yl_d[chunk], Ylt[:], reads=[Ylt], writes=[Yl_b[chunk]])
                        S.dma("pool", Qh_d[chunk], Qht[:], reads=[Qht], writes=[Qh_b[chunk]])
                        S.dma("pool", X_d[chunk], Xst[:], reads=[Xst], writes=[X_b[chunk]])
                        S.dma("pool", D_d[chunk], Dst[:], reads=[Dst], writes=[D_b[chunk]])

                for blk in range(NB):
                    phase1_block(blk)
                    if blk + 1 < NB and not BG_INJECT:
                        bg["g"] = abcd_gen(blk + 1)
                    ckpt(9)

                e1b.close()
                S.barrier()
                ckpt(10)
                Sf = sbt(e1, "Sf", [64, 16, 64], F32)
                Sb = sbt(e1, "Sb", [64, 16, 64], BF16)
                Xl = [sbt(e1, "Xl%d" % i, [64, 16, 64], BF16) for i in range(2)]
                Dl = [sbt(e1, "Dl%d" % i, [64, 16, 64], F32) for i in range(2)]
                Sfin = sbt(e1, "Sfin", [64, 16, 64], F32)
                SfB = [Buf() for _ in range(16)]

                def flat(t_, a=None, b=None):
                    ap = t_[:, :, :] if a is None else t_[:, a:b, :]
                    return ap.rearrange("p a b -> p (a b)")

                def run_seq(chunks, init_from_state, seq_out):
                    n = len(chunks)
                    if init_from_state:
                        cp("dve", flat(Sf), flat(Sf0), [Sf0], SfB)
                    else:
                        S.op("dve", lambda e: e.memset(flat(Sf), 0.0), writes=SfB)
                    cp("dve", flat(Sb), flat(Sf), SfB, [Sb])
                    for st in range(n):
                        cf, cbw = chunks[st], chunks[n - 1 - st]
                        S.dma("pool", S_d[cf][:, 0:512], flat(Sb, 0, 8), reads=[Sb], writes=[S_b[cf][0]])
                        S.dma("pool", S_d[cbw][:, 512:1024], flat(Sb, 8, 16), reads=[Sb], writes=[S_b[cbw][1]])
                        xl, dl = Xl[st % 2], Dl[st % 2]
                        S.dma("sp", flat(xl, 0, 8), X_d[cf][:, 0:512], reads=[X_b[cf]], writes=[xl])
                        S.dma("sp", flat(xl, 8, 16), X_d[cbw][:, 512:1024], reads=[X_b[cbw]], writes=[xl])
                        S.dma("sp", flat(dl, 0, 8), D_d[cf][:, 0:512], reads=[D_b[cf]], writes=[dl])
                        S.dma("sp", flat(dl, 8, 16), D_d[cbw][:, 512:1024], reads=[D_b[cbw]], writes=[dl])
                        for hd in range(16):
                            p = PS[hd // 8]
                            mm(p[0:64, (hd % 8) * 64:(hd % 8 + 1) * 64], xl[:, hd, :], Sb[:, hd, :], [xl, Sb], [p],
                               inc=(hd % 8 == 7))
                        for hd in range(16):
                            e, h = hd // 8, hd % 8
                            cch = cf if e == 0 else cbw
                            blk_, ck_ = cch // 2, cch % 2
                            c0 = ((blk_ * 2 + e) * 8 + h) * 2 + ck_
                            p = PS[hd // 8]
                            stt(Sf[:, hd, :], Sf[:, hd, :], WLh[:, c0:c0 + 1], p[0:64, (hd % 8) * 64:(hd % 8 + 1) * 64],
                                ALU.mult, ALU.add, [SfB[hd], WLh, p], [SfB[hd]])
                        tt("dve", flat(Sf), flat(Sf), flat(dl), ALU.add, SfB + [dl], SfB)
                        cp("act", flat(Sb), flat(Sf), SfB, [Sb])
                    if seq_out is not None:
                        for hd in range(16):
                            p = PS[2 + hd // 8]
                            mm(p[0:64, (hd % 8) * 64:(hd % 8 + 1) * 64], Sf[:, hd, :], identf[0:64, 0:64], [SfB[hd], identf], [p],
                               inc=(hd % 8 == 7))
                        for hf in range(2):
                            cp("dve", flat(Sfin, hf * 8, hf * 8 + 8), PS[2 + hf][0:64, :], [PS[2 + hf]], [Sfin])
                        S.dma("pool", so[seq_out], Sfin[:, :, :], reads=[Sfin])

                if NLB > 0:
                    run_seq(list(range(0, 2 * NLB)), True, None)
                for cs in range(NCS):
                    b0 = 2 * (NLB + cs)
                    run_seq([b0, b0 + 1], False, cs)

            ckpt(11)
            S.barrier()
            with ExitStack() as e3:
                wo_bf = sbt(e3, "wo_bf", [128, 8, 1024], BF16)
                FG = sbt(e3, "FG", [128, 1024], F32)
                stg3 = [sbt(e3, "stg3_%d" % i, [128, 2048], F32) for i in range(2)]
                for k in range(8):
                    g = stg3[k % 2]
                    S.dma("sp", g[:], w_in[:, k, 2048:4096], writes=[g])
                    cp("act" if k % 2 == 0 else "dve", w_bf[:, k, :], g[:], [g], [w_bf])
                for k in range(8):
                    g = stg3[k % 2]
                    S.dma("sp", g[:, 0:1024], w_out[:, k, :], writes=[g])
                    cp("act" if k % 2 == 0 else "dve", wo_bf[:, k, :], g[:, 0:1024], [g], [wo_bf])
                S.dma("sp", FG[:], fg_bc, writes=[FG])
                GATE = [sbt(e3, "GATE_%d" % s, [128, 1024], F32) for s in range(2)]
                modg = sbt(e3, "modg", [2, 1024], F32)
                sel3 = [sbt(e3, "sel3_%d" % s, [2, 128], F32) for s in range(2)]
                S.dma("sp", modg[:], mod_d, reads=[mod_b], writes=[modg])
                for s in range(2):
                    ts("dve", sel3[s][:], ones[0:2, :], identf[0:2, s:s + 1], None, ALU.mult, None, [ones, identf], [sel3[s]])
                    for n in range(2):
                        p = PS[s * 2 + n]
                        mm(p[:, :], sel3[s][:], modg[:, n * 512:(n + 1) * 512], [sel3[s], modg], [p])
                        cp("act", GATE[s][:, n * 512:(n + 1) * 512], p[:, :], [p], [GATE[s]])
                hTw2 = [sbt(e3, "hTw%d" % i, [128, 8, 384], BF16) for i in range(2)]
                x3 = [sbt(e3, "x3_%d" % i, [128, 1024], F32) for i in range(2)]
                Gt = [sbt(e3, "Gt%d" % cb, [128, 256], F32) for cb in range(4)]
                tmpc = sbt(e3, "tmpc", [128, 384], F32)
                cuA = sbt(e3, "cuA", [128, 4, 66], F32)
                cuB = sbt(e3, "cuB", [128, 384], F32)
                cuC = sbt(e3, "cuC", [128, 258], F32)
                cacc = sbt(e3, "cacc", [128, 256], F32)
                catT = sbt(e3, "catT", [128, 8, 256], BF16)
                Qhl2 = [sbt(e3, "Qhl%d" % i, [64, 2048], BF16) for i in range(2)]
                Sl2 = [sbt(e3, "Sl%d" % i, [64, 1024], BF16) for i in range(2)]
                Yll2 = [sbt(e3, "Yll%d" % i, [128, 512], F32) for i in range(2)]
                Yt = sbt(e3, "Yt", [128, 512], F32)
                gnY = sbt(e3, "gnY", [128, 512], BF16)
                bst = sbt(e3, "bst", [128, 8, 6], F32)
                mv = sbt(e3, "mv", [128, 8, 2], F32)
                rs = sbt(e3, "rs", [128, 8], F32)
                sgl2 = [sbt(e3, "sgl%d" % i, [128, 4, 256], F32) for i in range(2)]
                bvl2 = [sbt(e3, "bvl%d" % i, [128, 4, 256], F32) for i in range(2)]
                yat = sbt(e3, "yat", [128, 128], F32)
                yo = sbt(e3, "yo", [128, 1024], F32)
                junk3 = sbt(e3, "junk3", [128, 1024], BF16)
                ss3 = sbt(e3, "ss3", [128, 4], F32)
                S.op("dve", lambda e: e.memset(cuA[:, :, :].rearrange("p a b -> p (a b)"), 0.0), writes=[cuA])
                S.op("dve", lambda e: e.memset(cuC[:], 0.0), writes=[cuC])

                def hflat(a, b):
                    return hTw[:, :, a:b]

                def phase3_block(blk):
                    hTw, sgl, bvl = hTw2[blk % 2], sgl2[blk % 2], bvl2[blk % 2]
                    lat = blk < NLB
                    s = 0 if lat else 1
                    tok0 = blk * 256
                    hv = lambda b_: hT_d[b_].rearrange("p (k t) -> p k t", k=8)
                    if lat:
                        if blk == 0:
                            S.op("dve", lambda e: e.memset(hTw[:, :, 0:64], 0.0), writes=[hTw])
                        else:
                            S.dma("sp", hTw[:, :, 0:64], hv(blk - 1)[:, :, 192:256], reads=[hT_b[blk - 1]], writes=[hTw])
                        if blk == NLB - 1:
                            S.op("dve", lambda e: e.memset(hTw[:, :, 320:384], 0.0), writes=[hTw])
                        else:
                            S.dma("sp", hTw[:, :, 320:384], hv(blk + 1)[:, :, 0:64], reads=[hT_b[blk + 1]], writes=[hTw])
                    S.dma("sp", hTw[:, :, 64:320], hv(blk), reads=[hT_b[blk]], writes=[hTw])
                    S.dma("sp", sgl[:, :, :].rearrange("p a b -> p (a b)"), sg_d[blk], reads=[sg_b[blk]], writes=[sgl])
                    S.dma("sp", bvl[:, :, :].rearrange("p a b -> p (a b)"), bv_d[blk], reads=[bv_b[blk]], writes=[bvl])
                    for cb in range(4):
                        pb, pg = PS[(cb % 2) * 2], PS[(cb % 2) * 2 + 1]
                        hs = slice(0, 256)
                        for k in range(8):
                            mm(pb[:, hs], w_bf[:, k, cb * 128:(cb + 1) * 128], hTw[:, k, 64:320], [w_bf, hTw], [pb],
                               start=(k == 0), stop=(k == 7), inc=(k == 7))
                        for k in range(8):
                            mm(pg[:, hs], w_bf[:, k, (12 + cb) * 128:(13 + cb) * 128], hTw[:, k, 64:320], [w_bf, hTw], [pg],
                               start=(k == 0), stop=(k == 7), inc=(k == 7))
                        sigm(tmpc[:, 0:256], pg[:, hs], [pg], [tmpc])
                        tt("dve", tmpc[:, 0:256], pg[:, hs], tmpc[:, 0:256], ALU.mult, [pg, tmpc], [tmpc])
                        tt("dve", Gt[cb][:], pb[:, hs], tmpc[:, 0:256], ALU.mult, [pb, tmpc], [Gt[cb]])
                    for cb in range(4):
                        pc, pu = PS[4 + (cb % 2)], PS[6 + (cb % 2)]
                        wide = lat and cb >= 2
                        n0, n1 = (0, 384) if wide else (64, 320)
                        N = n1 - n0
                        for k in range(8):
                            mm(pc[:, 0:N], w_bf[:, k, (4 + cb) * 128:(5 + cb) * 128], hTw[:, k, n0:n1], [w_bf, hTw], [pc],
                               start=(k == 0), stop=(k == 7), inc=(k == 7))
                        for k in range(8):
                            mm(pu[:, 0:N], w_bf[:, k, (8 + cb) * 128:(9 + cb) * 128], hTw[:, k, n0:n1], [w_bf, hTw], [pu],
                               start=(k == 0), stop=(k == 7), inc=(k == 7))
                        cp("act", tmpc[:, 0:N], pc[:, 0:N], [pc], [tmpc])
                        cw = [col(48 + j * 4 + cb) for j in range(3)]
                        if not lat:
                            tt("dve", cuC[:, 1:257], tmpc[:, 0:256], pu[:, 0:256], ALU.mult, [tmpc, pu], [cuC])
                            prev, ctr, nxt, cub = cuC[:, 0:256], cuC[:, 1:257], cuC[:, 2:258], cuC
                            accv = cacc[:]
                        elif wide:
                            tt("dve", cuB[:], tmpc[:, 0:384], pu[:, 0:384], ALU.mult, [tmpc, pu], [cuB])
                            prev, ctr, nxt, cub = cuB[:, 0:256], cuB[:, 64:320], cuB[:, 128:384], cuB
                            accv = cacc[:]
                        else:
                            tt("dve", cuA[:, :, 1:65], tmpc[:, 0:256].rearrange("p (r w) -> p r w", r=4),
                               pu[:, 0:256].rearrange("p (r w) -> p r w", r=4), ALU.mult, [tmpc, pu], [cuA])
                            prev, ctr, nxt, cub = cuA[:, :, 0:64], cuA[:, :, 1:65], cuA[:, :, 2:66], cuA
                            accv = cacc[:].rearrange("p (r w) -> p r w", r=4)
                        ts("dve", accv, ctr, cw[1], None, ALU.mult, None, [cub, cols], [cacc])
                        stt(accv, prev, cw[0], accv, ALU.mult, ALU.add, [cub, cols, cacc], [cacc])
                        stt(accv, nxt, cw[2], accv, ALU.mult, ALU.add, [cub, cols, cacc], [cacc])
                        tt("pool", catT[:, 4 + cb, :], cacc[:], Gt[cb][:], ALU.mult, [cacc, Gt[cb]], [catT])
                    for ck in range(2):
                        chunk = blk * 2 + ck
                        tc = slice(ck * 128, (ck + 1) * 128)
                        Qhl, Sl, Yll = Qhl2[ck], Sl2[ck], Yll2[ck]
                        S.dma("sp", Qhl[:], Qh_d[chunk], reads=[Qh_b[chunk]], writes=[Qhl])
                        S.dma("sp", Sl[:], S_d[chunk], reads=[S_b[chunk][0], S_b[chunk][1]], writes=[Sl])
                        S.dma("sp", Yll[:], Yl_d[chunk], reads=[Yl_b[chunk]], writes=[Yll])
                        xt = x3[ck]
                        S.dma("sp", xt[:], xs[tok0 + ck * 128:tok0 + (ck + 1) * 128, :], writes=[xt])
                        yp = PS[6]
                        for h in range(8):
                            for e in range(2):
                                mm(yp[:, h * 64:(h + 1) * 64], Qhl[:, (e * 8 + h) * 128:(e * 8 + h + 1) * 128],
                                   Sl[:, (e * 8 + h) * 64:(e * 8 + h + 1) * 64], [Qhl, Sl], [yp], start=(e == 0), stop=(e == 1),
                                   inc=(h == 7 and e == 1))
                        tt("dve", Yt[:], yp[:, :], Yll[:], ALU.add, [yp, Yll], [Yt])
                        for h in range(8):
                            S.op("dve", lambda e_: e_.bn_stats(out=bst[:, h, :], in_=Yt[:, h * 64:(h + 1) * 64]), reads=[Yt],
                                 writes=[bst])
                        for h in range(8):
                            S.op("dve", lambda e_: e_.bn_aggr(out=mv[:, h, :], in_=bst[:, h, :]), reads=[bst], writes=[mv])
                        ts("dve", rs[:], mv[:, :, 1], GN_EPS, None, ALU.add, None, [mv], [rs])
                        act(rs[:], rs[:], AF.Ln, [rs], [rs])
                        act(rs[:], rs[:], AF.Exp, [rs], [rs], scale=-0.5)
                        for h in range(8):
                            ts("dve", gnY[:, h * 64:(h + 1) * 64], Yt[:, h * 64:(h + 1) * 64], mv[:, h, 0:1], rs[:, h:h + 1],
                               ALU.subtract, ALU.mult, [Yt, mv, rs], [gnY])
                        pt = PS[7]
                        for cb in range(4):
                            mm(pt[:, cb * 128:(cb + 1) * 128], gnY[:, cb * 128:(cb + 1) * 128], identb[:], [gnY, identb], [pt],
                               inc=(cb == 3))
                        for cb in range(4):
                            ts("dve", yat[:], pt[:, cb * 128:(cb + 1) * 128], col(40 + cb), col(44 + cb), ALU.mult, ALU.add,
                               [pt, cols], [yat])
                            tt("pool", yat[:], yat[:], bvl[:, cb, tc], ALU.add, [yat, bvl], [yat])
                            tt("pool", catT[:, cb, tc], yat[:], sgl[:, cb, tc], ALU.mult, [yat, sgl], [catT])
                        for n in range(2):
                            po = PS[4 + n]
                            for m in range(8):
                                mm(po[:, :], catT[:, m, tc], wo_bf[:, m, n * 512:(n + 1) * 512], [catT, wo_bf], [po],
                                   start=(m == 0), stop=(m == 7), inc=(m == 7))
                            hs = slice(n * 512, (n + 1) * 512)
                            tt("dve", yo[:, hs], po[:, :], GATE[s][:, hs], ALU.mult, [po, GATE[s]], [yo])
                        tt("pool", yo[:], yo[:], xt[:], ALU.add, [yo, xt], [yo])
                        act(junk3[:], yo[:], AF.Square, [yo], [junk3, ss3], accum=ss3[:, 0:1])
                        ts("dve", ss3[:, 1:2], ss3[:, 0:1], 1.0 / 1024, NORM_EPS, ALU.mult, ALU.add, [ss3], [ss3])
                        act(ss3[:, 2:3], ss3[:, 1:2], AF.Ln, [ss3], [ss3])
                        act(ss3[:, 3:4], ss3[:, 2:3], AF.Exp, [ss3], [ss3], scale=-0.5)
                        stt(yo[:], yo[:], ss3[:, 3:4], FG[:], ALU.mult, ALU.mult, [yo, ss3, FG], [yo])
                        S.dma("pool", ys[tok0 + ck * 128:tok0 + (ck + 1) * 128, :], yo[:], reads=[yo])

                for blk in range(NB):
                    phase3_block(blk)
                    ckpt(12)
        except _Stop:
            pass
        S.off = False
        S.finish("sp")
        S.finish("pool")
    nc._marks = SS[0].marks
    return nc


def _prep_shared(inp):
    f = np.float32
    d = {}
    d["w_ada"] = np.ascontiguousarray(inp["w_ada"][0].reshape(8, 128, 3072).transpose(1, 0, 2), dtype=f)
    d["b_ada2"] = np.ascontiguousarray(np.stack([inp["b_ada"][0], inp["b_ada"][0]], 0), dtype=f)
    d["ngc"] = np.ascontiguousarray(np.asarray(inp["norm_g"][0], dtype=f).reshape(8, 128).T)
    d["fg_bc"] = np.ascontiguousarray(np.broadcast_to(inp["final_g"][None, :], (128, 1024)), dtype=f)
    d["w_in"] = np.ascontiguousarray(inp["w_in"][0].reshape(8, 128, 4096).transpose(1, 0, 2), dtype=f)
    ld = np.concatenate([inp["decay_down"][0, 0], inp["decay_down"][0, 1], inp["iclr_down"][0, 0], inp["iclr_down"][0, 1]],
                        axis=1)
    d["lora_dn"] = np.ascontiguousarray(ld.reshape(8, 128, 256).transpose(1, 0, 2), dtype=f)
    d["dec_up"] = np.ascontiguousarray(inp["decay_up"][0].reshape(128, 512), dtype=f)
    d["icl_up"] = np.ascontiguousarray(inp["iclr_up"][0].reshape(128, 512), dtype=f)

    def c4(v):
        return np.asarray(v, dtype=f).reshape(4, 128).T

    cl = [c4(inp["shift_mu"][0, q]) for q in range(3)]
    cl += [c4(inp["decay_w0"][0, e]) for e in range(2)]
    cl += [c4(inp["iclr_bias"][0, e]) for e in range(2)]
    cl += [c4(inp["kk_scale"][0]), c4(inp["ka_scale"][0]), c4(inp["bonus_rk"][0]), c4(inp["gn_w"][0]), c4(inp["gn_b"][0])]
    cl += [c4(inp["conv_w"][0, j]) for j in range(3)]
    d["cols"] = np.ascontiguousarray(np.concatenate(cl, axis=1), dtype=f)
    d["w_out"] = np.ascontiguousarray(inp["w_out"][0].reshape(8, 128, 1024).transpose(1, 0, 2), dtype=f)
    return d


def _core_inputs(shared, x_lat, x_ctx, c_lat, c_ctx, st):
    f = np.float32
    m = dict(shared)
    parts = []
    if x_lat is not None:
        parts.append(np.asarray(x_lat, dtype=f).reshape(-1, 1024))
    if x_ctx is not None and len(x_ctx):
        parts.append(np.asarray(x_ctx, dtype=f).reshape(-1, 1024))
    m["xs"] = np.ascontiguousarray(np.concatenate(parts, 0))
    cv = np.stack([np.asarray(c_lat, dtype=f), np.asarray(c_ctx, dtype=f)], 0)
    m["cT"] = np.ascontiguousarray(cv.reshape(2, 8, 128).transpose(2, 1, 0).reshape(128, 16))
    m["st0"] = np.ascontiguousarray(np.asarray(st, dtype=f).transpose(2, 0, 1, 3).reshape(64, 16, 64))
    return m


_PROG = {}


def kernel(**inputs):
    inp = {k: np.asarray(v) for k, v in inputs.items()}
    NCORES = 8
    NLB, NCS = 16, 4
    shared = _prep_shared(inp)
    in_maps = []
    for b in range(NCORES):
        in_maps.append(_core_inputs(shared, inp["x_sample"][b], inp["x_prompt"][4 * b:4 * b + 4], inp["c"][b], inp["c_ctx"],
                                    inp["state_wkv"][b, 0]))
    key = (NLB, NCS)
    if key not in _PROG:
        _PROG[key] = build_program(NLB, NCS)
    res = run_bass_kernel_spmd(_PROG[key], in_maps, core_ids=list(range(NCORES)))
    y_prompt = np.zeros((32, 256, 1024), np.float32)
    y_sample = np.zeros((8, 4096, 1024), np.float32)
    new_state = np.zeros((32, 1, 2, 8, 64, 64), np.float32)
    for b in range(NCORES):
        r = res.results[b]
        ysb = np.asarray(r["ys"])
        y_sample[b] = ysb[:4096]
        y_prompt[4 * b:4 * b + 4] = ysb[4096:].reshape(4, 256, 1024)
        sob = np.asarray(r["so"]).reshape(4, 64, 2, 8, 64)
        new_state[4 * b:4 * b + 4, 0] = sob.transpose(0, 2, 3, 1, 4)
    return (y_prompt, y_sample, new_state)
```

```python
import os
import numpy as np
import concourse.bass as bass
import concourse.mybir as mybir
from concourse.bass_utils import run_bass_kernel_spmd
from contextlib import ExitStack

F32, BF16, I32 = mybir.dt.float32, mybir.dt.bfloat16, mybir.dt.int32
ALU = mybir.AluOpType
AF = mybir.ActivationFunctionType
C0 = 0.6065306597126334
NORM_EPS = 1e-6
GN_EPS = 64e-5
NDS = 24
BG_INJECT = os.environ.get("BG_INJECT", "1") == "1"
PUMP_EVERY = int(os.environ.get("PUMP_EVERY", "1"))
KCUT = int(os.environ.get("KCUT", "5"))
RAW_ONLY = os.environ.get("RAW_ONLY", "0") == "1"
SELF_SKIP = tuple(os.environ.get("SELF_SKIP", "pe").split(","))


class Buf:
    __slots__ = ("w", "r")

    def __init__(self):
        self.w = None
        self.r = {}


class T:
    def __init__(self, t):
        self.t = t
        self.b = Buf()

    def __getitem__(self, idx):
        return self.t[idx]


def _b(x):
    return x.b if hasattr(x, "b") else x


class Sched:
    def __init__(self, nc, es):
        self.nc = nc
        self.eng = {"pe": nc.tensor, "dve": nc.vector, "act": nc.scalar, "pool": nc.gpsimd, "sp": nc.sync}
        self.sem = {k: es.enter_context(nc.semaphore("s_" + k)) for k in self.eng}
        self.cnt = {k: 0 for k in self.eng}
        self.dsem = [es.enter_context(nc.semaphore("d%d" % i)) for i in range(NDS)]
        self.dcnt = [0] * NDS
        self.dnext = 0
        self.dnext2 = 0
        self.waited = {}
        self.nwait = 0
        self.off = False
        self.nops = {k: 0 for k in self.eng}
        self.marks = []

    def _semh(self, key):
        return self.sem[key] if isinstance(key, str) else self.dsem[key]

    def _wait(self, e, key, val):
        if val <= 0:
            return
        if e == key and e in SELF_SKIP:
            return
        if self.waited.get((e, key), 0) >= val:
            return
        self.eng[e].wait_ge(self._semh(key), val)
        self.waited[(e, key)] = val
        self.nwait += 1

    def _deps(self, e, reads, writes):
        for b in reads:
            b = _b(b)
            if b.w:
                self._wait(e, *b.w)
        raw_only = RAW_ONLY and e in ("act", "dve")
        for b in writes:
            b = _b(b)
            if b.w and not (raw_only and b.w[0] == e):
                self._wait(e, *b.w)
            for k, v in b.r.items():
                if raw_only and k == e:
                    continue
                self._wait(e, k, v)

    def _mark(self, key, tgt, reads, writes):
        for b in writes:
            b = _b(b)
            b.w = (key, tgt)
            b.r = {}
        for b in reads:
            b = _b(b)
            if b.r.get(key, 0) < tgt:
                b.r[key] = tgt

    def op(self, e, fn, reads=(), writes=(), inc=True):
        if self.off:
            return
        self._deps(e, reads, writes)
        inst = fn(self.eng[e])
        self.nops[e] += 1
        tgt = self.cnt[e] + 1
        if inc:
            inst.then_inc(self.sem[e], 1)
            self.cnt[e] = tgt
        self._mark(e, tgt, reads, writes)

    def dma(self, e, out, in_, reads=(), writes=()):
        if self.off:
            return
        if e == "pool":
            i = NDS - 8 + self.dnext2
            self.dnext2 = (self.dnext2 + 1) % 8
        else:
            i = self.dnext
            self.dnext = (i + 1) % (NDS - 8)
        self._deps(e, reads, writes)
        self._wait(e, i, self.dcnt[i])
        self.eng[e].dma_start(out=out, in_=in_).then_inc(self.dsem[i], 16)
        self.dcnt[i] += 16
        self._mark(i, self.dcnt[i], reads, writes)

    def mark(self, label):
        self.marks.append((label, dict(self.nops)))

    def barrier(self):
        if self.off:
            return
        for e in self.eng:
            for k in self.eng:
                if k != e:
                    self._wait(e, k, self.cnt[k])
            for i in range(NDS):
                self._wait(e, i, self.dcnt[i])

    def finish(self, e="sp"):
        for i in range(NDS):
            self._wait(e, i, self.dcnt[i])


class _Stop(Exception):
    pass


def build_program(NLB, NCS, debug=False, stop=None):
    SS = []

    def ckpt(n):
        SS[0].mark(n)
        if stop is not None and n == stop:
            SS[0].off = True

    NB = NLB + NCS
    NCH = 2 * NB
    NTOK = NB * 256
    nc = bass.Bass("TRN2", target_bir_lowering=False)

    def din(name, shape, dt=F32):
        return nc.dram_tensor(name, list(shape), dt, kind="ExternalInput").ap()

    def dout(name, shape, dt=F32):
        return nc.dram_tensor(name, list(shape), dt, kind="ExternalOutput").ap()

    def dscr(name, shape, dt):
        return nc.dram_tensor(name, list(shape), dt, kind=("ExternalOutput" if debug else "Internal")).ap()

    xs = din("xs", [NTOK, 1024])
    cT = din("cT", [128, 16])
    st0 = din("st0", [64, 16, 64])
    w_ada = din("w_ada", [128, 8, 3072])
    b_ada2 = din("b_ada2", [2, 3072])
    ngc_d = din("ngc", [128, 8])
    fg_bc = din("fg_bc", [128, 1024])
    w_in = din("w_in", [128, 8, 4096])
    lora_dn = din("lora_dn", [128, 8, 256])
    dec_up = din("dec_up", [128, 512])
    icl_up = din("icl_up", [128, 512])
    cols_d = din("cols", [128, 60])
    w_out = din("w_out", [128, 8, 1024])
    ys = dout("ys", [NTOK, 1024])
    so = dout("so", [max(NCS, 1), 64, 16, 64])

    hT_d = dscr("hT_d", [NB, 128, 8 * 256], BF16)
    Yl_d = dscr("Yl_d", [NCH, 128, 512], F32)
    Qh_d = dscr("Qh_d", [NCH, 64, 2048], BF16)
    X_d = dscr("X_d", [NCH, 64, 1024], BF16)
    D_d = dscr("D_d", [NCH, 64, 1024], F32)
    sg_d = dscr("sg_d", [NB, 128, 1024], F32)
    bv_d = dscr("bv_d", [NB, 128, 1024], F32)
    S_d = dscr("S_d", [NCH, 64, 1024], BF16)
    mod_d = dscr("mod_d", [2, 1024], F32)
    mod_b = Buf()
    hT_b = [Buf() for _ in range(NB)]
    Yl_b = [Buf() for _ in range(NCH)]
    Qh_b = [Buf() for _ in range(NCH)]
    X_b = [Buf() for _ in range(NCH)]
    D_b = [Buf() for _ in range(NCH)]
    sg_b = [Buf() for _ in range(NB)]
    bv_b = [Buf() for _ in range(NB)]
    S_b = [[Buf(), Buf()] for _ in range(NCH)]
    ys_b = Buf()
    so_b = Buf()

    with ExitStack() as es:
        S = Sched(nc, es)
        SS.append(S)

        try:
            def sbt(es_, name, shape, dt):
                return T(es_.enter_context(nc.sbuf_tensor("sb_" + name, list(shape), dt)))

            PS = [T(es.enter_context(nc.psum_tensor("ps%d" % i, [128, 512], F32))) for i in range(8)]

            def mm(out, lhsT, rhs, R, W, start=True, stop=True, inc=True):
                S.op("pe", lambda e: e.matmul(out, lhsT=lhsT, rhs=rhs, start=start, stop=stop, skip_group_check=True),
                     reads=R, writes=W, inc=inc)

            def tt(eng, out, a, b, op, R, W):
                S.op(eng, lambda e: e.tensor_tensor(out=out, in0=a, in1=b, op=op), reads=R, writes=W)

            def ts(eng, out, a, s1, s2, op0, op1, R, W):
                if op1 is None:
                    S.op(eng, lambda e: e.tensor_scalar(out=out, in0=a, scalar1=s1, scalar2=None, op0=op0), reads=R, writes=W)
                else:
                    S.op(eng, lambda e: e.tensor_scalar(out=out, in0=a, scalar1=s1, scalar2=s2, op0=op0, op1=op1),
                         reads=R, writes=W)

            def stt(out, a, sc, b, op0, op1, R, W):
                S.op("dve", lambda e: e.scalar_tensor_tensor(out=out, in0=a, scalar=sc, in1=b, op0=op0, op1=op1),
                     reads=R, writes=W)

            def act(out, in_, func, R, W, bias=None, scale=None, accum=None):
                kw = {}
                if bias is not None:
                    kw["bias"] = bias
                if scale is not None:
                    kw["scale"] = scale
                if accum is not None:
                    kw["accum_out"] = accum
                S.op("act", lambda e: e.activation(out=out, in_=in_, func=func, **kw), reads=R, writes=W)

            def sigm(out, in_, R, W, nbias=None):
                act(out, in_, AF.Exp, R, W, scale=-1.0, bias=nbias)
                act(out, out, AF.Ln, W, W, bias=1.0)
                act(out, out, AF.Exp, W, W, scale=-1.0)

            def cp(eng, out, in_, R, W):
                if eng == "act":
                    S.op("act", lambda e: e.copy(out=out, in_=in_), reads=R, writes=W)
                else:
                    S.op(eng, lambda e: e.tensor_copy(out=out, in_=in_), reads=R, writes=W)

            ioi = sbt(es, "ioi", [128, 128], I32)
            iof = sbt(es, "iof", [128, 128], F32)
            identf = sbt(es, "identf", [128, 128], F32)
            identb = sbt(es, "identb", [128, 128], BF16)
            bones = sbt(es, "bones", [128, 128], F32)
            ones = sbt(es, "ones", [128, 128], F32)
            MK = {k: sbt(es, "mk_" + k, [128, 512], BF16) for k in ("LT", "GT", "LE", "GE", "NLT", "NGT")}
            cols = sbt(es, "cols", [128, 60], F32)
            dcols = sbt(es, "dcols", [128, 44], F32)
            w_bf = sbt(es, "w_bf", [128, 8, 2048], BF16)
            WLh = sbt(es, "WLh", [64, NB * 32], F32)

            S.op("pool", lambda e: e.iota(ioi[:], pattern=[[1, 128]], base=0, channel_multiplier=-1), writes=[ioi])
            cp("dve", iof[:], ioi[:], [ioi], [iof])
            ts("dve", identf[:], iof[:], 0.0, None, ALU.is_equal, None, [iof], [identf])
            cp("dve", identb[:], identf[:], [identf], [identb])
            S.op("dve", lambda e: e.memset(ones[:], 1.0), writes=[ones])
            S.op("dve", lambda e: e.memset(bones[:], 0.0), writes=[bones])
            S.op("dve", lambda e: e.memset(bones[0:64, 0:64], 1.0), writes=[bones])
            S.op("dve", lambda e: e.memset(bones[64:128, 64:128], 1.0), writes=[bones])
            for j in range(4):
                sl = slice(j * 128, (j + 1) * 128)
                ts("dve", MK["LT"][:, sl], iof[:], 0.0, None, ALU.is_gt, None, [iof], [MK["LT"]])
                ts("dve", MK["GT"][:, sl], iof[:], 0.0, None, ALU.is_lt, None, [iof], [MK["GT"]])
                ts("dve", MK["LE"][:, sl], iof[:], 0.0, None, ALU.is_ge, None, [iof], [MK["LE"]])
                ts("dve", MK["GE"][:, sl], iof[:], 0.0, None, ALU.is_le, None, [iof], [MK["GE"]])
                ts("dve", MK["NLT"][:, sl], iof[:], 0.0, -1.0, ALU.is_gt, ALU.mult, [iof], [MK["NLT"]])
                ts("dve", MK["NGT"][:, sl], iof[:], 0.0, -1.0, ALU.is_lt, ALU.mult, [iof], [MK["NGT"]])
            S.dma("sp", cols[:], cols_d, writes=[cols])
            ts("dve", dcols[:, 0:12], cols[:, 0:12], -1.0, 1.0, ALU.mult, ALU.add, [cols], [dcols])
            ts("dve", dcols[:, 12:24], cols[:, 0:12], 0.5, None, ALU.mult, None, [cols], [dcols])
            ts("dve", dcols[:, 24:28], cols[:, 32:36], -1.0, 1.0, ALU.mult, ALU.add, [cols], [dcols])
            ts("dve", dcols[:, 28:44], cols[:, 12:28], -1.0, None, ALU.mult, None, [cols], [dcols])
            ckpt(1)

            def col(i):
                return cols[:, i:i + 1]

            def dcol(i):
                return dcols[:, i:i + 1]

            with ExitStack() as e1:
                g1c = [sbt(e1, "g1c%d" % s, [128, 8], F32) for s in range(2)]
                shc = [sbt(e1, "shc%d" % s, [128, 8], F32) for s in range(2)]
                lora_bf = sbt(e1, "lora_bf", [128, 8, 256], BF16)
                dup_bf = sbt(e1, "dup_bf", [128, 512], BF16)
                iup_bf = sbt(e1, "iup_bf", [128, 512], BF16)
                Sf0 = sbt(e1, "Sf0", [64, 16, 64], F32)
                with ExitStack() as e0:
                    stg = [sbt(e0, "stg%d" % i, [128, 3072], F32) for i in range(2)]
                    ngt = sbt(e0, "ngt", [128, 8], F32)
                    cTt = sbt(e0, "cTt", [128, 16], F32)
                    scT = sbt(e0, "scT", [128, 16], F32)
                    modv = sbt(e0, "modv", [2, 3072], F32)
                    bad = sbt(e0, "bad", [2, 3072], F32)
                    st_in = sbt(e0, "st_in", [64, 16, 64], F32)
                    for k in range(8):
                        g = stg[k % 2]
                        S.dma("sp", g[:, 0:2048], w_in[:, k, 0:2048], writes=[g])
                        cp("act" if k % 2 == 0 else "dve", w_bf[:, k, :], g[:, 0:2048], [g], [w_bf])
                    g = stg[0]
                    S.dma("sp", g[:, 0:2048], lora_dn.rearrange("p k n -> p (k n)"), writes=[g])
                    cp("dve", lora_bf[:, :, :].rearrange("p k n -> p (k n)"), g[:, 0:2048], [g], [lora_bf])
                    g = stg[1]
                    S.dma("sp", g[:, 0:512], dec_up, writes=[g])
                    S.dma("sp", g[:, 512:1024], icl_up, writes=[g])
                    cp("dve", dup_bf[:], g[:, 0:512], [g], [dup_bf])
                    cp("dve", iup_bf[:], g[:, 512:1024], [g], [iup_bf])
                    ckpt(2)
                    S.dma("sp", cTt[:], cT, writes=[cTt])
                    act(scT[:], cTt[:], AF.Silu, [cTt], [scT])
                    S.dma("sp", bad[:], b_ada2, writes=[bad])
                    S.dma("sp", ngt[:], ngc_d, writes=[ngt])
                    for k in range(8):
                        g = stg[k % 2]
                        S.dma("sp", g[:], w_ada[:, k, :], writes=[g])
                        for n in range(6):
                            mm(PS[n][0:2, :], scT[:, 2 * k:2 * k + 2], g[:, n * 512:(n + 1) * 512], [scT, g], [PS[n]],
                               start=(k == 0), stop=(k == 7))
                    for n in range(6):
                        tt("dve", modv[:, n * 512:(n + 1) * 512], PS[n][0:2, :], bad[:, n * 512:(n + 1) * 512], ALU.add,
                           [PS[n], bad], [modv])
                    S.dma("sp", mod_d, modv[:, 2048:3072], reads=[modv], writes=[mod_b])
                    for part in range(2):
                        for k in range(8):
                            c0 = (part * 8 + k) * 2
                            mm(PS[0][:, c0:c0 + 2], modv[0:2, part * 1024 + k * 128:part * 1024 + (k + 1) * 128],
                               identf[0:2, 0:2], [modv, identf], [PS[0]], inc=(part == 1 and k == 7))
                    mview = PS[0][:, 0:32].rearrange("p (a k s) -> p a k s", a=2, k=8)
                    for s in range(2):
                        cp("dve", shc[s][:], mview[:, 0, :, s], [PS[0]], [shc[s]])
                        stt(g1c[s][:], mview[:, 1, :, s], 1.0, ngt[:], ALU.add, ALU.mult, [PS[0], ngt], [g1c[s]])
                    ckpt(3)
                    S.dma("sp", st_in[:], st0, writes=[st_in])
                    for hd in range(16):
                        p = PS[hd // 8]
                        mm(p[0:64, (hd % 8) * 64:(hd % 8 + 1) * 64], st_in[:, hd, :], identf[0:64, 0:64], [st_in, identf], [p],
                           inc=(hd % 8 == 7))
                    for hf in range(2):
                        cp("dve", Sf0[:, hf * 8:(hf + 1) * 8, :].rearrange("p a b -> p (a b)"), PS[hf][0:64, :], [PS[hf]], [Sf0])

                S.barrier()
                ckpt(4)
                e1b = ExitStack()
                e1b.__enter__()
                xts = [sbt(e1b, "xt0", [128, 1024], F32)] * 2
                hb = sbt(e1b, "hb", [128, 1024], BF16)
                ss = sbt(e1b, "ss", [128, 4], F32)
                hT = sbt(e1b, "hT", [128, 8, 256], BF16)
                ppL = sbt(e1b, "ppL", [128, 4, 66], F32)
                ppC = sbt(e1b, "ppC", [128, 1, 258], F32)
                nbt = sbt(e1b, "nbt", [128, 256], F32)
                RKV = [[sbt(e1b, "rkv%d_%d" % (q, cb), [128, 256], F32) for cb in range(4)] for q in range(3)]
                vbf = [sbt(e1b, "vbf%d" % cb, [128, 256], BF16) for cb in range(4)]
                sgT = sbt(e1b, "sgT", [128, 4, 256], F32)
                bvT = sbt(e1b, "bvT", [128, 4, 256], F32)
                lwd = sbt(e1b, "lwd", [128, 256], BF16)
                lwi = sbt(e1b, "lwi", [128, 256], BF16)
                TMPC = [{n: sbt(e1b, "tmc%d_%s" % (i, n), [128, 256], F32) for n in ["sq", "kk", "rkd"]} for i in range(2)]
                TMPC[0]["rn"] = TMPC[1]["rn"] = sbt(e1b, "tmc_rn", [128, 256], F32)
                TMP = TMPC[0]
                TMPE = [{n: sbt(e1b, "tm0_%s" % n, [128, 256] if n != "tcol" else [128, 4], F32)
                         for n in ["sig", "pi", "px", "E1", "E2", "E3", "E4", "a", "bq", "kd", "tcol"]}]
                TMPE.append(TMPE[0])
                WLc = [sbt(e1b, "WLc%d" % i, [128, 4], F32) for i in range(2)]
                FM4 = {n: [[[sbt(e1b, "fm_%s%d%d%d" % (n, par, e, cb), [128, 256], BF16) for cb in range(4)] for e in range(2)]
                           for par in range(2)] for n in ("Qt", "KKt", "Kt", "Bt")}
                FMs = {n: [[sbt(e1b, "fm_%s%d%d" % (n, e, cb), [128, 256], BF16) for cb in range(4)] for e in range(2)]
                       for n in ("Kh", "Bh")}

                def FMt(n, par, e, cb):
                    return FMs[n][e][cb] if n in FMs else FM4[n][par][e][cb]

                Khtm = [[[sbt(e1b, "khtm%d%d%d" % (par, ck, e), [128, 512], BF16) for e in range(2)] for ck in range(2)]
                        for par in range(2)]
                Bhtm = [[[sbt(e1b, "bhtm%d%d%d" % (par, ck, e), [128, 512], BF16) for e in range(2)] for ck in range(2)]
                        for par in range(2)]
                Vtm = [[sbt(e1b, "vtm%d%d" % (par, ck), [128, 512], BF16) for ck in range(2)] for par in range(2)]
                XT = [[sbt(e1b, "XT%d%d" % (e, i), [128, 512], F32) for i in range(2)] for e in range(2)]
                XM = [[sbt(e1b, "XM%d%d" % (e, i), [128, 512], F32) for i in range(2)] for e in range(2)]
                AakT = [sbt(e1b, "AakT%d" % e, [128, 512], BF16) for e in range(2)]
                AqbT = [sbt(e1b, "AqbT%d" % e, [128, 512], BF16) for e in range(2)]
                AqkT = [sbt(e1b, "AqkT%d" % e, [128, 512], BF16) for e in range(2)]
                Zb = [sbt(e1b, "Zb%d" % e, [128, 512], F32) for e in range(2)]
                UGn = [sbt(e1b, "UGn%d" % e, [128, 512], BF16) for e in range(2)]
                Ylt = sbt(e1b, "Ylt", [128, 512], F32)
                Qht = sbt(e1b, "Qht", [64, 2048], BF16)
                Xst = sbt(e1b, "Xst", [64, 1024], BF16)
                Dst = sbt(e1b, "Dst", [64, 1024], F32)
                S.op("dve", lambda e: e.memset(ppL[:, :, :].rearrange("p a b -> p (a b)"), 0.0), writes=[ppL])
                S.op("dve", lambda e: e.memset(ppC[:, :, :].rearrange("p a b -> p (a b)"), 0.0), writes=[ppC])

                PB = PS[7]

                def ab_gen(blk):
                    lat = blk < NLB
                    s = 0 if lat else 1
                    tok0 = blk * 256
                    for i in range(2):
                        xt = xts[i]
                        S.dma("sp", xt[:], xs[tok0 + i * 128:tok0 + (i + 1) * 128, :], writes=[xt])
                        yield
                        act(hb[:], xt[:], AF.Square, [xt], [hb, ss], accum=ss[:, 0:1])
                        ts("dve", ss[:, 1:2], ss[:, 0:1], 1.0 / 1024, NORM_EPS, ALU.mult, ALU.add, [ss], [ss])
                        act(ss[:, 2:3], ss[:, 1:2], AF.Ln, [ss], [ss])
                        act(ss[:, 3:4], ss[:, 2:3], AF.Exp, [ss], [ss], scale=-0.5)
                        yield
                        ts("dve", hb[:], xt[:], ss[:, 3:4], None, ALU.mult, None, [xt, ss], [hb])
                        yield
                        for half in range(2):
                            p = PB
                            for k4 in range(4):
                                k = half * 4 + k4
                                mm(p[:, k4 * 128:(k4 + 1) * 128], hb[:, k * 128:(k + 1) * 128], identb[:], [hb, identb], [p],
                                   inc=(k4 == 3))
                            yield
                            for k4 in range(4):
                                k = half * 4 + k4
                                dsth = hT[:, k, i * 128:(i + 1) * 128]
                                srcp = p[:, k4 * 128:(k4 + 1) * 128]
                                if k4 % 2 == 0:
                                    act(dsth, srcp, AF.Identity, [p, g1c[s], shc[s]], [hT], scale=g1c[s][:, k:k + 1],
                                        bias=shc[s][:, k:k + 1])
                                else:
                                    ts("dve", dsth, srcp, g1c[s][:, k:k + 1], shc[s][:, k:k + 1], ALU.mult, ALU.add,
                                       [p, g1c[s], shc[s]], [hT])
                            yield
                    S.dma("pool", hT_d[blk], hT[:, :, :].rearrange("p k t -> p (k t)"), reads=[hT], writes=[hT_b[blk]])
                    pp = ppL if lat else ppC
                    R_, W_ = (4, 64) if lat else (1, 256)

                    def v3(ap):
                        return ap.rearrange("p (r w) -> p r w", r=R_)

                    hs = slice(0, 256)
                    for cbg in range(16):
                        p = PB
                        for k in range(8):
                            mm(p[:, hs], w_bf[:, k, cbg * 128:(cbg + 1) * 128], hT[:, k, :], [w_bf, hT], [p],
                               start=(k == 0), stop=(k == 7), inc=(k == 7))
                        q, cb = cbg // 4, cbg % 4
                        if q < 3:
                            cp("act", pp[:, :, 1:W_ + 1], v3(p[:, hs]), [p], [pp])
                            tt("dve", v3(nbt[:]), pp[:, :, 0:W_], pp[:, :, 2:W_ + 2], ALU.add, [pp], [nbt])
                            dst = RKV[q][cb]
                            act(v3(dst[:]), pp[:, :, 1:W_ + 1], AF.Identity, [pp, dcols], [dst], scale=dcol(q * 4 + cb))
                            stt(dst[:], nbt[:], dcol(12 + q * 4 + cb), dst[:], ALU.mult, ALU.add, [nbt, dcols, dst], [dst])
                            if q == 2:
                                cp("pool", vbf[cb][:], dst[:], [dst], [vbf[cb]])
                        else:
                            sigm(sgT[:, cb, :], p[:, hs], [p], [sgT])
                            tt("dve", sgT[:, cb, :], p[:, hs], sgT[:, cb, :], ALU.mult, [p, sgT], [sgT])
                        yield
                    S.dma("pool", sg_d[blk], sgT[:, :, :].rearrange("p a b -> p (a b)"), reads=[sgT], writes=[sg_b[blk]])
                    for mb in range(2):
                        p = PB
                        for k in range(8):
                            mm(p[:, hs], lora_bf[:, k, mb * 128:(mb + 1) * 128], hT[:, k, :], [lora_bf, hT], [p],
                               start=(k == 0), stop=(k == 7), inc=(k == 7))
                        if mb == 0:
                            tq = TMP["sq"]
                            act(tq[:], p[:, hs], AF.Exp, [p], [tq], scale=-2.0)
                            act(tq[:], tq[:], AF.Ln, [tq], [tq], bias=1.0)
                            act(tq[:], tq[:], AF.Exp, [tq], [tq], scale=-1.0)
                            ts("dve", lwd[:], tq[:], 2.0, -1.0, ALU.mult, ALU.add, [tq], [lwd])
                        else:
                            cp("dve", lwi[:], p[:, hs], [p], [lwi])
                        yield

                def cd_gen(blk):
                    par = blk % 2
                    p5 = PB

                    def c_kk(cb):
                        if False:
                            yield
                        k_ = RKV[1][cb]
                        T_ = TMPC[cb % 2]
                        act(T_["sq"][:], k_[:], AF.Square, [k_, cols], [T_["sq"]], scale=col(28 + cb))
                        mm(p5[:, 0:256], bones[:], T_["sq"][:], [bones, T_["sq"]], [p5])
                        ts("dve", T_["rn"][:], p5[:, 0:256], 1e-24, None, ALU.max, None, [p5], [T_["rn"]])
                        act(T_["rn"][:], T_["rn"][:], AF.Ln, [T_["rn"]], [T_["rn"]])
                        act(T_["rn"][:], T_["rn"][:], AF.Exp, [T_["rn"]], [T_["rn"]], scale=-0.5)
                        stt(T_["kk"][:], k_[:], col(28 + cb), T_["rn"][:], ALU.mult, ALU.mult, [k_, cols, T_["rn"]],
                            [T_["kk"]])

                    def c_front(cb, e):
                        TE = TMPE[e]
                        es_ = slice(e * 64, (e + 1) * 64)
                        pz = PB
                        mm(pz[:, 0:256], dup_bf[es_, cb * 128:(cb + 1) * 128], lwd[es_, :], [dup_bf, lwd], [pz])
                        mm(pz[:, 256:512], iup_bf[es_, cb * 128:(cb + 1) * 128], lwi[es_, :], [iup_bf, lwi], [pz])
                        sig, pi, px, tcl = TE["sig"], TE["pi"], TE["px"], TE["tcol"]
                        act(sig[:], pz[:, 0:256], AF.Exp, [pz, dcols], [sig], scale=-1.0, bias=dcol(28 + e * 4 + cb))
                        act(TE["a"][:], pz[:, 256:512], AF.Exp, [pz, dcols], [TE["a"]], scale=-1.0, bias=dcol(36 + e * 4 + cb))
                        act(sig[:], sig[:], AF.Ln, [sig], [sig], bias=1.0)
                        act(sig[:], sig[:], AF.Exp, [sig], [sig], scale=-1.0)
                        yield
                        for ck in range(2):
                            tc = slice(ck * 128, (ck + 1) * 128)
                            S.op("dve", lambda e_: e_.tensor_tensor_scan(out=pi[:, tc], data0=ones[:, 0:128],
                                                                         data1=sig[:, tc], initial=0.0,
                                                                         op0=ALU.mult, op1=ALU.add),
                                 reads=[ones, sig], writes=[pi])
                        tt("dve", px[:], pi[:], sig[:], ALU.subtract, [pi, sig], [px])
                        ts("dve", tcl[:, 0:2], pi[:, 127:256:128], -C0, None, ALU.mult, None, [pi], [tcl])
                        ts("dve", tcl[:, 2:4], pi[:, 127:256:128], C0, None, ALU.mult, None, [pi], [tcl])
                        yield
                        WL_ = WLc[cb % 2]
                        act(WL_[:, e * 2:e * 2 + 2], tcl[:, 0:2], AF.Exp, [tcl], [WL_])
                        E1, E2, E3, E4 = TE["E1"], TE["E2"], TE["E3"], TE["E4"]
                        if e == 0:
                            act(E1[:], pi[:], AF.Exp, [pi], [E1], scale=-C0)
                            act(E2[:], px[:], AF.Exp, [px], [E2], scale=-C0)
                            act(E3[:], pi[:], AF.Exp, [pi], [E3], scale=C0)
                            for ck in range(2):
                                tc = slice(ck * 128, (ck + 1) * 128)
                                act(E4[:, tc], pi[:, tc], AF.Exp, [pi, tcl], [E4], scale=C0, bias=tcl[:, ck:ck + 1])
                        else:
                            for ck in range(2):
                                tc = slice(ck * 128, (ck + 1) * 128)
                                act(E1[:, tc], px[:, tc], AF.Exp, [px, tcl], [E1], scale=C0, bias=tcl[:, ck:ck + 1])
                                act(E2[:, tc], pi[:, tc], AF.Exp, [pi, tcl], [E2], scale=C0, bias=tcl[:, ck:ck + 1])
                                act(E3[:, tc], px[:, tc], AF.Exp, [px, tcl], [E3], scale=-C0, bias=tcl[:, 2 + ck:3 + ck])
                            act(E4[:], px[:], AF.Exp, [px], [E4], scale=-C0)
                        yield
                        act(TE["a"][:], TE["a"][:], AF.Ln, [TE["a"]], [TE["a"]], bias=1.0)
                        act(TE["a"][:], TE["a"][:], AF.Exp, [TE["a"]], [TE["a"]], scale=-1.0)

                    def c_back(cb, e):
                        TE = TMPE[e]
                        T_ = TMPC[cb % 2]
                        r_, k_ = RKV[0][cb], RKV[1][cb]
                        E1, E2, E3, E4 = TE["E1"], TE["E2"], TE["E3"], TE["E4"]
                        a_, bq, kd, rkd = TE["a"], TE["bq"], TE["kd"], T_["rkd"]
                        tt("pool", bq[:], T_["kk"][:], a_[:], ALU.mult, [T_["kk"], a_], [bq])
                        ts("dve", kd[:], a_[:], col(32 + cb), dcol(24 + cb), ALU.mult, ALU.add, [a_, cols, dcols], [kd])
                        tt("dve", kd[:], kd[:], k_[:], ALU.mult, [kd, k_], [kd])
                        f = lambda n: FMt(n, par, e, cb)
                        tt("dve", f("Qt")[:], r_[:], E1[:], ALU.mult, [r_, E1], [f("Qt")])
                        tt("pool", f("KKt")[:], T_["kk"][:], E2[:], ALU.mult, [T_["kk"], E2], [f("KKt")])
                        tt("dve", f("Kt")[:], kd[:], E3[:], ALU.mult, [kd, E3], [f("Kt")])
                        yield
                        tt("pool", f("Bt")[:], bq[:], E3[:], ALU.mult, [bq, E3], [f("Bt")])
                        tt("dve", f("Kh")[:], kd[:], E4[:], ALU.mult, [kd, E4], [f("Kh")])
                        tt("pool", f("Bh")[:], bq[:], E4[:], ALU.mult, [bq, E4], [f("Bh")])
                        if e == 0:
                            tt("dve", rkd[:], r_[:], kd[:], ALU.mult, [r_, kd], [rkd])
                        else:
                            tt("dve", T_["sq"][:], r_[:], kd[:], ALU.mult, [r_, kd], [T_["sq"]])
                            stt(rkd[:], rkd[:], 1.0, T_["sq"][:], ALU.mult, ALU.add, [rkd, T_["sq"]], [rkd])

                    def c_tail(cb):
                        if False:
                            yield
                        T_ = TMPC[cb % 2]
                        v_ = RKV[2][cb]
                        rkd = T_["rkd"]
                        WL_ = WLc[cb % 2]
                        for hh in range(2):
                            mm(p5[0:64, 256 + hh * 4:256 + hh * 4 + 4], identf[:, hh * 64:(hh + 1) * 64], WL_[:, 0:4],
                               [identf, WL_], [p5])
                        for hh in range(2):
                            h = cb * 2 + hh
                            for e in range(2):
                                c0 = ((blk * 2 + e) * 8 + h) * 2
                                cp("dve", WLh[:, c0:c0 + 2], p5[0:64, 256 + hh * 4 + e * 2:256 + hh * 4 + e * 2 + 2], [p5], [WLh])
                        ts("dve", rkd[:], rkd[:], col(36 + cb), None, ALU.mult, None, [rkd, cols], [rkd])
                        mm(p5[:, 0:256], bones[:], rkd[:], [bones, rkd], [p5])
                        tt("dve", bvT[:, cb, :], p5[:, 0:256], v_[:], ALU.mult, [p5, v_], [bvT])

                    for cb in range(4):
                        yield from c_kk(cb)
                        yield
                        for e in range(2):
                            yield from c_front(cb, e)
                            yield
                            yield from c_back(cb, e)
                            yield
                        yield from c_tail(cb)
                        yield
                    S.dma("pool", bv_d[blk], bvT[:, :, :].rearrange("p a b -> p (a b)"), reads=[bvT], writes=[bv_b[blk]])

                    for ck in range(2):
                        tc = slice(ck * 128, (ck + 1) * 128)
                        jobs = [(Vtm[par][ck], vbf)] + [(Khtm[par][ck][e], FMs["Kh"][e]) for e in range(2)] + \
                               [(Bhtm[par][ck][e], FMs["Bh"][e]) for e in range(2)]
                        for ji, (dst, src) in enumerate(jobs):
                            p = PB
                            for cb in range(4):
                                mm(p[:, cb * 128:(cb + 1) * 128], src[cb][:, tc], identb[:], [src[cb], identb], [p],
                                   inc=(cb == 3))
                            cp("act" if ji % 2 == 0 else "dve", dst[:], p[:, :], [p], [dst])
                            yield


                def abcd_gen(blk):
                    yield from ab_gen(blk)
                    yield from cd_gen(blk)

                bg = {}

                def pump():
                    g = bg.get("g")
                    if g is not None:
                        try:
                            next(g)
                        except StopIteration:
                            bg["g"] = None

                def phase1_block(blk):
                    lat = blk < NLB
                    s = 0 if lat else 1
                    tok0 = blk * 256
                    if blk == 0:
                        bg["g"] = abcd_gen(0)
                    while bg.get("g") is not None:
                        pump()
                    ckpt(5)
                    if blk + 1 < NB and BG_INJECT:
                        bg["g"] = abcd_gen(blk + 1)
                    npump = [0]
                    par = blk % 2
                    ckpt(8)
                    for ck in range(2):
                        chunk = blk * 2 + ck
                        tc = slice(ck * 128, (ck + 1) * 128)
                        yps = PS[6]
                        for hg in range(2):
                            def fm(n, e, hq):
                                return FM4[n][par][e][hq][hg * 64:(hg + 1) * 64, tc]

                            def fmR(n, e, hq):
                                return [FM4[n][par][e][hq]]

                            def chain(e):
                                bA, bB, zps = PS[3 * e], PS[3 * e + 1], PS[3 * e + 2]
                                if e == 0:
                                    mSTn, mSn, mST, mIT = MK["NLT"], MK["NGT"], MK["LT"], MK["LE"]
                                else:
                                    mSTn, mSn, mST, mIT = MK["NGT"], MK["NLT"], MK["GT"], MK["GE"]
                                hsl = lambda hq: slice(hq * 128, (hq + 1) * 128)
                                for hq in range(4):
                                    mm(bA[:, hsl(hq)], fm("Bt", e, hq), fm("KKt", e, hq), fmR("Bt", e, hq) + fmR("KKt", e, hq),
                                       [bA], inc=(hq == 3))
                                tt("dve", XT[e][0][:], bA[:, :], mSTn[:], ALU.mult, [bA, mSTn], [XT[e][0]])
                                for hq in range(4):
                                    mm(bB[:, hsl(hq)], fm("KKt", e, hq), fm("Bt", e, hq), fmR("Bt", e, hq) + fmR("KKt", e, hq),
                                       [bB], inc=(hq == 3))
                                tt("dve", XM[e][0][:], bB[:, :], mSn[:], ALU.mult, [bB, mSn], [XM[e][0]])
                                yield
                                for hq in range(4):
                                    mm(bA[:, hsl(hq)], fm("Kt", e, hq), fm("KKt", e, hq), fmR("Kt", e, hq) + fmR("KKt", e, hq),
                                       [bA], inc=(hq == 3))
                                tt("dve", AakT[e][:], bA[:, :], mST[:], ALU.mult, [bA, mST], [AakT[e]])
                                for hq in range(4):
                                    h = hq * 2 + hg
                                    hh = hg
                                    mm(zps[:, hq * 128:hq * 128 + 64], AakT[e][:, hsl(hq)], Vtm[par][ck][:, h * 64:(h + 1) * 64],
                                       [AakT[e], Vtm[par][ck]], [zps], start=(hq == 0), stop=False, inc=False)
                                    mm(zps[:, hq * 128 + 64:(hq + 1) * 128], fm("KKt", e, hq),
                                       identb[hh * 64:(hh + 1) * 64, hh * 64:(hh + 1) * 64], fmR("KKt", e, hq) + [identb], [zps],
                                       start=False, stop=False, inc=(hq == 3))
                                cp("act", Zb[e][:], zps[:, :], [zps], [Zb[e]])
                                yield
                                for hq in range(4):
                                    mm(bB[:, hsl(hq)], fm("Bt", e, hq), fm("Qt", e, hq), fmR("Bt", e, hq) + fmR("Qt", e, hq),
                                       [bB], inc=(hq == 3))
                                cp("act", AqbT[e][:], bB[:, :], [bB], [AqbT[e]])
                                tt("pool", AqbT[e][:], AqbT[e][:], mIT[:], ALU.mult, [AqbT[e], mIT], [AqbT[e]])
                                for hq in range(4):
                                    mm(bA[:, hsl(hq)], fm("Kt", e, hq), fm("Qt", e, hq), fmR("Kt", e, hq) + fmR("Qt", e, hq),
                                       [bA], inc=(hq == 3))
                                cp("act", AqkT[e][:], bA[:, :], [bA], [AqkT[e]])
                                tt("pool", AqkT[e][:], AqkT[e][:], mIT[:], ALU.mult, [AqkT[e], mIT], [AqkT[e]])
                                yield
                                def xtv(i, lev_):
                                    if lev_ < KCUT:
                                        return XT[e][i][:, :], XM[e][i][:, :]
                                    return XT[e][i][:, :].bitcast(BF16)[:, 0:512], XM[e][i][:, :].bitcast(BF16)[:, 0:512]

                                for lev in range(7):
                                    cur, nxt = lev % 2, (lev + 1) % 2
                                    xt_c, xm_c = xtv(cur, lev)
                                    zsrc = Zb[e] if lev < KCUT else AakT[e]
                                    for hq in range(4):
                                        mm(zps[:, hsl(hq)], xt_c[:, hsl(hq)], zsrc[:, hsl(hq)], [XT[e][cur], zsrc], [zps],
                                           start=False, stop=(lev == 6), inc=(hq == 3))
                                    if lev < 6:
                                        xt_n, xm_n = xtv(nxt, lev + 1)
                                        for hq in range(4):
                                            mm(bA[:, hsl(hq)], xm_c[:, hsl(hq)], xt_c[:, hsl(hq)],
                                               [XM[e][cur], XT[e][cur]], [bA], inc=(hq == 3))
                                        cp("dve", xt_n, bA[:, :], [bA], [XT[e][nxt]])
                                        if lev < 5:
                                            for hq in range(4):
                                                mm(bB[:, hsl(hq)], xt_c[:, hsl(hq)], xm_c[:, hsl(hq)],
                                                   [XM[e][cur], XT[e][cur]], [bB], inc=(hq == 3))
                                            cp("act", xm_n, bB[:, :], [bB], [XM[e][nxt]])
                                        zdst = Zb[e] if lev + 1 < KCUT else AakT[e]
                                        cp("act", zdst[:], zps[:, :], [zps], [zdst])
                                    yield
                                act(UGn[e][:], zps[:, :], AF.Identity, [zps], [UGn[e]], scale=-1.0)
                                for hq in range(4):
                                    h = hq * 2 + hg
                                    hh = hg
                                    hc = slice(h * 64, (h + 1) * 64)
                                    mm(yps[:, hc], AqbT[e][:, hsl(hq)], UGn[e][:, hq * 128:hq * 128 + 64], [AqbT[e], UGn[e]],
                                       [yps], start=(e == 0 and hg == 0 and hq == 0), stop=False, inc=False)
                                    mm(yps[:, hc], AqkT[e][:, hsl(hq)], Vtm[par][ck][:, hc], [AqkT[e], Vtm[par][ck]], [yps],
                                       start=False, stop=(e == 1), inc=(hq == 3))
                                for hq in range(4):
                                    hh = hg
                                    mm(bA[0:64, hsl(hq)], identb[hh * 64:(hh + 1) * 64, hh * 64:(hh + 1) * 64], fm("Qt", e, hq),
                                       fmR("Qt", e, hq) + [identb], [bA], start=True, stop=False, inc=False)
                                    mm(bA[0:64, hsl(hq)], UGn[e][:, hq * 128 + 64:(hq + 1) * 128], AqbT[e][:, hsl(hq)],
                                       [UGn[e], AqbT[e]], [bA], start=False, stop=True, inc=(hq == 3))
                                cp("act", Qht[:, :].rearrange("p (e q g t) -> p e q g t", e=2, q=4, g=2)[:, e, :, hg, :],
                                   bA[0:64, :].rearrange("p (q t) -> p q t", q=4), [bA], [Qht])
                                for hq in range(4):
                                    h = hq * 2 + hg
                                    hc = slice(h * 64, (h + 1) * 64)
                                    mm(bB[0:64, hq * 64:(hq + 1) * 64], UGn[e][:, hq * 128 + 64:(hq + 1) * 128], Bhtm[par][ck][e][:, hc],
                                       [UGn[e], Bhtm[par][ck][e]], [bB], inc=False)
                                    mm(bB[0:64, 256 + hq * 64:256 + (hq + 1) * 64], Bhtm[par][ck][e][:, hc],
                                       UGn[e][:, hq * 128:hq * 128 + 64], [UGn[e], Bhtm[par][ck][e]], [bB], start=True, stop=False,
                                       inc=False)
                                    mm(bB[0:64, 256 + hq * 64:256 + (hq + 1) * 64], Khtm[par][ck][e][:, hc], Vtm[par][ck][:, hc],
                                       [Khtm[par][ck][e], Vtm[par][ck]], [bB], start=False, stop=True, inc=(hq == 3))
                                cp("dve", Xst[:, :].rearrange("p (e q g c) -> p e q g c", e=2, q=4, g=2)[:, e, :, hg, :],
                                   bB[0:64, 0:256].rearrange("p (q c) -> p q c", q=4), [bB], [Xst])
                                cp("dve", Dst[:, :].rearrange("p (e q g c) -> p e q g c", e=2, q=4, g=2)[:, e, :, hg, :],
                                   bB[0:64, 256:512].rearrange("p (q c) -> p q c", q=4), [bB], [Dst])
                                yield

                            gens = [chain(0), chain(1)]
                            alive = [True, True]
                            while any(alive):
                                for gi in range(2):
                                    if alive[gi]:
                                        try:
                                            next(gens[gi])
                                        except StopIteration:
                                            alive[gi] = False
                                        npump[0] += 1
                                        if npump[0] % PUMP_EVERY == 0:
                                            pump()
                        cp("act", Ylt[:], yps[:, :], [yps], [Ylt])
                        S.dma("pool", Yl_d[chunk], Ylt[:], reads=[Ylt], writes=[Yl_b[chunk]])
                        S.dma("pool", Qh_d[chunk], Qht[:], reads=[Qht], writes=[Qh_b[chunk]])
                        S.dma("pool", X_d[chunk], Xst[:], reads=[Xst], writes=[X_b[chunk]])
                        S.dma("pool", D_d[chunk], Dst[:], reads=[Dst], writes=[D_b[chunk]])

                for blk in range(NB):
                    phase1_block(blk)
                    if blk + 1 < NB and not BG_INJECT:
                        bg["g"] = abcd_gen(blk + 1)
                    ckpt(9)

                e1b.close()
                S.barrier()
                ckpt(10)
                Sf = sbt(e1, "Sf", [64, 16, 64], F32)
                Sb = sbt(e1, "Sb", [64, 16, 64], BF16)
                Xl = [sbt(e1, "Xl%d" % i, [64, 16, 64], BF16) for i in range(2)]
                Dl = [sbt(e1, "Dl%d" % i, [64, 16, 64], F32) for i in range(2)]
                Sfin = sbt(e1, "Sfin", [64, 16, 64], F32)
                SfB = [Buf() for _ in range(16)]

                def flat(t_, a=None, b=None):
                    ap = t_[:, :, :] if a is None else t_[:, a:b, :]
                    return ap.rearrange("p a b -> p (a b)")

                def run_seq(chunks, init_from_state, seq_out):
                    n = len(chunks)
                    if init_from_state:
                        cp("dve", flat(Sf), flat(Sf0), [Sf0], SfB)
                    else:
                        S.op("dve", lambda e: e.memset(flat(Sf), 0.0), writes=SfB)
                    cp("dve", flat(Sb), flat(Sf), SfB, [Sb])
                    for st in range(n):
                        cf, cbw = chunks[st], chunks[n - 1 - st]
                        S.dma("pool", S_d[cf][:, 0:512], flat(Sb, 0, 8), reads=[Sb], writes=[S_b[cf][0]])
                        S.dma("pool", S_d[cbw][:, 512:1024], flat(Sb, 8, 16), reads=[Sb], writes=[S_b[cbw][1]])
                        xl, dl = Xl[st % 2], Dl[st % 2]
                        S.dma("sp", flat(xl, 0, 8), X_d[cf][:, 0:512], reads=[X_b[cf]], writes=[xl])
                        S.dma("sp", flat(xl, 8, 16), X_d[cbw][:, 512:1024], reads=[X_b[cbw]], writes=[xl])
                        S.dma("sp", flat(dl, 0, 8), D_d[cf][:, 0:512], reads=[D_b[cf]], writes=[dl])
                        S.dma("sp", flat(dl, 8, 16), D_d[cbw][:, 512:1024], reads=[D_b[cbw]], writes=[dl])
                        for hd in range(16):
                            p = PS[hd // 8]
                            mm(p[0:64, (hd % 8) * 64:(hd % 8 + 1) * 64], xl[:, hd, :], Sb[:, hd, :], [xl, Sb], [p],
                               inc=(hd % 8 == 7))
                        for hd in range(16):
                            e, h = hd // 8, hd % 8
                            cch = cf if e == 0 else cbw
                            blk_, ck_ = cch // 2, cch % 2
                            c0 = ((blk_ * 2 + e) * 8 + h) * 2 + ck_
                            p = PS[hd // 8]
                            stt(Sf[:, hd, :], Sf[:, hd, :], WLh[:, c0:c0 + 1], p[0:64, (hd % 8) * 64:(hd % 8 + 1) * 64],
                                ALU.mult, ALU.add, [SfB[hd], WLh, p], [SfB[hd]])
                        tt("dve", flat(Sf), flat(Sf), flat(dl), ALU.add, SfB + [dl], SfB)
                        cp("act", flat(Sb), flat(Sf), SfB, [Sb])
                    if seq_out is not None:
                        for hd in range(16):
                            p = PS[2 + hd // 8]
                            mm(p[0:64, (hd % 8) * 64:(hd % 8 + 1) * 64], Sf[:, hd, :], identf[0:64, 0:64], [SfB[hd], identf], [p],
                               inc=(hd % 8 == 7))
                        for hf in range(2):
                            cp("dve", flat(Sfin, hf * 8, hf * 8 + 8), PS[2 + hf][0:64, :], [PS[2 + hf]], [Sfin])
                        S.dma("pool", so[seq_out], Sfin[:, :, :], reads=[Sfin])

                if NLB > 0:
                    run_seq(list(range(0, 2 * NLB)), True, None)
                for cs in range(NCS):
                    b0 = 2 * (NLB + cs)
                    run_seq([b0, b0 + 1], False, cs)

            ckpt(11)
            S.barrier()
            with ExitStack() as e3:
                wo_bf = sbt(e3, "wo_bf", [128, 8, 1024], BF16)
                FG = sbt(e3, "FG", [128, 1024], F32)
                stg3 = [sbt(e3, "stg3_%d" % i, [128, 2048], F32) for i in range(2)]
                for k in range(8):
                    g = stg3[k % 2]
                    S.dma("sp", g[:], w_in[:, k, 2048:4096], writes=[g])
                    cp("act" if k % 2 == 0 else "dve", w_bf[:, k, :], g[:], [g], [w_bf])
                for k in range(8):
                    g = stg3[k % 2]
                    S.dma("sp", g[:, 0:1024], w_out[:, k, :], writes=[g])
                    cp("act" if k % 2 == 0 else "dve", wo_bf[:, k, :], g[:, 0:1024], [g], [wo_bf])
                S.dma("sp", FG[:], fg_bc, writes=[FG])
                GATE = [sbt(e3, "GATE_%d" % s, [128, 1024], F32) for s in range(2)]
                modg = sbt(e3, "modg", [2, 1024], F32)
                sel3 = [sbt(e3, "sel3_%d" % s, [2, 128], F32) for s in range(2)]
                S.dma("sp", modg[:], mod_d, reads=[mod_b], writes=[modg])
                for s in range(2):
                    ts("dve", sel3[s][:], ones[0:2, :], identf[0:2, s:s + 1], None, ALU.mult, None, [ones, identf], [sel3[s]])
                    for n in range(2):
                        p = PS[s * 2 + n]
                        mm(p[:, :], sel3[s][:], modg[:, n * 512:(n + 1) * 512], [sel3[s], modg], [p])
                        cp("act", GATE[s][:, n * 512:(n + 1) * 512], p[:, :], [p], [GATE[s]])
                hTw2 = [sbt(e3, "hTw%d" % i, [128, 8, 384], BF16) for i in range(2)]
                x3 = [sbt(e3, "x3_%d" % i, [128, 1024], F32) for i in range(2)]
                Gt = [sbt(e3, "Gt%d" % cb, [128, 256], F32) for cb in range(4)]
                tmpc = sbt(e3, "tmpc", [128, 384], F32)
                cuA = sbt(e3, "cuA", [128, 4, 66], F32)
                cuB = sbt(e3, "cuB", [128, 384], F32)
                cuC = sbt(e3, "cuC", [128, 258], F32)
                cacc = sbt(e3, "cacc", [128, 256], F32)
                catT = sbt(e3, "catT", [128, 8, 256], BF16)
                Qhl2 = [sbt(e3, "Qhl%d" % i, [64, 2048], BF16) for i in range(2)]
                Sl2 = [sbt(e3, "Sl%d" % i, [64, 1024], BF16) for i in range(2)]
                Yll2 = [sbt(e3, "Yll%d" % i, [128, 512], F32) for i in range(2)]
                Yt2 = [sbt(e3, "Yt%d" % i, [128, 512], F32) for i in range(2)]
                gnY2 = [sbt(e3, "gnY%d" % i, [128, 512], BF16) for i in range(2)]
                bst2 = [sbt(e3, "bst%d" % i, [128, 8, 6], F32) for i in range(2)]
                mv2 = [sbt(e3, "mv%d" % i, [128, 8, 2], F32) for i in range(2)]
                rs2 = [sbt(e3, "rs%d" % i, [128, 8], F32) for i in range(2)]
                yat4 = [[sbt(e3, "yat%d%d" % (i, cb), [128, 128], F32) for cb in range(4)] for i in range(2)]
                catB = [Buf(), Buf()]
                catBB = Buf()
                sgl2 = [sbt(e3, "sgl%d" % i, [128, 4, 256], F32) for i in range(2)]
                bvl2 = [sbt(e3, "bvl%d" % i, [128, 4, 256], F32) for i in range(2)]
                yo2 = [sbt(e3, "yo%d" % i, [128, 1024], F32) for i in range(2)]
                junk32 = [sbt(e3, "junk3_%d" % i, [128, 1024], BF16) for i in range(2)]
                ss32 = [sbt(e3, "ss3_%d" % i, [128, 4], F32) for i in range(2)]
                S.op("dve", lambda e: e.memset(cuA[:, :, :].rearrange("p a b -> p (a b)"), 0.0), writes=[cuA])
                S.op("dve", lambda e: e.memset(cuC[:], 0.0), writes=[cuC])

                def hflat(a, b):
                    return hTw[:, :, a:b]

                def phase3_block(blk):
                    hTw, sgl, bvl = hTw2[blk % 2], sgl2[blk % 2], bvl2[blk % 2]
                    lat = blk < NLB
                    s = 0 if lat else 1
                    tok0 = blk * 256
                    hv = lambda b_: hT_d[b_].rearrange("p (k t) -> p k t", k=8)
                    if lat:
                        if blk == 0:
                            S.op("dve", lambda e: e.memset(hTw[:, :, 0:64], 0.0), writes=[hTw])
                        else:
                            S.dma("sp", hTw[:, :, 0:64], hv(blk - 1)[:, :, 192:256], reads=[hT_b[blk - 1]], writes=[hTw])
                        if blk == NLB - 1:
                            S.op("dve", lambda e: e.memset(hTw[:, :, 320:384], 0.0), writes=[hTw])
                        else:
                            S.dma("sp", hTw[:, :, 320:384], hv(blk + 1)[:, :, 0:64], reads=[hT_b[blk + 1]], writes=[hTw])
                    S.dma("sp", hTw[:, :, 64:320], hv(blk), reads=[hT_b[blk]], writes=[hTw])
                    S.dma("sp", sgl[:, :, :].rearrange("p a b -> p (a b)"), sg_d[blk], reads=[sg_b[blk]], writes=[sgl])
                    S.dma("sp", bvl[:, :, :].rearrange("p a b -> p (a b)"), bv_d[blk], reads=[bv_b[blk]], writes=[bvl])
                    for cb in range(4):
                        pb, pg = PS[(cb % 2) * 2], PS[(cb % 2) * 2 + 1]
                        hs = slice(0, 256)
                        for k in range(8):
                            mm(pb[:, hs], w_bf[:, k, cb * 128:(cb + 1) * 128], hTw[:, k, 64:320], [w_bf, hTw], [pb],
                               start=(k == 0), stop=(k == 7), inc=(k == 7))
                        for k in range(8):
                            mm(pg[:, hs], w_bf[:, k, (12 + cb) * 128:(13 + cb) * 128], hTw[:, k, 64:320], [w_bf, hTw], [pg],
                               start=(k == 0), stop=(k == 7), inc=(k == 7))
                        sigm(tmpc[:, 0:256], pg[:, hs], [pg], [tmpc])
                        tt("dve", tmpc[:, 0:256], pg[:, hs], tmpc[:, 0:256], ALU.mult, [pg, tmpc], [tmpc])
                        tt("dve", Gt[cb][:], pb[:, hs], tmpc[:, 0:256], ALU.mult, [pb, tmpc], [Gt[cb]])
                    for cb in range(4):
                        pc, pu = PS[4 + (cb % 2)], PS[6 + (cb % 2)]
                        wide = lat and cb >= 2
                        n0, n1 = (0, 384) if wide else (64, 320)
                        N = n1 - n0
                        for k in range(8):
                            mm(pc[:, 0:N], w_bf[:, k, (4 + cb) * 128:(5 + cb) * 128], hTw[:, k, n0:n1], [w_bf, hTw], [pc],
                               start=(k == 0), stop=(k == 7), inc=(k == 7))
                        for k in range(8):
                            mm(pu[:, 0:N], w_bf[:, k, (8 + cb) * 128:(9 + cb) * 128], hTw[:, k, n0:n1], [w_bf, hTw], [pu],
                               start=(k == 0), stop=(k == 7), inc=(k == 7))
                        cp("act", tmpc[:, 0:N], pc[:, 0:N], [pc], [tmpc])
                        cw = [col(48 + j * 4 + cb) for j in range(3)]
                        if not lat:
                            tt("dve", cuC[:, 1:257], tmpc[:, 0:256], pu[:, 0:256], ALU.mult, [tmpc, pu], [cuC])
                            prev, ctr, nxt, cub = cuC[:, 0:256], cuC[:, 1:257], cuC[:, 2:258], cuC
                            accv = cacc[:]
                        elif wide:
                            tt("dve", cuB[:], tmpc[:, 0:384], pu[:, 0:384], ALU.mult, [tmpc, pu], [cuB])
                            prev, ctr, nxt, cub = cuB[:, 0:256], cuB[:, 64:320], cuB[:, 128:384], cuB
                            accv = cacc[:]
                        else:
                            tt("dve", cuA[:, :, 1:65], tmpc[:, 0:256].rearrange("p (r w) -> p r w", r=4),
                               pu[:, 0:256].rearrange("p (r w) -> p r w", r=4), ALU.mult, [tmpc, pu], [cuA])
                            prev, ctr, nxt, cub = cuA[:, :, 0:64], cuA[:, :, 1:65], cuA[:, :, 2:66], cuA
                            accv = cacc[:].rearrange("p (r w) -> p r w", r=4)
                        ts("dve", accv, ctr, cw[1], None, ALU.mult, None, [cub, cols], [cacc])
                        stt(accv, prev, cw[0], accv, ALU.mult, ALU.add, [cub, cols, cacc], [cacc])
                        stt(accv, nxt, cw[2], accv, ALU.mult, ALU.add, [cub, cols, cacc], [cacc])
                        tt("pool", catT[:, 4 + cb, :], cacc[:], Gt[cb][:], ALU.mult, [cacc, Gt[cb]], [catBB])
                    for ck in range(2):
                        chunk = blk * 2 + ck
                        S.dma("sp", Qhl2[ck][:], Qh_d[chunk], reads=[Qh_b[chunk]], writes=[Qhl2[ck]])
                        S.dma("sp", Sl2[ck][:], S_d[chunk], reads=[S_b[chunk][0], S_b[chunk][1]], writes=[Sl2[ck]])
                        S.dma("sp", Yll2[ck][:], Yl_d[chunk], reads=[Yl_b[chunk]], writes=[Yll2[ck]])
                        S.dma("sp", x3[ck][:], xs[tok0 + ck * 128:tok0 + (ck + 1) * 128, :], writes=[x3[ck]])

                    def a_tile(ck):
                        tc = slice(ck * 128, (ck + 1) * 128)
                        Qhl, Sl, Yll, xt = Qhl2[ck], Sl2[ck], Yll2[ck], x3[ck]
                        Yt, gnY, bst, mv, rs, yo, junk3, ss3 = Yt2[ck], gnY2[ck], bst2[ck], mv2[ck], rs2[ck], yo2[ck], junk32[ck], ss32[ck]
                        yp, pt, po2 = (PS[6], PS[7], [PS[4], PS[5]]) if ck == 0 else (PS[2], PS[3], [PS[0], PS[1]])
                        for h in range(8):
                            for e in range(2):
                                mm(yp[:, h * 64:(h + 1) * 64], Qhl[:, (e * 8 + h) * 128:(e * 8 + h + 1) * 128],
                                   Sl[:, (e * 8 + h) * 64:(e * 8 + h + 1) * 64], [Qhl, Sl], [yp], start=(e == 0), stop=(e == 1),
                                   inc=(h == 7 and e == 1))
                        yield
                        tt("dve", Yt[:], yp[:, :], Yll[:], ALU.add, [yp, Yll], [Yt])
                        for h in range(8):
                            S.op("dve", lambda e_: e_.bn_stats(out=bst[:, h, :], in_=Yt[:, h * 64:(h + 1) * 64]), reads=[Yt],
                                 writes=[bst])
                        for h in range(8):
                            S.op("dve", lambda e_: e_.bn_aggr(out=mv[:, h, :], in_=bst[:, h, :]), reads=[bst], writes=[mv])
                        ts("dve", rs[:], mv[:, :, 1], GN_EPS, None, ALU.add, None, [mv], [rs])
                        yield
                        act(rs[:], rs[:], AF.Ln, [rs], [rs])
                        act(rs[:], rs[:], AF.Exp, [rs], [rs], scale=-0.5)
                        yield
                        for h in range(8):
                            ts("dve", gnY[:, h * 64:(h + 1) * 64], Yt[:, h * 64:(h + 1) * 64], mv[:, h, 0:1], rs[:, h:h + 1],
                               ALU.subtract, ALU.mult, [Yt, mv, rs], [gnY])
                        yield
                        for cb in range(4):
                            mm(pt[:, cb * 128:(cb + 1) * 128], gnY[:, cb * 128:(cb + 1) * 128], identb[:], [gnY, identb], [pt],
                               inc=(cb == 3))
                        yield
                        for cb in range(4):
                            yat = yat4[ck][cb]
                            ts("dve", yat[:], pt[:, cb * 128:(cb + 1) * 128], col(40 + cb), col(44 + cb), ALU.mult, ALU.add,
                               [pt, cols], [yat])
                            tt("pool", yat[:], yat[:], bvl[:, cb, tc], ALU.add, [yat, bvl], [yat])
                            tt("pool", catT[:, cb, tc], yat[:], sgl[:, cb, tc], ALU.mult, [yat, sgl], [catB[ck]])
                        yield
                        for n in range(2):
                            po = po2[n]
                            for m in range(8):
                                mm(po[:, :], catT[:, m, tc], wo_bf[:, m, n * 512:(n + 1) * 512], [catB[ck], catBB, wo_bf], [po],
                                   start=(m == 0), stop=(m == 7), inc=(m == 7))
                        yield
                        for n in range(2):
                            hs = slice(n * 512, (n + 1) * 512)
                            tt("dve", yo[:, hs], po2[n][:, :], GATE[s][:, hs], ALU.mult, [po2[n], GATE[s]], [yo])
                        tt("dve", yo[:], yo[:], xt[:], ALU.add, [yo, xt], [yo])
                        yield
                        act(junk3[:], yo[:], AF.Square, [yo], [junk3, ss3], accum=ss3[:, 0:1])
                        yield
                        ts("dve", ss3[:, 1:2], ss3[:, 0:1], 1.0 / 1024, NORM_EPS, ALU.mult, ALU.add, [ss3], [ss3])
                        yield
                        act(ss3[:, 2:3], ss3[:, 1:2], AF.Ln, [ss3], [ss3])
                        act(ss3[:, 3:4], ss3[:, 2:3], AF.Exp, [ss3], [ss3], scale=-0.5)
                        yield
                        stt(yo[:], yo[:], ss3[:, 3:4], FG[:], ALU.mult, ALU.mult, [yo, ss3, FG], [yo])
                        S.dma("pool", ys[tok0 + ck * 128:tok0 + (ck + 1) * 128, :], yo[:], reads=[yo])

                    gens3 = [a_tile(0), a_tile(1)]
                    alive3 = [True, True]
                    while any(alive3):
                        for gi in range(2):
                            if alive3[gi]:
                                try:
                                    next(gens3[gi])
                                except StopIteration:
                                    alive3[gi] = False

                for blk in range(NB):
                    phase3_block(blk)
                    ckpt(12)
        except _Stop:
            pass
        S.off = False
        S.finish("sp")
        S.finish("pool")
    nc._marks = SS[0].marks
    return nc


def _prep_shared(inp):
    f = np.float32
    d = {}
    d["w_ada"] = np.ascontiguousarray(inp["w_ada"][0].reshape(8, 128, 3072).transpose(1, 0, 2), dtype=f)
    d["b_ada2"] = np.ascontiguousarray(np.stack([inp["b_ada"][0], inp["b_ada"][0]], 0), dtype=f)
    d["ngc"] = np.ascontiguousarray(np.asarray(inp["norm_g"][0], dtype=f).reshape(8, 128).T)
    d["fg_bc"] = np.ascontiguousarray(np.broadcast_to(inp["final_g"][None, :], (128, 1024)), dtype=f)
    d["w_in"] = np.ascontiguousarray(inp["w_in"][0].reshape(8, 128, 4096).transpose(1, 0, 2), dtype=f)
    ld = np.concatenate([inp["decay_down"][0, 0], inp["decay_down"][0, 1], inp["iclr_down"][0, 0], inp["iclr_down"][0, 1]],
                        axis=1)
    d["lora_dn"] = np.ascontiguousarray(ld.reshape(8, 128, 256).transpose(1, 0, 2), dtype=f)
    d["dec_up"] = np.ascontiguousarray(inp["decay_up"][0].reshape(128, 512), dtype=f)
    d["icl_up"] = np.ascontiguousarray(inp["iclr_up"][0].reshape(128, 512), dtype=f)

    def c4(v):
        return np.asarray(v, dtype=f).reshape(4, 128).T

    cl = [c4(inp["shift_mu"][0, q]) for q in range(3)]
    cl += [c4(inp["decay_w0"][0, e]) for e in range(2)]
    cl += [c4(inp["iclr_bias"][0, e]) for e in range(2)]
    cl += [c4(inp["kk_scale"][0]), c4(inp["ka_scale"][0]), c4(inp["bonus_rk"][0]), c4(inp["gn_w"][0]), c4(inp["gn_b"][0])]
    cl += [c4(inp["conv_w"][0, j]) for j in range(3)]
    d["cols"] = np.ascontiguousarray(np.concatenate(cl, axis=1), dtype=f)
    d["w_out"] = np.ascontiguousarray(inp["w_out"][0].reshape(8, 128, 1024).transpose(1, 0, 2), dtype=f)
    return d


def _core_inputs(shared, x_lat, x_ctx, c_lat, c_ctx, st):
    f = np.float32
    m = dict(shared)
    parts = []
    if x_lat is not None:
        parts.append(np.asarray(x_lat, dtype=f).reshape(-1, 1024))
    if x_ctx is not None and len(x_ctx):
        parts.append(np.asarray(x_ctx, dtype=f).reshape(-1, 1024))
    m["xs"] = np.ascontiguousarray(np.concatenate(parts, 0))
    cv = np.stack([np.asarray(c_lat, dtype=f), np.asarray(c_ctx, dtype=f)], 0)
    m["cT"] = np.ascontiguousarray(cv.reshape(2, 8, 128).transpose(2, 1, 0).reshape(128, 16))
    m["st0"] = np.ascontiguousarray(np.asarray(st, dtype=f).transpose(2, 0, 1, 3).reshape(64, 16, 64))
    return m


_PROG = {}


def kernel(**inputs):
    inp = {k: np.asarray(v) for k, v in inputs.items()}
    NCORES = 8
    NLB, NCS = 16, 4
    shared = _prep_shared(inp)
    in_maps = []
    for b in range(NCORES):
        in_maps.append(_core_inputs(shared, inp["x_sample"][b], inp["x_prompt"][4 * b:4 * b + 4], inp["c"][b], inp["c_ctx"],
                                    inp["state_wkv"][b, 0]))
    key = (NLB, NCS)
    if key not in _PROG:
        _PROG[key] = build_program(NLB, NCS)
    res = run_bass_kernel_spmd(_PROG[key], in_maps, core_ids=list(range(NCORES)))
    y_prompt = np.zeros((32, 256, 1024), np.float32)
    y_sample = np.zeros((8, 4096, 1024), np.float32)
    new_state = np.zeros((32, 1, 2, 8, 64, 64), np.float32)
    for b in range(NCORES):
        r = res.results[b]
        ysb = np.asarray(r["ys"])
        y_sample[b] = ysb[:4096]
        y_prompt[4 * b:4 * b + 4] = ysb[4096:].reshape(4, 256, 1024)
        sob = np.asarray(r["so"]).reshape(4, 64, 2, 8, 64)
        new_state[4 * b:4 * b + 4, 0] = sob.transpose(0, 2, 3, 1, 4)
    return (y_prompt, y_sample, new_state)
```

```python
import os
import numpy as np
import concourse.bass as bass
import concourse.mybir as mybir
from concourse.bass_utils import run_bass_kernel_spmd
from contextlib import ExitStack

F32, BF16, I32 = mybir.dt.float32, mybir.dt.bfloat16, mybir.dt.int32
ALU = mybir.AluOpType
AF = mybir.ActivationFunctionType
C0 = 0.6065306597126334
NORM_EPS = 1e-6
GN_EPS = 64e-5
NDS = 24
BG_INJECT = os.environ.get("BG_INJECT", "1") == "1"
PUMP_EVERY = int(os.environ.get("PUMP_EVERY", "1"))
KCUT = int(os.environ.get("KCUT", "5"))
RAW_ONLY = os.environ.get("RAW_ONLY", "0") == "1"
SELF_SKIP = tuple(os.environ.get("SELF_SKIP", "pe").split(","))


class Buf:
    __slots__ = ("w", "r")

    def __init__(self):
        self.w = None
        self.r = {}


class T:
    def __init__(self, t):
        self.t = t
        self.b = Buf()

    def __getitem__(self, idx):
        return self.t[idx]


def _b(x):
    return x.b if hasattr(x, "b") else x


class Sched:
    def __init__(self, nc, es):
        self.nc = nc
        self.eng = {"pe": nc.tensor, "dve": nc.vector, "act": nc.scalar, "pool": nc.gpsimd, "sp": nc.sync}
        self.sem = {k: es.enter_context(nc.semaphore("s_" + k)) for k in self.eng}
        self.cnt = {k: 0 for k in self.eng}
        self.dsem = [es.enter_context(nc.semaphore("d%d" % i)) for i in range(NDS)]
        self.dcnt = [0] * NDS
        self.dnext = 0
        self.dnext2 = 0
        self.waited = {}
        self.nwait = 0
        self.off = False
        self.nops = {k: 0 for k in self.eng}
        self.marks = []

    def _semh(self, key):
        return self.sem[key] if isinstance(key, str) else self.dsem[key]

    def _wait(self, e, key, val):
        if val <= 0:
            return
        if e == key and e in SELF_SKIP:
            return
        if self.waited.get((e, key), 0) >= val:
            return
        self.eng[e].wait_ge(self._semh(key), val)
        self.waited[(e, key)] = val
        self.nwait += 1

    def _deps(self, e, reads, writes):
        for b in reads:
            b = _b(b)
            if b.w:
                self._wait(e, *b.w)
        raw_only = RAW_ONLY and e in ("act", "dve")
        for b in writes:
            b = _b(b)
            if b.w and not (raw_only and b.w[0] == e):
                self._wait(e, *b.w)
            for k, v in b.r.items():
                if raw_only and k == e:
                    continue
                self._wait(e, k, v)

    def _mark(self, key, tgt, reads, writes):
        for b in writes:
            b = _b(b)
            b.w = (key, tgt)
            b.r = {}
        for b in reads:
            b = _b(b)
            if b.r.get(key, 0) < tgt:
                b.r[key] = tgt

    def op(self, e, fn, reads=(), writes=(), inc=True):
        if self.off:
            return
        self._deps(e, reads, writes)
        inst = fn(self.eng[e])
        self.nops[e] += 1
        tgt = self.cnt[e] + 1
        if inc:
            inst.then_inc(self.sem[e], 1)
            self.cnt[e] = tgt
        self._mark(e, tgt, reads, writes)

    def dma(self, e, out, in_, reads=(), writes=()):
        if self.off:
            return
        if e == "pool":
            i = NDS - 8 + self.dnext2
            self.dnext2 = (self.dnext2 + 1) % 8
        else:
            i = self.dnext
            self.dnext = (i + 1) % (NDS - 8)
        self._deps(e, reads, writes)
        self._wait(e, i, self.dcnt[i])
        self.eng[e].dma_start(out=out, in_=in_).then_inc(self.dsem[i], 16)
        self.dcnt[i] += 16
        self._mark(i, self.dcnt[i], reads, writes)

    def mark(self, label):
        self.marks.append((label, dict(self.nops)))

    def barrier(self):
        if self.off:
            return
        for e in self.eng:
            for k in self.eng:
                if k != e:
                    self._wait(e, k, self.cnt[k])
            for i in range(NDS):
                self._wait(e, i, self.dcnt[i])

    def finish(self, e="sp"):
        for i in range(NDS):
            self._wait(e, i, self.dcnt[i])


class _Stop(Exception):
    pass


def build_program(NLB, NCS, debug=False, stop=None):
    SS = []

    def ckpt(n):
        SS[0].mark(n)
        if stop is not None and n == stop:
            SS[0].off = True

    NB = NLB + NCS
    NCH = 2 * NB
    NTOK = NB * 256
    nc = bass.Bass("TRN2", target_bir_lowering=False)

    def din(name, shape, dt=F32):
        return nc.dram_tensor(name, list(shape), dt, kind="ExternalInput").ap()

    def dout(name, shape, dt=F32):
        return nc.dram_tensor(name, list(shape), dt, kind="ExternalOutput").ap()

    def dscr(name, shape, dt):
        return nc.dram_tensor(name, list(shape), dt, kind=("ExternalOutput" if debug else "Internal")).ap()

    xs = din("xs", [NTOK, 1024])
    cT = din("cT", [128, 16])
    st0 = din("st0", [64, 16, 64])
    w_ada = din("w_ada", [128, 8, 3072])
    b_ada2 = din("b_ada2", [2, 3072])
    ngc_d = din("ngc", [128, 8])
    fg_bc = din("fg_bc", [128, 1024])
    w_in = din("w_in", [128, 8, 4096])
    lora_dn = din("lora_dn", [128, 8, 256])
    dec_up = din("dec_up", [128, 512])
    icl_up = din("icl_up", [128, 512])
    cols_d = din("cols", [128, 60])
    w_out = din("w_out", [128, 8, 1024])
    ys = dout("ys", [NTOK, 1024])
    so = dout("so", [max(NCS, 1), 64, 16, 64])

    hT_d = dscr("hT_d", [NB, 128, 8 * 256], BF16)
    Yl_d = dscr("Yl_d", [NCH, 128, 512], F32)
    Qh_d = dscr("Qh_d", [NCH, 64, 2048], BF16)
    X_d = dscr("X_d", [NCH, 64, 1024], BF16)
    D_d = dscr("D_d", [NCH, 64, 1024], F32)
    sg_d = dscr("sg_d", [NB, 128, 1024], F32)
    bv_d = dscr("bv_d", [NB, 128, 1024], F32)
    S_d = dscr("S_d", [NCH, 64, 1024], BF16)
    mod_d = dscr("mod_d", [2, 1024], F32)
    mod_b = Buf()
    hT_b = [Buf() for _ in range(NB)]
    Yl_b = [Buf() for _ in range(NCH)]
    Qh_b = [Buf() for _ in range(NCH)]
    X_b = [Buf() for _ in range(NCH)]
    D_b = [Buf() for _ in range(NCH)]
    sg_b = [Buf() for _ in range(NB)]
    bv_b = [Buf() for _ in range(NB)]
    S_b = [[Buf(), Buf()] for _ in range(NCH)]
    ys_b = Buf()
    so_b = Buf()

    with ExitStack() as es:
        S = Sched(nc, es)
        SS.append(S)

        try:
            def sbt(es_, name, shape, dt):
                return T(es_.enter_context(nc.sbuf_tensor("sb_" + name, list(shape), dt)))

            PS = [T(es.enter_context(nc.psum_tensor("ps%d" % i, [128, 512], F32))) for i in range(8)]

            def mm(out, lhsT, rhs, R, W, start=True, stop=True, inc=True):
                S.op("pe", lambda e: e.matmul(out, lhsT=lhsT, rhs=rhs, start=start, stop=stop, skip_group_check=True),
                     reads=R, writes=W, inc=inc)

            def tt(eng, out, a, b, op, R, W):
                S.op(eng, lambda e: e.tensor_tensor(out=out, in0=a, in1=b, op=op), reads=R, writes=W)

            def ts(eng, out, a, s1, s2, op0, op1, R, W):
                if op1 is None:
                    S.op(eng, lambda e: e.tensor_scalar(out=out, in0=a, scalar1=s1, scalar2=None, op0=op0), reads=R, writes=W)
                else:
                    S.op(eng, lambda e: e.tensor_scalar(out=out, in0=a, scalar1=s1, scalar2=s2, op0=op0, op1=op1),
                         reads=R, writes=W)

            def stt(out, a, sc, b, op0, op1, R, W):
                S.op("dve", lambda e: e.scalar_tensor_tensor(out=out, in0=a, scalar=sc, in1=b, op0=op0, op1=op1),
                     reads=R, writes=W)

            def act(out, in_, func, R, W, bias=None, scale=None, accum=None):
                kw = {}
                if bias is not None:
                    kw["bias"] = bias
                if scale is not None:
                    kw["scale"] = scale
                if accum is not None:
                    kw["accum_out"] = accum
                S.op("act", lambda e: e.activation(out=out, in_=in_, func=func, **kw), reads=R, writes=W)

            def sigm(out, in_, R, W, nbias=None):
                act(out, in_, AF.Exp, R, W, scale=-1.0, bias=nbias)
                act(out, out, AF.Ln, W, W, bias=1.0)
                act(out, out, AF.Exp, W, W, scale=-1.0)

            def cp(eng, out, in_, R, W):
                if eng == "act":
                    S.op("act", lambda e: e.copy(out=out, in_=in_), reads=R, writes=W)
                else:
                    S.op(eng, lambda e: e.tensor_copy(out=out, in_=in_), reads=R, writes=W)

            ioi = sbt(es, "ioi", [128, 128], I32)
            iof = sbt(es, "iof", [128, 128], F32)
            identf = sbt(es, "identf", [128, 128], F32)
            identb = sbt(es, "identb", [128, 128], BF16)
            bones = sbt(es, "bones", [128, 128], F32)
            ones = sbt(es, "ones", [128, 128], F32)
            MK = {k: sbt(es, "mk_" + k, [128, 512], BF16) for k in ("LT", "GT", "LE", "GE", "NLT", "NGT")}
            cols = sbt(es, "cols", [128, 60], F32)
            dcols = sbt(es, "dcols", [128, 44], F32)
            w_bf = sbt(es, "w_bf", [128, 8, 2048], BF16)
            WLh = sbt(es, "WLh", [64, NB * 32], F32)

            S.op("pool", lambda e: e.iota(ioi[:], pattern=[[1, 128]], base=0, channel_multiplier=-1), writes=[ioi])
            cp("dve", iof[:], ioi[:], [ioi], [iof])
            ts("dve", identf[:], iof[:], 0.0, None, ALU.is_equal, None, [iof], [identf])
            cp("dve", identb[:], identf[:], [identf], [identb])
            S.op("dve", lambda e: e.memset(ones[:], 1.0), writes=[ones])
            S.op("dve", lambda e: e.memset(bones[:], 0.0), writes=[bones])
            S.op("dve", lambda e: e.memset(bones[0:64, 0:64], 1.0), writes=[bones])
            S.op("dve", lambda e: e.memset(bones[64:128, 64:128], 1.0), writes=[bones])
            for j in range(4):
                sl = slice(j * 128, (j + 1) * 128)
                ts("dve", MK["LT"][:, sl], iof[:], 0.0, None, ALU.is_gt, None, [iof], [MK["LT"]])
                ts("dve", MK["GT"][:, sl], iof[:], 0.0, None, ALU.is_lt, None, [iof], [MK["GT"]])
                ts("dve", MK["LE"][:, sl], iof[:], 0.0, None, ALU.is_ge, None, [iof], [MK["LE"]])
                ts("dve", MK["GE"][:, sl], iof[:], 0.0, None, ALU.is_le, None, [iof], [MK["GE"]])
                ts("dve", MK["NLT"][:, sl], iof[:], 0.0, -1.0, ALU.is_gt, ALU.mult, [iof], [MK["NLT"]])
                ts("dve", MK["NGT"][:, sl], iof[:], 0.0, -1.0, ALU.is_lt, ALU.mult, [iof], [MK["NGT"]])
            S.dma("sp", cols[:], cols_d, writes=[cols])
            ts("dve", dcols[:, 0:12], cols[:, 0:12], -1.0, 1.0, ALU.mult, ALU.add, [cols], [dcols])
            ts("dve", dcols[:, 12:24], cols[:, 0:12], 0.5, None, ALU.mult, None, [cols], [dcols])
            ts("dve", dcols[:, 24:28], cols[:, 32:36], -1.0, 1.0, ALU.mult, ALU.add, [cols], [dcols])
            ts("dve", dcols[:, 28:44], cols[:, 12:28], -1.0, None, ALU.mult, None, [cols], [dcols])
            ckpt(1)

            def col(i):
                return cols[:, i:i + 1]

            def dcol(i):
                return dcols[:, i:i + 1]

            with ExitStack() as e1:
                g1c = [sbt(e1, "g1c%d" % s, [128, 8], F32) for s in range(2)]
                shc = [sbt(e1, "shc%d" % s, [128, 8], F32) for s in range(2)]
                lora_bf = sbt(e1, "lora_bf", [128, 8, 256], BF16)
                dup_bf = sbt(e1, "dup_bf", [128, 512], BF16)
                iup_bf = sbt(e1, "iup_bf", [128, 512], BF16)
                Sf0 = sbt(e1, "Sf0", [64, 16, 64], F32)
                with ExitStack() as e0:
                    stg = [sbt(e0, "stg%d" % i, [128, 3072], F32) for i in range(2)]
                    ngt = sbt(e0, "ngt", [128, 8], F32)
                    cTt = sbt(e0, "cTt", [128, 16], F32)
                    scT = sbt(e0, "scT", [128, 16], F32)
                    modv = sbt(e0, "modv", [2, 3072], F32)
                    bad = sbt(e0, "bad", [2, 3072], F32)
                    st_in = sbt(e0, "st_in", [64, 16, 64], F32)
                    for k in range(8):
                        g = stg[k % 2]
                        S.dma("sp", g[:, 0:2048], w_in[:, k, 0:2048], writes=[g])
                        cp("act" if k % 2 == 0 else "dve", w_bf[:, k, :], g[:, 0:2048], [g], [w_bf])
                    g = stg[0]
                    S.dma("sp", g[:, 0:2048], lora_dn.rearrange("p k n -> p (k n)"), writes=[g])
                    cp("dve", lora_bf[:, :, :].rearrange("p k n -> p (k n)"), g[:, 0:2048], [g], [lora_bf])
                    g = stg[1]
                    S.dma("sp", g[:, 0:512], dec_up, writes=[g])
                    S.dma("sp", g[:, 512:1024], icl_up, writes=[g])
                    cp("dve", dup_bf[:], g[:, 0:512], [g], [dup_bf])
                    cp("dve", iup_bf[:], g[:, 512:1024], [g], [iup_bf])
                    ckpt(2)
                    S.dma("sp", cTt[:], cT, writes=[cTt])
                    act(scT[:], cTt[:], AF.Silu, [cTt], [scT])
                    S.dma("sp", bad[:], b_ada2, writes=[bad])
                    S.dma("sp", ngt[:], ngc_d, writes=[ngt])
                    for k in range(8):
                        g = stg[k % 2]
                        S.dma("sp", g[:], w_ada[:, k, :], writes=[g])
                        for n in range(6):
                            mm(PS[n][0:2, :], scT[:, 2 * k:2 * k + 2], g[:, n * 512:(n + 1) * 512], [scT, g], [PS[n]],
                               start=(k == 0), stop=(k == 7))
                    for n in range(6):
                        tt("dve", modv[:, n * 512:(n + 1) * 512], PS[n][0:2, :], bad[:, n * 512:(n + 1) * 512], ALU.add,
                           [PS[n], bad], [modv])
                    S.dma("sp", mod_d, modv[:, 2048:3072], reads=[modv], writes=[mod_b])
                    for part in range(2):
                        for k in range(8):
                            c0 = (part * 8 + k) * 2
                            mm(PS[0][:, c0:c0 + 2], modv[0:2, part * 1024 + k * 128:part * 1024 + (k + 1) * 128],
                               identf[0:2, 0:2], [modv, identf], [PS[0]], inc=(part == 1 and k == 7))
                    mview = PS[0][:, 0:32].rearrange("p (a k s) -> p a k s", a=2, k=8)
                    for s in range(2):
                        cp("dve", shc[s][:], mview[:, 0, :, s], [PS[0]], [shc[s]])
                        stt(g1c[s][:], mview[:, 1, :, s], 1.0, ngt[:], ALU.add, ALU.mult, [PS[0], ngt], [g1c[s]])
                    ckpt(3)
                    S.dma("sp", st_in[:], st0, writes=[st_in])
                    for hd in range(16):
                        p = PS[hd // 8]
                        mm(p[0:64, (hd % 8) * 64:(hd % 8 + 1) * 64], st_in[:, hd, :], identf[0:64, 0:64], [st_in, identf], [p],
                           inc=(hd % 8 == 7))
                    for hf in range(2):
                        cp("dve", Sf0[:, hf * 8:(hf + 1) * 8, :].rearrange("p a b -> p (a b)"), PS[hf][0:64, :], [PS[hf]], [Sf0])

                S.barrier()
                ckpt(4)
                e1b = ExitStack()
                e1b.__enter__()
                xts = [sbt(e1b, "xt0", [128, 1024], F32)] * 2
                hb = sbt(e1b, "hb", [128, 1024], BF16)
                ss = sbt(e1b, "ss", [128, 4], F32)
                hT = sbt(e1b, "hT", [128, 8, 256], BF16)
                ppL = sbt(e1b, "ppL", [128, 4, 66], F32)
                ppC = sbt(e1b, "ppC", [128, 1, 258], F32)
                nbt = sbt(e1b, "nbt", [128, 256], F32)
                RKV = [[sbt(e1b, "rkv%d_%d" % (q, cb), [128, 256], F32) for cb in range(4)] for q in range(3)]
                vbf = [sbt(e1b, "vbf%d" % cb, [128, 256], BF16) for cb in range(4)]
                sgT = sbt(e1b, "sgT", [128, 4, 256], F32)
                bvT = sbt(e1b, "bvT", [128, 4, 256], F32)
                lwd = sbt(e1b, "lwd", [128, 256], BF16)
                lwi = sbt(e1b, "lwi", [128, 256], BF16)
                TMPC = [{n: sbt(e1b, "tmc%d_%s" % (i, n), [128, 256], F32) for n in ["sq", "kk", "rkd"]} for i in range(2)]
                TMPC[0]["rn"] = TMPC[1]["rn"] = sbt(e1b, "tmc_rn", [128, 256], F32)
                TMP = TMPC[0]
                TMPE = [{n: sbt(e1b, "tm0_%s" % n, [128, 256] if n != "tcol" else [128, 4], F32)
                         for n in ["sig", "pi", "px", "E1", "E2", "E3", "E4", "a", "bq", "kd", "tcol"]}]
                TMPE.append(TMPE[0])
                WLc = [sbt(e1b, "WLc%d" % i, [128, 4], F32) for i in range(2)]
                FM4 = {n: [[[sbt(e1b, "fm_%s%d%d%d" % (n, par, e, cb), [128, 256], BF16) for cb in range(4)] for e in range(2)]
                           for par in range(2)] for n in ("Qt", "KKt", "Kt", "Bt")}
                FMs = {n: [[sbt(e1b, "fm_%s%d%d" % (n, e, cb), [128, 256], BF16) for cb in range(4)] for e in range(2)]
                       for n in ("Kh", "Bh")}

                def FMt(n, par, e, cb):
                    return FMs[n][e][cb] if n in FMs else FM4[n][par][e][cb]

                Khtm = [[[sbt(e1b, "khtm%d%d%d" % (par, ck, e), [128, 512], BF16) for e in range(2)] for ck in range(2)]
                        for par in range(2)]
                Bhtm = [[[sbt(e1b, "bhtm%d%d%d" % (par, ck, e), [128, 512], BF16) for e in range(2)] for ck in range(2)]
                        for par in range(2)]
                Vtm = [[sbt(e1b, "vtm%d%d" % (par, ck), [128, 512], BF16) for ck in range(2)] for par in range(2)]
                XT = [[sbt(e1b, "XT%d%d" % (e, i), [128, 512], F32) for i in range(2)] for e in range(2)]
                XM = [[sbt(e1b, "XM%d%d" % (e, i), [128, 512], F32) for i in range(2)] for e in range(2)]
                AakT = [sbt(e1b, "AakT%d" % e, [128, 512], BF16) for e in range(2)]
                AqbT = [sbt(e1b, "AqbT%d" % e, [128, 512], BF16) for e in range(2)]
                AqkT = [sbt(e1b, "AqkT%d" % e, [128, 512], BF16) for e in range(2)]
                Zb = [sbt(e1b, "Zb%d" % e, [128, 512], F32) for e in range(2)]
                UGn = [sbt(e1b, "UGn%d" % e, [128, 512], BF16) for e in range(2)]
                Ylt = sbt(e1b, "Ylt", [128, 512], F32)
                Qht = sbt(e1b, "Qht", [64, 2048], BF16)
                Xst = sbt(e1b, "Xst", [64, 1024], BF16)
                Dst = sbt(e1b, "Dst", [64, 1024], F32)
                S.op("dve", lambda e: e.memset(ppL[:, :, :].rearrange("p a b -> p (a b)"), 0.0), writes=[ppL])
                S.op("dve", lambda e: e.memset(ppC[:, :, :].rearrange("p a b -> p (a b)"), 0.0), writes=[ppC])

                PB = PS[7]

                def ab_gen(blk):
                    lat = blk < NLB
                    s = 0 if lat else 1
                    tok0 = blk * 256
                    for i in range(2):
                        xt = xts[i]
                        S.dma("sp", xt[:], xs[tok0 + i * 128:tok0 + (i + 1) * 128, :], writes=[xt])
                        yield
                        act(hb[:], xt[:], AF.Square, [xt], [hb, ss], accum=ss[:, 0:1])
                        ts("dve", ss[:, 1:2], ss[:, 0:1], 1.0 / 1024, NORM_EPS, ALU.mult, ALU.add, [ss], [ss])
                        act(ss[:, 2:3], ss[:, 1:2], AF.Ln, [ss], [ss])
                        act(ss[:, 3:4], ss[:, 2:3], AF.Exp, [ss], [ss], scale=-0.5)
                        yield
                        ts("dve", hb[:], xt[:], ss[:, 3:4], None, ALU.mult, None, [xt, ss], [hb])
                        yield
                        for half in range(2):
                            p = PB
                            for k4 in range(4):
                                k = half * 4 + k4
                                mm(p[:, k4 * 128:(k4 + 1) * 128], hb[:, k * 128:(k + 1) * 128], identb[:], [hb, identb], [p],
                                   inc=(k4 == 3))
                            yield
                            for k4 in range(4):
                                k = half * 4 + k4
                                dsth = hT[:, k, i * 128:(i + 1) * 128]
                                srcp = p[:, k4 * 128:(k4 + 1) * 128]
                                if k4 % 2 == 0:
                                    act(dsth, srcp, AF.Identity, [p, g1c[s], shc[s]], [hT], scale=g1c[s][:, k:k + 1],
                                        bias=shc[s][:, k:k + 1])
                                else:
                                    ts("dve", dsth, srcp, g1c[s][:, k:k + 1], shc[s][:, k:k + 1], ALU.mult, ALU.add,
                                       [p, g1c[s], shc[s]], [hT])
                            yield
                    S.dma("pool", hT_d[blk], hT[:, :, :].rearrange("p k t -> p (k t)"), reads=[hT], writes=[hT_b[blk]])
                    pp = ppL if lat else ppC
                    R_, W_ = (4, 64) if lat else (1, 256)

                    def v3(ap):
                        return ap.rearrange("p (r w) -> p r w", r=R_)

                    hs = slice(0, 256)
                    for cbg in range(16):
                        p = PB
                        for k in range(8):
                            mm(p[:, hs], w_bf[:, k, cbg * 128:(cbg + 1) * 128], hT[:, k, :], [w_bf, hT], [p],
                               start=(k == 0), stop=(k == 7), inc=(k == 7))
                        q, cb = cbg // 4, cbg % 4
                        if q < 3:
                            cp("act", pp[:, :, 1:W_ + 1], v3(p[:, hs]), [p], [pp])
                            tt("dve", v3(nbt[:]), pp[:, :, 0:W_], pp[:, :, 2:W_ + 2], ALU.add, [pp], [nbt])
                            dst = RKV[q][cb]
                            act(v3(dst[:]), pp[:, :, 1:W_ + 1], AF.Identity, [pp, dcols], [dst], scale=dcol(q * 4 + cb))
                            stt(dst[:], nbt[:], dcol(12 + q * 4 + cb), dst[:], ALU.mult, ALU.add, [nbt, dcols, dst], [dst])
                            if q == 2:
                                cp("pool", vbf[cb][:], dst[:], [dst], [vbf[cb]])
                        else:
                            sigm(sgT[:, cb, :], p[:, hs], [p], [sgT])
                            tt("dve", sgT[:, cb, :], p[:, hs], sgT[:, cb, :], ALU.mult, [p, sgT], [sgT])
                        yield
                    S.dma("pool", sg_d[blk], sgT[:, :, :].rearrange("p a b -> p (a b)"), reads=[sgT], writes=[sg_b[blk]])
                    for mb in range(2):
                        p = PB
                        for k in range(8):
                            mm(p[:, hs], lora_bf[:, k, mb * 128:(mb + 1) * 128], hT[:, k, :], [lora_bf, hT], [p],
                               start=(k == 0), stop=(k == 7), inc=(k == 7))
                        if mb == 0:
                            tq = TMP["sq"]
                            act(tq[:], p[:, hs], AF.Exp, [p], [tq], scale=-2.0)
                            act(tq[:], tq[:], AF.Ln, [tq], [tq], bias=1.0)
                            act(tq[:], tq[:], AF.Exp, [tq], [tq], scale=-1.0)
                            ts("dve", lwd[:], tq[:], 2.0, -1.0, ALU.mult, ALU.add, [tq], [lwd])
                        else:
                            cp("dve", lwi[:], p[:, hs], [p], [lwi])
                        yield

                def cd_gen(blk):
                    par = blk % 2
                    p5 = PB

                    def c_kk(cb):
                        if False:
                            yield
                        k_ = RKV[1][cb]
                        T_ = TMPC[cb % 2]
                        act(T_["sq"][:], k_[:], AF.Square, [k_, cols], [T_["sq"]], scale=col(28 + cb))
                        mm(p5[:, 0:256], bones[:], T_["sq"][:], [bones, T_["sq"]], [p5])
                        ts("dve", T_["rn"][:], p5[:, 0:256], 1e-24, None, ALU.max, None, [p5], [T_["rn"]])
                        act(T_["rn"][:], T_["rn"][:], AF.Ln, [T_["rn"]], [T_["rn"]])
                        act(T_["rn"][:], T_["rn"][:], AF.Exp, [T_["rn"]], [T_["rn"]], scale=-0.5)
                        stt(T_["kk"][:], k_[:], col(28 + cb), T_["rn"][:], ALU.mult, ALU.mult, [k_, cols, T_["rn"]],
                            [T_["kk"]])

                    def c_front(cb, e):
                        TE = TMPE[e]
                        es_ = slice(e * 64, (e + 1) * 64)
                        pz = PB
                        mm(pz[:, 0:256], dup_bf[es_, cb * 128:(cb + 1) * 128], lwd[es_, :], [dup_bf, lwd], [pz])
                        mm(pz[:, 256:512], iup_bf[es_, cb * 128:(cb + 1) * 128], lwi[es_, :], [iup_bf, lwi], [pz])
                        sig, pi, px, tcl = TE["sig"], TE["pi"], TE["px"], TE["tcol"]
                        act(sig[:], pz[:, 0:256], AF.Exp, [pz, dcols], [sig], scale=-1.0, bias=dcol(28 + e * 4 + cb))
                        act(TE["a"][:], pz[:, 256:512], AF.Exp, [pz, dcols], [TE["a"]], scale=-1.0, bias=dcol(36 + e * 4 + cb))
                        act(sig[:], sig[:], AF.Ln, [sig], [sig], bias=1.0)
                        act(sig[:], sig[:], AF.Exp, [sig], [sig], scale=-1.0)
                        yield
                        for ck in range(2):
                            tc = slice(ck * 128, (ck + 1) * 128)
                            S.op("dve", lambda e_: e_.tensor_tensor_scan(out=pi[:, tc], data0=ones[:, 0:128],
                                                                         data1=sig[:, tc], initial=0.0,
                                                                         op0=ALU.mult, op1=ALU.add),
                                 reads=[ones, sig], writes=[pi])
                        tt("dve", px[:], pi[:], sig[:], ALU.subtract, [pi, sig], [px])
                        ts("dve", tcl[:, 0:2], pi[:, 127:256:128], -C0, None, ALU.mult, None, [pi], [tcl])
                        ts("dve", tcl[:, 2:4], pi[:, 127:256:128], C0, None, ALU.mult, None, [pi], [tcl])
                        yield
                        WL_ = WLc[cb % 2]
                        act(WL_[:, e * 2:e * 2 + 2], tcl[:, 0:2], AF.Exp, [tcl], [WL_])
                        E1, E2, E3, E4 = TE["E1"], TE["E2"], TE["E3"], TE["E4"]
                        if e == 0:
                            act(E1[:], pi[:], AF.Exp, [pi], [E1], scale=-C0)
                            act(E2[:], px[:], AF.Exp, [px], [E2], scale=-C0)
                            act(E3[:], pi[:], AF.Exp, [pi], [E3], scale=C0)
                            for ck in range(2):
                                tc = slice(ck * 128, (ck + 1) * 128)
                                act(E4[:, tc], pi[:, tc], AF.Exp, [pi, tcl], [E4], scale=C0, bias=tcl[:, ck:ck + 1])
                        else:
                            for ck in range(2):
                                tc = slice(ck * 128, (ck + 1) * 128)
                                act(E1[:, tc], px[:, tc], AF.Exp, [px, tcl], [E1], scale=C0, bias=tcl[:, ck:ck + 1])
                                act(E2[:, tc], pi[:, tc], AF.Exp, [pi, tcl], [E2], scale=C0, bias=tcl[:, ck:ck + 1])
                                act(E3[:, tc], px[:, tc], AF.Exp, [px, tcl], [E3], scale=-C0, bias=tcl[:, 2 + ck:3 + ck])
                            act(E4[:], px[:], AF.Exp, [px], [E4], scale=-C0)
                        yield
                        act(TE["a"][:], TE["a"][:], AF.Ln, [TE["a"]], [TE["a"]], bias=1.0)
                        act(TE["a"][:], TE["a"][:], AF.Exp, [TE["a"]], [TE["a"]], scale=-1.0)

                    def c_back(cb, e):
                        TE = TMPE[e]
                        T_ = TMPC[cb % 2]
                        r_, k_ = RKV[0][cb], RKV[1][cb]
                        E1, E2, E3, E4 = TE["E1"], TE["E2"], TE["E3"], TE["E4"]
                        a_, bq, kd, rkd = TE["a"], TE["bq"], TE["kd"], T_["rkd"]
                        tt("pool", bq[:], T_["kk"][:], a_[:], ALU.mult, [T_["kk"], a_], [bq])
                        ts("dve", kd[:], a_[:], col(32 + cb), dcol(24 + cb), ALU.mult, ALU.add, [a_, cols, dcols], [kd])
                        tt("dve", kd[:], kd[:], k_[:], ALU.mult, [kd, k_], [kd])
                        f = lambda n: FMt(n, par, e, cb)
                        tt("dve", f("Qt")[:], r_[:], E1[:], ALU.mult, [r_, E1], [f("Qt")])
                        tt("pool", f("KKt")[:], T_["kk"][:], E2[:], ALU.mult, [T_["kk"], E2], [f("KKt")])
                        tt("dve", f("Kt")[:], kd[:], E3[:], ALU.mult, [kd, E3], [f("Kt")])
                        yield
                        tt("pool", f("Bt")[:], bq[:], E3[:], ALU.mult, [bq, E3], [f("Bt")])
                        tt("dve", f("Kh")[:], kd[:], E4[:], ALU.mult, [kd, E4], [f("Kh")])
                        tt("pool", f("Bh")[:], bq[:], E4[:], ALU.mult, [bq, E4], [f("Bh")])
                        if e == 0:
                            tt("dve", rkd[:], r_[:], kd[:], ALU.mult, [r_, kd], [rkd])
                        else:
                            tt("dve", T_["sq"][:], r_[:], kd[:], ALU.mult, [r_, kd], [T_["sq"]])
                            stt(rkd[:], rkd[:], 1.0, T_["sq"][:], ALU.mult, ALU.add, [rkd, T_["sq"]], [rkd])

                    def c_tail(cb):
                        if False:
                            yield
                        T_ = TMPC[cb % 2]
                        v_ = RKV[2][cb]
                        rkd = T_["rkd"]
                        WL_ = WLc[cb % 2]
                        for hh in range(2):
                            mm(p5[0:64, 256 + hh * 4:256 + hh * 4 + 4], identf[:, hh * 64:(hh + 1) * 64], WL_[:, 0:4],
                               [identf, WL_], [p5])
                        for hh in range(2):
                            h = cb * 2 + hh
                            for e in range(2):
                                c0 = ((blk * 2 + e) * 8 + h) * 2
                                cp("dve", WLh[:, c0:c0 + 2], p5[0:64, 256 + hh * 4 + e * 2:256 + hh * 4 + e * 2 + 2], [p5], [WLh])
                        ts("dve", rkd[:], rkd[:], col(36 + cb), None, ALU.mult, None, [rkd, cols], [rkd])
                        mm(p5[:, 0:256], bones[:], rkd[:], [bones, rkd], [p5])
                        tt("dve", bvT[:, cb, :], p5[:, 0:256], v_[:], ALU.mult, [p5, v_], [bvT])

                    for cb in range(4):
                        yield from c_kk(cb)
                        yield
                        for e in range(2):
                            yield from c_front(cb, e)
                            yield
                            yield from c_back(cb, e)
                            yield
                        yield from c_tail(cb)
                        yield
                    S.dma("pool", bv_d[blk], bvT[:, :, :].rearrange("p a b -> p (a b)"), reads=[bvT], writes=[bv_b[blk]])

                    for ck in range(2):
                        tc = slice(ck * 128, (ck + 1) * 128)
                        jobs = [(Vtm[par][ck], vbf)] + [(Khtm[par][ck][e], FMs["Kh"][e]) for e in range(2)] + \
                               [(Bhtm[par][ck][e], FMs["Bh"][e]) for e in range(2)]
                        for ji, (dst, src) in enumerate(jobs):
                            p = PB
                            for cb in range(4):
                                mm(p[:, cb * 128:(cb + 1) * 128], src[cb][:, tc], identb[:], [src[cb], identb], [p],
                                   inc=(cb == 3))
                            cp("act" if ji % 2 == 0 else "dve", dst[:], p[:, :], [p], [dst])
                            yield


                def abcd_gen(blk):
                    yield from ab_gen(blk)
                    yield from cd_gen(blk)

                bg = {}

                def pump():
                    g = bg.get("g")
                    if g is not None:
                        try:
                            next(g)
                        except StopIteration:
                            bg["g"] = None

                def phase1_block(blk):
                    lat = blk < NLB
                    s = 0 if lat else 1
                    tok0 = blk * 256
                    if blk == 0:
                        bg["g"] = abcd_gen(0)
                    while bg.get("g") is not None:
                        pump()
                    ckpt(5)
                    if blk + 1 < NB and BG_INJECT:
                        bg["g"] = abcd_gen(blk + 1)
                    npump = [0]
                    par = blk % 2
                    ckpt(8)
                    for ck in range(2):
                        chunk = blk * 2 + ck
                        tc = slice(ck * 128, (ck + 1) * 128)
                        yps = PS[6]
                        for hg in range(2):
                            def fm(n, e, hq):
                                return FM4[n][par][e][hq][hg * 64:(hg + 1) * 64, tc]

                            def fmR(n, e, hq):
                                return [FM4[n][par][e][hq]]

                            def chain(e):
                                bA, bB, zps = PS[3 * e], PS[3 * e + 1], PS[3 * e + 2]
                                if e == 0:
                                    mSTn, mSn, mST, mIT = MK["NLT"], MK["NGT"], MK["LT"], MK["LE"]
                                else:
                                    mSTn, mSn, mST, mIT = MK["NGT"], MK["NLT"], MK["GT"], MK["GE"]
                                hsl = lambda hq: slice(hq * 128, (hq + 1) * 128)
                                for hq in range(4):
                                    mm(bA[:, hsl(hq)], fm("Bt", e, hq), fm("KKt", e, hq), fmR("Bt", e, hq) + fmR("KKt", e, hq),
                                       [bA], inc=(hq == 3))
                                tt("dve", XT[e][0][:], bA[:, :], mSTn[:], ALU.mult, [bA, mSTn], [XT[e][0]])
                                for hq in range(4):
                                    mm(bB[:, hsl(hq)], fm("KKt", e, hq), fm("Bt", e, hq), fmR("Bt", e, hq) + fmR("KKt", e, hq),
                                       [bB], inc=(hq == 3))
                                tt("dve", XM[e][0][:], bB[:, :], mSn[:], ALU.mult, [bB, mSn], [XM[e][0]])
                                yield
                                for hq in range(4):
                                    mm(bA[:, hsl(hq)], fm("Kt", e, hq), fm("KKt", e, hq), fmR("Kt", e, hq) + fmR("KKt", e, hq),
                                       [bA], inc=(hq == 3))
                                tt("dve", AakT[e][:], bA[:, :], mST[:], ALU.mult, [bA, mST], [AakT[e]])
                                for hq in range(4):
                                    h = hq * 2 + hg
                                    hh = hg
                                    mm(zps[:, hq * 128:hq * 128 + 64], AakT[e][:, hsl(hq)], Vtm[par][ck][:, h * 64:(h + 1) * 64],
                                       [AakT[e], Vtm[par][ck]], [zps], start=(hq == 0), stop=False, inc=False)
                                    mm(zps[:, hq * 128 + 64:(hq + 1) * 128], fm("KKt", e, hq),
                                       identb[hh * 64:(hh + 1) * 64, hh * 64:(hh + 1) * 64], fmR("KKt", e, hq) + [identb], [zps],
                                       start=False, stop=False, inc=(hq == 3))
                                cp("act", Zb[e][:], zps[:, :], [zps], [Zb[e]])
                                yield
                                for hq in range(4):
                                    mm(bB[:, hsl(hq)], fm("Bt", e, hq), fm("Qt", e, hq), fmR("Bt", e, hq) + fmR("Qt", e, hq),
                                       [bB], inc=(hq == 3))
                                cp("act", AqbT[e][:], bB[:, :], [bB], [AqbT[e]])
                                tt("pool", AqbT[e][:], AqbT[e][:], mIT[:], ALU.mult, [AqbT[e], mIT], [AqbT[e]])
                                for hq in range(4):
                                    mm(bA[:, hsl(hq)], fm("Kt", e, hq), fm("Qt", e, hq), fmR("Kt", e, hq) + fmR("Qt", e, hq),
                                       [bA], inc=(hq == 3))
                                cp("act", AqkT[e][:], bA[:, :], [bA], [AqkT[e]])
                                tt("pool", AqkT[e][:], AqkT[e][:], mIT[:], ALU.mult, [AqkT[e], mIT], [AqkT[e]])
                                yield
                                def xtv(i, lev_):
                                    if lev_ < KCUT:
                                        return XT[e][i][:, :], XM[e][i][:, :]
                                    return XT[e][i][:, :].bitcast(BF16)[:, 0:512], XM[e][i][:, :].bitcast(BF16)[:, 0:512]

                                for lev in range(7):
                                    cur, nxt = lev % 2, (lev + 1) % 2
                                    xt_c, xm_c = xtv(cur, lev)
                                    zsrc = Zb[e] if lev < KCUT else AakT[e]
                                    for hq in range(4):
                                        mm(zps[:, hsl(hq)], xt_c[:, hsl(hq)], zsrc[:, hsl(hq)], [XT[e][cur], zsrc], [zps],
                                           start=False, stop=(lev == 6), inc=(hq == 3))
                                    if lev < 6:
                                        xt_n, xm_n = xtv(nxt, lev + 1)
                                        for hq in range(4):
                                            mm(bA[:, hsl(hq)], xm_c[:, hsl(hq)], xt_c[:, hsl(hq)],
                                               [XM[e][cur], XT[e][cur]], [bA], inc=(hq == 3))
                                        cp("dve", xt_n, bA[:, :], [bA], [XT[e][nxt]])
                                        if lev < 5:
                                            for hq in range(4):
                                                mm(bB[:, hsl(hq)], xt_c[:, hsl(hq)], xm_c[:, hsl(hq)],
                                                   [XM[e][cur], XT[e][cur]], [bB], inc=(hq == 3))
                                            cp("act", xm_n, bB[:, :], [bB], [XM[e][nxt]])
                                        zdst = Zb[e] if lev + 1 < KCUT else AakT[e]
                                        cp("act", zdst[:], zps[:, :], [zps], [zdst])
                                    yield
                                act(UGn[e][:], zps[:, :], AF.Identity, [zps], [UGn[e]], scale=-1.0)
                                for hq in range(4):
                                    h = hq * 2 + hg
                                    hh = hg
                                    hc = slice(h * 64, (h + 1) * 64)
                                    mm(yps[:, hc], AqbT[e][:, hsl(hq)], UGn[e][:, hq * 128:hq * 128 + 64], [AqbT[e], UGn[e]],
                                       [yps], start=(e == 0 and hg == 0 and hq == 0), stop=False, inc=False)
                                    mm(yps[:, hc], AqkT[e][:, hsl(hq)], Vtm[par][ck][:, hc], [AqkT[e], Vtm[par][ck]], [yps],
                                       start=False, stop=(e == 1), inc=(hq == 3))
                                for hq in range(4):
                                    hh = hg
                                    mm(bA[0:64, hsl(hq)], identb[hh * 64:(hh + 1) * 64, hh * 64:(hh + 1) * 64], fm("Qt", e, hq),
                                       fmR("Qt", e, hq) + [identb], [bA], start=True, stop=False, inc=False)
                                    mm(bA[0:64, hsl(hq)], UGn[e][:, hq * 128 + 64:(hq + 1) * 128], AqbT[e][:, hsl(hq)],
                                       [UGn[e], AqbT[e]], [bA], start=False, stop=True, inc=(hq == 3))
                                cp("act", Qht[:, :].rearrange("p (e q g t) -> p e q g t", e=2, q=4, g=2)[:, e, :, hg, :],
                                   bA[0:64, :].rearrange("p (q t) -> p q t", q=4), [bA], [Qht])
                                for hq in range(4):
                                    h = hq * 2 + hg
                                    hc = slice(h * 64, (h + 1) * 64)
                                    mm(bB[0:64, hq * 64:(hq + 1) * 64], UGn[e][:, hq * 128 + 64:(hq + 1) * 128], Bhtm[par][ck][e][:, hc],
                                       [UGn[e], Bhtm[par][ck][e]], [bB], inc=False)
                                    mm(bB[0:64, 256 + hq * 64:256 + (hq + 1) * 64], Bhtm[par][ck][e][:, hc],
                                       UGn[e][:, hq * 128:hq * 128 + 64], [UGn[e], Bhtm[par][ck][e]], [bB], start=True, stop=False,
                                       inc=False)
                                    mm(bB[0:64, 256 + hq * 64:256 + (hq + 1) * 64], Khtm[par][ck][e][:, hc], Vtm[par][ck][:, hc],
                                       [Khtm[par][ck][e], Vtm[par][ck]], [bB], start=False, stop=True, inc=(hq == 3))
                                cp("dve", Xst[:, :].rearrange("p (e q g c) -> p e q g c", e=2, q=4, g=2)[:, e, :, hg, :],
                                   bB[0:64, 0:256].rearrange("p (q c) -> p q c", q=4), [bB], [Xst])
                                cp("dve", Dst[:, :].rearrange("p (e q g c) -> p e q g c", e=2, q=4, g=2)[:, e, :, hg, :],
                                   bB[0:64, 256:512].rearrange("p (q c) -> p q c", q=4), [bB], [Dst])
                                yield

                            gens = [chain(0), chain(1)]
                            alive = [True, True]
                            while any(alive):
                                for gi in range(2):
                                    if alive[gi]:
                                        try:
                                            next(gens[gi])
                                        except StopIteration:
                                            alive[gi] = False
                                        npump[0] += 1
                                        if npump[0] % PUMP_EVERY == 0:
                                            pump()
                        cp("act", Ylt[:], yps[:, :], [yps], [Ylt])
                        S.dma("pool", Yl_d[chunk], Ylt[:], reads=[Ylt], writes=[Yl_b[chunk]])
                        S.dma("pool", Qh_d[chunk], Qht[:], reads=[Qht], writes=[Qh_b[chunk]])
                        S.dma("pool", X_d[chunk], Xst[:], reads=[Xst], writes=[X_b[chunk]])
                        S.dma("pool", D_d[chunk], Dst[:], reads=[Dst], writes=[D_b[chunk]])

                for blk in range(NB):
                    phase1_block(blk)
                    if blk + 1 < NB and not BG_INJECT:
                        bg["g"] = abcd_gen(blk + 1)
                    ckpt(9)

                e1b.close()
                S.barrier()
                ckpt(10)
                Sf = sbt(e1, "Sf", [64, 16, 64], F32)
                Sb = sbt(e1, "Sb", [64, 16, 64], BF16)
                Xl = [sbt(e1, "Xl%d" % i, [64, 16, 64], BF16) for i in range(2)]
                Dl = [sbt(e1, "Dl%d" % i, [64, 16, 64], F32) for i in range(2)]
                Sfin = sbt(e1, "Sfin", [64, 16, 64], F32)
                SfB = [Buf() for _ in range(16)]

                def flat(t_, a=None, b=None):
                    ap = t_[:, :, :] if a is None else t_[:, a:b, :]
                    return ap.rearrange("p a b -> p (a b)")

                def run_seq(chunks, init_from_state, seq_out):
                    n = len(chunks)
                    if init_from_state:
                        cp("dve", flat(Sf), flat(Sf0), [Sf0], SfB)
                    else:
                        S.op("dve", lambda e: e.memset(flat(Sf), 0.0), writes=SfB)
                    cp("dve", flat(Sb), flat(Sf), SfB, [Sb])
                    for st in range(n):
                        cf, cbw = chunks[st], chunks[n - 1 - st]
                        S.dma("pool", S_d[cf][:, 0:512], flat(Sb, 0, 8), reads=[Sb], writes=[S_b[cf][0]])
                        S.dma("pool", S_d[cbw][:, 512:1024], flat(Sb, 8, 16), reads=[Sb], writes=[S_b[cbw][1]])
                        xl, dl = Xl[st % 2], Dl[st % 2]
                        S.dma("sp", flat(xl, 0, 8), X_d[cf][:, 0:512], reads=[X_b[cf]], writes=[xl])
                        S.dma("sp", flat(xl, 8, 16), X_d[cbw][:, 512:1024], reads=[X_b[cbw]], writes=[xl])
                        S.dma("sp", flat(dl, 0, 8), D_d[cf][:, 0:512], reads=[D_b[cf]], writes=[dl])
                        S.dma("sp", flat(dl, 8, 16), D_d[cbw][:, 512:1024], reads=[D_b[cbw]], writes=[dl])
                        for hd in range(16):
                            p = PS[hd // 8]
                            mm(p[0:64, (hd % 8) * 64:(hd % 8 + 1) * 64], xl[:, hd, :], Sb[:, hd, :], [xl, Sb], [p],
                               inc=(hd % 8 == 7))
                        for hd in range(16):
                            e, h = hd // 8, hd % 8
                            cch = cf if e == 0 else cbw
                            blk_, ck_ = cch // 2, cch % 2
                            c0 = ((blk_ * 2 + e) * 8 + h) * 2 + ck_
                            p = PS[hd // 8]
                            stt(Sf[:, hd, :], Sf[:, hd, :], WLh[:, c0:c0 + 1], p[0:64, (hd % 8) * 64:(hd % 8 + 1) * 64],
                                ALU.mult, ALU.add, [SfB[hd], WLh, p], [SfB[hd]])
                        tt("dve", flat(Sf), flat(Sf), flat(dl), ALU.add, SfB + [dl], SfB)
                        cp("act", flat(Sb), flat(Sf), SfB, [Sb])
                    if seq_out is not None:
                        for hd in range(16):
                            p = PS[2 + hd // 8]
                            mm(p[0:64, (hd % 8) * 64:(hd % 8 + 1) * 64], Sf[:, hd, :], identf[0:64, 0:64], [SfB[hd], identf], [p],
                               inc=(hd % 8 == 7))
                        for hf in range(2):
                            cp("dve", flat(Sfin, hf * 8, hf * 8 + 8), PS[2 + hf][0:64, :], [PS[2 + hf]], [Sfin])
                        S.dma("pool", so[seq_out], Sfin[:, :, :], reads=[Sfin])

                if NLB > 0:
                    run_seq(list(range(0, 2 * NLB)), True, None)
                for cs in range(NCS):
                    b0 = 2 * (NLB + cs)
                    run_seq([b0, b0 + 1], False, cs)

            ckpt(11)
            S.barrier()
            with ExitStack() as e3:
                wo_bf = sbt(e3, "wo_bf", [128, 8, 1024], BF16)
                FG = sbt(e3, "FG", [128, 1024], F32)
                stg3 = [sbt(e3, "stg3_%d" % i, [128, 2048], F32) for i in range(2)]
                for k in range(8):
                    g = stg3[k % 2]
                    S.dma("sp", g[:], w_in[:, k, 2048:4096], writes=[g])
                    cp("act" if k % 2 == 0 else "dve", w_bf[:, k, :], g[:], [g], [w_bf])
                for k in range(8):
                    g = stg3[k % 2]
                    S.dma("sp", g[:, 0:1024], w_out[:, k, :], writes=[g])
                    cp("act" if k % 2 == 0 else "dve", wo_bf[:, k, :], g[:, 0:1024], [g], [wo_bf])
                S.dma("sp", FG[:], fg_bc, writes=[FG])
                GATE = [sbt(e3, "GATE_%d" % s, [128, 1024], F32) for s in range(2)]
                modg = sbt(e3, "modg", [2, 1024], F32)
                sel3 = [sbt(e3, "sel3_%d" % s, [2, 128], F32) for s in range(2)]
                S.dma("sp", modg[:], mod_d, reads=[mod_b], writes=[modg])
                for s in range(2):
                    ts("dve", sel3[s][:], ones[0:2, :], identf[0:2, s:s + 1], None, ALU.mult, None, [ones, identf], [sel3[s]])
                    for n in range(2):
                        p = PS[s * 2 + n]
                        mm(p[:, :], sel3[s][:], modg[:, n * 512:(n + 1) * 512], [sel3[s], modg], [p])
                        cp("act", GATE[s][:, n * 512:(n + 1) * 512], p[:, :], [p], [GATE[s]])
                hTw2 = [sbt(e3, "hTw%d" % i, [128, 8, 384], BF16) for i in range(2)]
                x3 = [sbt(e3, "x3_%d" % i, [128, 1024], F32) for i in range(2)]
                Gt = [sbt(e3, "Gt%d" % cb, [128, 256], F32) for cb in range(4)]
                tmpc = sbt(e3, "tmpc", [128, 384], F32)
                cuA = sbt(e3, "cuA", [128, 4, 66], F32)
                cuB = sbt(e3, "cuB", [128, 384], F32)
                cuC = sbt(e3, "cuC", [128, 258], F32)
                cacc = sbt(e3, "cacc", [128, 256], F32)
                catT = sbt(e3, "catT", [128, 8, 256], BF16)
                Qhl2 = [sbt(e3, "Qhl%d" % i, [64, 2048], BF16) for i in range(2)]
                Sl2 = [sbt(e3, "Sl%d" % i, [64, 1024], BF16) for i in range(2)]
                Yll2 = [sbt(e3, "Yll%d" % i, [128, 512], F32) for i in range(2)]
                Yt2 = [sbt(e3, "Yt%d" % i, [128, 512], F32) for i in range(2)]
                gnY2 = [sbt(e3, "gnY%d" % i, [128, 512], BF16) for i in range(2)]
                bst2 = [sbt(e3, "bst%d" % i, [128, 8, 6], F32) for i in range(2)]
                mv2 = [sbt(e3, "mv%d" % i, [128, 8, 2], F32) for i in range(2)]
                rs2 = [sbt(e3, "rs%d" % i, [128, 8], F32) for i in range(2)]
                yat4 = [[sbt(e3, "yat%d%d" % (i, cb), [128, 128], F32) for cb in range(4)] for i in range(2)]
                catB = [Buf(), Buf()]
                catTB2 = [sbt(e3, "catTB%d" % i, [128, 4, 256], BF16) for i in range(2)]
                sgl2 = [sbt(e3, "sgl%d" % i, [128, 4, 256], F32) for i in range(2)]
                bvl2 = [sbt(e3, "bvl%d" % i, [128, 4, 256], F32) for i in range(2)]
                yo2 = [sbt(e3, "yo%d" % i, [128, 1024], F32) for i in range(2)]
                junk32 = [sbt(e3, "junk3_%d" % i, [128, 1024], BF16) for i in range(2)]
                ss32 = [sbt(e3, "ss3_%d" % i, [128, 4], F32) for i in range(2)]
                S.op("dve", lambda e: e.memset(cuA[:, :, :].rearrange("p a b -> p (a b)"), 0.0), writes=[cuA])
                S.op("dve", lambda e: e.memset(cuC[:], 0.0), writes=[cuC])

                def hflat(a, b):
                    return hTw[:, :, a:b]

                def b_gen(blk):
                    hTw, sgl, bvl = hTw2[blk % 2], sgl2[blk % 2], bvl2[blk % 2]
                    catTB = catTB2[blk % 2]
                    lat = blk < NLB
                    s = 0 if lat else 1
                    tok0 = blk * 256
                    hv = lambda b_: hT_d[b_].rearrange("p (k t) -> p k t", k=8)
                    if lat:
                        if blk == 0:
                            S.op("dve", lambda e: e.memset(hTw[:, :, 0:64], 0.0), writes=[hTw])
                        else:
                            S.dma("sp", hTw[:, :, 0:64], hv(blk - 1)[:, :, 192:256], reads=[hT_b[blk - 1]], writes=[hTw])
                        if blk == NLB - 1:
                            S.op("dve", lambda e: e.memset(hTw[:, :, 320:384], 0.0), writes=[hTw])
                        else:
                            S.dma("sp", hTw[:, :, 320:384], hv(blk + 1)[:, :, 0:64], reads=[hT_b[blk + 1]], writes=[hTw])
                    S.dma("sp", hTw[:, :, 64:320], hv(blk), reads=[hT_b[blk]], writes=[hTw])
                    S.dma("sp", sgl[:, :, :].rearrange("p a b -> p (a b)"), sg_d[blk], reads=[sg_b[blk]], writes=[sgl])
                    S.dma("sp", bvl[:, :, :].rearrange("p a b -> p (a b)"), bv_d[blk], reads=[bv_b[blk]], writes=[bvl])
                    for cb in range(4):
                        pb, pg = PS[4], PS[5]
                        hs = slice(0, 256)
                        for k in range(8):
                            mm(pb[:, hs], w_bf[:, k, cb * 128:(cb + 1) * 128], hTw[:, k, 64:320], [w_bf, hTw], [pb],
                               start=(k == 0), stop=(k == 7), inc=(k == 7))
                        for k in range(8):
                            mm(pg[:, hs], w_bf[:, k, (12 + cb) * 128:(13 + cb) * 128], hTw[:, k, 64:320], [w_bf, hTw], [pg],
                               start=(k == 0), stop=(k == 7), inc=(k == 7))
                        yield
                        sigm(tmpc[:, 0:256], pg[:, hs], [pg], [tmpc])
                        yield
                        tt("dve", tmpc[:, 0:256], pg[:, hs], tmpc[:, 0:256], ALU.mult, [pg, tmpc], [tmpc])
                        tt("dve", Gt[cb][:], pb[:, hs], tmpc[:, 0:256], ALU.mult, [pb, tmpc], [Gt[cb]])
                        yield
                    for cb in range(4):
                        pc, pu = PS[6], PS[7]
                        wide = lat and cb >= 2
                        n0, n1 = (0, 384) if wide else (64, 320)
                        N = n1 - n0
                        for k in range(8):
                            mm(pc[:, 0:N], w_bf[:, k, (4 + cb) * 128:(5 + cb) * 128], hTw[:, k, n0:n1], [w_bf, hTw], [pc],
                               start=(k == 0), stop=(k == 7), inc=(k == 7))
                        for k in range(8):
                            mm(pu[:, 0:N], w_bf[:, k, (8 + cb) * 128:(9 + cb) * 128], hTw[:, k, n0:n1], [w_bf, hTw], [pu],
                               start=(k == 0), stop=(k == 7), inc=(k == 7))
                        yield
                        cp("act", tmpc[:, 0:N], pc[:, 0:N], [pc], [tmpc])
                        yield
                        cw = [col(48 + j * 4 + cb) for j in range(3)]
                        if not lat:
                            tt("dve", cuC[:, 1:257], tmpc[:, 0:256], pu[:, 0:256], ALU.mult, [tmpc, pu], [cuC])
                            prev, ctr, nxt, cub = cuC[:, 0:256], cuC[:, 1:257], cuC[:, 2:258], cuC
                            accv = cacc[:]
                        elif wide:
                            tt("dve", cuB[:], tmpc[:, 0:384], pu[:, 0:384], ALU.mult, [tmpc, pu], [cuB])
                            prev, ctr, nxt, cub = cuB[:, 0:256], cuB[:, 64:320], cuB[:, 128:384], cuB
                            accv = cacc[:]
                        else:
                            tt("dve", cuA[:, :, 1:65], tmpc[:, 0:256].rearrange("p (r w) -> p r w", r=4),
                               pu[:, 0:256].rearrange("p (r w) -> p r w", r=4), ALU.mult, [tmpc, pu], [cuA])
                            prev, ctr, nxt, cub = cuA[:, :, 0:64], cuA[:, :, 1:65], cuA[:, :, 2:66], cuA
                            accv = cacc[:].rearrange("p (r w) -> p r w", r=4)
                        yield
                        ts("dve", accv, ctr, cw[1], None, ALU.mult, None, [cub, cols], [cacc])
                        stt(accv, prev, cw[0], accv, ALU.mult, ALU.add, [cub, cols, cacc], [cacc])
                        stt(accv, nxt, cw[2], accv, ALU.mult, ALU.add, [cub, cols, cacc], [cacc])
                        yield
                        tt("pool", catTB[:, cb, :], cacc[:], Gt[cb][:], ALU.mult, [cacc, Gt[cb]], [catTB])
                    yield

                def a_part(blk, bgen):
                    hTw, sgl, bvl = hTw2[blk % 2], sgl2[blk % 2], bvl2[blk % 2]
                    catTB = catTB2[blk % 2]
                    lat = blk < NLB
                    s = 0 if lat else 1
                    tok0 = blk * 256
                    for ck in range(2):
                        chunk = blk * 2 + ck
                        S.dma("sp", Qhl2[ck][:], Qh_d[chunk], reads=[Qh_b[chunk]], writes=[Qhl2[ck]])
                        S.dma("sp", Sl2[ck][:], S_d[chunk], reads=[S_b[chunk][0], S_b[chunk][1]], writes=[Sl2[ck]])
                        S.dma("sp", Yll2[ck][:], Yl_d[chunk], reads=[Yl_b[chunk]], writes=[Yll2[ck]])
                        S.dma("sp", x3[ck][:], xs[tok0 + ck * 128:tok0 + (ck + 1) * 128, :], writes=[x3[ck]])

                    def a_tile(ck):
                        tc = slice(ck * 128, (ck + 1) * 128)
                        Qhl, Sl, Yll, xt = Qhl2[ck], Sl2[ck], Yll2[ck], x3[ck]
                        Yt, gnY, bst, mv, rs, yo, junk3, ss3 = Yt2[ck], gnY2[ck], bst2[ck], mv2[ck], rs2[ck], yo2[ck], junk32[ck], ss32[ck]
                        yp, pt, po2 = (PS[0], PS[0], [PS[1], PS[1]]) if ck == 0 else (PS[2], PS[2], [PS[3], PS[3]])
                        for h in range(8):
                            for e in range(2):
                                mm(yp[:, h * 64:(h + 1) * 64], Qhl[:, (e * 8 + h) * 128:(e * 8 + h + 1) * 128],
                                   Sl[:, (e * 8 + h) * 64:(e * 8 + h + 1) * 64], [Qhl, Sl], [yp], start=(e == 0), stop=(e == 1),
                                   inc=(h == 7 and e == 1))
                        yield
                        tt("dve", Yt[:], yp[:, :], Yll[:], ALU.add, [yp, Yll], [Yt])
                        for h in range(8):
                            S.op("dve", lambda e_: e_.bn_stats(out=bst[:, h, :], in_=Yt[:, h * 64:(h + 1) * 64]), reads=[Yt],
                                 writes=[bst])
                        for h in range(8):
                            S.op("dve", lambda e_: e_.bn_aggr(out=mv[:, h, :], in_=bst[:, h, :]), reads=[bst], writes=[mv])
                        ts("dve", rs[:], mv[:, :, 1], GN_EPS, None, ALU.add, None, [mv], [rs])
                        yield
                        act(rs[:], rs[:], AF.Ln, [rs], [rs])
                        act(rs[:], rs[:], AF.Exp, [rs], [rs], scale=-0.5)
                        yield
                        for h in range(8):
                            ts("dve", gnY[:, h * 64:(h + 1) * 64], Yt[:, h * 64:(h + 1) * 64], mv[:, h, 0:1], rs[:, h:h + 1],
                               ALU.subtract, ALU.mult, [Yt, mv, rs], [gnY])
                        yield
                        for cb in range(4):
                            mm(pt[:, cb * 128:(cb + 1) * 128], gnY[:, cb * 128:(cb + 1) * 128], identb[:], [gnY, identb], [pt],
                               inc=(cb == 3))
                        yield
                        for cb in range(4):
                            yat = yat4[ck][cb]
                            ts("dve", yat[:], pt[:, cb * 128:(cb + 1) * 128], col(40 + cb), col(44 + cb), ALU.mult, ALU.add,
                               [pt, cols], [yat])
                            tt("pool", yat[:], yat[:], bvl[:, cb, tc], ALU.add, [yat, bvl], [yat])
                            tt("pool", catT[:, cb, tc], yat[:], sgl[:, cb, tc], ALU.mult, [yat, sgl], [catB[ck]])
                        yield
                        for n in range(2):
                            po = po2[n]
                            hs = slice(n * 512, (n + 1) * 512)
                            for m in range(8):
                                lhs = catT[:, m, tc] if m < 4 else catTB[:, m - 4, tc]
                                mm(po[:, :], lhs, wo_bf[:, m, n * 512:(n + 1) * 512], [catB[ck], catTB, wo_bf], [po],
                                   start=(m == 0), stop=(m == 7), inc=(m == 7))
                            yield
                            tt("dve", yo[:, hs], po[:, :], GATE[s][:, hs], ALU.mult, [po, GATE[s]], [yo])
                            yield
                        tt("dve", yo[:], yo[:], xt[:], ALU.add, [yo, xt], [yo])
                        yield
                        act(junk3[:], yo[:], AF.Square, [yo], [junk3, ss3], accum=ss3[:, 0:1])
                        yield
                        ts("dve", ss3[:, 1:2], ss3[:, 0:1], 1.0 / 1024, NORM_EPS, ALU.mult, ALU.add, [ss3], [ss3])
                        yield
                        act(ss3[:, 2:3], ss3[:, 1:2], AF.Ln, [ss3], [ss3])
                        act(ss3[:, 3:4], ss3[:, 2:3], AF.Exp, [ss3], [ss3], scale=-0.5)
                        yield
                        stt(yo[:], yo[:], ss3[:, 3:4], FG[:], ALU.mult, ALU.mult, [yo, ss3, FG], [yo])
                        S.dma("pool", ys[tok0 + ck * 128:tok0 + (ck + 1) * 128, :], yo[:], reads=[yo])

                    gens3 = [a_tile(0), a_tile(1)]
                    alive3 = [True, True]
                    while any(alive3):
                        for gi in range(2):
                            if alive3[gi]:
                                try:
                                    next(gens3[gi])
                                except StopIteration:
                                    alive3[gi] = False
                                if bgen is not None:
                                    try:
                                        next(bgen)
                                    except StopIteration:
                                        bgen = None
                    if bgen is not None:
                        for _ in bgen:
                            pass

                for _ in b_gen(0):
                    pass
                for blk in range(NB):
                    a_part(blk, b_gen(blk + 1) if blk + 1 < NB else None)
                    ckpt(12)
        except _Stop:
            pass
        S.off = False
        S.finish("sp")
        S.finish("pool")
    nc._marks = SS[0].marks
    return nc


def _prep_shared(inp):
    f = np.float32
    d = {}
    d["w_ada"] = np.ascontiguousarray(inp["w_ada"][0].reshape(8, 128, 3072).transpose(1, 0, 2), dtype=f)
    d["b_ada2"] = np.ascontiguousarray(np.stack([inp["b_ada"][0], inp["b_ada"][0]], 0), dtype=f)
    d["ngc"] = np.ascontiguousarray(np.asarray(inp["norm_g"][0], dtype=f).reshape(8, 128).T)
    d["fg_bc"] = np.ascontiguousarray(np.broadcast_to(inp["final_g"][None, :], (128, 1024)), dtype=f)
    d["w_in"] = np.ascontiguousarray(inp["w_in"][0].reshape(8, 128, 4096).transpose(1, 0, 2), dtype=f)
    ld = np.concatenate([inp["decay_down"][0, 0], inp["decay_down"][0, 1], inp["iclr_down"][0, 0], inp["iclr_down"][0, 1]],
                        axis=1)
    d["lora_dn"] = np.ascontiguousarray(ld.reshape(8, 128, 256).transpose(1, 0, 2), dtype=f)
    d["dec_up"] = np.ascontiguousarray(inp["decay_up"][0].reshape(128, 512), dtype=f)
    d["icl_up"] = np.ascontiguousarray(inp["iclr_up"][0].reshape(128, 512), dtype=f)

    def c4(v):
        return np.asarray(v, dtype=f).reshape(4, 128).T

    cl = [c4(inp["shift_mu"][0, q]) for q in range(3)]
    cl += [c4(inp["decay_w0"][0, e]) for e in range(2)]
    cl += [c4(inp["iclr_bias"][0, e]) for e in range(2)]
    cl += [c4(inp["kk_scale"][0]), c4(inp["ka_scale"][0]), c4(inp["bonus_rk"][0]), c4(inp["gn_w"][0]), c4(inp["gn_b"][0])]
    cl += [c4(inp["conv_w"][0, j]) for j in range(3)]
    d["cols"] = np.ascontiguousarray(np.concatenate(cl, axis=1), dtype=f)
    d["w_out"] = np.ascontiguousarray(inp["w_out"][0].reshape(8, 128, 1024).transpose(1, 0, 2), dtype=f)
    return d


def _core_inputs(shared, x_lat, x_ctx, c_lat, c_ctx, st):
    f = np.float32
    m = dict(shared)
    parts = []
    if x_lat is not None:
        parts.append(np.asarray(x_lat, dtype=f).reshape(-1, 1024))
    if x_ctx is not None and len(x_ctx):
        parts.append(np.asarray(x_ctx, dtype=f).reshape(-1, 1024))
    m["xs"] = np.ascontiguousarray(np.concatenate(parts, 0))
    cv = np.stack([np.asarray(c_lat, dtype=f), np.asarray(c_ctx, dtype=f)], 0)
    m["cT"] = np.ascontiguousarray(cv.reshape(2, 8, 128).transpose(2, 1, 0).reshape(128, 16))
    m["st0"] = np.ascontiguousarray(np.asarray(st, dtype=f).transpose(2, 0, 1, 3).reshape(64, 16, 64))
    return m


_PROG = {}


def kernel(**inputs):
    inp = {k: np.asarray(v) for k, v in inputs.items()}
    NCORES = 8
    NLB, NCS = 16, 4
    shared = _prep_shared(inp)
    in_maps = []
    for b in range(NCORES):
        in_maps.append(_core_inputs(shared, inp["x_sample"][b], inp["x_prompt"][4 * b:4 * b + 4], inp["c"][b], inp["c_ctx"],
                                    inp["state_wkv"][b, 0]))
    key = (NLB, NCS)
    if key not in _PROG:
        _PROG[key] = build_program(NLB, NCS)
    res = run_bass_kernel_spmd(_PROG[key], in_maps, core_ids=list(range(NCORES)))
    y_prompt = np.zeros((32, 256, 1024), np.float32)
    y_sample = np.zeros((8, 4096, 1024), np.float32)
    new_state = np.zeros((32, 1, 2, 8, 64, 64), np.float32)
    for b in range(NCORES):
        r = res.results[b]
        ysb = np.asarray(r["ys"])
        y_sample[b] = ysb[:4096]
        y_prompt[4 * b:4 * b + 4] = ysb[4096:].reshape(4, 256, 1024)
        sob = np.asarray(r["so"]).reshape(4, 64, 2, 8, 64)
        new_state[4 * b:4 * b + 4, 0] = sob.transpose(0, 2, 3, 1, 4)
    return (y_prompt, y_sample, new_state)
```

```python
import os
import numpy as np
import concourse.bass as bass
import concourse.mybir as mybir
from concourse.bass_utils import run_bass_kernel_spmd
from contextlib import ExitStack

F32, BF16, I32 = mybir.dt.float32, mybir.dt.bfloat16, mybir.dt.int32
ALU = mybir.AluOpType
AF = mybir.ActivationFunctionType
C0 = 0.6065306597126334
NORM_EPS = 1e-6
GN_EPS = 64e-5
NDS = 24
BG_INJECT = os.environ.get("BG_INJECT", "1") == "1"
PUMP_EVERY = int(os.environ.get("PUMP_EVERY", "1"))
KCUT = int(os.environ.get("KCUT", "5"))
RAW_ONLY = os.environ.get("RAW_ONLY", "0") == "1"
SELF_SKIP = tuple(os.environ.get("SELF_SKIP", "pe").split(","))


class Buf:
    __slots__ = ("w", "r")

    def __init__(self):
        self.w = None
        self.r = {}


class T:
    def __init__(self, t):
        self.t = t
        self.b = Buf()

    def __getitem__(self, idx):
        return self.t[idx]


def _b(x):
    return x.b if hasattr(x, "b") else x


class Sched:
    def __init__(self, nc, es):
        self.nc = nc
        self.eng = {"pe": nc.tensor, "dve": nc.vector, "act": nc.scalar, "pool": nc.gpsimd, "sp": nc.sync}
        self.sem = {k: es.enter_context(nc.semaphore("s_" + k)) for k in self.eng}
        self.cnt = {k: 0 for k in self.eng}
        self.dsem = [es.enter_context(nc.semaphore("d%d" % i)) for i in range(NDS)]
        self.dcnt = [0] * NDS
        self.dnext = 0
        self.dnext2 = 0
        self.waited = {}
        self.nwait = 0
        self.off = False
        self.nops = {k: 0 for k in self.eng}
        self.marks = []

    def _semh(self, key):
        return self.sem[key] if isinstance(key, str) else self.dsem[key]

    def _wait(self, e, key, val):
        if val <= 0:
            return
        if e == key and e in SELF_SKIP:
            return
        if self.waited.get((e, key), 0) >= val:
            return
        self.eng[e].wait_ge(self._semh(key), val)
        self.waited[(e, key)] = val
        self.nwait += 1

    def _deps(self, e, reads, writes):
        for b in reads:
            b = _b(b)
            if b.w:
                self._wait(e, *b.w)
        raw_only = RAW_ONLY and e in ("act", "dve")
        for b in writes:
            b = _b(b)
            if b.w and not (raw_only and b.w[0] == e):
                self._wait(e, *b.w)
            for k, v in b.r.items():
                if raw_only and k == e:
                    continue
                self._wait(e, k, v)

    def _mark(self, key, tgt, reads, writes):
        for b in writes:
            b = _b(b)
            b.w = (key, tgt)
            b.r = {}
        for b in reads:
            b = _b(b)
            if b.r.get(key, 0) < tgt:
                b.r[key] = tgt

    def op(self, e, fn, reads=(), writes=(), inc=True):
        if self.off:
            return
        self._deps(e, reads, writes)
        inst = fn(self.eng[e])
        self.nops[e] += 1
        tgt = self.cnt[e] + 1
        if inc:
            inst.then_inc(self.sem[e], 1)
            self.cnt[e] = tgt
        self._mark(e, tgt, reads, writes)

    def dma(self, e, out, in_, reads=(), writes=()):
        if self.off:
            return
        if e == "pool":
            i = NDS - 8 + self.dnext2
            self.dnext2 = (self.dnext2 + 1) % 8
        else:
            i = self.dnext
            self.dnext = (i + 1) % (NDS - 8)
        self._deps(e, reads, writes)
        self._wait(e, i, self.dcnt[i])
        self.eng[e].dma_start(out=out, in_=in_).then_inc(self.dsem[i], 16)
        self.dcnt[i] += 16
        self._mark(i, self.dcnt[i], reads, writes)

    def mark(self, label):
        self.marks.append((label, dict(self.nops)))

    def barrier(self):
        if self.off:
            return
        for e in self.eng:
            for k in self.eng:
                if k != e:
                    self._wait(e, k, self.cnt[k])
            for i in range(NDS):
                self._wait(e, i, self.dcnt[i])

    def finish(self, e="sp"):
        for i in range(NDS):
            self._wait(e, i, self.dcnt[i])


class _Stop(Exception):
    pass


def build_program(NLB, NCS, debug=False, stop=None):
    SS = []

    def ckpt(n):
        SS[0].mark(n)
        if stop is not None and n == stop:
            SS[0].off = True

    NB = NLB + NCS
    NCH = 2 * NB
    NTOK = NB * 256
    nc = bass.Bass("TRN2", target_bir_lowering=False)

    def din(name, shape, dt=F32):
        return nc.dram_tensor(name, list(shape), dt, kind="ExternalInput").ap()

    def dout(name, shape, dt=F32):
        return nc.dram_tensor(name, list(shape), dt, kind="ExternalOutput").ap()

    def dscr(name, shape, dt):
        return nc.dram_tensor(name, list(shape), dt, kind=("ExternalOutput" if debug else "Internal")).ap()

    xs = din("xs", [NTOK, 1024])
    cT = din("cT", [128, 16])
    st0 = din("st0", [64, 16, 64])
    w_ada = din("w_ada", [128, 8, 3072])
    b_ada2 = din("b_ada2", [2, 3072])
    ngc_d = din("ngc", [128, 8])
    fg_bc = din("fg_bc", [128, 1024])
    w_in = din("w_in", [128, 8, 4096])
    lora_dn = din("lora_dn", [128, 8, 256])
    dec_up = din("dec_up", [128, 512])
    icl_up = din("icl_up", [128, 512])
    cols_d = din("cols", [128, 60])
    w_out = din("w_out", [128, 8, 1024])
    ys = dout("ys", [NTOK, 1024])
    so = dout("so", [max(NCS, 1), 64, 16, 64])

    hT_d = dscr("hT_d", [NB, 128, 8 * 256], BF16)
    Yl_d = dscr("Yl_d", [NCH, 128, 512], F32)
    Qh_d = dscr("Qh_d", [NCH, 64, 2048], BF16)
    X_d = dscr("X_d", [NCH, 64, 1024], BF16)
    D_d = dscr("D_d", [NCH, 64, 1024], F32)
    sg_d = dscr("sg_d", [NB, 128, 1024], F32)
    bv_d = dscr("bv_d", [NB, 128, 1024], F32)
    S_d = dscr("S_d", [NCH, 64, 1024], BF16)
    mod_d = dscr("mod_d", [2, 1024], F32)
    mod_b = Buf()
    hT_b = [Buf() for _ in range(NB)]
    Yl_b = [Buf() for _ in range(NCH)]
    Qh_b = [Buf() for _ in range(NCH)]
    X_b = [Buf() for _ in range(NCH)]
    D_b = [Buf() for _ in range(NCH)]
    sg_b = [Buf() for _ in range(NB)]
    bv_b = [Buf() for _ in range(NB)]
    S_b = [[Buf(), Buf()] for _ in range(NCH)]
    ys_b = Buf()
    so_b = Buf()

    with ExitStack() as es:
        S = Sched(nc, es)
        SS.append(S)

        try:
            def sbt(es_, name, shape, dt):
                return T(es_.enter_context(nc.sbuf_tensor("sb_" + name, list(shape), dt)))

            PS = [T(es.enter_context(nc.psum_tensor("ps%d" % i, [128, 512], F32))) for i in range(8)]

            def mm(out, lhsT, rhs, R, W, start=True, stop=True, inc=True):
                S.op("pe", lambda e: e.matmul(out, lhsT=lhsT, rhs=rhs, start=start, stop=stop, skip_group_check=True),
                     reads=R, writes=W, inc=inc)

            def tt(eng, out, a, b, op, R, W):
                S.op(eng, lambda e: e.tensor_tensor(out=out, in0=a, in1=b, op=op), reads=R, writes=W)

            def ts(eng, out, a, s1, s2, op0, op1, R, W):
                if op1 is None:
                    S.op(eng, lambda e: e.tensor_scalar(out=out, in0=a, scalar1=s1, scalar2=None, op0=op0), reads=R, writes=W)
                else:
                    S.op(eng, lambda e: e.tensor_scalar(out=out, in0=a, scalar1=s1, scalar2=s2, op0=op0, op1=op1),
                         reads=R, writes=W)

            def stt(out, a, sc, b, op0, op1, R, W):
                S.op("dve", lambda e: e.scalar_tensor_tensor(out=out, in0=a, scalar=sc, in1=b, op0=op0, op1=op1),
                     reads=R, writes=W)

            def act(out, in_, func, R, W, bias=None, scale=None, accum=None):
                kw = {}
                if bias is not None:
                    kw["bias"] = bias
                if scale is not None:
                    kw["scale"] = scale
                if accum is not None:
                    kw["accum_out"] = accum
                S.op("act", lambda e: e.activation(out=out, in_=in_, func=func, **kw), reads=R, writes=W)

            def sigm(out, in_, R, W, nbias=None):
                act(out, in_, AF.Exp, R, W, scale=-1.0, bias=nbias)
                act(out, out, AF.Ln, W, W, bias=1.0)
                act(out, out, AF.Exp, W, W, scale=-1.0)

            def cp(eng, out, in_, R, W):
                if eng == "act":
                    S.op("act", lambda e: e.copy(out=out, in_=in_), reads=R, writes=W)
                else:
                    S.op(eng, lambda e: e.tensor_copy(out=out, in_=in_), reads=R, writes=W)

            ioi = sbt(es, "ioi", [128, 128], I32)
            iof = sbt(es, "iof", [128, 128], F32)
            identf = sbt(es, "identf", [128, 128], F32)
            identb = sbt(es, "identb", [128, 128], BF16)
            bones = sbt(es, "bones", [128, 128], F32)
            ones = sbt(es, "ones", [128, 128], F32)
            MK = {k: sbt(es, "mk_" + k, [128, 512], BF16) for k in ("LT", "GT", "LE", "GE", "NLT", "NGT")}
            cols = sbt(es, "cols", [128, 60], F32)
            dcols = sbt(es, "dcols", [128, 44], F32)
            w_bf = sbt(es, "w_bf", [128, 8, 2048], BF16)
            WLh = sbt(es, "WLh", [64, NB * 32], F32)

            S.op("pool", lambda e: e.iota(ioi[:], pattern=[[1, 128]], base=0, channel_multiplier=-1), writes=[ioi])
            cp("dve", iof[:], ioi[:], [ioi], [iof])
            ts("dve", identf[:], iof[:], 0.0, None, ALU.is_equal, None, [iof], [identf])
            cp("dve", identb[:], identf[:], [identf], [identb])
            S.op("dve", lambda e: e.memset(ones[:], 1.0), writes=[ones])
            S.op("dve", lambda e: e.memset(bones[:], 0.0), writes=[bones])
            S.op("dve", lambda e: e.memset(bones[0:64, 0:64], 1.0), writes=[bones])
            S.op("dve", lambda e: e.memset(bones[64:128, 64:128], 1.0), writes=[bones])
            for j in range(4):
                sl = slice(j * 128, (j + 1) * 128)
                ts("dve", MK["LT"][:, sl], iof[:], 0.0, None, ALU.is_gt, None, [iof], [MK["LT"]])
                ts("dve", MK["GT"][:, sl], iof[:], 0.0, None, ALU.is_lt, None, [iof], [MK["GT"]])
                ts("dve", MK["LE"][:, sl], iof[:], 0.0, None, ALU.is_ge, None, [iof], [MK["LE"]])
                ts("dve", MK["GE"][:, sl], iof[:], 0.0, None, ALU.is_le, None, [iof], [MK["GE"]])
                ts("dve", MK["NLT"][:, sl], iof[:], 0.0, -1.0, ALU.is_gt, ALU.mult, [iof], [MK["NLT"]])
                ts("dve", MK["NGT"][:, sl], iof[:], 0.0, -1.0, ALU.is_lt, ALU.mult, [iof], [MK["NGT"]])
            S.dma("sp", cols[:], cols_d, writes=[cols])
            ts("dve", dcols[:, 0:12], cols[:, 0:12], -1.0, 1.0, ALU.mult, ALU.add, [cols], [dcols])
            ts("dve", dcols[:, 12:24], cols[:, 0:12], 0.5, None, ALU.mult, None, [cols], [dcols])
            ts("dve", dcols[:, 24:28], cols[:, 32:36], -1.0, 1.0, ALU.mult, ALU.add, [cols], [dcols])
            ts("dve", dcols[:, 28:44], cols[:, 12:28], -1.0, None, ALU.mult, None, [cols], [dcols])
            ckpt(1)

            def col(i):
                return cols[:, i:i + 1]

            def dcol(i):
                return dcols[:, i:i + 1]

            with ExitStack() as e1:
                g1c = [sbt(e1, "g1c%d" % s, [128, 8], F32) for s in range(2)]
                shc = [sbt(e1, "shc%d" % s, [128, 8], F32) for s in range(2)]
                lora_bf = sbt(e1, "lora_bf", [128, 8, 256], BF16)
                dup_bf = sbt(e1, "dup_bf", [128, 512], BF16)
                iup_bf = sbt(e1, "iup_bf", [128, 512], BF16)
                Sf0 = sbt(e1, "Sf0", [64, 16, 64], F32)
                with ExitStack() as e0:
                    stg = [sbt(e0, "stg%d" % i, [128, 3072], F32) for i in range(2)]
                    ngt = sbt(e0, "ngt", [128, 8], F32)
                    cTt = sbt(e0, "cTt", [128, 16], F32)
                    scT = sbt(e0, "scT", [128, 16], F32)
                    modv = sbt(e0, "modv", [2, 3072], F32)
                    bad = sbt(e0, "bad", [2, 3072], F32)
                    st_in = sbt(e0, "st_in", [64, 16, 64], F32)
                    for k in range(8):
                        g = stg[k % 2]
                        S.dma("sp", g[:, 0:2048], w_in[:, k, 0:2048], writes=[g])
                        cp("act" if k % 2 == 0 else "dve", w_bf[:, k, :], g[:, 0:2048], [g], [w_bf])
                    g = stg[0]
                    S.dma("sp", g[:, 0:2048], lora_dn.rearrange("p k n -> p (k n)"), writes=[g])
                    cp("dve", lora_bf[:, :, :].rearrange("p k n -> p (k n)"), g[:, 0:2048], [g], [lora_bf])
                    g = stg[1]
                    S.dma("sp", g[:, 0:512], dec_up, writes=[g])
                    S.dma("sp", g[:, 512:1024], icl_up, writes=[g])
                    cp("dve", dup_bf[:], g[:, 0:512], [g], [dup_bf])
                    cp("dve", iup_bf[:], g[:, 512:1024], [g], [iup_bf])
                    ckpt(2)
                    S.dma("sp", cTt[:], cT, writes=[cTt])
                    act(scT[:], cTt[:], AF.Silu, [cTt], [scT])
                    S.dma("sp", bad[:], b_ada2, writes=[bad])
                    S.dma("sp", ngt[:], ngc_d, writes=[ngt])
                    for k in range(8):
                        g = stg[k % 2]
                        S.dma("sp", g[:], w_ada[:, k, :], writes=[g])
                        for n in range(6):
                            mm(PS[n][0:2, :], scT[:, 2 * k:2 * k + 2], g[:, n * 512:(n + 1) * 512], [scT, g], [PS[n]],
                               start=(k == 0), stop=(k == 7))
                    for n in range(6):
                        tt("dve", modv[:, n * 512:(n + 1) * 512], PS[n][0:2, :], bad[:, n * 512:(n + 1) * 512], ALU.add,
                           [PS[n], bad], [modv])
                    S.dma("sp", mod_d, modv[:, 2048:3072], reads=[modv], writes=[mod_b])
                    for part in range(2):
                        for k in range(8):
                            c0 = (part * 8 + k) * 2
                            mm(PS[0][:, c0:c0 + 2], modv[0:2, part * 1024 + k * 128:part * 1024 + (k + 1) * 128],
                               identf[0:2, 0:2], [modv, identf], [PS[0]], inc=(part == 1 and k == 7))
                    mview = PS[0][:, 0:32].rearrange("p (a k s) -> p a k s", a=2, k=8)
                    for s in range(2):
                        cp("dve", shc[s][:], mview[:, 0, :, s], [PS[0]], [shc[s]])
                        stt(g1c[s][:], mview[:, 1, :, s], 1.0, ngt[:], ALU.add, ALU.mult, [PS[0], ngt], [g1c[s]])
                    ckpt(3)
                    S.dma("sp", st_in[:], st0, writes=[st_in])
                    for hd in range(16):
                        p = PS[hd // 8]
                        mm(p[0:64, (hd % 8) * 64:(hd % 8 + 1) * 64], st_in[:, hd, :], identf[0:64, 0:64], [st_in, identf], [p],
                           inc=(hd % 8 == 7))
                    for hf in range(2):
                        cp("dve", Sf0[:, hf * 8:(hf + 1) * 8, :].rearrange("p a b -> p (a b)"), PS[hf][0:64, :], [PS[hf]], [Sf0])

                S.barrier()
                ckpt(4)
                e1b = ExitStack()
                e1b.__enter__()
                xts = [sbt(e1b, "xt0", [128, 1024], F32)] * 2
                hb = sbt(e1b, "hb", [128, 1024], BF16)
                ss = sbt(e1b, "ss", [128, 4], F32)
                hT = sbt(e1b, "hT", [128, 8, 256], BF16)
                ppL = sbt(e1b, "ppL", [128, 4, 66], F32)
                ppC = sbt(e1b, "ppC", [128, 1, 258], F32)
                nbt = sbt(e1b, "nbt", [128, 256], F32)
                RKV = [[sbt(e1b, "rkv%d_%d" % (q, cb), [128, 256], F32) for cb in range(4)] for q in range(3)]
                vbf = [sbt(e1b, "vbf%d" % cb, [128, 256], BF16) for cb in range(4)]
                sgT = sbt(e1b, "sgT", [128, 4, 256], F32)
                bvT = sbt(e1b, "bvT", [128, 4, 256], F32)
                lwd = sbt(e1b, "lwd", [128, 256], BF16)
                lwi = sbt(e1b, "lwi", [128, 256], BF16)
                TMPC = [{n: sbt(e1b, "tmc%d_%s" % (i, n), [128, 256], F32) for n in ["sq", "kk", "rkd"]} for i in range(2)]
                TMPC[0]["rn"] = TMPC[1]["rn"] = sbt(e1b, "tmc_rn", [128, 256], F32)
                TMP = TMPC[0]
                TMPE = [{n: sbt(e1b, "tm0_%s" % n, [128, 256] if n != "tcol" else [128, 4], F32)
                         for n in ["sig", "pi", "px", "E1", "E2", "E3", "E4", "a", "bq", "kd", "tcol"]}]
                TMPE.append(TMPE[0])
                WLc = [sbt(e1b, "WLc%d" % i, [128, 4], F32) for i in range(2)]
                FM4 = {n: [[[sbt(e1b, "fm_%s%d%d%d" % (n, par, e, cb), [128, 256], BF16) for cb in range(4)] for e in range(2)]
                           for par in range(2)] for n in ("Qt", "KKt", "Kt", "Bt")}
                FMs = {n: [[sbt(e1b, "fm_%s%d%d" % (n, e, cb), [128, 256], BF16) for cb in range(4)] for e in range(2)]
                       for n in ("Kh", "Bh")}

                def FMt(n, par, e, cb):
                    return FMs[n][e][cb] if n in FMs else FM4[n][par][e][cb]

                Khtm = [[[sbt(e1b, "khtm%d%d%d" % (par, ck, e), [128, 512], BF16) for e in range(2)] for ck in range(2)]
                        for par in range(2)]
                Bhtm = [[[sbt(e1b, "bhtm%d%d%d" % (par, ck, e), [128, 512], BF16) for e in range(2)] for ck in range(2)]
                        for par in range(2)]
                Vtm = [[sbt(e1b, "vtm%d%d" % (par, ck), [128, 512], BF16) for ck in range(2)] for par in range(2)]
                XT = [[sbt(e1b, "XT%d%d" % (e, i), [128, 512], F32) for i in range(2)] for e in range(2)]
                XM = [[sbt(e1b, "XM%d%d" % (e, i), [128, 512], F32) for i in range(2)] for e in range(2)]
                AakT = [sbt(e1b, "AakT%d" % e, [128, 512], BF16) for e in range(2)]
                AqbT = [sbt(e1b, "AqbT%d" % e, [128, 512], BF16) for e in range(2)]
                AqkT = [sbt(e1b, "AqkT%d" % e, [128, 512], BF16) for e in range(2)]
                Zb = [sbt(e1b, "Zb%d" % e, [128, 512], F32) for e in range(2)]
                UGn = [sbt(e1b, "UGn%d" % e, [128, 512], BF16) for e in range(2)]
                Ylt = sbt(e1b, "Ylt", [128, 512], F32)
                Qht = sbt(e1b, "Qht", [64, 2048], BF16)
                Xst = sbt(e1b, "Xst", [64, 1024], BF16)
                Dst = sbt(e1b, "Dst", [64, 1024], F32)
                S.op("dve", lambda e: e.memset(ppL[:, :, :].rearrange("p a b -> p (a b)"), 0.0), writes=[ppL])
                S.op("dve", lambda e: e.memset(ppC[:, :, :].rearrange("p a b -> p (a b)"), 0.0), writes=[ppC])

                PB = PS[7]

                def ab_gen(blk):
                    lat = blk < NLB
                    s = 0 if lat else 1
                    tok0 = blk * 256
                    for i in range(2):
                        xt = xts[i]
                        S.dma("sp", xt[:], xs[tok0 + i * 128:tok0 + (i + 1) * 128, :], writes=[xt])
                        yield
                        act(hb[:], xt[:], AF.Square, [xt], [hb, ss], accum=ss[:, 0:1])
                        act(ss[:, 2:3], ss[:, 0:1], AF.Ln, [ss], [ss], scale=1.0 / 1024, bias=NORM_EPS)
                        act(ss[:, 3:4], ss[:, 2:3], AF.Exp, [ss], [ss], scale=-0.5)
                        yield
                        ts("dve", hb[:], xt[:], ss[:, 3:4], None, ALU.mult, None, [xt, ss], [hb])
                        yield
                        for half in range(2):
                            p = PB
                            for k4 in range(4):
                                k = half * 4 + k4
                                mm(p[:, k4 * 128:(k4 + 1) * 128], hb[:, k * 128:(k + 1) * 128], identb[:], [hb, identb], [p],
                                   inc=(k4 == 3))
                            yield
                            for k4 in range(4):
                                k = half * 4 + k4
                                dsth = hT[:, k, i * 128:(i + 1) * 128]
                                srcp = p[:, k4 * 128:(k4 + 1) * 128]
                                if k4 % 2 == 0:
                                    act(dsth, srcp, AF.Identity, [p, g1c[s], shc[s]], [hT], scale=g1c[s][:, k:k + 1],
                                        bias=shc[s][:, k:k + 1])
                                else:
                                    ts("dve", dsth, srcp, g1c[s][:, k:k + 1], shc[s][:, k:k + 1], ALU.mult, ALU.add,
                                       [p, g1c[s], shc[s]], [hT])
                            yield
                    S.dma("pool", hT_d[blk], hT[:, :, :].rearrange("p k t -> p (k t)"), reads=[hT], writes=[hT_b[blk]])
                    pp = ppL if lat else ppC
                    R_, W_ = (4, 64) if lat else (1, 256)

                    def v3(ap):
                        return ap.rearrange("p (r w) -> p r w", r=R_)

                    hs = slice(0, 256)
                    for cbg in range(16):
                        p = PB
                        for k in range(8):
                            mm(p[:, hs], w_bf[:, k, cbg * 128:(cbg + 1) * 128], hT[:, k, :], [w_bf, hT], [p],
                               start=(k == 0), stop=(k == 7), inc=(k == 7))
                        q, cb = cbg // 4, cbg % 4
                        if q < 3:
                            cp("act", pp[:, :, 1:W_ + 1], v3(p[:, hs]), [p], [pp])
                            tt("dve", v3(nbt[:]), pp[:, :, 0:W_], pp[:, :, 2:W_ + 2], ALU.add, [pp], [nbt])
                            dst = RKV[q][cb]
                            act(v3(dst[:]), pp[:, :, 1:W_ + 1], AF.Identity, [pp, dcols], [dst], scale=dcol(q * 4 + cb))
                            stt(dst[:], nbt[:], dcol(12 + q * 4 + cb), dst[:], ALU.mult, ALU.add, [nbt, dcols, dst], [dst])
                            if q == 2:
                                cp("pool", vbf[cb][:], dst[:], [dst], [vbf[cb]])
                        else:
                            sigm(sgT[:, cb, :], p[:, hs], [p], [sgT])
                            tt("dve", sgT[:, cb, :], p[:, hs], sgT[:, cb, :], ALU.mult, [p, sgT], [sgT])
                        yield
                    S.dma("pool", sg_d[blk], sgT[:, :, :].rearrange("p a b -> p (a b)"), reads=[sgT], writes=[sg_b[blk]])
                    for mb in range(2):
                        p = PB
                        for k in range(8):
                            mm(p[:, hs], lora_bf[:, k, mb * 128:(mb + 1) * 128], hT[:, k, :], [lora_bf, hT], [p],
                               start=(k == 0), stop=(k == 7), inc=(k == 7))
                        if mb == 0:
                            tq = TMP["sq"]
                            act(tq[:], p[:, hs], AF.Exp, [p], [tq], scale=-2.0)
                            act(tq[:], tq[:], AF.Ln, [tq], [tq], bias=1.0)
                            act(tq[:], tq[:], AF.Exp, [tq], [tq], scale=-1.0)
                            ts("dve", lwd[:], tq[:], 2.0, -1.0, ALU.mult, ALU.add, [tq], [lwd])
                        else:
                            cp("dve", lwi[:], p[:, hs], [p], [lwi])
                        yield

                def cd_gen(blk):
                    par = blk % 2
                    p5 = PB

                    def c_kk(cb):
                        if False:
                            yield
                        k_ = RKV[1][cb]
                        T_ = TMPC[cb % 2]
                        act(T_["sq"][:], k_[:], AF.Square, [k_, cols], [T_["sq"]], scale=col(28 + cb))
                        mm(p5[:, 0:256], bones[:], T_["sq"][:], [bones, T_["sq"]], [p5])
                        ts("dve", T_["rn"][:], p5[:, 0:256], 1e-24, None, ALU.max, None, [p5], [T_["rn"]])
                        act(T_["rn"][:], T_["rn"][:], AF.Ln, [T_["rn"]], [T_["rn"]])
                        act(T_["rn"][:], T_["rn"][:], AF.Exp, [T_["rn"]], [T_["rn"]], scale=-0.5)
                        stt(T_["kk"][:], k_[:], col(28 + cb), T_["rn"][:], ALU.mult, ALU.mult, [k_, cols, T_["rn"]],
                            [T_["kk"]])

                    def c_front(cb, e):
                        TE = TMPE[e]
                        es_ = slice(e * 64, (e + 1) * 64)
                        pz = PB
                        mm(pz[:, 0:256], dup_bf[es_, cb * 128:(cb + 1) * 128], lwd[es_, :], [dup_bf, lwd], [pz])
                        mm(pz[:, 256:512], iup_bf[es_, cb * 128:(cb + 1) * 128], lwi[es_, :], [iup_bf, lwi], [pz])
                        sig, pi, px, tcl = TE["sig"], TE["pi"], TE["px"], TE["tcol"]
                        act(sig[:], pz[:, 0:256], AF.Exp, [pz, dcols], [sig], scale=-1.0, bias=dcol(28 + e * 4 + cb))
                        act(TE["a"][:], pz[:, 256:512], AF.Exp, [pz, dcols], [TE["a"]], scale=-1.0, bias=dcol(36 + e * 4 + cb))
                        act(sig[:], sig[:], AF.Ln, [sig], [sig], bias=1.0)
                        act(sig[:], sig[:], AF.Exp, [sig], [sig], scale=-1.0)
                        yield
                        for ck in range(2):
                            tc = slice(ck * 128, (ck + 1) * 128)
                            S.op("dve", lambda e_: e_.tensor_tensor_scan(out=pi[:, tc], data0=ones[:, 0:128],
                                                                         data1=sig[:, tc], initial=0.0,
                                                                         op0=ALU.mult, op1=ALU.add),
                                 reads=[ones, sig], writes=[pi])
                        tt("dve", px[:], pi[:], sig[:], ALU.subtract, [pi, sig], [px])
                        ts("dve", tcl[:, 0:2], pi[:, 127:256:128], -C0, None, ALU.mult, None, [pi], [tcl])
                        ts("dve", tcl[:, 2:4], pi[:, 127:256:128], C0, None, ALU.mult, None, [pi], [tcl])
                        yield
                        WL_ = WLc[cb % 2]
                        act(WL_[:, e * 2:e * 2 + 2], tcl[:, 0:2], AF.Exp, [tcl], [WL_])
                        E1, E2, E3, E4 = TE["E1"], TE["E2"], TE["E3"], TE["E4"]
                        if e == 0:
                            act(E1[:], pi[:], AF.Exp, [pi], [E1], scale=-C0)
                            act(E2[:], px[:], AF.Exp, [px], [E2], scale=-C0)
                            act(E3[:], pi[:], AF.Exp, [pi], [E3], scale=C0)
                            for ck in range(2):
                                tc = slice(ck * 128, (ck + 1) * 128)
                                act(E4[:, tc], pi[:, tc], AF.Exp, [pi, tcl], [E4], scale=C0, bias=tcl[:, ck:ck + 1])
                        else:
                            for ck in range(2):
                                tc = slice(ck * 128, (ck + 1) * 128)
                                act(E1[:, tc], px[:, tc], AF.Exp, [px, tcl], [E1], scale=C0, bias=tcl[:, ck:ck + 1])
                                act(E2[:, tc], pi[:, tc], AF.Exp, [pi, tcl], [E2], scale=C0, bias=tcl[:, ck:ck + 1])
                                act(E3[:, tc], px[:, tc], AF.Exp, [px, tcl], [E3], scale=-C0, bias=tcl[:, 2 + ck:3 + ck])
                            act(E4[:], px[:], AF.Exp, [px], [E4], scale=-C0)
                        yield
                        act(TE["a"][:], TE["a"][:], AF.Ln, [TE["a"]], [TE["a"]], bias=1.0)
                        act(TE["a"][:], TE["a"][:], AF.Exp, [TE["a"]], [TE["a"]], scale=-1.0)

                    def c_back(cb, e):
                        TE = TMPE[e]
                        T_ = TMPC[cb % 2]
                        r_, k_ = RKV[0][cb], RKV[1][cb]
                        E1, E2, E3, E4 = TE["E1"], TE["E2"], TE["E3"], TE["E4"]
                        a_, bq, kd, rkd = TE["a"], TE["bq"], TE["kd"], T_["rkd"]
                        tt("pool", bq[:], T_["kk"][:], a_[:], ALU.mult, [T_["kk"], a_], [bq])
                        ts("dve", kd[:], a_[:], col(32 + cb), dcol(24 + cb), ALU.mult, ALU.add, [a_, cols, dcols], [kd])
                        tt("dve", kd[:], kd[:], k_[:], ALU.mult, [kd, k_], [kd])
                        f = lambda n: FMt(n, par, e, cb)
                        tt("dve", f("Qt")[:], r_[:], E1[:], ALU.mult, [r_, E1], [f("Qt")])
                        tt("pool", f("KKt")[:], T_["kk"][:], E2[:], ALU.mult, [T_["kk"], E2], [f("KKt")])
                        tt("dve", f("Kt")[:], kd[:], E3[:], ALU.mult, [kd, E3], [f("Kt")])
                        yield
                        tt("pool", f("Bt")[:], bq[:], E3[:], ALU.mult, [bq, E3], [f("Bt")])
                        tt("dve", f("Kh")[:], kd[:], E4[:], ALU.mult, [kd, E4], [f("Kh")])
                        tt("pool", f("Bh")[:], bq[:], E4[:], ALU.mult, [bq, E4], [f("Bh")])
                        if e == 0:
                            tt("dve", rkd[:], r_[:], kd[:], ALU.mult, [r_, kd], [rkd])
                        else:
                            tt("dve", T_["sq"][:], r_[:], kd[:], ALU.mult, [r_, kd], [T_["sq"]])
                            stt(rkd[:], rkd[:], 1.0, T_["sq"][:], ALU.mult, ALU.add, [rkd, T_["sq"]], [rkd])

                    def c_tail(cb):
                        if False:
                            yield
                        T_ = TMPC[cb % 2]
                        v_ = RKV[2][cb]
                        rkd = T_["rkd"]
                        WL_ = WLc[cb % 2]
                        for hh in range(2):
                            mm(p5[0:64, 256 + hh * 4:256 + hh * 4 + 4], identf[:, hh * 64:(hh + 1) * 64], WL_[:, 0:4],
                               [identf, WL_], [p5])
                        for hh in range(2):
                            h = cb * 2 + hh
                            for e in range(2):
                                c0 = ((blk * 2 + e) * 8 + h) * 2
                                cp("dve", WLh[:, c0:c0 + 2], p5[0:64, 256 + hh * 4 + e * 2:256 + hh * 4 + e * 2 + 2], [p5], [WLh])
                        ts("dve", rkd[:], rkd[:], col(36 + cb), None, ALU.mult, None, [rkd, cols], [rkd])
                        mm(p5[:, 0:256], bones[:], rkd[:], [bones, rkd], [p5])
                        tt("dve", bvT[:, cb, :], p5[:, 0:256], v_[:], ALU.mult, [p5, v_], [bvT])

                    for cb in range(4):
                        yield from c_kk(cb)
                        yield
                        for e in range(2):
                            yield from c_front(cb, e)
                            yield
                            yield from c_back(cb, e)
                            yield
                        yield from c_tail(cb)
                        yield
                    S.dma("pool", bv_d[blk], bvT[:, :, :].rearrange("p a b -> p (a b)"), reads=[bvT], writes=[bv_b[blk]])

                    for ck in range(2):
                        tc = slice(ck * 128, (ck + 1) * 128)
                        jobs = [(Vtm[par][ck], vbf)] + [(Khtm[par][ck][e], FMs["Kh"][e]) for e in range(2)] + \
                               [(Bhtm[par][ck][e], FMs["Bh"][e]) for e in range(2)]
                        for ji, (dst, src) in enumerate(jobs):
                            p = PB
                            for cb in range(4):
                                mm(p[:, cb * 128:(cb + 1) * 128], src[cb][:, tc], identb[:], [src[cb], identb], [p],
                                   inc=(cb == 3))
                            cp("act" if ji % 2 == 0 else "dve", dst[:], p[:, :], [p], [dst])
                            yield


                def abcd_gen(blk):
                    yield from ab_gen(blk)
                    yield from cd_gen(blk)

                bg = {}

                def pump():
                    g = bg.get("g")
                    if g is not None:
                        try:
                            next(g)
                        except StopIteration:
                            bg["g"] = None

                def phase1_block(blk):
                    lat = blk < NLB
                    s = 0 if lat else 1
                    tok0 = blk * 256
                    if blk == 0:
                        bg["g"] = abcd_gen(0)
                    while bg.get("g") is not None:
                        pump()
                    ckpt(5)
                    if blk + 1 < NB and BG_INJECT:
                        bg["g"] = abcd_gen(blk + 1)
                    npump = [0]
                    par = blk % 2
                    ckpt(8)
                    for ck in range(2):
                        chunk = blk * 2 + ck
                        tc = slice(ck * 128, (ck + 1) * 128)
                        yps = PS[6]
                        for hg in range(2):
                            def fm(n, e, hq):
                                return FM4[n][par][e][hq][hg * 64:(hg + 1) * 64, tc]

                            def fmR(n, e, hq):
                                return [FM4[n][par][e][hq]]

                            def chain(e):
                                bA, bB, zps = PS[3 * e], PS[3 * e + 1], PS[3 * e + 2]
                                if e == 0:
                                    mSTn, mSn, mST, mIT = MK["NLT"], MK["NGT"], MK["LT"], MK["LE"]
                                else:
                                    mSTn, mSn, mST, mIT = MK["NGT"], MK["NLT"], MK["GT"], MK["GE"]
                                hsl = lambda hq: slice(hq * 128, (hq + 1) * 128)
                                for hq in range(4):
                                    mm(bA[:, hsl(hq)], fm("Bt", e, hq), fm("KKt", e, hq), fmR("Bt", e, hq) + fmR("KKt", e, hq),
                                       [bA], inc=(hq == 3))
                                tt("dve", XT[e][0][:], bA[:, :], mSTn[:], ALU.mult, [bA, mSTn], [XT[e][0]])
                                for hq in range(4):
                                    mm(bB[:, hsl(hq)], fm("KKt", e, hq), fm("Bt", e, hq), fmR("Bt", e, hq) + fmR("KKt", e, hq),
                                       [bB], inc=(hq == 3))
                                tt("dve", XM[e][0][:], bB[:, :], mSn[:], ALU.mult, [bB, mSn], [XM[e][0]])
                                yield
                                for hq in range(4):
                                    mm(bA[:, hsl(hq)], fm("Kt", e, hq), fm("KKt", e, hq), fmR("Kt", e, hq) + fmR("KKt", e, hq),
                                       [bA], inc=(hq == 3))
                                tt("dve", AakT[e][:], bA[:, :], mST[:], ALU.mult, [bA, mST], [AakT[e]])
                                for hq in range(4):
                                    h = hq * 2 + hg
                                    hh = hg
                                    mm(zps[:, hq * 128:hq * 128 + 64], AakT[e][:, hsl(hq)], Vtm[par][ck][:, h * 64:(h + 1) * 64],
                                       [AakT[e], Vtm[par][ck]], [zps], start=(hq == 0), stop=False, inc=False)
                                    mm(zps[:, hq * 128 + 64:(hq + 1) * 128], fm("KKt", e, hq),
                                       identb[hh * 64:(hh + 1) * 64, hh * 64:(hh + 1) * 64], fmR("KKt", e, hq) + [identb], [zps],
                                       start=False, stop=False, inc=(hq == 3))
                                cp("act", Zb[e][:], zps[:, :], [zps], [Zb[e]])
                                yield
                                for hq in range(4):
                                    mm(bB[:, hsl(hq)], fm("Bt", e, hq), fm("Qt", e, hq), fmR("Bt", e, hq) + fmR("Qt", e, hq),
                                       [bB], inc=(hq == 3))
                                cp("act", AqbT[e][:], bB[:, :], [bB], [AqbT[e]])
                                tt("pool", AqbT[e][:], AqbT[e][:], mIT[:], ALU.mult, [AqbT[e], mIT], [AqbT[e]])
                                for hq in range(4):
                                    mm(bA[:, hsl(hq)], fm("Kt", e, hq), fm("Qt", e, hq), fmR("Kt", e, hq) + fmR("Qt", e, hq),
                                       [bA], inc=(hq == 3))
                                cp("act", AqkT[e][:], bA[:, :], [bA], [AqkT[e]])
                                tt("pool", AqkT[e][:], AqkT[e][:], mIT[:], ALU.mult, [AqkT[e], mIT], [AqkT[e]])
                                yield
                                def xtv(i, lev_):
                                    if lev_ < KCUT:
                                        return XT[e][i][:, :], XM[e][i][:, :]
                                    return XT[e][i][:, :].bitcast(BF16)[:, 0:512], XM[e][i][:, :].bitcast(BF16)[:, 0:512]

                                for lev in range(7):
                                    cur, nxt = lev % 2, (lev + 1) % 2
                                    xt_c, xm_c = xtv(cur, lev)
                                    zsrc = Zb[e] if lev < KCUT else AakT[e]
                                    for hq in range(4):
                                        mm(zps[:, hsl(hq)], xt_c[:, hsl(hq)], zsrc[:, hsl(hq)], [XT[e][cur], zsrc], [zps],
                                           start=False, stop=(lev == 6), inc=(hq == 3))
                                    if lev < 6:
                                        xt_n, xm_n = xtv(nxt, lev + 1)
                                        for hq in range(4):
                                            mm(bA[:, hsl(hq)], xm_c[:, hsl(hq)], xt_c[:, hsl(hq)],
                                               [XM[e][cur], XT[e][cur]], [bA], inc=(hq == 3))
                                        cp("dve", xt_n, bA[:, :], [bA], [XT[e][nxt]])
                                        if lev < 5:
                                            for hq in range(4):
                                                mm(bB[:, hsl(hq)], xt_c[:, hsl(hq)], xm_c[:, hsl(hq)],
                                                   [XM[e][cur], XT[e][cur]], [bB], inc=(hq == 3))
                                            cp("act", xm_n, bB[:, :], [bB], [XM[e][nxt]])
                                        zdst = Zb[e] if lev + 1 < KCUT else AakT[e]
                                        cp("act", zdst[:], zps[:, :], [zps], [zdst])
                                    yield
                                act(UGn[e][:], zps[:, :], AF.Identity, [zps], [UGn[e]], scale=-1.0)
                                for hq in range(4):
                                    h = hq * 2 + hg
                                    hh = hg
                                    hc = slice(h * 64, (h + 1) * 64)
                                    mm(yps[:, hc], AqbT[e][:, hsl(hq)], UGn[e][:, hq * 128:hq * 128 + 64], [AqbT[e], UGn[e]],
                                       [yps], start=(e == 0 and hg == 0 and hq == 0), stop=False, inc=False)
                                    mm(yps[:, hc], AqkT[e][:, hsl(hq)], Vtm[par][ck][:, hc], [AqkT[e], Vtm[par][ck]], [yps],
                                       start=False, stop=(e == 1), inc=(hq == 3))
                                for hq in range(4):
                                    hh = hg
                                    mm(bA[0:64, hsl(hq)], identb[hh * 64:(hh + 1) * 64, hh * 64:(hh + 1) * 64], fm("Qt", e, hq),
                                       fmR("Qt", e, hq) + [identb], [bA], start=True, stop=False, inc=False)
                                    mm(bA[0:64, hsl(hq)], UGn[e][:, hq * 128 + 64:(hq + 1) * 128], AqbT[e][:, hsl(hq)],
                                       [UGn[e], AqbT[e]], [bA], start=False, stop=True, inc=(hq == 3))
                                cp("act", Qht[:, :].rearrange("p (e q g t) -> p e q g t", e=2, q=4, g=2)[:, e, :, hg, :],
                                   bA[0:64, :].rearrange("p (q t) -> p q t", q=4), [bA], [Qht])
                                for hq in range(4):
                                    h = hq * 2 + hg
                                    hc = slice(h * 64, (h + 1) * 64)
                                    mm(bB[0:64, hq * 64:(hq + 1) * 64], UGn[e][:, hq * 128 + 64:(hq + 1) * 128], Bhtm[par][ck][e][:, hc],
                                       [UGn[e], Bhtm[par][ck][e]], [bB], inc=False)
                                    mm(bB[0:64, 256 + hq * 64:256 + (hq + 1) * 64], Bhtm[par][ck][e][:, hc],
                                       UGn[e][:, hq * 128:hq * 128 + 64], [UGn[e], Bhtm[par][ck][e]], [bB], start=True, stop=False,
                                       inc=False)
                                    mm(bB[0:64, 256 + hq * 64:256 + (hq + 1) * 64], Khtm[par][ck][e][:, hc], Vtm[par][ck][:, hc],
                                       [Khtm[par][ck][e], Vtm[par][ck]], [bB], start=False, stop=True, inc=(hq == 3))
                                cp("dve", Xst[:, :].rearrange("p (e q g c) -> p e q g c", e=2, q=4, g=2)[:, e, :, hg, :],
                                   bB[0:64, 0:256].rearrange("p (q c) -> p q c", q=4), [bB], [Xst])
                                cp("dve", Dst[:, :].rearrange("p (e q g c) -> p e q g c", e=2, q=4, g=2)[:, e, :, hg, :],
                                   bB[0:64, 256:512].rearrange("p (q c) -> p q c", q=4), [bB], [Dst])
                                yield

                            gens = [chain(0), chain(1)]
                            alive = [True, True]
                            while any(alive):
                                for gi in range(2):
                                    if alive[gi]:
                                        try:
                                            next(gens[gi])
                                        except StopIteration:
                                            alive[gi] = False
                                        npump[0] += 1
                                        if npump[0] % PUMP_EVERY == 0:
                                            pump()
                        cp("act", Ylt[:], yps[:, :], [yps], [Ylt])
                        S.dma("pool", Yl_d[chunk], Ylt[:], reads=[Ylt], writes=[Yl_b[chunk]])
                        S.dma("pool", Qh_d[chunk], Qht[:], reads=[Qht], writes=[Qh_b[chunk]])
                        S.dma("pool", X_d[chunk], Xst[:], reads=[Xst], writes=[X_b[chunk]])
                        S.dma("pool", D_d[chunk], Dst[:], reads=[Dst], writes=[D_b[chunk]])

                for blk in range(NB):
                    phase1_block(blk)
                    if blk + 1 < NB and not BG_INJECT:
                        bg["g"] = abcd_gen(blk + 1)
                    ckpt(9)

                e1b.close()
                S.barrier()
                ckpt(10)
                Sf = sbt(e1, "Sf", [64, 16, 64], F32)
                Sb = sbt(e1, "Sb", [64, 16, 64], BF16)
                Xl = [sbt(e1, "Xl%d" % i, [64, 16, 64], BF16) for i in range(2)]
                Dl = [sbt(e1, "Dl%d" % i, [64, 16, 64], F32) for i in range(2)]
                Sfin = sbt(e1, "Sfin", [64, 16, 64], F32)
                SfB = [Buf() for _ in range(16)]

                def flat(t_, a=None, b=None):
                    ap = t_[:, :, :] if a is None else t_[:, a:b, :]
                    return ap.rearrange("p a b -> p (a b)")

                def run_seq(chunks, init_from_state, seq_out):
                    n = len(chunks)
                    if init_from_state:
                        cp("dve", flat(Sf), flat(Sf0), [Sf0], SfB)
                    else:
                        S.op("dve", lambda e: e.memset(flat(Sf), 0.0), writes=SfB)
                    cp("dve", flat(Sb), flat(Sf), SfB, [Sb])
                    for st in range(n):
                        cf, cbw = chunks[st], chunks[n - 1 - st]
                        S.dma("pool", S_d[cf][:, 0:512], flat(Sb, 0, 8), reads=[Sb], writes=[S_b[cf][0]])
                        S.dma("pool", S_d[cbw][:, 512:1024], flat(Sb, 8, 16), reads=[Sb], writes=[S_b[cbw][1]])
                        xl, dl = Xl[st % 2], Dl[st % 2]
                        S.dma("sp", flat(xl, 0, 8), X_d[cf][:, 0:512], reads=[X_b[cf]], writes=[xl])
                        S.dma("sp", flat(xl, 8, 16), X_d[cbw][:, 512:1024], reads=[X_b[cbw]], writes=[xl])
                        S.dma("sp", flat(dl, 0, 8), D_d[cf][:, 0:512], reads=[D_b[cf]], writes=[dl])
                        S.dma("sp", flat(dl, 8, 16), D_d[cbw][:, 512:1024], reads=[D_b[cbw]], writes=[dl])
                        for hd in range(16):
                            p = PS[hd // 8]
                            mm(p[0:64, (hd % 8) * 64:(hd % 8 + 1) * 64], xl[:, hd, :], Sb[:, hd, :], [xl, Sb], [p],
                               inc=(hd % 8 == 7))
                        for hd in range(16):
                            e, h = hd // 8, hd % 8
                            cch = cf if e == 0 else cbw
                            blk_, ck_ = cch // 2, cch % 2
                            c0 = ((blk_ * 2 + e) * 8 + h) * 2 + ck_
                            p = PS[hd // 8]
                            stt(Sf[:, hd, :], Sf[:, hd, :], WLh[:, c0:c0 + 1], p[0:64, (hd % 8) * 64:(hd % 8 + 1) * 64],
                                ALU.mult, ALU.add, [SfB[hd], WLh, p], [SfB[hd]])
                        tt("dve", flat(Sf), flat(Sf), flat(dl), ALU.add, SfB + [dl], SfB)
                        cp("act", flat(Sb), flat(Sf), SfB, [Sb])
                    if seq_out is not None:
                        for hd in range(16):
                            p = PS[2 + hd // 8]
                            mm(p[0:64, (hd % 8) * 64:(hd % 8 + 1) * 64], Sf[:, hd, :], identf[0:64, 0:64], [SfB[hd], identf], [p],
                               inc=(hd % 8 == 7))
                        for hf in range(2):
                            cp("dve", flat(Sfin, hf * 8, hf * 8 + 8), PS[2 + hf][0:64, :], [PS[2 + hf]], [Sfin])
                        S.dma("pool", so[seq_out], Sfin[:, :, :], reads=[Sfin])

                if NLB > 0:
                    run_seq(list(range(0, 2 * NLB)), True, None)
                for cs in range(NCS):
                    b0 = 2 * (NLB + cs)
                    run_seq([b0, b0 + 1], False, cs)

            ckpt(11)
            S.barrier()
            with ExitStack() as e3:
                wo_bf = sbt(e3, "wo_bf", [128, 8, 1024], BF16)
                FG = sbt(e3, "FG", [128, 1024], F32)
                stg3 = [sbt(e3, "stg3_%d" % i, [128, 2048], F32) for i in range(2)]
                for k in range(8):
                    g = stg3[k % 2]
                    S.dma("sp", g[:], w_in[:, k, 2048:4096], writes=[g])
                    cp("act" if k % 2 == 0 else "dve", w_bf[:, k, :], g[:], [g], [w_bf])
                for k in range(8):
                    g = stg3[k % 2]
                    S.dma("sp", g[:, 0:1024], w_out[:, k, :], writes=[g])
                    cp("act" if k % 2 == 0 else "dve", wo_bf[:, k, :], g[:, 0:1024], [g], [wo_bf])
                S.dma("sp", FG[:], fg_bc, writes=[FG])
                GATE = [sbt(e3, "GATE_%d" % s, [128, 1024], F32) for s in range(2)]
                modg = sbt(e3, "modg", [2, 1024], F32)
                sel3 = [sbt(e3, "sel3_%d" % s, [2, 128], F32) for s in range(2)]
                S.dma("sp", modg[:], mod_d, reads=[mod_b], writes=[modg])
                for s in range(2):
                    ts("dve", sel3[s][:], ones[0:2, :], identf[0:2, s:s + 1], None, ALU.mult, None, [ones, identf], [sel3[s]])
                    for n in range(2):
                        p = PS[s * 2 + n]
                        mm(p[:, :], sel3[s][:], modg[:, n * 512:(n + 1) * 512], [sel3[s], modg], [p])
                        cp("act", GATE[s][:, n * 512:(n + 1) * 512], p[:, :], [p], [GATE[s]])
                hTw2 = [sbt(e3, "hTw%d" % i, [128, 8, 384], BF16) for i in range(2)]
                x3 = [sbt(e3, "x3_%d" % i, [128, 1024], F32) for i in range(2)]
                Gt = [sbt(e3, "Gt%d" % cb, [128, 256], F32) for cb in range(4)]
                tmpc = sbt(e3, "tmpc", [128, 384], F32)
                cuA = sbt(e3, "cuA", [128, 4, 66], F32)
                cuB = sbt(e3, "cuB", [128, 384], F32)
                cuC = sbt(e3, "cuC", [128, 258], F32)
                cacc = sbt(e3, "cacc", [128, 256], F32)
                catT = sbt(e3, "catT", [128, 8, 256], BF16)
                Qhl2 = [sbt(e3, "Qhl%d" % i, [64, 2048], BF16) for i in range(2)]
                Sl2 = [sbt(e3, "Sl%d" % i, [64, 1024], BF16) for i in range(2)]
                Yll2 = [sbt(e3, "Yll%d" % i, [128, 512], F32) for i in range(2)]
                Yt2 = [sbt(e3, "Yt%d" % i, [128, 512], F32) for i in range(2)]
                gnY2 = [sbt(e3, "gnY%d" % i, [128, 512], BF16) for i in range(2)]
                bst2 = [sbt(e3, "bst%d" % i, [128, 8, 6], F32) for i in range(2)]
                mv2 = [sbt(e3, "mv%d" % i, [128, 8, 2], F32) for i in range(2)]
                rs2 = [sbt(e3, "rs%d" % i, [128, 8], F32) for i in range(2)]
                yat4 = [[sbt(e3, "yat%d%d" % (i, cb), [128, 128], F32) for cb in range(4)] for i in range(2)]
                catB = [Buf(), Buf()]
                catTB2 = [sbt(e3, "catTB%d" % i, [128, 4, 256], BF16) for i in range(2)]
                sgl2 = [sbt(e3, "sgl%d" % i, [128, 4, 256], F32) for i in range(2)]
                bvl2 = [sbt(e3, "bvl%d" % i, [128, 4, 256], F32) for i in range(2)]
                yo2 = [sbt(e3, "yo%d" % i, [128, 1024], F32) for i in range(2)]
                junk32 = [sbt(e3, "junk3_%d" % i, [128, 1024], BF16) for i in range(2)]
                ss32 = [sbt(e3, "ss3_%d" % i, [128, 4], F32) for i in range(2)]
                S.op("dve", lambda e: e.memset(cuA[:, :, :].rearrange("p a b -> p (a b)"), 0.0), writes=[cuA])
                S.op("dve", lambda e: e.memset(cuC[:], 0.0), writes=[cuC])

                def hflat(a, b):
                    return hTw[:, :, a:b]

                def b_gen(blk):
                    hTw, sgl, bvl = hTw2[blk % 2], sgl2[blk % 2], bvl2[blk % 2]
                    catTB = catTB2[blk % 2]
                    lat = blk < NLB
                    s = 0 if lat else 1
                    tok0 = blk * 256
                    hv = lambda b_: hT_d[b_].rearrange("p (k t) -> p k t", k=8)
                    if lat:
                        if blk == 0:
                            S.op("dve", lambda e: e.memset(hTw[:, :, 0:64], 0.0), writes=[hTw])
                        else:
                            S.dma("sp", hTw[:, :, 0:64], hv(blk - 1)[:, :, 192:256], reads=[hT_b[blk - 1]], writes=[hTw])
                        if blk == NLB - 1:
                            S.op("dve", lambda e: e.memset(hTw[:, :, 320:384], 0.0), writes=[hTw])
                        else:
                            S.dma("sp", hTw[:, :, 320:384], hv(blk + 1)[:, :, 0:64], reads=[hT_b[blk + 1]], writes=[hTw])
                    S.dma("sp", hTw[:, :, 64:320], hv(blk), reads=[hT_b[blk]], writes=[hTw])
                    S.dma("sp", sgl[:, :, :].rearrange("p a b -> p (a b)"), sg_d[blk], reads=[sg_b[blk]], writes=[sgl])
                    S.dma("sp", bvl[:, :, :].rearrange("p a b -> p (a b)"), bv_d[blk], reads=[bv_b[blk]], writes=[bvl])
                    for cb in range(4):
                        pb, pg = PS[4], PS[5]
                        hs = slice(0, 256)
                        for k in range(8):
                            mm(pb[:, hs], w_bf[:, k, cb * 128:(cb + 1) * 128], hTw[:, k, 64:320], [w_bf, hTw], [pb],
                               start=(k == 0), stop=(k == 7), inc=(k == 7))
                        for k in range(8):
                            mm(pg[:, hs], w_bf[:, k, (12 + cb) * 128:(13 + cb) * 128], hTw[:, k, 64:320], [w_bf, hTw], [pg],
                               start=(k == 0), stop=(k == 7), inc=(k == 7))
                        yield
                        sigm(tmpc[:, 0:256], pg[:, hs], [pg], [tmpc])
                        yield
                        tt("dve", tmpc[:, 0:256], pg[:, hs], tmpc[:, 0:256], ALU.mult, [pg, tmpc], [tmpc])
                        tt("dve", Gt[cb][:], pb[:, hs], tmpc[:, 0:256], ALU.mult, [pb, tmpc], [Gt[cb]])
                        yield
                    for cb in range(4):
                        pc, pu = PS[6], PS[7]
                        wide = lat and cb >= 2
                        n0, n1 = (0, 384) if wide else (64, 320)
                        N = n1 - n0
                        for k in range(8):
                            mm(pc[:, 0:N], w_bf[:, k, (4 + cb) * 128:(5 + cb) * 128], hTw[:, k, n0:n1], [w_bf, hTw], [pc],
                               start=(k == 0), stop=(k == 7), inc=(k == 7))
                        for k in range(8):
                            mm(pu[:, 0:N], w_bf[:, k, (8 + cb) * 128:(9 + cb) * 128], hTw[:, k, n0:n1], [w_bf, hTw], [pu],
                               start=(k == 0), stop=(k == 7), inc=(k == 7))
                        yield
                        cp("act", tmpc[:, 0:N], pc[:, 0:N], [pc], [tmpc])
                        yield
                        cw = [col(48 + j * 4 + cb) for j in range(3)]
                        if not lat:
                            tt("dve", cuC[:, 1:257], tmpc[:, 0:256], pu[:, 0:256], ALU.mult, [tmpc, pu], [cuC])
                            prev, ctr, nxt, cub = cuC[:, 0:256], cuC[:, 1:257], cuC[:, 2:258], cuC
                            accv = cacc[:]
                        elif wide:
                            tt("dve", cuB[:], tmpc[:, 0:384], pu[:, 0:384], ALU.mult, [tmpc, pu], [cuB])
                            prev, ctr, nxt, cub = cuB[:, 0:256], cuB[:, 64:320], cuB[:, 128:384], cuB
                            accv = cacc[:]
                        else:
                            tt("dve", cuA[:, :, 1:65], tmpc[:, 0:256].rearrange("p (r w) -> p r w", r=4),
                               pu[:, 0:256].rearrange("p (r w) -> p r w", r=4), ALU.mult, [tmpc, pu], [cuA])
                            prev, ctr, nxt, cub = cuA[:, :, 0:64], cuA[:, :, 1:65], cuA[:, :, 2:66], cuA
                            accv = cacc[:].rearrange("p (r w) -> p r w", r=4)
                        yield
                        ts("dve", accv, ctr, cw[1], None, ALU.mult, None, [cub, cols], [cacc])
                        stt(accv, prev, cw[0], accv, ALU.mult, ALU.add, [cub, cols, cacc], [cacc])
                        stt(accv, nxt, cw[2], accv, ALU.mult, ALU.add, [cub, cols, cacc], [cacc])
                        yield
                        tt("pool", catTB[:, cb, :], cacc[:], Gt[cb][:], ALU.mult, [cacc, Gt[cb]], [catTB])
                    yield

                def a_part(blk, bgen):
                    hTw, sgl, bvl = hTw2[blk % 2], sgl2[blk % 2], bvl2[blk % 2]
                    catTB = catTB2[blk % 2]
                    lat = blk < NLB
                    s = 0 if lat else 1
                    tok0 = blk * 256
                    for ck in range(2):
                        chunk = blk * 2 + ck
                        S.dma("sp", Qhl2[ck][:], Qh_d[chunk], reads=[Qh_b[chunk]], writes=[Qhl2[ck]])
                        S.dma("sp", Sl2[ck][:], S_d[chunk], reads=[S_b[chunk][0], S_b[chunk][1]], writes=[Sl2[ck]])
                        S.dma("sp", Yll2[ck][:], Yl_d[chunk], reads=[Yl_b[chunk]], writes=[Yll2[ck]])
                        S.dma("sp", x3[ck][:], xs[tok0 + ck * 128:tok0 + (ck + 1) * 128, :], writes=[x3[ck]])

                    def a_tile(ck):
                        tc = slice(ck * 128, (ck + 1) * 128)
                        Qhl, Sl, Yll, xt = Qhl2[ck], Sl2[ck], Yll2[ck], x3[ck]
                        Yt, gnY, bst, mv, rs, yo, junk3, ss3 = Yt2[ck], gnY2[ck], bst2[ck], mv2[ck], rs2[ck], yo2[ck], junk32[ck], ss32[ck]
                        yp, pt, po2 = (PS[0], PS[0], [PS[1], PS[1]]) if ck == 0 else (PS[2], PS[2], [PS[3], PS[3]])
                        for h in range(8):
                            for e in range(2):
                                mm(yp[:, h * 64:(h + 1) * 64], Qhl[:, (e * 8 + h) * 128:(e * 8 + h + 1) * 128],
                                   Sl[:, (e * 8 + h) * 64:(e * 8 + h + 1) * 64], [Qhl, Sl], [yp], start=(e == 0), stop=(e == 1),
                                   inc=(h == 7 and e == 1))
                        yield
                        tt("dve", Yt[:], yp[:, :], Yll[:], ALU.add, [yp, Yll], [Yt])
                        for h in range(8):
                            S.op("dve", lambda e_: e_.bn_stats(out=bst[:, h, :], in_=Yt[:, h * 64:(h + 1) * 64]), reads=[Yt],
                                 writes=[bst])
                        for h in range(8):
                            S.op("dve", lambda e_: e_.bn_aggr(out=mv[:, h, :], in_=bst[:, h, :]), reads=[bst], writes=[mv])
                        yield
                        act(rs[:], mv[:, :, 1], AF.Ln, [mv], [rs], bias=GN_EPS)
                        act(rs[:], rs[:], AF.Exp, [rs], [rs], scale=-0.5)
                        yield
                        for h in range(8):
                            ts("dve", gnY[:, h * 64:(h + 1) * 64], Yt[:, h * 64:(h + 1) * 64], mv[:, h, 0:1], rs[:, h:h + 1],
                               ALU.subtract, ALU.mult, [Yt, mv, rs], [gnY])
                        yield
                        for cb in range(4):
                            mm(pt[:, cb * 128:(cb + 1) * 128], gnY[:, cb * 128:(cb + 1) * 128], identb[:], [gnY, identb], [pt],
                               inc=(cb == 3))
                        yield
                        for cb in range(4):
                            yat = yat4[ck][cb]
                            ts("dve", yat[:], pt[:, cb * 128:(cb + 1) * 128], col(40 + cb), col(44 + cb), ALU.mult, ALU.add,
                               [pt, cols], [yat])
                            tt("pool", yat[:], yat[:], bvl[:, cb, tc], ALU.add, [yat, bvl], [yat])
                            tt("pool", catT[:, cb, tc], yat[:], sgl[:, cb, tc], ALU.mult, [yat, sgl], [catB[ck]])
                        yield
                        for n in range(2):
                            po = po2[n]
                            hs = slice(n * 512, (n + 1) * 512)
                            for m in range(8):
                                lhs = catT[:, m, tc] if m < 4 else catTB[:, m - 4, tc]
                                mm(po[:, :], lhs, wo_bf[:, m, n * 512:(n + 1) * 512], [catB[ck], catTB, wo_bf], [po],
                                   start=(m == 0), stop=(m == 7), inc=(m == 7))
                            yield
                            tt("dve", yo[:, hs], po[:, :], GATE[s][:, hs], ALU.mult, [po, GATE[s]], [yo])
                            yield
                        tt("dve", yo[:], yo[:], xt[:], ALU.add, [yo, xt], [yo])
                        yield
                        act(junk3[:], yo[:], AF.Square, [yo], [junk3, ss3], accum=ss3[:, 0:1])
                        yield
                        act(ss3[:, 2:3], ss3[:, 0:1], AF.Ln, [ss3], [ss3], scale=1.0 / 1024, bias=NORM_EPS)
                        act(ss3[:, 3:4], ss3[:, 2:3], AF.Exp, [ss3], [ss3], scale=-0.5)
                        yield
                        stt(yo[:], yo[:], ss3[:, 3:4], FG[:], ALU.mult, ALU.mult, [yo, ss3, FG], [yo])
                        S.dma("pool", ys[tok0 + ck * 128:tok0 + (ck + 1) * 128, :], yo[:], reads=[yo])

                    gens3 = [a_tile(0), a_tile(1)]
                    alive3 = [True, True]
                    while any(alive3):
                        for gi in range(2):
                            if alive3[gi]:
                                try:
                                    next(gens3[gi])
                                except StopIteration:
                                    alive3[gi] = False
                                if bgen is not None:
                                    try:
                                        next(bgen)
                                    except StopIteration:
                                        bgen = None
                    if bgen is not None:
                        for _ in bgen:
                            pass

                for _ in b_gen(0):
                    pass
                for blk in range(NB):
                    a_part(blk, b_gen(blk + 1) if blk + 1 < NB else None)
                    ckpt(12)
        except _Stop:
            pass
        S.off = False
        S.finish("sp")
        S.finish("pool")
    nc._marks = SS[0].marks
    return nc


def _prep_shared(inp):
    f = np.float32
    d = {}
    d["w_ada"] = np.ascontiguousarray(inp["w_ada"][0].reshape(8, 128, 3072).transpose(1, 0, 2), dtype=f)
    d["b_ada2"] = np.ascontiguousarray(np.stack([inp["b_ada"][0], inp["b_ada"][0]], 0), dtype=f)
    d["ngc"] = np.ascontiguousarray(np.asarray(inp["norm_g"][0], dtype=f).reshape(8, 128).T)
    d["fg_bc"] = np.ascontiguousarray(np.broadcast_to(inp["final_g"][None, :], (128, 1024)), dtype=f)
    d["w_in"] = np.ascontiguousarray(inp["w_in"][0].reshape(8, 128, 4096).transpose(1, 0, 2), dtype=f)
    ld = np.concatenate([inp["decay_down"][0, 0], inp["decay_down"][0, 1], inp["iclr_down"][0, 0], inp["iclr_down"][0, 1]],
                        axis=1)
    d["lora_dn"] = np.ascontiguousarray(ld.reshape(8, 128, 256).transpose(1, 0, 2), dtype=f)
    d["dec_up"] = np.ascontiguousarray(inp["decay_up"][0].reshape(128, 512), dtype=f)
    d["icl_up"] = np.ascontiguousarray(inp["iclr_up"][0].reshape(128, 512), dtype=f)

    def c4(v):
        return np.asarray(v, dtype=f).reshape(4, 128).T

    cl = [c4(inp["shift_mu"][0, q]) for q in range(3)]
    cl += [c4(inp["decay_w0"][0, e]) for e in range(2)]
    cl += [c4(inp["iclr_bias"][0, e]) for e in range(2)]
    cl += [c4(inp["kk_scale"][0]), c4(inp["ka_scale"][0]), c4(inp["bonus_rk"][0]), c4(inp["gn_w"][0]), c4(inp["gn_b"][0])]
    cl += [c4(inp["conv_w"][0, j]) for j in range(3)]
    d["cols"] = np.ascontiguousarray(np.concatenate(cl, axis=1), dtype=f)
    d["w_out"] = np.ascontiguousarray(inp["w_out"][0].reshape(8, 128, 1024).transpose(1, 0, 2), dtype=f)
    return d


def _core_inputs(shared, x_lat, x_ctx, c_lat, c_ctx, st):
    f = np.float32
    m = dict(shared)
    parts = []
    if x_lat is not None:
        parts.append(np.asarray(x_lat, dtype=f).reshape(-1, 1024))
    if x_ctx is not None and len(x_ctx):
        parts.append(np.asarray(x_ctx, dtype=f).reshape(-1, 1024))
    m["xs"] = np.ascontiguousarray(np.concatenate(parts, 0))
    cv = np.stack([np.asarray(c_lat, dtype=f), np.asarray(c_ctx, dtype=f)], 0)
    m["cT"] = np.ascontiguousarray(cv.reshape(2, 8, 128).transpose(2, 1, 0).reshape(128, 16))
    m["st0"] = np.ascontiguousarray(np.asarray(st, dtype=f).transpose(2, 0, 1, 3).reshape(64, 16, 64))
    return m


_PROG = {}


def kernel(**inputs):
    inp = {k: np.asarray(v) for k, v in inputs.items()}
    NCORES = 8
    NLB, NCS = 16, 4
    shared = _prep_shared(inp)
    in_maps = []
    for b in range(NCORES):
        in_maps.append(_core_inputs(shared, inp["x_sample"][b], inp["x_prompt"][4 * b:4 * b + 4], inp["c"][b], inp["c_ctx"],
                                    inp["state_wkv"][b, 0]))
    key = (NLB, NCS)
    if key not in _PROG:
        _PROG[key] = build_program(NLB, NCS)
    res = run_bass_kernel_spmd(_PROG[key], in_maps, core_ids=list(range(NCORES)))
    y_prompt = np.zeros((32, 256, 1024), np.float32)
    y_sample = np.zeros((8, 4096, 1024), np.float32)
    new_state = np.zeros((32, 1, 2, 8, 64, 64), np.float32)
    for b in range(NCORES):
        r = res.results[b]
        ysb = np.asarray(r["ys"])
        y_sample[b] = ysb[:4096]
        y_prompt[4 * b:4 * b + 4] = ysb[4096:].reshape(4, 256, 1024)
        sob = np.asarray(r["so"]).reshape(4, 64, 2, 8, 64)
        new_state[4 * b:4 * b + 4, 0] = sob.transpose(0, 2, 3, 1, 4)
    return (y_prompt, y_sample, new_state)
```

```python
import os
import numpy as np
import concourse.bass as bass
import concourse.mybir as mybir
from concourse.bass_utils import run_bass_kernel_spmd
from contextlib import ExitStack

F32, BF16, I32 = mybir.dt.float32, mybir.dt.bfloat16, mybir.dt.int32
ALU = mybir.AluOpType
AF = mybir.ActivationFunctionType
C0 = 0.6065306597126334
NORM_EPS = 1e-6
GN_EPS = 64e-5
NDS = 24
BG_INJECT = os.environ.get("BG_INJECT", "1") == "1"
PUMP_EVERY = int(os.environ.get("PUMP_EVERY", "1"))
KCUT = int(os.environ.get("KCUT", "5"))
RAW_ONLY = os.environ.get("RAW_ONLY", "1") == "1"
SELF_SKIP = tuple(os.environ.get("SELF_SKIP", "pe").split(","))


class Buf:
    __slots__ = ("w", "r")

    def __init__(self):
        self.w = None
        self.r = {}


class T:
    def __init__(self, t):
        self.t = t
        self.b = Buf()

    def __getitem__(self, idx):
        return self.t[idx]


def _b(x):
    return x.b if hasattr(x, "b") else x


class Sched:
    def __init__(self, nc, es):
        self.nc = nc
        self.eng = {"pe": nc.tensor, "dve": nc.vector, "act": nc.scalar, "pool": nc.gpsimd, "sp": nc.sync}
        self.sem = {k: es.enter_context(nc.semaphore("s_" + k)) for k in self.eng}
        self.cnt = {k: 0 for k in self.eng}
        self.dsem = [es.enter_context(nc.semaphore("d%d" % i)) for i in range(NDS)]
        self.dcnt = [0] * NDS
        self.dnext = 0
        self.dnext2 = 0
        self.waited = {}
        self.nwait = 0
        self.off = False
        self.nops = {k: 0 for k in self.eng}
        self.marks = []

    def _semh(self, key):
        return self.sem[key] if isinstance(key, str) else self.dsem[key]

    def _wait(self, e, key, val):
        if val <= 0:
            return
        if e == key and e in SELF_SKIP:
            return
        if self.waited.get((e, key), 0) >= val:
            return
        self.eng[e].wait_ge(self._semh(key), val)
        self.waited[(e, key)] = val
        self.nwait += 1

    def _deps(self, e, reads, writes):
        for b in reads:
            b = _b(b)
            if b.w:
                self._wait(e, *b.w)
        raw_only = RAW_ONLY and e in ("act", "dve")
        for b in writes:
            b = _b(b)
            if b.w and not (raw_only and b.w[0] == e):
                self._wait(e, *b.w)
            for k, v in b.r.items():
                if raw_only and k == e:
                    continue
                self._wait(e, k, v)

    def _mark(self, key, tgt, reads, writes):
        for b in writes:
            b = _b(b)
            b.w = (key, tgt)
            b.r = {}
        for b in reads:
            b = _b(b)
            if b.r.get(key, 0) < tgt:
                b.r[key] = tgt

    def op(self, e, fn, reads=(), writes=(), inc=True):
        if self.off:
            return
        self._deps(e, reads, writes)
        inst = fn(self.eng[e])
        self.nops[e] += 1
        tgt = self.cnt[e] + 1
        if inc:
            inst.then_inc(self.sem[e], 1)
            self.cnt[e] = tgt
        self._mark(e, tgt, reads, writes)

    def dma(self, e, out, in_, reads=(), writes=()):
        if self.off:
            return
        if e == "pool":
            i = NDS - 8 + self.dnext2
            self.dnext2 = (self.dnext2 + 1) % 8
        else:
            i = self.dnext
            self.dnext = (i + 1) % (NDS - 8)
        self._deps(e, reads, writes)
        self._wait(e, i, self.dcnt[i])
        self.eng[e].dma_start(out=out, in_=in_).then_inc(self.dsem[i], 16)
        self.dcnt[i] += 16
        self._mark(i, self.dcnt[i], reads, writes)

    def mark(self, label):
        self.marks.append((label, dict(self.nops)))

    def barrier(self):
        if self.off:
            return
        for e in self.eng:
            for k in self.eng:
                if k != e:
                    self._wait(e, k, self.cnt[k])
            for i in range(NDS):
                self._wait(e, i, self.dcnt[i])

    def finish(self, e="sp"):
        for i in range(NDS):
            self._wait(e, i, self.dcnt[i])


class _Stop(Exception):
    pass


def build_program(NLB, NCS, debug=False, stop=None):
    SS = []

    def ckpt(n):
        SS[0].mark(n)
        if stop is not None and n == stop:
            SS[0].off = True

    NB = NLB + NCS
    NCH = 2 * NB
    NTOK = NB * 256
    nc = bass.Bass("TRN2", target_bir_lowering=False)

    def din(name, shape, dt=F32):
        return nc.dram_tensor(name, list(shape), dt, kind="ExternalInput").ap()

    def dout(name, shape, dt=F32):
        return nc.dram_tensor(name, list(shape), dt, kind="ExternalOutput").ap()

    def dscr(name, shape, dt):
        return nc.dram_tensor(name, list(shape), dt, kind=("ExternalOutput" if debug else "Internal")).ap()

    xs = din("xs", [NTOK, 1024])
    cT = din("cT", [128, 16])
    st0 = din("st0", [64, 16, 64])
    w_ada = din("w_ada", [128, 8, 3072])
    b_ada2 = din("b_ada2", [2, 3072])
    ngc_d = din("ngc", [128, 8])
    fg_bc = din("fg_bc", [128, 1024])
    w_in = din("w_in", [128, 8, 4096])
    lora_dn = din("lora_dn", [128, 8, 256])
    dec_up = din("dec_up", [128, 512])
    icl_up = din("icl_up", [128, 512])
    cols_d = din("cols", [128, 60])
    w_out = din("w_out", [128, 8, 1024])
    ys = dout("ys", [NTOK, 1024])
    so = dout("so", [max(NCS, 1), 64, 16, 64])

    hT_d = dscr("hT_d", [NB, 128, 8 * 256], BF16)
    Yl_d = dscr("Yl_d", [NCH, 128, 512], F32)
    Qh_d = dscr("Qh_d", [NCH, 64, 2048], BF16)
    X_d = dscr("X_d", [NCH, 64, 1024], BF16)
    D_d = dscr("D_d", [NCH, 64, 1024], F32)
    sg_d = dscr("sg_d", [NB, 128, 1024], F32)
    bv_d = dscr("bv_d", [NB, 128, 1024], F32)
    S_d = dscr("S_d", [NCH, 64, 1024], BF16)
    mod_d = dscr("mod_d", [2, 1024], F32)
    mod_b = Buf()
    hT_b = [Buf() for _ in range(NB)]
    Yl_b = [Buf() for _ in range(NCH)]
    Qh_b = [Buf() for _ in range(NCH)]
    X_b = [Buf() for _ in range(NCH)]
    D_b = [Buf() for _ in range(NCH)]
    sg_b = [Buf() for _ in range(NB)]
    bv_b = [Buf() for _ in range(NB)]
    S_b = [[Buf(), Buf()] for _ in range(NCH)]
    ys_b = Buf()
    so_b = Buf()

    with ExitStack() as es:
        S = Sched(nc, es)
        SS.append(S)

        try:
            def sbt(es_, name, shape, dt):
                return T(es_.enter_context(nc.sbuf_tensor("sb_" + name, list(shape), dt)))

            PS = [T(es.enter_context(nc.psum_tensor("ps%d" % i, [128, 512], F32))) for i in range(8)]

            def mm(out, lhsT, rhs, R, W, start=True, stop=True, inc=True):
                S.op("pe", lambda e: e.matmul(out, lhsT=lhsT, rhs=rhs, start=start, stop=stop, skip_group_check=True),
                     reads=R, writes=W, inc=inc)

            def tt(eng, out, a, b, op, R, W):
                S.op(eng, lambda e: e.tensor_tensor(out=out, in0=a, in1=b, op=op), reads=R, writes=W)

            def ts(eng, out, a, s1, s2, op0, op1, R, W):
                if op1 is None:
                    S.op(eng, lambda e: e.tensor_scalar(out=out, in0=a, scalar1=s1, scalar2=None, op0=op0), reads=R, writes=W)
                else:
                    S.op(eng, lambda e: e.tensor_scalar(out=out, in0=a, scalar1=s1, scalar2=s2, op0=op0, op1=op1),
                         reads=R, writes=W)

            def stt(out, a, sc, b, op0, op1, R, W):
                S.op("dve", lambda e: e.scalar_tensor_tensor(out=out, in0=a, scalar=sc, in1=b, op0=op0, op1=op1),
                     reads=R, writes=W)

            def act(out, in_, func, R, W, bias=None, scale=None, accum=None):
                kw = {}
                if bias is not None:
                    kw["bias"] = bias
                if scale is not None:
                    kw["scale"] = scale
                if accum is not None:
                    kw["accum_out"] = accum
                S.op("act", lambda e: e.activation(out=out, in_=in_, func=func, **kw), reads=R, writes=W)

            def sigm(out, in_, R, W, nbias=None):
                act(out, in_, AF.Exp, R, W, scale=-1.0, bias=nbias)
                act(out, out, AF.Ln, W, W, bias=1.0)
                act(out, out, AF.Exp, W, W, scale=-1.0)

            def cp(eng, out, in_, R, W):
                if eng == "act":
                    S.op("act", lambda e: e.copy(out=out, in_=in_), reads=R, writes=W)
                else:
                    S.op(eng, lambda e: e.tensor_copy(out=out, in_=in_), reads=R, writes=W)

            ioi = sbt(es, "ioi", [128, 128], I32)
            iof = sbt(es, "iof", [128, 128], F32)
            identf = sbt(es, "identf", [128, 128], F32)
            identb = sbt(es, "identb", [128, 128], BF16)
            bones = sbt(es, "bones", [128, 128], F32)
            ones = sbt(es, "ones", [128, 128], F32)
            MK = {k: sbt(es, "mk_" + k, [128, 512], BF16) for k in ("LT", "GT", "LE", "GE", "NLT", "NGT")}
            cols = sbt(es, "cols", [128, 60], F32)
            dcols = sbt(es, "dcols", [128, 44], F32)
            w_bf = sbt(es, "w_bf", [128, 8, 2048], BF16)
            WLh = sbt(es, "WLh", [64, NB * 32], F32)

            S.op("pool", lambda e: e.iota(ioi[:], pattern=[[1, 128]], base=0, channel_multiplier=-1), writes=[ioi])
            cp("dve", iof[:], ioi[:], [ioi], [iof])
            ts("dve", identf[:], iof[:], 0.0, None, ALU.is_equal, None, [iof], [identf])
            cp("dve", identb[:], identf[:], [identf], [identb])
            S.op("dve", lambda e: e.memset(ones[:], 1.0), writes=[ones])
            S.op("dve", lambda e: e.memset(bones[:], 0.0), writes=[bones])
            S.op("dve", lambda e: e.memset(bones[0:64, 0:64], 1.0), writes=[bones])
            S.op("dve", lambda e: e.memset(bones[64:128, 64:128], 1.0), writes=[bones])
            for j in range(4):
                sl = slice(j * 128, (j + 1) * 128)
                ts("dve", MK["LT"][:, sl], iof[:], 0.0, None, ALU.is_gt, None, [iof], [MK["LT"]])
                ts("dve", MK["GT"][:, sl], iof[:], 0.0, None, ALU.is_lt, None, [iof], [MK["GT"]])
                ts("dve", MK["LE"][:, sl], iof[:], 0.0, None, ALU.is_ge, None, [iof], [MK["LE"]])
                ts("dve", MK["GE"][:, sl], iof[:], 0.0, None, ALU.is_le, None, [iof], [MK["GE"]])
                ts("dve", MK["NLT"][:, sl], iof[:], 0.0, -1.0, ALU.is_gt, ALU.mult, [iof], [MK["NLT"]])
                ts("dve", MK["NGT"][:, sl], iof[:], 0.0, -1.0, ALU.is_lt, ALU.mult, [iof], [MK["NGT"]])
            S.dma("sp", cols[:], cols_d, writes=[cols])
            ts("dve", dcols[:, 0:12], cols[:, 0:12], -1.0, 1.0, ALU.mult, ALU.add, [cols], [dcols])
            ts("dve", dcols[:, 12:24], cols[:, 0:12], 0.5, None, ALU.mult, None, [cols], [dcols])
            ts("dve", dcols[:, 24:28], cols[:, 32:36], -1.0, 1.0, ALU.mult, ALU.add, [cols], [dcols])
            ts("dve", dcols[:, 28:44], cols[:, 12:28], -1.0, None, ALU.mult, None, [cols], [dcols])
            ckpt(1)

            def col(i):
                return cols[:, i:i + 1]

            def dcol(i):
                return dcols[:, i:i + 1]

            with ExitStack() as e1:
                g1c = [sbt(e1, "g1c%d" % s, [128, 8], F32) for s in range(2)]
                shc = [sbt(e1, "shc%d" % s, [128, 8], F32) for s in range(2)]
                lora_bf = sbt(e1, "lora_bf", [128, 8, 256], BF16)
                dup_bf = sbt(e1, "dup_bf", [128, 512], BF16)
                iup_bf = sbt(e1, "iup_bf", [128, 512], BF16)
                Sf0 = sbt(e1, "Sf0", [64, 16, 64], F32)
                with ExitStack() as e0:
                    stg = [sbt(e0, "stg%d" % i, [128, 3072], F32) for i in range(2)]
                    ngt = sbt(e0, "ngt", [128, 8], F32)
                    cTt = sbt(e0, "cTt", [128, 16], F32)
                    scT = sbt(e0, "scT", [128, 16], F32)
                    modv = sbt(e0, "modv", [2, 3072], F32)
                    bad = sbt(e0, "bad", [2, 3072], F32)
                    st_in = sbt(e0, "st_in", [64, 16, 64], F32)
                    for k in range(8):
                        g = stg[k % 2]
                        S.dma("sp", g[:, 0:2048], w_in[:, k, 0:2048], writes=[g])
                        cp("act" if k % 2 == 0 else "dve", w_bf[:, k, :], g[:, 0:2048], [g], [w_bf])
                    g = stg[0]
                    S.dma("sp", g[:, 0:2048], lora_dn.rearrange("p k n -> p (k n)"), writes=[g])
                    cp("dve", lora_bf[:, :, :].rearrange("p k n -> p (k n)"), g[:, 0:2048], [g], [lora_bf])
                    g = stg[1]
                    S.dma("sp", g[:, 0:512], dec_up, writes=[g])
                    S.dma("sp", g[:, 512:1024], icl_up, writes=[g])
                    cp("dve", dup_bf[:], g[:, 0:512], [g], [dup_bf])
                    cp("dve", iup_bf[:], g[:, 512:1024], [g], [iup_bf])
                    ckpt(2)
                    S.dma("sp", cTt[:], cT, writes=[cTt])
                    act(scT[:], cTt[:], AF.Silu, [cTt], [scT])
                    S.dma("sp", bad[:], b_ada2, writes=[bad])
                    S.dma("sp", ngt[:], ngc_d, writes=[ngt])
                    for k in range(8):
                        g = stg[k % 2]
                        S.dma("sp", g[:], w_ada[:, k, :], writes=[g])
                        for n in range(6):
                            mm(PS[n][0:2, :], scT[:, 2 * k:2 * k + 2], g[:, n * 512:(n + 1) * 512], [scT, g], [PS[n]],
                               start=(k == 0), stop=(k == 7))
                    for n in range(6):
                        tt("dve", modv[:, n * 512:(n + 1) * 512], PS[n][0:2, :], bad[:, n * 512:(n + 1) * 512], ALU.add,
                           [PS[n], bad], [modv])
                    S.dma("sp", mod_d, modv[:, 2048:3072], reads=[modv], writes=[mod_b])
                    for part in range(2):
                        for k in range(8):
                            c0 = (part * 8 + k) * 2
                            mm(PS[0][:, c0:c0 + 2], modv[0:2, part * 1024 + k * 128:part * 1024 + (k + 1) * 128],
                               identf[0:2, 0:2], [modv, identf], [PS[0]], inc=(part == 1 and k == 7))
                    mview = PS[0][:, 0:32].rearrange("p (a k s) -> p a k s", a=2, k=8)
                    for s in range(2):
                        cp("dve", shc[s][:], mview[:, 0, :, s], [PS[0]], [shc[s]])
                        stt(g1c[s][:], mview[:, 1, :, s], 1.0, ngt[:], ALU.add, ALU.mult, [PS[0], ngt], [g1c[s]])
                    ckpt(3)
                    S.dma("sp", st_in[:], st0, writes=[st_in])
                    for hd in range(16):
                        p = PS[hd // 8]
                        mm(p[0:64, (hd % 8) * 64:(hd % 8 + 1) * 64], st_in[:, hd, :], identf[0:64, 0:64], [st_in, identf], [p],
                           inc=(hd % 8 == 7))
                    for hf in range(2):
                        cp("dve", Sf0[:, hf * 8:(hf + 1) * 8, :].rearrange("p a b -> p (a b)"), PS[hf][0:64, :], [PS[hf]], [Sf0])

                S.barrier()
                ckpt(4)
                e1b = ExitStack()
                e1b.__enter__()
                xts = [sbt(e1b, "xt0", [128, 1024], F32)] * 2
                hb = sbt(e1b, "hb", [128, 1024], BF16)
                ss = sbt(e1b, "ss", [128, 4], F32)
                hT = sbt(e1b, "hT", [128, 8, 256], BF16)
                ppL = sbt(e1b, "ppL", [128, 4, 66], F32)
                ppC = sbt(e1b, "ppC", [128, 1, 258], F32)
                nbt = sbt(e1b, "nbt", [128, 256], F32)
                RKV = [[sbt(e1b, "rkv%d_%d" % (q, cb), [128, 256], F32) for cb in range(4)] for q in range(3)]
                vbf = [sbt(e1b, "vbf%d" % cb, [128, 256], BF16) for cb in range(4)]
                sgT = sbt(e1b, "sgT", [128, 4, 256], F32)
                bvT = sbt(e1b, "bvT", [128, 4, 256], F32)
                lwd = sbt(e1b, "lwd", [128, 256], BF16)
                lwi = sbt(e1b, "lwi", [128, 256], BF16)
                TMPC = [{n: sbt(e1b, "tmc%d_%s" % (i, n), [128, 256], F32) for n in ["sq", "kk", "rkd"]} for i in range(2)]
                TMPC[0]["rn"] = TMPC[1]["rn"] = sbt(e1b, "tmc_rn", [128, 256], F32)
                TMP = TMPC[0]
                TMPE = [{n: sbt(e1b, "tm0_%s" % n, [128, 256] if n != "tcol" else [128, 4], F32)
                         for n in ["sig", "pi", "px", "E1", "E2", "E3", "E4", "a", "bq", "kd", "tcol"]}]
                TMPE.append(TMPE[0])
                WLc = [sbt(e1b, "WLc%d" % i, [128, 4], F32) for i in range(2)]
                FM4 = {n: [[[sbt(e1b, "fm_%s%d%d%d" % (n, par, e, cb), [128, 256], BF16) for cb in range(4)] for e in range(2)]
                           for par in range(2)] for n in ("Qt", "KKt", "Kt", "Bt")}
                FMs = {n: [[sbt(e1b, "fm_%s%d%d" % (n, e, cb), [128, 256], BF16) for cb in range(4)] for e in range(2)]
                       for n in ("Kh", "Bh")}

                def FMt(n, par, e, cb):
                    return FMs[n][e][cb] if n in FMs else FM4[n][par][e][cb]

                Khtm = [[[sbt(e1b, "khtm%d%d%d" % (par, ck, e), [128, 512], BF16) for e in range(2)] for ck in range(2)]
                        for par in range(2)]
                Bhtm = [[[sbt(e1b, "bhtm%d%d%d" % (par, ck, e), [128, 512], BF16) for e in range(2)] for ck in range(2)]
                        for par in range(2)]
                Vtm = [[sbt(e1b, "vtm%d%d" % (par, ck), [128, 512], BF16) for ck in range(2)] for par in range(2)]
                XT = [[sbt(e1b, "XT%d%d" % (e, i), [128, 512], F32) for i in range(2)] for e in range(2)]
                XM = [[sbt(e1b, "XM%d%d" % (e, i), [128, 512], F32) for i in range(2)] for e in range(2)]
                AakT = [sbt(e1b, "AakT%d" % e, [128, 512], BF16) for e in range(2)]
                AqbT = [sbt(e1b, "AqbT%d" % e, [128, 512], BF16) for e in range(2)]
                AqkT = [sbt(e1b, "AqkT%d" % e, [128, 512], BF16) for e in range(2)]
                Zb = [sbt(e1b, "Zb%d" % e, [128, 512], F32) for e in range(2)]
                UGn = [sbt(e1b, "UGn%d" % e, [128, 512], BF16) for e in range(2)]
                Ylt = sbt(e1b, "Ylt", [128, 512], F32)
                Qht = sbt(e1b, "Qht", [64, 2048], BF16)
                Xst = sbt(e1b, "Xst", [64, 1024], BF16)
                Dst = sbt(e1b, "Dst", [64, 1024], F32)
                S.op("dve", lambda e: e.memset(ppL[:, :, :].rearrange("p a b -> p (a b)"), 0.0), writes=[ppL])
                S.op("dve", lambda e: e.memset(ppC[:, :, :].rearrange("p a b -> p (a b)"), 0.0), writes=[ppC])

                PB = PS[7]

                def ab_gen(blk):
                    lat = blk < NLB
                    s = 0 if lat else 1
                    tok0 = blk * 256
                    for i in range(2):
                        xt = xts[i]
                        S.dma("sp", xt[:], xs[tok0 + i * 128:tok0 + (i + 1) * 128, :], writes=[xt])
                        yield
                        act(hb[:], xt[:], AF.Square, [xt], [hb, ss], accum=ss[:, 0:1])
                        act(ss[:, 2:3], ss[:, 0:1], AF.Ln, [ss], [ss], scale=1.0 / 1024, bias=NORM_EPS)
                        act(ss[:, 3:4], ss[:, 2:3], AF.Exp, [ss], [ss], scale=-0.5)
                        yield
                        ts("dve", hb[:], xt[:], ss[:, 3:4], None, ALU.mult, None, [xt, ss], [hb])
                        yield
                        for half in range(2):
                            p = PB
                            for k4 in range(4):
                                k = half * 4 + k4
                                mm(p[:, k4 * 128:(k4 + 1) * 128], hb[:, k * 128:(k + 1) * 128], identb[:], [hb, identb], [p],
                                   inc=(k4 == 3))
                            yield
                            for k4 in range(4):
                                k = half * 4 + k4
                                dsth = hT[:, k, i * 128:(i + 1) * 128]
                                srcp = p[:, k4 * 128:(k4 + 1) * 128]
                                if k4 % 2 == 0:
                                    act(dsth, srcp, AF.Identity, [p, g1c[s], shc[s]], [hT], scale=g1c[s][:, k:k + 1],
                                        bias=shc[s][:, k:k + 1])
                                else:
                                    ts("dve", dsth, srcp, g1c[s][:, k:k + 1], shc[s][:, k:k + 1], ALU.mult, ALU.add,
                                       [p, g1c[s], shc[s]], [hT])
                            yield
                    S.dma("pool", hT_d[blk], hT[:, :, :].rearrange("p k t -> p (k t)"), reads=[hT], writes=[hT_b[blk]])
                    pp = ppL if lat else ppC
                    R_, W_ = (4, 64) if lat else (1, 256)

                    def v3(ap):
                        return ap.rearrange("p (r w) -> p r w", r=R_)

                    hs = slice(0, 256)
                    for cbg in range(16):
                        p = PB
                        for k in range(8):
                            mm(p[:, hs], w_bf[:, k, cbg * 128:(cbg + 1) * 128], hT[:, k, :], [w_bf, hT], [p],
                               start=(k == 0), stop=(k == 7), inc=(k == 7))
                        q, cb = cbg // 4, cbg % 4
                        if q < 3:
                            cp("act", pp[:, :, 1:W_ + 1], v3(p[:, hs]), [p], [pp])
                            tt("dve", v3(nbt[:]), pp[:, :, 0:W_], pp[:, :, 2:W_ + 2], ALU.add, [pp], [nbt])
                            dst = RKV[q][cb]
                            act(v3(dst[:]), pp[:, :, 1:W_ + 1], AF.Identity, [pp, dcols], [dst], scale=dcol(q * 4 + cb))
                            stt(dst[:], nbt[:], dcol(12 + q * 4 + cb), dst[:], ALU.mult, ALU.add, [nbt, dcols, dst], [dst])
                            if q == 2:
                                cp("pool", vbf[cb][:], dst[:], [dst], [vbf[cb]])
                        else:
                            sigm(sgT[:, cb, :], p[:, hs], [p], [sgT])
                            tt("dve", sgT[:, cb, :], p[:, hs], sgT[:, cb, :], ALU.mult, [p, sgT], [sgT])
                        yield
                    S.dma("pool", sg_d[blk], sgT[:, :, :].rearrange("p a b -> p (a b)"), reads=[sgT], writes=[sg_b[blk]])
                    for mb in range(2):
                        p = PB
                        for k in range(8):
                            mm(p[:, hs], lora_bf[:, k, mb * 128:(mb + 1) * 128], hT[:, k, :], [lora_bf, hT], [p],
                               start=(k == 0), stop=(k == 7), inc=(k == 7))
                        if mb == 0:
                            tq = TMP["sq"]
                            act(tq[:], p[:, hs], AF.Exp, [p], [tq], scale=-2.0)
                            act(tq[:], tq[:], AF.Ln, [tq], [tq], bias=1.0)
                            act(tq[:], tq[:], AF.Exp, [tq], [tq], scale=-1.0)
                            ts("dve", lwd[:], tq[:], 2.0, -1.0, ALU.mult, ALU.add, [tq], [lwd])
                        else:
                            cp("dve", lwi[:], p[:, hs], [p], [lwi])
                        yield

                def cd_gen(blk):
                    par = blk % 2
                    p5 = PB

                    def c_kk(cb):
                        if False:
                            yield
                        k_ = RKV[1][cb]
                        T_ = TMPC[cb % 2]
                        act(T_["sq"][:], k_[:], AF.Square, [k_, cols], [T_["sq"]], scale=col(28 + cb))
                        mm(p5[:, 0:256], bones[:], T_["sq"][:], [bones, T_["sq"]], [p5])
                        ts("dve", T_["rn"][:], p5[:, 0:256], 1e-24, None, ALU.max, None, [p5], [T_["rn"]])
                        act(T_["rn"][:], T_["rn"][:], AF.Ln, [T_["rn"]], [T_["rn"]])
                        act(T_["rn"][:], T_["rn"][:], AF.Exp, [T_["rn"]], [T_["rn"]], scale=-0.5)
                        stt(T_["kk"][:], k_[:], col(28 + cb), T_["rn"][:], ALU.mult, ALU.mult, [k_, cols, T_["rn"]],
                            [T_["kk"]])

                    def c_front(cb, e):
                        TE = TMPE[e]
                        es_ = slice(e * 64, (e + 1) * 64)
                        pz = PB
                        mm(pz[:, 0:256], dup_bf[es_, cb * 128:(cb + 1) * 128], lwd[es_, :], [dup_bf, lwd], [pz])
                        mm(pz[:, 256:512], iup_bf[es_, cb * 128:(cb + 1) * 128], lwi[es_, :], [iup_bf, lwi], [pz])
                        sig, pi, px, tcl = TE["sig"], TE["pi"], TE["px"], TE["tcol"]
                        act(sig[:], pz[:, 0:256], AF.Exp, [pz, dcols], [sig], scale=-1.0, bias=dcol(28 + e * 4 + cb))
                        act(TE["a"][:], pz[:, 256:512], AF.Exp, [pz, dcols], [TE["a"]], scale=-1.0, bias=dcol(36 + e * 4 + cb))
                        act(sig[:], sig[:], AF.Ln, [sig], [sig], bias=1.0)
                        act(sig[:], sig[:], AF.Exp, [sig], [sig], scale=-1.0)
                        yield
                        for ck in range(2):
                            tc = slice(ck * 128, (ck + 1) * 128)
                            S.op("dve", lambda e_: e_.tensor_tensor_scan(out=pi[:, tc], data0=ones[:, 0:128],
                                                                         data1=sig[:, tc], initial=0.0,
                                                                         op0=ALU.mult, op1=ALU.add),
                                 reads=[ones, sig], writes=[pi])
                        tt("dve", px[:], pi[:], sig[:], ALU.subtract, [pi, sig], [px])
                        ts("dve", tcl[:, 0:2], pi[:, 127:256:128], -C0, None, ALU.mult, None, [pi], [tcl])
                        ts("dve", tcl[:, 2:4], pi[:, 127:256:128], C0, None, ALU.mult, None, [pi], [tcl])
                        yield
                        WL_ = WLc[cb % 2]
                        act(WL_[:, e * 2:e * 2 + 2], tcl[:, 0:2], AF.Exp, [tcl], [WL_])
                        E1, E2, E3, E4 = TE["E1"], TE["E2"], TE["E3"], TE["E4"]
                        if e == 0:
                            act(E1[:], pi[:], AF.Exp, [pi], [E1], scale=-C0)
                            act(E2[:], px[:], AF.Exp, [px], [E2], scale=-C0)
                            act(E3[:], pi[:], AF.Exp, [pi], [E3], scale=C0)
                            for ck in range(2):
                                tc = slice(ck * 128, (ck + 1) * 128)
                                act(E4[:, tc], pi[:, tc], AF.Exp, [pi, tcl], [E4], scale=C0, bias=tcl[:, ck:ck + 1])
                        else:
                            for ck in range(2):
                                tc = slice(ck * 128, (ck + 1) * 128)
                                act(E1[:, tc], px[:, tc], AF.Exp, [px, tcl], [E1], scale=C0, bias=tcl[:, ck:ck + 1])
                                act(E2[:, tc], pi[:, tc], AF.Exp, [pi, tcl], [E2], scale=C0, bias=tcl[:, ck:ck + 1])
                                act(E3[:, tc], px[:, tc], AF.Exp, [px, tcl], [E3], scale=-C0, bias=tcl[:, 2 + ck:3 + ck])
                            act(E4[:], px[:], AF.Exp, [px], [E4], scale=-C0)
                        yield
                        act(TE["a"][:], TE["a"][:], AF.Ln, [TE["a"]], [TE["a"]], bias=1.0)
                        act(TE["a"][:], TE["a"][:], AF.Exp, [TE["a"]], [TE["a"]], scale=-1.0)

                    def c_back(cb, e):
                        TE = TMPE[e]
                        T_ = TMPC[cb % 2]
                        r_, k_ = RKV[0][cb], RKV[1][cb]
                        E1, E2, E3, E4 = TE["E1"], TE["E2"], TE["E3"], TE["E4"]
                        a_, bq, kd, rkd = TE["a"], TE["bq"], TE["kd"], T_["rkd"]
                        tt("pool", bq[:], T_["kk"][:], a_[:], ALU.mult, [T_["kk"], a_], [bq])
                        ts("dve", kd[:], a_[:], col(32 + cb), dcol(24 + cb), ALU.mult, ALU.add, [a_, cols, dcols], [kd])
                        tt("dve", kd[:], kd[:], k_[:], ALU.mult, [kd, k_], [kd])
                        f = lambda n: FMt(n, par, e, cb)
                        tt("dve", f("Qt")[:], r_[:], E1[:], ALU.mult, [r_, E1], [f("Qt")])
                        tt("pool", f("KKt")[:], T_["kk"][:], E2[:], ALU.mult, [T_["kk"], E2], [f("KKt")])
                        tt("dve", f("Kt")[:], kd[:], E3[:], ALU.mult, [kd, E3], [f("Kt")])
                        yield
                        tt("pool", f("Bt")[:], bq[:], E3[:], ALU.mult, [bq, E3], [f("Bt")])
                        tt("dve", f("Kh")[:], kd[:], E4[:], ALU.mult, [kd, E4], [f("Kh")])
                        tt("pool", f("Bh")[:], bq[:], E4[:], ALU.mult, [bq, E4], [f("Bh")])
                        if e == 0:
                            tt("dve", rkd[:], r_[:], kd[:], ALU.mult, [r_, kd], [rkd])
                        else:
                            tt("dve", T_["sq"][:], r_[:], kd[:], ALU.mult, [r_, kd], [T_["sq"]])
                            stt(rkd[:], rkd[:], 1.0, T_["sq"][:], ALU.mult, ALU.add, [rkd, T_["sq"]], [rkd])

                    def c_tail(cb):
                        if False:
                            yield
                        T_ = TMPC[cb % 2]
                        v_ = RKV[2][cb]
                        rkd = T_["rkd"]
                        WL_ = WLc[cb % 2]
                        for hh in range(2):
                            mm(p5[0:64, 256 + hh * 4:256 + hh * 4 + 4], identf[:, hh * 64:(hh + 1) * 64], WL_[:, 0:4],
                               [identf, WL_], [p5])
                        for hh in range(2):
                            h = cb * 2 + hh
                            for e in range(2):
                                c0 = ((blk * 2 + e) * 8 + h) * 2
                                cp("dve", WLh[:, c0:c0 + 2], p5[0:64, 256 + hh * 4 + e * 2:256 + hh * 4 + e * 2 + 2], [p5], [WLh])
                        ts("dve", rkd[:], rkd[:], col(36 + cb), None, ALU.mult, None, [rkd, cols], [rkd])
                        mm(p5[:, 0:256], bones[:], rkd[:], [bones, rkd], [p5])
                        tt("dve", bvT[:, cb, :], p5[:, 0:256], v_[:], ALU.mult, [p5, v_], [bvT])

                    for cb in range(4):
                        yield from c_kk(cb)
                        yield
                        for e in range(2):
                            yield from c_front(cb, e)
                            yield
                            yield from c_back(cb, e)
                            yield
                        yield from c_tail(cb)
                        yield
                    S.dma("pool", bv_d[blk], bvT[:, :, :].rearrange("p a b -> p (a b)"), reads=[bvT], writes=[bv_b[blk]])

                    for ck in range(2):
                        tc = slice(ck * 128, (ck + 1) * 128)
                        jobs = [(Vtm[par][ck], vbf)] + [(Khtm[par][ck][e], FMs["Kh"][e]) for e in range(2)] + \
                               [(Bhtm[par][ck][e], FMs["Bh"][e]) for e in range(2)]
                        for ji, (dst, src) in enumerate(jobs):
                            p = PB
                            for cb in range(4):
                                mm(p[:, cb * 128:(cb + 1) * 128], src[cb][:, tc], identb[:], [src[cb], identb], [p],
                                   inc=(cb == 3))
                            cp("act" if ji % 2 == 0 else "dve", dst[:], p[:, :], [p], [dst])
                            yield


                def abcd_gen(blk):
                    yield from ab_gen(blk)
                    yield from cd_gen(blk)

                bg = {}

                def pump():
                    g = bg.get("g")
                    if g is not None:
                        try:
                            next(g)
                        except StopIteration:
                            bg["g"] = None

                def phase1_block(blk):
                    lat = blk < NLB
                    s = 0 if lat else 1
                    tok0 = blk * 256
                    if blk == 0:
                        bg["g"] = abcd_gen(0)
                    while bg.get("g") is not None:
                        pump()
                    ckpt(5)
                    if blk + 1 < NB and BG_INJECT:
                        bg["g"] = abcd_gen(blk + 1)
                    npump = [0]
                    par = blk % 2
                    ckpt(8)
                    for ck in range(2):
                        chunk = blk * 2 + ck
                        tc = slice(ck * 128, (ck + 1) * 128)
                        yps = PS[6]
                        for hg in range(2):
                            def fm(n, e, hq):
                                return FM4[n][par][e][hq][hg * 64:(hg + 1) * 64, tc]

                            def fmR(n, e, hq):
                                return [FM4[n][par][e][hq]]

                            def chain(e):
                                bA, bB, zps = PS[3 * e], PS[3 * e + 1], PS[3 * e + 2]
                                if e == 0:
                                    mSTn, mSn, mST, mIT = MK["NLT"], MK["NGT"], MK["LT"], MK["LE"]
                                else:
                                    mSTn, mSn, mST, mIT = MK["NGT"], MK["NLT"], MK["GT"], MK["GE"]
                                hsl = lambda hq: slice(hq * 128, (hq + 1) * 128)
                                for hq in range(4):
                                    mm(bA[:, hsl(hq)], fm("Bt", e, hq), fm("KKt", e, hq), fmR("Bt", e, hq) + fmR("KKt", e, hq),
                                       [bA], inc=(hq == 3))
                                tt("dve", XT[e][0][:], bA[:, :], mSTn[:], ALU.mult, [bA, mSTn], [XT[e][0]])
                                for hq in range(4):
                                    mm(bB[:, hsl(hq)], fm("KKt", e, hq), fm("Bt", e, hq), fmR("Bt", e, hq) + fmR("KKt", e, hq),
                                       [bB], inc=(hq == 3))
                                tt("dve", XM[e][0][:], bB[:, :], mSn[:], ALU.mult, [bB, mSn], [XM[e][0]])
                                yield
                                for hq in range(4):
                                    mm(bA[:, hsl(hq)], fm("Kt", e, hq), fm("KKt", e, hq), fmR("Kt", e, hq) + fmR("KKt", e, hq),
                                       [bA], inc=(hq == 3))
                                tt("dve", AakT[e][:], bA[:, :], mST[:], ALU.mult, [bA, mST], [AakT[e]])
                                for hq in range(4):
                                    h = hq * 2 + hg
                                    hh = hg
                                    mm(zps[:, hq * 128:hq * 128 + 64], AakT[e][:, hsl(hq)], Vtm[par][ck][:, h * 64:(h + 1) * 64],
                                       [AakT[e], Vtm[par][ck]], [zps], start=(hq == 0), stop=False, inc=False)
                                    mm(zps[:, hq * 128 + 64:(hq + 1) * 128], fm("KKt", e, hq),
                                       identb[hh * 64:(hh + 1) * 64, hh * 64:(hh + 1) * 64], fmR("KKt", e, hq) + [identb], [zps],
                                       start=False, stop=False, inc=(hq == 3))
                                cp("act", Zb[e][:], zps[:, :], [zps], [Zb[e]])
                                yield
                                for hq in range(4):
                                    mm(bB[:, hsl(hq)], fm("Bt", e, hq), fm("Qt", e, hq), fmR("Bt", e, hq) + fmR("Qt", e, hq),
                                       [bB], inc=(hq == 3))
                                cp("act", AqbT[e][:], bB[:, :], [bB], [AqbT[e]])
                                tt("pool", AqbT[e][:], AqbT[e][:], mIT[:], ALU.mult, [AqbT[e], mIT], [AqbT[e]])
                                for hq in range(4):
                                    mm(bA[:, hsl(hq)], fm("Kt", e, hq), fm("Qt", e, hq), fmR("Kt", e, hq) + fmR("Qt", e, hq),
                                       [bA], inc=(hq == 3))
                                cp("act", AqkT[e][:], bA[:, :], [bA], [AqkT[e]])
                                tt("pool", AqkT[e][:], AqkT[e][:], mIT[:], ALU.mult, [AqkT[e], mIT], [AqkT[e]])
                                yield
                                def xtv(i, lev_):
                                    if lev_ < KCUT:
                                        return XT[e][i][:, :], XM[e][i][:, :]
                                    return XT[e][i][:, :].bitcast(BF16)[:, 0:512], XM[e][i][:, :].bitcast(BF16)[:, 0:512]

                                for lev in range(7):
                                    cur, nxt = lev % 2, (lev + 1) % 2
                                    xt_c, xm_c = xtv(cur, lev)
                                    zsrc = Zb[e] if lev < KCUT else AakT[e]
                                    for hq in range(4):
                                        mm(zps[:, hsl(hq)], xt_c[:, hsl(hq)], zsrc[:, hsl(hq)], [XT[e][cur], zsrc], [zps],
                                           start=False, stop=(lev == 6), inc=(hq == 3))
                                    if lev < 6:
                                        xt_n, xm_n = xtv(nxt, lev + 1)
                                        for hq in range(4):
                                            mm(bA[:, hsl(hq)], xm_c[:, hsl(hq)], xt_c[:, hsl(hq)],
                                               [XM[e][cur], XT[e][cur]], [bA], inc=(hq == 3))
                                        cp("dve", xt_n, bA[:, :], [bA], [XT[e][nxt]])
                                        if lev < 5:
                                            for hq in range(4):
                                                mm(bB[:, hsl(hq)], xt_c[:, hsl(hq)], xm_c[:, hsl(hq)],
                                                   [XM[e][cur], XT[e][cur]], [bB], inc=(hq == 3))
                                            cp("act", xm_n, bB[:, :], [bB], [XM[e][nxt]])
                                        zdst = Zb[e] if lev + 1 < KCUT else AakT[e]
                                        cp("act", zdst[:], zps[:, :], [zps], [zdst])
                                    yield
                                act(UGn[e][:], zps[:, :], AF.Identity, [zps], [UGn[e]], scale=-1.0)
                                for hq in range(4):
                                    h = hq * 2 + hg
                                    hh = hg
                                    hc = slice(h * 64, (h + 1) * 64)
                                    mm(yps[:, hc], AqbT[e][:, hsl(hq)], UGn[e][:, hq * 128:hq * 128 + 64], [AqbT[e], UGn[e]],
                                       [yps], start=(e == 0 and hg == 0 and hq == 0), stop=False, inc=False)
                                    mm(yps[:, hc], AqkT[e][:, hsl(hq)], Vtm[par][ck][:, hc], [AqkT[e], Vtm[par][ck]], [yps],
                                       start=False, stop=(e == 1), inc=(hq == 3))
                                for hq in range(4):
                                    hh = hg
                                    mm(bA[0:64, hsl(hq)], identb[hh * 64:(hh + 1) * 64, hh * 64:(hh + 1) * 64], fm("Qt", e, hq),
                                       fmR("Qt", e, hq) + [identb], [bA], start=True, stop=False, inc=False)
                                    mm(bA[0:64, hsl(hq)], UGn[e][:, hq * 128 + 64:(hq + 1) * 128], AqbT[e][:, hsl(hq)],
                                       [UGn[e], AqbT[e]], [bA], start=False, stop=True, inc=(hq == 3))
                                cp("act", Qht[:, :].rearrange("p (e q g t) -> p e q g t", e=2, q=4, g=2)[:, e, :, hg, :],
                                   bA[0:64, :].rearrange("p (q t) -> p q t", q=4), [bA], [Qht])
                                for hq in range(4):
                                    h = hq * 2 + hg
                                    hc = slice(h * 64, (h + 1) * 64)
                                    mm(bB[0:64, hq * 64:(hq + 1) * 64], UGn[e][:, hq * 128 + 64:(hq + 1) * 128], Bhtm[par][ck][e][:, hc],
                                       [UGn[e], Bhtm[par][ck][e]], [bB], inc=False)
                                    mm(bB[0:64, 256 + hq * 64:256 + (hq + 1) * 64], Bhtm[par][ck][e][:, hc],
                                       UGn[e][:, hq * 128:hq * 128 + 64], [UGn[e], Bhtm[par][ck][e]], [bB], start=True, stop=False,
                                       inc=False)
                                    mm(bB[0:64, 256 + hq * 64:256 + (hq + 1) * 64], Khtm[par][ck][e][:, hc], Vtm[par][ck][:, hc],
                                       [Khtm[par][ck][e], Vtm[par][ck]], [bB], start=False, stop=True, inc=(hq == 3))
                                cp("dve", Xst[:, :].rearrange("p (e q g c) -> p e q g c", e=2, q=4, g=2)[:, e, :, hg, :],
                                   bB[0:64, 0:256].rearrange("p (q c) -> p q c", q=4), [bB], [Xst])
                                cp("dve", Dst[:, :].rearrange("p (e q g c) -> p e q g c", e=2, q=4, g=2)[:, e, :, hg, :],
                                   bB[0:64, 256:512].rearrange("p (q c) -> p q c", q=4), [bB], [Dst])
                                yield

                            gens = [chain(0), chain(1)]
                            alive = [True, True]
                            while any(alive):
                                for gi in range(2):
                                    if alive[gi]:
                                        try:
                                            next(gens[gi])
                                        except StopIteration:
                                            alive[gi] = False
                                        npump[0] += 1
                                        if npump[0] % PUMP_EVERY == 0:
                                            pump()
                        cp("act", Ylt[:], yps[:, :], [yps], [Ylt])
                        S.dma("pool", Yl_d[chunk], Ylt[:], reads=[Ylt], writes=[Yl_b[chunk]])
                        S.dma("pool", Qh_d[chunk], Qht[:], reads=[Qht], writes=[Qh_b[chunk]])
                        S.dma("pool", X_d[chunk], Xst[:], reads=[Xst], writes=[X_b[chunk]])
                        S.dma("pool", D_d[chunk], Dst[:], reads=[Dst], writes=[D_b[chunk]])

                for blk in range(NB):
                    phase1_block(blk)
                    if blk + 1 < NB and not BG_INJECT:
                        bg["g"] = abcd_gen(blk + 1)
                    ckpt(9)

                e1b.close()
                S.barrier()
                ckpt(10)
                Sf = sbt(e1, "Sf", [64, 16, 64], F32)
                Sb = sbt(e1, "Sb", [64, 16, 64], BF16)
                Xl = [sbt(e1, "Xl%d" % i, [64, 16, 64], BF16) for i in range(2)]
                Dl = [sbt(e1, "Dl%d" % i, [64, 16, 64], F32) for i in range(2)]
                Sfin = sbt(e1, "Sfin", [64, 16, 64], F32)
                SfB = [Buf() for _ in range(16)]

                def flat(t_, a=None, b=None):
                    ap = t_[:, :, :] if a is None else t_[:, a:b, :]
                    return ap.rearrange("p a b -> p (a b)")

                def run_seq(chunks, init_from_state, seq_out):
                    n = len(chunks)
                    if init_from_state:
                        cp("dve", flat(Sf), flat(Sf0), [Sf0], SfB)
                    else:
                        S.op("dve", lambda e: e.memset(flat(Sf), 0.0), writes=SfB)
                    cp("dve", flat(Sb), flat(Sf), SfB, [Sb])
                    for st in range(n):
                        cf, cbw = chunks[st], chunks[n - 1 - st]
                        S.dma("pool", S_d[cf][:, 0:512], flat(Sb, 0, 8), reads=[Sb], writes=[S_b[cf][0]])
                        S.dma("pool", S_d[cbw][:, 512:1024], flat(Sb, 8, 16), reads=[Sb], writes=[S_b[cbw][1]])
                        xl, dl = Xl[st % 2], Dl[st % 2]
                        S.dma("sp", flat(xl, 0, 8), X_d[cf][:, 0:512], reads=[X_b[cf]], writes=[xl])
                        S.dma("sp", flat(xl, 8, 16), X_d[cbw][:, 512:1024], reads=[X_b[cbw]], writes=[xl])
                        S.dma("sp", flat(dl, 0, 8), D_d[cf][:, 0:512], reads=[D_b[cf]], writes=[dl])
                        S.dma("sp", flat(dl, 8, 16), D_d[cbw][:, 512:1024], reads=[D_b[cbw]], writes=[dl])
                        for hd in range(16):
                            p = PS[hd // 8]
                            mm(p[0:64, (hd % 8) * 64:(hd % 8 + 1) * 64], xl[:, hd, :], Sb[:, hd, :], [xl, Sb], [p],
                               inc=(hd % 8 == 7))
                        for hd in range(16):
                            e, h = hd // 8, hd % 8
                            cch = cf if e == 0 else cbw
                            blk_, ck_ = cch // 2, cch % 2
                            c0 = ((blk_ * 2 + e) * 8 + h) * 2 + ck_
                            p = PS[hd // 8]
                            stt(Sf[:, hd, :], Sf[:, hd, :], WLh[:, c0:c0 + 1], p[0:64, (hd % 8) * 64:(hd % 8 + 1) * 64],
                                ALU.mult, ALU.add, [SfB[hd], WLh, p], [SfB[hd]])
                        tt("dve", flat(Sf), flat(Sf), flat(dl), ALU.add, SfB + [dl], SfB)
                        cp("act", flat(Sb), flat(Sf), SfB, [Sb])
                    if seq_out is not None:
                        for hd in range(16):
                            p = PS[2 + hd // 8]
                            mm(p[0:64, (hd % 8) * 64:(hd % 8 + 1) * 64], Sf[:, hd, :], identf[0:64, 0:64], [SfB[hd], identf], [p],
                               inc=(hd % 8 == 7))
                        for hf in range(2):
                            cp("dve", flat(Sfin, hf * 8, hf * 8 + 8), PS[2 + hf][0:64, :], [PS[2 + hf]], [Sfin])
                        S.dma("pool", so[seq_out], Sfin[:, :, :], reads=[Sfin])

                if NLB > 0:
                    run_seq(list(range(0, 2 * NLB)), True, None)
                for cs in range(NCS):
                    b0 = 2 * (NLB + cs)
                    run_seq([b0, b0 + 1], False, cs)

            ckpt(11)
            S.barrier()
            with ExitStack() as e3:
                wo_bf = sbt(e3, "wo_bf", [128, 8, 1024], BF16)
                FG = sbt(e3, "FG", [128, 1024], F32)
                stg3 = [sbt(e3, "stg3_%d" % i, [128, 2048], F32) for i in range(2)]
                for k in range(8):
                    g = stg3[k % 2]
                    S.dma("sp", g[:], w_in[:, k, 2048:4096], writes=[g])
                    cp("act" if k % 2 == 0 else "dve", w_bf[:, k, :], g[:], [g], [w_bf])
                for k in range(8):
                    g = stg3[k % 2]
                    S.dma("sp", g[:, 0:1024], w_out[:, k, :], writes=[g])
                    cp("act" if k % 2 == 0 else "dve", wo_bf[:, k, :], g[:, 0:1024], [g], [wo_bf])
                S.dma("sp", FG[:], fg_bc, writes=[FG])
                GATE = [sbt(e3, "GATE_%d" % s, [128, 1024], F32) for s in range(2)]
                modg = sbt(e3, "modg", [2, 1024], F32)
                sel3 = [sbt(e3, "sel3_%d" % s, [2, 128], F32) for s in range(2)]
                S.dma("sp", modg[:], mod_d, reads=[mod_b], writes=[modg])
                for s in range(2):
                    ts("dve", sel3[s][:], ones[0:2, :], identf[0:2, s:s + 1], None, ALU.mult, None, [ones, identf], [sel3[s]])
                    for n in range(2):
                        p = PS[s * 2 + n]
                        mm(p[:, :], sel3[s][:], modg[:, n * 512:(n + 1) * 512], [sel3[s], modg], [p])
                        cp("act", GATE[s][:, n * 512:(n + 1) * 512], p[:, :], [p], [GATE[s]])
                hTw2 = [sbt(e3, "hTw%d" % i, [128, 8, 384], BF16) for i in range(2)]
                x3 = [sbt(e3, "x3_%d" % i, [128, 1024], F32) for i in range(2)]
                Gt = [sbt(e3, "Gt%d" % cb, [128, 256], F32) for cb in range(4)]
                tmpc = sbt(e3, "tmpc", [128, 384], F32)
                cuA = sbt(e3, "cuA", [128, 4, 66], F32)
                cuB = sbt(e3, "cuB", [128, 384], F32)
                cuC = sbt(e3, "cuC", [128, 258], F32)
                cacc = sbt(e3, "cacc", [128, 256], F32)
                catT = sbt(e3, "catT", [128, 8, 256], BF16)
                Qhl2 = [sbt(e3, "Qhl%d" % i, [64, 2048], BF16) for i in range(2)]
                Sl2 = [sbt(e3, "Sl%d" % i, [64, 1024], BF16) for i in range(2)]
                Yll2 = [sbt(e3, "Yll%d" % i, [128, 512], F32) for i in range(2)]
                Yt2 = [sbt(e3, "Yt%d" % i, [128, 512], F32) for i in range(2)]
                gnY2 = [sbt(e3, "gnY%d" % i, [128, 512], BF16) for i in range(2)]
                bst2 = [sbt(e3, "bst%d" % i, [128, 8, 6], F32) for i in range(2)]
                mv2 = [sbt(e3, "mv%d" % i, [128, 8, 2], F32) for i in range(2)]
                rs2 = [sbt(e3, "rs%d" % i, [128, 8], F32) for i in range(2)]
                yat4 = [[sbt(e3, "yat%d%d" % (i, cb), [128, 128], F32) for cb in range(4)] for i in range(2)]
                catB = [Buf(), Buf()]
                catTB2 = [sbt(e3, "catTB%d" % i, [128, 4, 256], BF16) for i in range(2)]
                sgl2 = [sbt(e3, "sgl%d" % i, [128, 4, 256], F32) for i in range(2)]
                bvl2 = [sbt(e3, "bvl%d" % i, [128, 4, 256], F32) for i in range(2)]
                yo2 = [sbt(e3, "yo%d" % i, [128, 1024], F32) for i in range(2)]
                junk32 = [sbt(e3, "junk3_%d" % i, [128, 1024], BF16) for i in range(2)]
                ss32 = [sbt(e3, "ss3_%d" % i, [128, 4], F32) for i in range(2)]
                S.op("dve", lambda e: e.memset(cuA[:, :, :].rearrange("p a b -> p (a b)"), 0.0), writes=[cuA])
                S.op("dve", lambda e: e.memset(cuC[:], 0.0), writes=[cuC])

                def hflat(a, b):
                    return hTw[:, :, a:b]

                def b_gen(blk):
                    hTw, sgl, bvl = hTw2[blk % 2], sgl2[blk % 2], bvl2[blk % 2]
                    catTB = catTB2[blk % 2]
                    lat = blk < NLB
                    s = 0 if lat else 1
                    tok0 = blk * 256
                    hv = lambda b_: hT_d[b_].rearrange("p (k t) -> p k t", k=8)
                    if lat:
                        if blk == 0:
                            S.op("dve", lambda e: e.memset(hTw[:, :, 0:64], 0.0), writes=[hTw])
                        else:
                            S.dma("sp", hTw[:, :, 0:64], hv(blk - 1)[:, :, 192:256], reads=[hT_b[blk - 1]], writes=[hTw])
                        if blk == NLB - 1:
                            S.op("dve", lambda e: e.memset(hTw[:, :, 320:384], 0.0), writes=[hTw])
                        else:
                            S.dma("sp", hTw[:, :, 320:384], hv(blk + 1)[:, :, 0:64], reads=[hT_b[blk + 1]], writes=[hTw])
                    S.dma("sp", hTw[:, :, 64:320], hv(blk), reads=[hT_b[blk]], writes=[hTw])
                    S.dma("sp", sgl[:, :, :].rearrange("p a b -> p (a b)"), sg_d[blk], reads=[sg_b[blk]], writes=[sgl])
                    S.dma("sp", bvl[:, :, :].rearrange("p a b -> p (a b)"), bv_d[blk], reads=[bv_b[blk]], writes=[bvl])
                    for cb in range(4):
                        pb, pg = PS[4], PS[5]
                        hs = slice(0, 256)
                        for k in range(8):
                            mm(pb[:, hs], w_bf[:, k, cb * 128:(cb + 1) * 128], hTw[:, k, 64:320], [w_bf, hTw], [pb],
                               start=(k == 0), stop=(k == 7), inc=(k == 7))
                        for k in range(8):
                            mm(pg[:, hs], w_bf[:, k, (12 + cb) * 128:(13 + cb) * 128], hTw[:, k, 64:320], [w_bf, hTw], [pg],
                               start=(k == 0), stop=(k == 7), inc=(k == 7))
                        yield
                        sigm(tmpc[:, 0:256], pg[:, hs], [pg], [tmpc])
                        yield
                        tt("dve", tmpc[:, 0:256], pg[:, hs], tmpc[:, 0:256], ALU.mult, [pg, tmpc], [tmpc])
                        tt("dve", Gt[cb][:], pb[:, hs], tmpc[:, 0:256], ALU.mult, [pb, tmpc], [Gt[cb]])
                        yield
                    for cb in range(4):
                        pc, pu = PS[6], PS[7]
                        wide = lat and cb >= 2
                        n0, n1 = (0, 384) if wide else (64, 320)
                        N = n1 - n0
                        for k in range(8):
                            mm(pc[:, 0:N], w_bf[:, k, (4 + cb) * 128:(5 + cb) * 128], hTw[:, k, n0:n1], [w_bf, hTw], [pc],
                               start=(k == 0), stop=(k == 7), inc=(k == 7))
                        for k in range(8):
                            mm(pu[:, 0:N], w_bf[:, k, (8 + cb) * 128:(9 + cb) * 128], hTw[:, k, n0:n1], [w_bf, hTw], [pu],
                               start=(k == 0), stop=(k == 7), inc=(k == 7))
                        yield
                        cp("act", tmpc[:, 0:N], pc[:, 0:N], [pc], [tmpc])
                        yield
                        cw = [col(48 + j * 4 + cb) for j in range(3)]
                        if not lat:
                            tt("dve", cuC[:, 1:257], tmpc[:, 0:256], pu[:, 0:256], ALU.mult, [tmpc, pu], [cuC])
                            prev, ctr, nxt, cub = cuC[:, 0:256], cuC[:, 1:257], cuC[:, 2:258], cuC
                            accv = cacc[:]
                        elif wide:
                            tt("dve", cuB[:], tmpc[:, 0:384], pu[:, 0:384], ALU.mult, [tmpc, pu], [cuB])
                            prev, ctr, nxt, cub = cuB[:, 0:256], cuB[:, 64:320], cuB[:, 128:384], cuB
                            accv = cacc[:]
                        else:
                            tt("dve", cuA[:, :, 1:65], tmpc[:, 0:256].rearrange("p (r w) -> p r w", r=4),
                               pu[:, 0:256].rearrange("p (r w) -> p r w", r=4), ALU.mult, [tmpc, pu], [cuA])
                            prev, ctr, nxt, cub = cuA[:, :, 0:64], cuA[:, :, 1:65], cuA[:, :, 2:66], cuA
                            accv = cacc[:].rearrange("p (r w) -> p r w", r=4)
                        yield
                        ts("dve", accv, ctr, cw[1], None, ALU.mult, None, [cub, cols], [cacc])
                        stt(accv, prev, cw[0], accv, ALU.mult, ALU.add, [cub, cols, cacc], [cacc])
                        stt(accv, nxt, cw[2], accv, ALU.mult, ALU.add, [cub, cols, cacc], [cacc])
                        yield
                        tt("pool", catTB[:, cb, :], cacc[:], Gt[cb][:], ALU.mult, [cacc, Gt[cb]], [catTB])
                    yield

                def a_part(blk, bgen):
                    hTw, sgl, bvl = hTw2[blk % 2], sgl2[blk % 2], bvl2[blk % 2]
                    catTB = catTB2[blk % 2]
                    lat = blk < NLB
                    s = 0 if lat else 1
                    tok0 = blk * 256
                    for ck in range(2):
                        chunk = blk * 2 + ck
                        S.dma("sp", Qhl2[ck][:], Qh_d[chunk], reads=[Qh_b[chunk]], writes=[Qhl2[ck]])
                        S.dma("sp", Sl2[ck][:], S_d[chunk], reads=[S_b[chunk][0], S_b[chunk][1]], writes=[Sl2[ck]])
                        S.dma("sp", Yll2[ck][:], Yl_d[chunk], reads=[Yl_b[chunk]], writes=[Yll2[ck]])
                        S.dma("sp", x3[ck][:], xs[tok0 + ck * 128:tok0 + (ck + 1) * 128, :], writes=[x3[ck]])

                    def a_tile(ck):
                        tc = slice(ck * 128, (ck + 1) * 128)
                        Qhl, Sl, Yll, xt = Qhl2[ck], Sl2[ck], Yll2[ck], x3[ck]
                        Yt, gnY, bst, mv, rs, yo, junk3, ss3 = Yt2[ck], gnY2[ck], bst2[ck], mv2[ck], rs2[ck], yo2[ck], junk32[ck], ss32[ck]
                        yp, pt, po2 = (PS[0], PS[0], [PS[1], PS[1]]) if ck == 0 else (PS[2], PS[2], [PS[3], PS[3]])
                        for h in range(8):
                            for e in range(2):
                                mm(yp[:, h * 64:(h + 1) * 64], Qhl[:, (e * 8 + h) * 128:(e * 8 + h + 1) * 128],
                                   Sl[:, (e * 8 + h) * 64:(e * 8 + h + 1) * 64], [Qhl, Sl], [yp], start=(e == 0), stop=(e == 1),
                                   inc=(h == 7 and e == 1))
                        yield
                        tt("dve", Yt[:], yp[:, :], Yll[:], ALU.add, [yp, Yll], [Yt])
                        for h in range(8):
                            S.op("dve", lambda e_: e_.bn_stats(out=bst[:, h, :], in_=Yt[:, h * 64:(h + 1) * 64]), reads=[Yt],
                                 writes=[bst])
                        for h in range(8):
                            S.op("dve", lambda e_: e_.bn_aggr(out=mv[:, h, :], in_=bst[:, h, :]), reads=[bst], writes=[mv])
                        yield
                        act(rs[:], mv[:, :, 1], AF.Ln, [mv], [rs], bias=GN_EPS)
                        act(rs[:], rs[:], AF.Exp, [rs], [rs], scale=-0.5)
                        yield
                        for h in range(8):
                            ts("dve", gnY[:, h * 64:(h + 1) * 64], Yt[:, h * 64:(h + 1) * 64], mv[:, h, 0:1], rs[:, h:h + 1],
                               ALU.subtract, ALU.mult, [Yt, mv, rs], [gnY])
                        yield
                        for cb in range(4):
                            mm(pt[:, cb * 128:(cb + 1) * 128], gnY[:, cb * 128:(cb + 1) * 128], identb[:], [gnY, identb], [pt],
                               inc=(cb == 3))
                        yield
                        for cb in range(4):
                            yat = yat4[ck][cb]
                            ts("dve", yat[:], pt[:, cb * 128:(cb + 1) * 128], col(40 + cb), col(44 + cb), ALU.mult, ALU.add,
                               [pt, cols], [yat])
                            tt("pool", yat[:], yat[:], bvl[:, cb, tc], ALU.add, [yat, bvl], [yat])
                            tt("pool", catT[:, cb, tc], yat[:], sgl[:, cb, tc], ALU.mult, [yat, sgl], [catB[ck]])
                        yield
                        for n in range(2):
                            po = po2[n]
                            hs = slice(n * 512, (n + 1) * 512)
                            for m in range(8):
                                lhs = catT[:, m, tc] if m < 4 else catTB[:, m - 4, tc]
                                mm(po[:, :], lhs, wo_bf[:, m, n * 512:(n + 1) * 512], [catB[ck], catTB, wo_bf], [po],
                                   start=(m == 0), stop=(m == 7), inc=(m == 7))
                            yield
                            tt("dve", yo[:, hs], po[:, :], GATE[s][:, hs], ALU.mult, [po, GATE[s]], [yo])
                            yield
                        tt("dve", yo[:], yo[:], xt[:], ALU.add, [yo, xt], [yo])
                        yield
                        act(junk3[:], yo[:], AF.Square, [yo], [junk3, ss3], accum=ss3[:, 0:1])
                        yield
                        act(ss3[:, 2:3], ss3[:, 0:1], AF.Ln, [ss3], [ss3], scale=1.0 / 1024, bias=NORM_EPS)
                        act(ss3[:, 3:4], ss3[:, 2:3], AF.Exp, [ss3], [ss3], scale=-0.5)
                        yield
                        stt(yo[:], yo[:], ss3[:, 3:4], FG[:], ALU.mult, ALU.mult, [yo, ss3, FG], [yo])
                        S.dma("pool", ys[tok0 + ck * 128:tok0 + (ck + 1) * 128, :], yo[:], reads=[yo])

                    gens3 = [a_tile(0), a_tile(1)]
                    alive3 = [True, True]
                    while any(alive3):
                        for gi in range(2):
                            if alive3[gi]:
                                try:
                                    next(gens3[gi])
                                except StopIteration:
                                    alive3[gi] = False
                                if bgen is not None:
                                    try:
                                        next(bgen)
                                    except StopIteration:
                                        bgen = None
                    if bgen is not None:
                        for _ in bgen:
                            pass

                for _ in b_gen(0):
                    pass
                for blk in range(NB):
                    a_part(blk, b_gen(blk + 1) if blk + 1 < NB else None)
                    ckpt(12)
        except _Stop:
            pass
        S.off = False
        S.finish("sp")
        S.finish("pool")
    nc._marks = SS[0].marks
    return nc


def _prep_shared(inp):
    f = np.float32
    d = {}
    d["w_ada"] = np.ascontiguousarray(inp["w_ada"][0].reshape(8, 128, 3072).transpose(1, 0, 2), dtype=f)
    d["b_ada2"] = np.ascontiguousarray(np.stack([inp["b_ada"][0], inp["b_ada"][0]], 0), dtype=f)
    d["ngc"] = np.ascontiguousarray(np.asarray(inp["norm_g"][0], dtype=f).reshape(8, 128).T)
    d["fg_bc"] = np.ascontiguousarray(np.broadcast_to(inp["final_g"][None, :], (128, 1024)), dtype=f)
    d["w_in"] = np.ascontiguousarray(inp["w_in"][0].reshape(8, 128, 4096).transpose(1, 0, 2), dtype=f)
    ld = np.concatenate([inp["decay_down"][0, 0], inp["decay_down"][0, 1], inp["iclr_down"][0, 0], inp["iclr_down"][0, 1]],
                        axis=1)
    d["lora_dn"] = np.ascontiguousarray(ld.reshape(8, 128, 256).transpose(1, 0, 2), dtype=f)
    d["dec_up"] = np.ascontiguousarray(inp["decay_up"][0].reshape(128, 512), dtype=f)
    d["icl_up"] = np.ascontiguousarray(inp["iclr_up"][0].reshape(128, 512), dtype=f)

    def c4(v):
        return np.asarray(v, dtype=f).reshape(4, 128).T

    cl = [c4(inp["shift_mu"][0, q]) for q in range(3)]
    cl += [c4(inp["decay_w0"][0, e]) for e in range(2)]
    cl += [c4(inp["iclr_bias"][0, e]) for e in range(2)]
    cl += [c4(inp["kk_scale"][0]), c4(inp["ka_scale"][0]), c4(inp["bonus_rk"][0]), c4(inp["gn_w"][0]), c4(inp["gn_b"][0])]
    cl += [c4(inp["conv_w"][0, j]) for j in range(3)]
    d["cols"] = np.ascontiguousarray(np.concatenate(cl, axis=1), dtype=f)
    d["w_out"] = np.ascontiguousarray(inp["w_out"][0].reshape(8, 128, 1024).transpose(1, 0, 2), dtype=f)
    return d


def _core_inputs(shared, x_lat, x_ctx, c_lat, c_ctx, st):
    f = np.float32
    m = dict(shared)
    parts = []
    if x_lat is not None:
        parts.append(np.asarray(x_lat, dtype=f).reshape(-1, 1024))
    if x_ctx is not None and len(x_ctx):
        parts.append(np.asarray(x_ctx, dtype=f).reshape(-1, 1024))
    m["xs"] = np.ascontiguousarray(np.concatenate(parts, 0))
    cv = np.stack([np.asarray(c_lat, dtype=f), np.asarray(c_ctx, dtype=f)], 0)
    m["cT"] = np.ascontiguousarray(cv.reshape(2, 8, 128).transpose(2, 1, 0).reshape(128, 16))
    m["st0"] = np.ascontiguousarray(np.asarray(st, dtype=f).transpose(2, 0, 1, 3).reshape(64, 16, 64))
    return m


_PROG = {}


def kernel(**inputs):
    inp = {k: np.asarray(v) for k, v in inputs.items()}
    NCORES = 8
    NLB, NCS = 16, 4
    shared = _prep_shared(inp)
    in_maps = []
    for b in range(NCORES):
        in_maps.append(_core_inputs(shared, inp["x_sample"][b], inp["x_prompt"][4 * b:4 * b + 4], inp["c"][b], inp["c_ctx"],
                                    inp["state_wkv"][b, 0]))
    key = (NLB, NCS)
    if key not in _PROG:
        _PROG[key] = build_program(NLB, NCS)
    res = run_bass_kernel_spmd(_PROG[key], in_maps, core_ids=list(range(NCORES)))
    y_prompt = np.zeros((32, 256, 1024), np.float32)
    y_sample = np.zeros((8, 4096, 1024), np.float32)
    new_state = np.zeros((32, 1, 2, 8, 64, 64), np.float32)
    for b in range(NCORES):
        r = res.results[b]
        ysb = np.asarray(r["ys"])
        y_sample[b] = ysb[:4096]
        y_prompt[4 * b:4 * b + 4] = ysb[4096:].reshape(4, 256, 1024)
        sob = np.asarray(r["so"]).reshape(4, 64, 2, 8, 64)
        new_state[4 * b:4 * b + 4, 0] = sob.transpose(0, 2, 3, 1, 4)
    return (y_prompt, y_sample, new_state)
```
